# Optimizing a Trainium2 kernel written in Bass

```python
import math
import jax
import jax.numpy as jnp
from jax import lax
import numpy as np

D_MODEL = 1024
BATCH = 8
SEQ = 4096
DEPTH = 2

GRID_W = 64
CTX_LEN = 256
ROPE_THETA = 10000.0
Q_BLOCK = 128
NORM_EPS = 1e-6
L2_EPS = 1e-6

DIFF_HEADS = 4
DIFF_QK_DIM = 32
DIFF_V_DIM = 2 * DIFF_QK_DIM
DIFF_WIDTH = DIFF_HEADS * DIFF_V_DIM
GQA_Q_HEADS = 6
GQA_KV_HEADS = 2
GQA_HEAD_DIM = 64
GQA_WIDTH = GQA_Q_HEADS * GQA_HEAD_DIM
GDN_HEADS = 6
GDN_HEAD_DIM = 64
GDN_WIDTH = GDN_HEADS * GDN_HEAD_DIM
GDN_CONV = 5
GDN_CHUNK = 64

MIX_WIDTH = DIFF_WIDTH + GQA_WIDTH + GDN_WIDTH
FFN_HIDDEN = -(-(8 * D_MODEL) // (3 * 256)) * 256

IN_SPLITS = (
    DIFF_HEADS * 2 * DIFF_QK_DIM,
    DIFF_HEADS * 2 * DIFF_QK_DIM,
    DIFF_WIDTH,
    GQA_Q_HEADS * GQA_HEAD_DIM,
    GQA_KV_HEADS * GQA_HEAD_DIM,
    GQA_KV_HEADS * GQA_HEAD_DIM,
    3 * GDN_WIDTH,
    GDN_WIDTH,
    2 * GDN_HEADS,
    2 * GDN_HEADS,
)
IN_DIM = sum(IN_SPLITS)

kernel_name = "hybrid_head_group_dit"


def rms_norm(x, g):
    xf = x.astype(jnp.float32)
    y = xf * lax.rsqrt(jnp.mean(xf * xf, axis=-1, keepdims=True) + NORM_EPS)
    return (y * g.astype(jnp.float32)).astype(x.dtype)


def l2_normalize(x):
    xf = x.astype(jnp.float32)
    return xf * lax.rsqrt(jnp.sum(xf * xf, axis=-1, keepdims=True) + L2_EPS)


def modulate(h, shift, scale):
    return h * (1.0 + scale) + shift


def axial_rope_tables(rows, rot_dim):
    row = jnp.repeat(jnp.arange(rows), GRID_W).astype(jnp.float32)
    col = jnp.tile(jnp.arange(GRID_W), rows).astype(jnp.float32)
    n_freq = rot_dim // 4
    inv_freq = ROPE_THETA ** (-jnp.arange(n_freq, dtype=jnp.float32) / n_freq)
    ang = jnp.stack([row[:, None] * inv_freq, col[:, None] * inv_freq], axis=1)
    return jnp.cos(ang), jnp.sin(ang)


def apply_axial_rope(x, cos, sin):
    shp = x.shape
    nf = cos.shape[-1]
    xr = x.reshape(shp[0], shp[1], -1, 2, 2, nf).astype(jnp.float32)
    x1, x2 = xr[..., 0, :], xr[..., 1, :]
    c, s = cos[:, None], sin[:, None]
    out = jnp.stack([x1 * c - x2 * s, x2 * c + x1 * s], axis=-2)
    return out.reshape(shp).astype(x.dtype)


def sweep_query_blocks(attend, queries):
    t = queries[0].shape[1]
    nb = t // Q_BLOCK
    blocks = tuple(jnp.moveaxis(q.reshape((q.shape[0], nb, Q_BLOCK) + q.shape[2:]), 1, 0) for q in queries)
    out = lax.map(lambda qb: attend(*qb), blocks)
    out = jnp.moveaxis(out, 0, 1)
    return out.reshape((out.shape[0], t) + out.shape[3:])


def centred_depthwise_conv(x, w):
    taps = w.shape[0]
    return lax.conv_general_dilated(
        x, w[:, None, :].astype(x.dtype), window_strides=(1,),
        padding=[(taps // 2, taps // 2)], dimension_numbers=('NWC', 'WIO', 'NWC'),
        feature_group_count=x.shape[-1])


def chunked_gated_delta(q, k, v, g, beta, s0):
    bsz, t, h, dk = q.shape
    dv = v.shape[-1]
    cs = GDN_CHUNK
    n = t // cs

    def chunks(a):
        a = a.reshape((bsz, n, cs, h) + a.shape[3:])
        return jnp.moveaxis(jnp.moveaxis(a, 1, 0), 2, 3)

    qc, kc, vc, gc, bc = (chunks(a) for a in (q, k, v, g, beta))
    gcum = jnp.cumsum(gc, axis=-1)
    idx = jnp.arange(cs)
    incl = idx[:, None] >= idx[None, :]
    strict = idx[:, None] > idx[None, :]
    decay = jnp.exp(jnp.where(incl, gcum[..., :, None] - gcum[..., None, :], -jnp.inf))
    kk = jnp.einsum('nbhid,nbhjd->nbhij', kc, kc)
    a_mat = jnp.where(strict, kk * decay * bc[..., :, None], 0.0) + jnp.eye(cs, dtype=jnp.float32)
    rhs = jnp.concatenate([vc * bc[..., None], kc * (bc * jnp.exp(gcum))[..., None]], axis=-1)
    sol = lax.linalg.triangular_solve(a_mat, rhs, left_side=True, lower=True, unit_diagonal=True)
    u, w = sol[..., :dv], sol[..., dv:]
    qk = jnp.einsum('nbhid,nbhjd->nbhij', qc, kc) * decay
    q_dec = qc * jnp.exp(gcum)[..., None]
    k_dec = kc * jnp.exp(gcum[..., -1:] - gcum)[..., None]
    g_last = jnp.exp(gcum[..., -1])

    def step(state, xs):
        u_i, w_i, qk_i, qd_i, kd_i, gl_i = xs
        v_new = u_i - jnp.einsum('bhck,bhkv->bhcv', w_i, state)
        o_i = jnp.einsum('bhck,bhkv->bhcv', qd_i, state) + jnp.einsum('bhij,bhjv->bhiv', qk_i, v_new)
        state = state * gl_i[..., None, None] + jnp.einsum('bhck,bhcv->bhkv', kd_i, v_new)
        return state, o_i

    s_final, o = lax.scan(step, s0, (u, w, qk, q_dec, k_dec, g_last))
    o = jnp.moveaxis(jnp.moveaxis(o, 3, 2), 0, 1).reshape(bsz, t, h, dv)
    return o, s_final


def diff_mixer(q_l, k_l, v_l, q_c, k_c, v_c, lam_vecs, norm_g, lambda_init, cos, sin, need_ctx):
    def heads(q, k, v):
        b, t = q.shape[:2]
        return (q.reshape(b, t, DIFF_HEADS, 2, DIFF_QK_DIM),
                k.reshape(b, t, DIFF_HEADS, 2, DIFF_QK_DIM),
                v.reshape(b, t, DIFF_HEADS, DIFF_V_DIM))

    ql, kl, vl = heads(q_l, k_l, v_l)
    qc, kc, vc = heads(q_c, k_c, v_c)
    ql = apply_axial_rope(ql, cos, sin)
    kl = apply_axial_rope(kl, cos, sin)
    lf = lam_vecs.astype(jnp.float32)
    lam = jnp.exp(jnp.sum(lf[0] * lf[1])) - jnp.exp(jnp.sum(lf[2] * lf[3])) + lambda_init
    scale = DIFF_QK_DIM ** -0.5

    def attend(q, k, v):
        s = jnp.einsum('bqhcd,bkhcd->bhcqk', q, k).astype(jnp.float32) * scale
        p = jax.nn.softmax(s, axis=-1)
        p = p[:, :, 0] - lam * p[:, :, 1]
        return jnp.einsum('bhqk,bkhe->bqhe', p.astype(v.dtype), v)

    def post(o):
        b, t = o.shape[:2]
        return (rms_norm(o, norm_g) * (1.0 - lambda_init)).reshape(b, t, DIFF_WIDTH)

    k_all = jnp.concatenate([kc, kl], axis=1)
    v_all = jnp.concatenate([vc, vl], axis=1)
    y_l = post(sweep_query_blocks(lambda qb: attend(qb, k_all, v_all), (ql,)))
    y_c = post(attend(qc, kc, vc)) if need_ctx else None
    return y_l, y_c


def gqa_attend(q, k, v):
    b, tq, h, dh = q.shape
    hkv = k.shape[2]
    qg = q.reshape(b, tq, hkv, h // hkv, dh)
    s = jnp.einsum('bqhgd,bkhd->bhgqk', qg, k).astype(jnp.float32) * dh ** -0.5
    p = jax.nn.softmax(s, axis=-1)
    o = jnp.einsum('bhgqk,bkhd->bqhgd', p.astype(v.dtype), v)
    return o.reshape(b, tq, h * dh)


def gqa_mixer(q_l, k_l, v_l, q_c, k_c, v_c, qn_g, kn_g, cos, sin, need_ctx):
    def heads(q, k, v):
        b, t = q.shape[:2]
        q = rms_norm(q.reshape(b, t, GQA_Q_HEADS, GQA_HEAD_DIM), qn_g)
        k = rms_norm(k.reshape(b, t, GQA_KV_HEADS, GQA_HEAD_DIM), kn_g)
        return q, k, v.reshape(b, t, GQA_KV_HEADS, GQA_HEAD_DIM)

    ql, kl, vl = heads(q_l, k_l, v_l)
    qc, kc, vc = heads(q_c, k_c, v_c)
    ql = apply_axial_rope(ql, cos, sin)
    kl = apply_axial_rope(kl, cos, sin)
    k_all = jnp.concatenate([kc, kl], axis=1)
    v_all = jnp.concatenate([vc, vl], axis=1)
    y_l = sweep_query_blocks(lambda qb: gqa_attend(qb, k_all, v_all), (ql,))
    y_c = gqa_attend(qc, kc, vc) if need_ctx else None
    return y_l, y_c


def gdn_mixer(qkv_l, z_l, a_l, b_l, qkv_c, z_c, a_c, b_c, conv_w, a_log, dt_bias, norm_g, need_ctx):
    def prep(qkv, a, b):
        bs, t = qkv.shape[:2]
        qkv = jax.nn.silu(centred_depthwise_conv(qkv, conv_w))
        q, k, v = jnp.split(qkv, 3, axis=-1)
        q = l2_normalize(q.reshape(bs, t, GDN_HEADS, GDN_HEAD_DIM)) * GDN_HEAD_DIM ** -0.5
        k = l2_normalize(k.reshape(bs, t, GDN_HEADS, GDN_HEAD_DIM))
        v = v.reshape(bs, t, GDN_HEADS, GDN_HEAD_DIM).astype(jnp.float32)
        a = a.reshape(bs, t, 2, GDN_HEADS).astype(jnp.float32)
        b = b.reshape(bs, t, 2, GDN_HEADS).astype(jnp.float32)
        g = -jnp.exp(a_log.astype(jnp.float32)) * jax.nn.softplus(a + dt_bias.astype(jnp.float32))
        return q, k, v, g, jax.nn.sigmoid(b)

    ql, kl, vl, gl, bl = prep(qkv_l, a_l, b_l)
    qc, kc, vc, gc, bc = prep(qkv_c, a_c, b_c)
    zero = jnp.zeros((ql.shape[0], GDN_HEADS, GDN_HEAD_DIM, GDN_HEAD_DIM), jnp.float32)
    flip = lambda a: a[:, ::-1]
    oc_f, sc_f = chunked_gated_delta(qc, kc, vc, gc[:, :, 0], bc[:, :, 0], zero)
    ol_f, _ = chunked_gated_delta(ql, kl, vl, gl[:, :, 0], bl[:, :, 0], sc_f)
    oc_b, sc_b = chunked_gated_delta(flip(qc), flip(kc), flip(vc), flip(gc[:, :, 1]), flip(bc[:, :, 1]), zero)
    ol_b, _ = chunked_gated_delta(flip(ql), flip(kl), flip(vl), flip(gl[:, :, 1]), flip(bl[:, :, 1]), sc_b)

    def readout(o, z):
        bs, t = z.shape[:2]
        zh = z.reshape(bs, t, GDN_HEADS, GDN_HEAD_DIM).astype(jnp.float32)
        y = rms_norm(o, norm_g) * jax.nn.silu(zh)
        return y.reshape(bs, t, GDN_WIDTH).astype(z.dtype)

    y_l = readout(ol_f + flip(ol_b), z_l)
    y_c = readout(oc_f + flip(oc_b), z_c) if need_ctx else None
    return y_l, y_c


def swiglu(h, w_gu, w_down):
    gate, up = jnp.split(h @ w_gu, 2, axis=-1)
    return (jax.nn.silu(gate) * up) @ w_down


def setup_inputs(seed: int = 0) -> dict:
    key = jax.random.key(seed)
    ks = jax.random.split(key, 21)
    f32 = jnp.float32

    def nrm(k, shape, scale):
        return jax.random.normal(k, shape, f32) * scale

    dt = jnp.exp(jax.random.uniform(ks[14], (DEPTH, 2, GDN_HEADS), f32, math.log(1e-3), math.log(1e-1)))
    return {
        'x': nrm(ks[0], (BATCH, SEQ, D_MODEL), 1.0),
        'c': nrm(ks[1], (BATCH, D_MODEL), 1.0),
        'ctx': nrm(ks[2], (BATCH, CTX_LEN, D_MODEL), 1.0),
        'c_ctx': nrm(ks[3], (D_MODEL,), 1.0),
        'norm1_g': 1.0 + nrm(ks[4], (DEPTH, D_MODEL), 0.1),
        'ada_w': nrm(ks[5], (DEPTH, D_MODEL, 6 * D_MODEL), 0.5 * D_MODEL ** -0.5),
        'ada_b': nrm(ks[6], (DEPTH, 6 * D_MODEL), 0.02),
        'w_in': nrm(ks[7], (DEPTH, D_MODEL, IN_DIM), D_MODEL ** -0.5),
        'diff_lambda': nrm(ks[8], (DEPTH, 4, DIFF_QK_DIM), 0.1),
        'diff_norm_g': 1.0 + nrm(ks[9], (DEPTH, DIFF_V_DIM), 0.1),
        'q_norm_g': 1.0 + nrm(ks[10], (DEPTH, GQA_HEAD_DIM), 0.1),
        'k_norm_g': 1.0 + nrm(ks[11], (DEPTH, GQA_HEAD_DIM), 0.1),
        'gdn_conv_w': nrm(ks[12], (DEPTH, GDN_CONV, 3 * GDN_WIDTH), GDN_CONV ** -0.5),
        'gdn_a_log': jnp.log(jax.random.uniform(ks[13], (DEPTH, 2, GDN_HEADS), f32, 1.0, 16.0)),
        'gdn_dt_bias': dt + jnp.log(-jnp.expm1(-dt)),
        'gdn_norm_g': 1.0 + nrm(ks[15], (DEPTH, GDN_HEAD_DIM), 0.1),
        'w_out': nrm(ks[16], (DEPTH, MIX_WIDTH, D_MODEL), MIX_WIDTH ** -0.5),
        'norm2_g': 1.0 + nrm(ks[17], (DEPTH, D_MODEL), 0.1),
        'ffn_w_gu': nrm(ks[18], (DEPTH, D_MODEL, 2 * FFN_HIDDEN), D_MODEL ** -0.5),
        'ffn_w_down': nrm(ks[19], (DEPTH, FFN_HIDDEN, D_MODEL), FFN_HIDDEN ** -0.5),
        'final_norm_g': 1.0 + nrm(ks[20], (D_MODEL,), 0.1),
    }


def reference(x, c, ctx, c_ctx, norm1_g, ada_w, ada_b, w_in, diff_lambda, diff_norm_g,
              q_norm_g, k_norm_g, gdn_conv_w, gdn_a_log, gdn_dt_bias, gdn_norm_g, w_out,
              norm2_g, ffn_w_gu, ffn_w_down, final_norm_g):
    ROWS = x.shape[1] // GRID_W
    rope_diff = axial_rope_tables(ROWS, DIFF_QK_DIM)
    rope_gqa = axial_rope_tables(ROWS, GQA_HEAD_DIM)
    offs = np.cumsum(IN_SPLITS)[:-1].tolist()
    silu_c = jax.nn.silu(c)
    silu_cc = jax.nn.silu(c_ctx)
    h = x
    hc = ctx
    for layer in range(DEPTH):
        need_ctx = layer < DEPTH - 1
        lambda_init = 0.8 - 0.6 * math.exp(-0.3 * layer)
        sh1, sc1, g1, sh2, sc2, g2 = jnp.split((silu_c @ ada_w[layer] + ada_b[layer])[:, None, :], 6, axis=-1)
        csh1, csc1, cg1, csh2, csc2, cg2 = jnp.split((silu_cc @ ada_w[layer] + ada_b[layer])[None, None, :], 6, axis=-1)

        a_l = modulate(rms_norm(h, norm1_g[layer]), sh1, sc1)
        a_c = modulate(rms_norm(hc, norm1_g[layer]), csh1, csc1)
        pl = jnp.split(a_l @ w_in[layer], offs, axis=-1)
        pc = jnp.split(a_c @ w_in[layer], offs, axis=-1)
        d_l, d_c = diff_mixer(pl[0], pl[1], pl[2], pc[0], pc[1], pc[2], diff_lambda[layer],
                              diff_norm_g[layer], lambda_init, rope_diff[0], rope_diff[1], need_ctx)
        q_l, q_c = gqa_mixer(pl[3], pl[4], pl[5], pc[3], pc[4], pc[5], q_norm_g[layer], k_norm_g[layer],
                             rope_gqa[0], rope_gqa[1], need_ctx)
        n_l, n_c = gdn_mixer(pl[6], pl[7], pl[8], pl[9], pc[6], pc[7], pc[8], pc[9], gdn_conv_w[layer],
                             gdn_a_log[layer], gdn_dt_bias[layer], gdn_norm_g[layer], need_ctx)
        h = h + g1 * (jnp.concatenate([d_l, q_l, n_l], axis=-1) @ w_out[layer])
        h = h + g2 * swiglu(modulate(rms_norm(h, norm2_g[layer]), sh2, sc2), ffn_w_gu[layer], ffn_w_down[layer])
        if need_ctx:
            hc = hc + cg1 * (jnp.concatenate([d_c, q_c, n_c], axis=-1) @ w_out[layer])
            hc = hc + cg2 * swiglu(modulate(rms_norm(hc, norm2_g[layer]), csh2, csc2), ffn_w_gu[layer], ffn_w_down[layer])
    return rms_norm(h, final_norm_g)
```

```python
import numpy as np, contextlib, math
import concourse.bass as bass
import concourse.mybir as mybir
from concourse.bass_utils import run_bass_kernel_spmd

F32 = mybir.dt.float32
BF16 = mybir.dt.bfloat16
AF = mybir.ActivationFunctionType
ALU = mybir.AluOpType
AX = mybir.AxisListType


class Buf:
    __slots__ = ("name", "w", "r", "dsem", "excl")

    def __init__(self, name, excl=False):
        self.name = name
        self.excl = excl
        self.w = None
        self.r = {}
        self.dsem = None


class Sched:
    ENG = ("pe", "act", "dve", "pool", "sp")

    def __init__(self, nc, es):
        self.nc = nc
        self.es = es
        self.eng = {"pe": nc.tensor, "act": nc.scalar, "dve": nc.vector,
                    "pool": nc.gpsimd, "sp": nc.sync}
        self.sems = {}
        self.cnt = {}
        for e in self.ENG:
            self.sems[e] = es.enter_context(nc.semaphore("s_" + e))
            self.cnt[e] = 0
        self.seen = {e: {} for e in self.ENG}
        self.free_dsems = []
        self.n_dsem = 0
        self.live_dsems = []

    def _dsem(self, buf):
        if buf.dsem is None:
            if self.free_dsems:
                key = self.free_dsems.pop()
            else:
                key = "d%d" % self.n_dsem
                self.n_dsem += 1
                self.sems[key] = self.es.enter_context(self.nc.semaphore("s_" + key))
                self.cnt[key] = 0
            buf.dsem = key
            self.live_dsems.append(buf)
        return buf.dsem

    def _deps(self, e, r, w):
        deps = {}

        def need(ev, raw):
            if ev is None:
                return
            key, val = ev
            if key == e and (e == "pe" or not raw):
                return
            if self.seen[e].get(key, 0) >= val:
                return
            if deps.get(key, 0) < val:
                deps[key] = val
        for b in r:
            need(b.w, True)
            if b.excl:
                for k, v in b.r.items():
                    need((k, v), False)
        for b in w:
            need(b.w, False)
            for k, v in b.r.items():
                need((k, v), False)
        return deps

    def _emit(self, e, deps, fn):
        eng = self.eng[e]
        items = list(deps.items())
        for key, val in items[:-1]:
            eng.wait_ge(self.sems[key], val)
        ins = fn(eng)
        if isinstance(ins, (list, tuple)):
            first, last = ins[0], ins[-1]
            multi = len(ins) > 1
        else:
            first = last = ins
            multi = False
        if items:
            key, val = items[-1]
            if multi:
                raise RuntimeError("multi-instruction op must use op_group")
            first._wait_ge(self.sems[key], val)
        for key, val in items:
            self.seen[e][key] = val
        return last

    def op(self, e, fn, r=(), w=()):
        deps = self._deps(e, r, w)
        last = self._emit(e, deps, fn)
        self.cnt[e] += 1
        last.then_inc(self.sems[e], 1)
        ev = (e, self.cnt[e])
        self._mark(ev, r, w)
        return ev

    def group(self, e, fn, r=(), w=()):
        deps = self._deps(e, r, w)
        eng = self.eng[e]
        for key, val in deps.items():
            eng.wait_ge(self.sems[key], val)
            self.seen[e][key] = val
        last = fn(eng)
        self.cnt[e] += 1
        last.then_inc(self.sems[e], 1)
        ev = (e, self.cnt[e])
        self._mark(ev, r, w)
        return ev

    def _mark(self, ev, r, w):
        key, val = ev
        for b in r:
            if b.excl:
                b.r = {key: val}
            elif b.r.get(key, 0) < val:
                b.r[key] = val
        for b in w:
            b.w = ev
            b.r = {}

    def dma(self, out, in_, r=(), w=(), sem=None, q="sp", **kw):
        deps = self._deps(q, r, w)
        key = self._dsem(sem)
        eng = self.eng[q]
        for k, v in deps.items():
            eng.wait_ge(self.sems[k], v)
            self.seen[q][k] = v
        ins = eng.dma_start(out=out, in_=in_, **kw)
        self.cnt[key] += 16
        ins.then_inc(self.sems[key], 16)
        ev = (key, self.cnt[key])
        self._mark(ev, r, w)
        return ev

    def barrier(self):
        sp = self.eng["sp"]
        for key in list(self.sems.keys()):
            if key == "sp":
                continue
            v = self.cnt[key]
            if v > 0 and self.seen["sp"].get(key, 0) < v:
                sp.wait_ge(self.sems[key], v)
                self.seen["sp"][key] = v
        self.cnt["sp"] += 1
        sp.sem_inc(self.sems["sp"], 1)
        for e in self.ENG:
            if e == "sp":
                continue
            self.eng[e].wait_ge(self.sems["sp"], self.cnt["sp"])
        for e in self.ENG:
            for key in self.sems:
                self.seen[e][key] = self.cnt[key]
        for b in self.live_dsems:
            self.free_dsems.append(b.dsem)
            b.dsem = None
        self.live_dsems = []

    def finish(self):
        self.barrier()


import os
import ml_dtypes

T = 4352
NCTX = 256
D = 1024
IN_DIM = 2968
EPS = 1e-6
BLOCKS = [(0, 256)] + [(256 + 512 * i, 512) for i in range(8)]


class K:
    pass


def build(depth=2, stop_after=None, debug=False):
    nc = bass.Bass("TRN2", target_bir_lowering=False)
    es = contextlib.ExitStack()
    S = Sched(nc, es)
    k = K()
    k.nc, k.S, k.es = nc, S, es
    k.debug = debug

    def din(name, shape, dt=F32):
        return nc.dram_tensor(name, list(shape), dt, kind="ExternalInput").ap()

    def dscr(name, shape, dt=F32):
        kind = "ExternalOutput" if debug else "Internal"
        return nc.dram_tensor(name, list(shape), dt, kind=kind).ap()

    k.xT = din("xT", [D, T])
    k.ccin = din("ccin", [128, 8, 2])
    k.ada_w = din("ada_w", [depth, D, 6 * D])
    k.ada_b = din("ada_b", [depth, 128, 48])
    k.norm1_g = din("norm1_g", [depth, 128, 8])
    k.norm2_g = din("norm2_g", [depth, 128, 8])
    k.final_g = din("final_g", [128, 8])
    k.w_in = din("w_in", [depth, D, IN_DIM])
    k.w_out = din("w_out", [depth, D, D])
    k.w_gu = din("w_gu", [depth, D, 2 * 2816])
    k.w_down = din("w_down", [depth, 2816, D])
    k.outT = nc.dram_tensor("outT", [D, 4096], F32, kind="ExternalOutput").ap()
    k.qk_g = din("qk_g", [depth, 128, 2])
    k.c_ones_d = din("c_ones_d", [128, 128], BF16)
    k.c_blk64 = din("c_blk64", [128, 128], BF16)
    k.c_rp_d = din("c_rp_d", [128, 128], BF16)
    k.c_rp_g = din("c_rp_g", [128, 128], BF16)
    k.c_rope = din("c_rope", [4, 128, 4096])
    k.c_place = din("c_place", [64, 2, 128], BF16)
    k.diff_lambda = din("diff_lambda", [depth, 128])
    k.diff_g = din("diff_g", [depth, 128, 1])
    k.conv_w = din("conv_w", [depth, 128, 9, 5])
    k.c_blk64s = din("c_blk64s", [128, 128], BF16)
    k.c_identb = din("c_identb", [128, 128], BF16)
    k.c_identf = din("c_identf", [128, 128])
    k.c_MI = din("c_MI", [128, 2, 128])
    k.c_SU = din("c_SU", [128, 2, 128])
    k.c_EMK = din("c_EMK", [128, 2, 256])
    k.c_MK = din("c_MK", [3, 128, 128])
    k.gdn_g = din("gdn_g", [depth, 64])
    k.gdn_alog = din("gdn_alog", [depth, 12])
    k.gdn_dtb = din("gdn_dtb", [depth, 12])
    k.hA = dscr("hA", [D, T])
    k.hB = dscr("hB", [D, T])
    k.QdT = dscr("QdT", [2, 128, T], BF16)
    k.KdT = dscr("KdT", [2, 128, T], BF16)
    k.QgT = dscr("QgT", [3, 128, T], BF16)
    k.KgT = dscr("KgT", [1, 128, T], BF16)
    k.Vtok = dscr("Vtok", [T, 390], BF16)
    k.gqkvT = dscr("gqkvT", [9, 128, T])
    k.zab = dscr("zab", [T, 408])
    k.mixT = dscr("mixT", [8, 128, T], BF16)
    k.QnT = dscr("QnT", [3, 128, T], BF16)
    k.KnT = dscr("KnT", [3, 128, T], BF16)
    k.Ktok = dscr("Ktok", [T, 384], BF16)
    k.Vtok2 = dscr("Vtok2", [T, 384], BF16)
    if debug:
        k.dbg_mod = nc.dram_tensor("dbg_mod", [128, 96], F32, kind="ExternalOutput").ap()

    def sbp(name, shape, dt):
        t = es.enter_context(nc.sbuf_tensor(name, list(shape), dt))
        return t, Buf(name)

    k.ones_d, k.b_ones_d = sbp("ones_d", [128, 128], BF16)
    k.blk64, k.b_blk64 = sbp("blk64", [128, 128], BF16)
    k.rp_d, k.b_rp_d = sbp("rp_d", [128, 128], BF16)
    k.rp_g, k.b_rp_g = sbp("rp_g", [128, 128], BF16)
    k.scc, k.b_scc = sbp("scc", [128, 8, 2], F32)
    k.mod, k.b_mod = sbp("mod", [128, 48, 2], F32)
    k.G1, k.b_G1 = sbp("G1", [128, 8, 2], F32)
    k.G2, k.b_G2 = sbp("G2", [128, 8, 2], F32)
    k.qkg, k.b_qkg = sbp("qkg", [128, 2], F32)
    k.epsc, k.b_epsc = sbp("epsc", [128, 2], F32)
    k.place, k.b_place = sbp("place", [64, 2, 128], BF16)
    k.blk64s, k.b_blk64s = sbp("blk64s", [128, 128], BF16)
    k.identb, k.b_identb = sbp("identb", [128, 128], BF16)
    S.dma(k.blk64s[:], k.c_blk64s, w=[k.b_blk64s], sem=k.b_blk64s)
    S.dma(k.identb[:], k.c_identb, w=[k.b_identb], sem=k.b_identb)
    S.dma(k.place[:], k.c_place, w=[k.b_place], sem=k.b_place)
    S.op("pool", lambda e: e.memset(k.epsc[:], EPS), w=[k.b_epsc])

    S.dma(k.ones_d[:], k.c_ones_d, w=[k.b_ones_d], sem=k.b_ones_d)
    S.dma(k.blk64[:], k.c_blk64, w=[k.b_blk64], sem=k.b_blk64)
    S.dma(k.rp_d[:], k.c_rp_d, w=[k.b_rp_d], sem=k.b_rp_d)
    S.dma(k.rp_g[:], k.c_rp_g, w=[k.b_rp_g], sem=k.b_rp_g)
    S.dma(k.scc[:], k.ccin, w=[k.b_scc], sem=k.b_scc)
    S.op("act", lambda e: e.activation(out=k.scc[:], in_=k.scc[:], func=AF.Silu), r=[k.b_scc], w=[k.b_scc])
    S.barrier()

    for l in range(depth):
        ph_mod(k, l)
        S.barrier()
        if stop_after == ("mod", l):
            break
        ph_proj(k, l, k.xT if l == 0 else k.hB)
        S.barrier()
        if stop_after == ("proj", l):
            break
        ph_att(k, l, need_ctx=(l < depth - 1))
        S.barrier()
        if stop_after == ("att", l):
            break
        ph_gdn_a(k, l)
        S.barrier()
        if stop_after == ("gdna", l):
            break
        ph_gdn_b(k, l, need_ctx=(l < depth - 1))
        S.barrier()
        if stop_after == ("gdnb", l):
            break
        last = (l == depth - 1)
        hsrc = k.xT if l == 0 else k.hB
        ph_out(k, l, hsrc, k.hA, need_ctx=not last)
        S.barrier()
        if stop_after == ("out", l):
            break
        ph_ffn(k, l, k.hA, None if last else k.hB, need_ctx=not last, final=last)
        S.barrier()
        if stop_after == ("ffn", l):
            break
    S.finish()
    es.close()
    return nc


def rsqrt_eps(k, dst, dst_b, src, src_b, eps=EPS):
    S = k.S
    npart = dst.shape[0]
    S.op("act", lambda e: e.activation(out=dst, in_=src, func=AF.Ln, bias=k.epsc[0:npart, 0:1]), r=[src_b, k.b_epsc], w=[dst_b])
    S.op("act", lambda e: e.activation(out=dst, in_=dst, func=AF.Exp, scale=-0.5), r=[dst_b], w=[dst_b])


class Phase:
    def __init__(self, k):
        self.k = k
        self.es = contextlib.ExitStack()
        self.nps = 0

    _uid = [0]

    def sb(self, name, shape, dt):
        Phase._uid[0] += 1
        name = "%s_u%d" % (name, Phase._uid[0])
        t = self.es.enter_context(self.k.nc.sbuf_tensor(name, list(shape), dt))
        return t, Buf(name)

    def ps(self, name, shape, dt=F32):
        Phase._uid[0] += 1
        name = "%s_u%d" % (name, Phase._uid[0])
        t = self.es.enter_context(self.k.nc.psum_tensor(name, list(shape), dt))
        return t, Buf(name, excl=True)

    def close(self):
        self.es.close()


def ph_mod(k, l):
    S, nc = k.S, k.nc
    P = Phase(k)
    wst = [P.sb("adaw%d" % i, [128, 8, 512], F32) for i in range(2)]
    adab, b_adab = P.sb("adab", [128, 48], F32)
    n1, b_n1 = P.sb("n1g", [128, 8], F32)
    n2, b_n2 = P.sb("n2g", [128, 8], F32)
    pm, b_pm = P.ps("pm", [128, 96])
    S.dma(adab[:], k.ada_b[l], w=[b_adab], sem=b_adab)
    S.dma(n1[:], k.norm1_g[l], w=[b_n1], sem=b_n1)
    S.dma(n2[:], k.norm2_g[l], w=[b_n2], sem=b_n2)
    S.dma(k.qkg[:], k.qk_g[l], w=[k.b_qkg], sem=k.b_qkg)
    wv = k.ada_w[l].rearrange("(k p) n -> p k n", p=128)
    for ci in range(12):
        wt, wb = wst[ci % 2]
        S.dma(wt[:], wv[:, :, ci * 512:(ci + 1) * 512], w=[wb], sem=wb)
        for j in range(4):
            c = ci * 4 + j

            def mm(e, c=c, j=j, wt=wt):
                last = None
                for kk in range(8):
                    last = e.matmul(pm[:, 2 * c:2 * c + 2], lhsT=wt[:, kk, j * 128:(j + 1) * 128],
                                    rhs=k.scc[:, kk, :], start=(kk == 0), stop=(kk == 7))
                return last
            S.group("pe", mm, r=[wb, k.b_scc], w=[b_pm])
    S.op("dve", lambda e: e.tensor_tensor(k.mod[:], pm[:].rearrange("p (c j) -> p c j", j=2),
                                          adab[:].unsqueeze(2).broadcast_to([128, 48, 2]), ALU.add),
         r=[b_pm, b_adab], w=[k.b_mod])
    S.op("dve", lambda e: e.scalar_tensor_tensor(out=k.G1[:], in0=k.mod[:, 8:16, :], scalar=1.0,
                                                 in1=n1[:].unsqueeze(2).broadcast_to([128, 8, 2]),
                                                 op0=ALU.add, op1=ALU.mult),
         r=[k.b_mod, b_n1], w=[k.b_G1])
    S.op("dve", lambda e: e.scalar_tensor_tensor(out=k.G2[:], in0=k.mod[:, 32:40, :], scalar=1.0,
                                                 in1=n2[:].unsqueeze(2).broadcast_to([128, 8, 2]),
                                                 op0=ALU.add, op1=ALU.mult),
         r=[k.b_mod, b_n2], w=[k.b_G2])
    if k.debug:
        S.dma(k.dbg_mod, k.mod[:].rearrange("p c j -> p (c j)"), r=[k.b_mod], sem=k.b_mod)
    S.barrier()
    P.close()


def ph_proj(k, l, hsrc):
    S, nc = k.S, k.nc
    P = Phase(k)
    wbf, b_w = P.sb("w_in_bf", [128, 8, IN_DIM], BF16)
    wv = k.w_in[l].rearrange("(k p) n -> p k n", p=128)
    hb = [P.sb("hblk%d" % i, [128, 8, 512], F32) for i in range(2)]
    tmp, b_tmp = P.sb("tmp", [128, 8, 512], F32)
    stg = [(tmp, b_tmp), hb[1]]
    for ci, c0 in enumerate(range(0, IN_DIM, 512)):
        n = min(512, IN_DIM - c0)
        st, sbuf_ = stg[ci % 2]
        S.dma(st[:, :, :n], wv[:, :, c0:c0 + n], w=[sbuf_], sem=sbuf_)
        S.op("pool", lambda e, st=st, c0=c0, n=n: e.tensor_copy(wbf[:, :, c0:c0 + n], st[:, :, :n]),
             r=[sbuf_], w=[b_w])

    sq, b_sq = P.sb("sq", [128, 8, 512], BF16)
    rstd, b_rstd = P.sb("rstd", [128, 512], F32)
    aT = [P.sb("aT%d" % i, [128, 8, 512], BF16) for i in range(2)]
    rope = [P.sb("rope%d" % i, [128, 4, 512], F32) for i in range(2)]
    p_ss, b_pss = P.ps("p_ss", [128, 512])
    p_fm = [P.ps("p_fm%d" % i, [128, 512]) for i in range(3)]
    p_aux = [P.ps("p_aux%d" % i, [128, 512]) for i in range(2)]
    p_tm = [P.ps("p_tm%d" % i, [128, 512]) for i in range(2)]
    qb = [P.sb("qb%d" % i, [128, 512], BF16) for i in range(2)]
    sqh = [P.sb("sqh%d" % i, [128, 512], BF16) for i in range(2)]
    t1 = [P.sb("t1_%d" % i, [128, 512], F32) for i in range(2)]
    t2 = [P.sb("t2_%d" % i, [128, 512], F32) for i in range(2)]
    rs2 = [P.sb("rs2_%d" % i, [128, 512], F32) for i in range(2)]
    ob = [P.sb("ob%d" % i, [128, 512], BF16) for i in range(3)]
    of = [P.sb("of%d" % i, [128, 512], F32) for i in range(3)]
    vt = [P.sb("vt%d" % i, [128, 4, 390], BF16) for i in range(2)]
    zt = [P.sb("zt%d" % i, [128, 4, 408], F32) for i in range(2)]
    for i in range(2):
        S.op("pool", lambda e, i=i: e.memset(vt[i][0][:], 1.0), w=[vt[i][1]])

    hv = hsrc.rearrange("(k p) t -> p k t", p=128)
    cnt = {"fm": 0, "aux": 0, "tm": 0, "w": 0, "ob": 0, "of": 0}

    def load_block(b):
        t0, nt = BLOCKS[b]
        ht, hbuf = hb[b % 2]
        S.dma(ht[:, :, :nt], hv[:, :, t0:t0 + nt], w=[hbuf], sem=hbuf)
        if b > 0:
            rt, rbuf = rope[b % 2]
            S.dma(rt[:, :, :nt], k.c_rope[:, :, t0 - NCTX:t0 - NCTX + nt].rearrange("c p t -> p c t"),
                  w=[rbuf], sem=rbuf)

    load_block(0)
    for b in range(len(BLOCKS)):
        t0, nt = BLOCKS[b]
        j = 1 if b == 0 else 0
        if b + 1 < len(BLOCKS):
            load_block(b + 1)
        ht, hbuf = hb[b % 2]
        rt, rbuf = rope[b % 2]
        at, abuf = aT[b % 2]
        S.op("act", lambda e: e.activation(out=sq[:, :, :nt], in_=ht[:, :, :nt], func=AF.Square),
             r=[hbuf], w=[b_sq])

        def mm_ss(e):
            last = None
            for kk in range(8):
                last = e.matmul(p_ss[:, :nt], lhsT=k.ones_d[:], rhs=sq[:, kk, :nt], start=(kk == 0), stop=(kk == 7))
            return last
        S.group("pe", mm_ss, r=[b_sq, k.b_ones_d], w=[b_pss])
        rsqrt_eps(k, rstd[:, :nt], b_rstd, p_ss[:, :nt], b_pss)
        S.op("dve", lambda e: e.tensor_tensor(tmp[:, :, :nt], ht[:, :, :nt],
                                              rstd[:, :nt].unsqueeze(1).broadcast_to([128, 8, nt]), ALU.mult),
             r=[hbuf, b_rstd], w=[b_tmp])
        for kk in range(8):
            S.op("act", lambda e, kk=kk: e.activation(out=at[:, kk, :nt], in_=tmp[:, kk, :nt], func=AF.Identity,
                                                       scale=k.G1[:, kk, j:j + 1], bias=k.mod[:, kk, j:j + 1]),
                 r=[b_tmp, k.b_G1, k.b_mod], w=[abuf])

        if os.environ.get("BISECT") == "1":
            continue
        def fm_matmul(c0):
            pt, pb = p_fm[cnt["fm"] % 3]
            cnt["fm"] += 1

            def mm(e):
                last = None
                for kk in range(8):
                    last = e.matmul(pt[:, :nt], lhsT=wbf[:, kk, c0:c0 + 128], rhs=at[:, kk, :nt],
                                    start=(kk == 0), stop=(kk == 7))
                return last
            S.group("pe", mm, r=[b_w, abuf], w=[pb])
            return pt, pb

        def store(dst, src_t, src_b):
            S.dma(dst, src_t, r=[src_b], sem=src_b)

        def rope_chunk(pt, pb, kind, dst, gcol=None):
            i = cnt["w"] % 2
            cnt["w"] += 1
            qt, qbuf = qb[i]
            o_t, o_b = ob[cnt["ob"] % 3]
            cnt["ob"] += 1
            rp = k.rp_d if kind == "d" else k.rp_g
            rpb = k.b_rp_d if kind == "d" else k.b_rp_g
            ci, si = (0, 1) if kind == "d" else (2, 3)
            if kind == "d":
                S.op("act", lambda e: e.activation(out=qt[:, :nt], in_=pt[:, :nt], func=AF.Copy), r=[pb], w=[qbuf])
            else:
                S.op("act", lambda e: e.activation(out=qt[:, :nt], in_=pt[:, :nt], func=AF.Copy, scale=gcol),
                     r=[pb, k.b_qkg], w=[qbuf])
                st_, sb_ = sqh[i]
                S.op("act", lambda e: e.activation(out=st_[:, :nt], in_=pt[:, :nt], func=AF.Square), r=[pb], w=[sb_])
                pa, pab = p_aux[cnt["aux"] % 2]
                cnt["aux"] += 1
                S.op("pe", lambda e: e.matmul(pa[:, :nt], lhsT=k.blk64[:], rhs=st_[:, :nt], start=True, stop=True),
                     r=[sb_, k.b_blk64], w=[pab])
                r2, r2b = rs2[i]
                rsqrt_eps(k, r2[:, :nt], r2b, pa[:, :nt], pab)
            if b == 0:
                if kind == "d":
                    store(dst, qt[:, :nt], qbuf)
                else:
                    S.op("pool", lambda e: e.tensor_tensor(o_t[:, :nt], qt[:, :nt], r2[:, :nt], ALU.mult),
                         r=[qbuf, r2b], w=[o_b])
                    store(dst, o_t[:, :nt], o_b)
                return
            pa, pab = p_aux[cnt["aux"] % 2]
            cnt["aux"] += 1
            S.op("pe", lambda e: e.matmul(pa[:, :nt], lhsT=rp[:], rhs=qt[:, :nt], start=True, stop=True),
                 r=[qbuf, rpb], w=[pab])
            a1, a1b = t1[i]
            a2, a2b = t2[i]
            if kind == "d":
                S.op("dve", lambda e: e.tensor_tensor(a1[:, :nt], pt[:, :nt], rt[:, ci, :nt], ALU.mult),
                     r=[pb, rbuf], w=[a1b])
            else:
                S.op("pool", lambda e: e.tensor_tensor(a1[:, :nt], qt[:, :nt], rt[:, ci, :nt], ALU.mult),
                     r=[qbuf, rbuf], w=[a1b])
            S.op("dve", lambda e: e.tensor_tensor(a2[:, :nt], pa[:, :nt], rt[:, si, :nt], ALU.mult),
                 r=[pab, rbuf], w=[a2b])
            if kind == "d":
                S.op("pool", lambda e: e.tensor_tensor(o_t[:, :nt], a1[:, :nt], a2[:, :nt], ALU.add),
                     r=[a1b, a2b], w=[o_b])
            else:
                S.op("pool", lambda e: e.tensor_tensor(a1[:, :nt], a1[:, :nt], a2[:, :nt], ALU.add),
                     r=[a1b, a2b], w=[a1b])
                S.op("pool", lambda e: e.tensor_tensor(o_t[:, :nt], a1[:, :nt], r2[:, :nt], ALU.mult),
                     r=[a1b, r2b], w=[o_b])
            store(dst, o_t[:, :nt], o_b)

        B2 = os.environ.get("BISECT2", "dgn")
        for c in range(2 if "d" in B2 else 0):
            pt, pb = fm_matmul(c * 128)
            rope_chunk(pt, pb, "d", k.QdT[c, :, t0:t0 + nt])
        for c in range(2 if "d" in B2 else 0):
            pt, pb = fm_matmul(256 + c * 128)
            rope_chunk(pt, pb, "d", k.KdT[c, :, t0:t0 + nt])
        for c in range(3 if "g" in B2 else 0):
            pt, pb = fm_matmul(768 + c * 128)
            rope_chunk(pt, pb, "g", k.QgT[c, :, t0:t0 + nt], gcol=k.qkg[:, 0:1])
        if "g" in B2:
            pt, pb = fm_matmul(1152)
            rope_chunk(pt, pb, "g", k.KgT[0, :, t0:t0 + nt], gcol=k.qkg[:, 1:2])
        for c in range(9 if "n" in B2 else 0):
            pt, pb = fm_matmul(1408 + c * 128)
            o_t, o_b = of[cnt["of"] % 3]
            cnt["of"] += 1
            if c % 2 == 0:
                S.op("dve", lambda e: e.tensor_copy(o_t[:, :nt], pt[:, :nt]), r=[pb], w=[o_b])
            else:
                S.op("act", lambda e: e.activation(out=o_t[:, :nt], in_=pt[:, :nt], func=AF.Copy), r=[pb], w=[o_b])
            store(k.gqkvT[c, :, t0:t0 + nt], o_t[:, :nt], o_b)

        if os.environ.get("BISECT") == "2":
            continue
        v_t, v_b = vt[b % 2]
        z_t, z_b = zt[b % 2]
        ntile = nt // 128
        for tt in range(ntile):
            for (c0, n, kind) in ((512, 256, "dv"), (1280, 128, "gv"), (2560, 408, "zab")):
                pt, pb = p_tm[cnt["tm"] % 2]
                cnt["tm"] += 1

                def mm(e, c0=c0, n=n, pt=pt):
                    last = None
                    for kk in range(8):
                        last = e.matmul(pt[:, :n], lhsT=at[:, kk, tt * 128:(tt + 1) * 128], rhs=wbf[:, kk, c0:c0 + n],
                                        start=(kk == 0), stop=(kk == 7))
                    return last
                S.group("pe", mm, r=[b_w, abuf], w=[pb])
                if kind == "dv":
                    S.op("act", lambda e, pt=pt: e.activation(
                        out=v_t[:, tt, 0:260].rearrange("p (h d) -> p h d", d=65)[:, :, 0:64],
                        in_=pt[:, 0:256].rearrange("p (h d) -> p h d", d=64), func=AF.Copy), r=[pb], w=[v_b])
                elif kind == "gv":
                    S.op("act", lambda e, pt=pt: e.activation(
                        out=v_t[:, tt, 260:390].rearrange("p (h d) -> p h d", d=65)[:, :, 0:64],
                        in_=pt[:, 0:128].rearrange("p (h d) -> p h d", d=64), func=AF.Copy), r=[pb], w=[v_b])
                else:
                    S.op("dve", lambda e, pt=pt: e.tensor_copy(z_t[:, tt, :], pt[:, 0:408]), r=[pb], w=[z_b])
        S.dma(k.Vtok[t0:t0 + nt, :].rearrange("(n p) c -> p n c", p=128), v_t[:, :ntile, :], r=[v_b], sem=v_b)
        S.dma(k.zab[t0:t0 + nt, :].rearrange("(n p) c -> p n c", p=128), z_t[:, :ntile, :], r=[z_b], sem=z_b)
    S.barrier()
    P.close()


def host_consts():
    bf = ml_dtypes.bfloat16
    c = {}
    c["c_ones_d"] = np.full((128, 128), 1.0 / 1024, np.float32).astype(bf)
    blk = np.zeros((128, 128), np.float32)
    blk[:64, :64] = 1.0 / 64
    blk[64:, 64:] = 1.0 / 64
    c["c_blk64"] = blk.astype(bf)
    c["c_blk64s"] = (blk * 64).astype(bf)
    c["c_identb"] = np.eye(128, dtype=np.float32).astype(bf)
    c["c_identf"] = np.eye(128, dtype=np.float32)
    ti = np.arange(128)
    le = (ti[:, None] <= ti[None, :]).astype(np.float32)
    gt = (ti[:, None] > ti[None, :]).astype(np.float32)
    c["c_MI"] = np.ascontiguousarray(np.stack([le, le.T], axis=1))
    c["c_SU"] = np.ascontiguousarray(np.stack([gt, gt.T], axis=1))
    incl_f = (ti[None, :] >= ti[:, None]).astype(np.float32); strict_f = (ti[None, :] > ti[:, None]).astype(np.float32)
    emk_f = np.concatenate([incl_f, -strict_f], axis=1)
    emk_b = np.concatenate([incl_f.T, -strict_f.T], axis=1)
    c["c_EMK"] = np.ascontiguousarray(np.stack([emk_f, emk_b], axis=1))
    blk = lambda b_: (ti[:, None] // b_) == (ti[None, :] // b_)
    c["c_MK"] = np.stack([blk(64) & ~blk(32), blk(128) & ~blk(64), blk(32)]).astype(np.float32)

    def rp(group, nf):
        Rp = np.zeros((128, 128), np.float32)
        for g0 in range(0, 128, group):
            for a in range(2):
                for f in range(nf):
                    i0 = g0 + a * 2 * nf + f
                    i1 = g0 + a * 2 * nf + nf + f
                    Rp[i0, i1] = -1.0
                    Rp[i1, i0] = 1.0
        return np.ascontiguousarray(Rp.T).astype(bf)
    c["c_rp_d"] = rp(32, 8)
    c["c_rp_g"] = rp(64, 16)
    t = np.arange(4096)
    row = (t // 64).astype(np.float32)
    col = (t % 64).astype(np.float32)

    def tables(group, nf):
        inv = (np.float32(10000.0) ** (-np.arange(nf, dtype=np.float32) / np.float32(nf))).astype(np.float32)
        cos = np.zeros((128, 4096), np.float32)
        sin = np.zeros((128, 4096), np.float32)
        for p in range(128):
            w = p % group
            a = w // (2 * nf)
            f = w % nf
            pos = row if a == 0 else col
            ang = (pos * inv[f]).astype(np.float32)
            cos[p] = np.cos(ang)
            sin[p] = np.sin(ang)
        return cos, sin
    cd, sd = tables(32, 8)
    cg, sg = tables(64, 16)
    c["c_rope"] = np.stack([cd, sd, cg, sg]).astype(np.float32)
    pl = np.zeros((64, 2, 128), np.float32)
    for i in range(64):
        pl[i, 0, i] = 1.0
        pl[i, 1, 64 + i] = 1.0
    c["c_place"] = pl.astype(bf)
    return c


def col_layout(v, nch):
    v = np.asarray(v)
    return np.ascontiguousarray(np.swapaxes(v.reshape(v.shape[:-1] + (nch, 128)), -1, -2))


def host_inputs(inp, b):
    d = {}
    xT = np.concatenate([inp["ctx"][b], inp["x"][b]], axis=0).T
    d["xT"] = np.ascontiguousarray(xT)
    cc = np.stack([col_layout(inp["c"][b], 8), col_layout(inp["c_ctx"], 8)], axis=-1)
    d["ccin"] = np.ascontiguousarray(cc.astype(np.float32))
    d["ada_w"] = inp["ada_w"]
    d["ada_b"] = col_layout(inp["ada_b"], 48)
    d["norm1_g"] = col_layout(inp["norm1_g"], 8)
    d["norm2_g"] = col_layout(inp["norm2_g"], 8)
    d["final_g"] = col_layout(inp["final_norm_g"], 8)
    d["w_in"] = inp["w_in"]
    d["w_out"] = inp["w_out"]
    d["w_gu"] = inp["ffn_w_gu"]
    d["w_down"] = inp["ffn_w_down"]
    d["gdn_g"] = inp["gdn_norm_g"]
    d["gdn_alog"] = np.ascontiguousarray(inp["gdn_a_log"].reshape(-1, 12))
    d["gdn_dtb"] = np.ascontiguousarray(inp["gdn_dt_bias"].reshape(-1, 12))
    cwl = inp["gdn_conv_w"]
    d["conv_w"] = np.ascontiguousarray(cwl.reshape(cwl.shape[0], 5, 9, 128).transpose(0, 3, 2, 1))
    d["diff_lambda"] = np.ascontiguousarray(inp["diff_lambda"].reshape(-1, 128))
    d["diff_g"] = np.ascontiguousarray(np.tile(inp["diff_norm_g"], (1, 2))[:, :, None].astype(np.float32))
    qg = np.tile(inp["q_norm_g"], (1, 2))
    kg = np.tile(inp["k_norm_g"], (1, 2))
    d["qk_g"] = np.ascontiguousarray(np.stack([qg, kg], axis=-1).astype(np.float32))
    return d


def ph_att(k, l, need_ctx):
    S, nc = k.S, k.nc
    P = Phase(k)
    lam_init = 0.8 - 0.6 * math.exp(-0.3 * l)
    Kd, b_Kd = P.sb("Kd", [128, 2, 2, T], BF16)
    Kg, b_Kg = P.sb("Kg", [128, 3, T], BF16)
    V, b_V = P.sb("V", [128, 34, 390], BF16)
    S.op("pool", lambda e: e.memset(Kd[:], 0.0), w=[b_Kd])
    for c in range(2):
        for hi in range(2):
            for cc in range(2):
                r0 = hi * 64 + cc * 32
                S.dma(Kd[r0:r0 + 32, c, cc, :], k.KdT[c, r0:r0 + 32, :], w=[b_Kd], sem=b_Kd)
    for vi, (ka, kb) in enumerate(((0, 0), (0, 1), (1, 1))):
        S.dma(Kg[0:64, vi, :], k.KgT[0, ka * 64:ka * 64 + 64, :], w=[b_Kg], sem=b_Kg)
        S.dma(Kg[64:128, vi, :], k.KgT[0, kb * 64:kb * 64 + 64, :], w=[b_Kg], sem=b_Kg)
    S.dma(V[:], k.Vtok.rearrange("(n p) c -> p n c", p=128), w=[b_V], sem=b_V)
    lamt, b_lamt = P.sb("lamt", [128, 4, 32], F32)
    lamp, b_lamp = P.sb("lamp", [128, 2, 32], F32)
    lams, b_lams = P.sb("lams", [128, 5], F32)
    gd, b_gd = P.sb("gd", [128, 1], F32)
    ones_r, b_ones_r = P.sb("ones_r", [128, 64], F32)
    S.op("pool", lambda e: e.memset(ones_r[:], 1.0), w=[b_ones_r])
    S.dma(lamt[:].rearrange("p a b -> p (a b)"), k.diff_lambda[l:l + 1, :].partition_broadcast(128), w=[b_lamt], sem=b_lamt)
    S.dma(gd[:], k.diff_g[l], w=[b_gd], sem=b_gd)
    S.op("act", lambda e: e.mul(gd[:], gd[:], 1.0 - lam_init), r=[b_gd], w=[b_gd])
    S.op("dve", lambda e: e.tensor_tensor(lamp[:], lamt[:, 0:4:2, :], lamt[:, 1:4:2, :], ALU.mult), r=[b_lamt], w=[b_lamp])
    S.op("dve", lambda e: e.reduce_sum(lams[:, 0:2], lamp[:], axis=AX.X), r=[b_lamp], w=[b_lams])
    S.op("act", lambda e: e.activation(out=lams[:, 0:2], in_=lams[:, 0:2], func=AF.Exp), r=[b_lams], w=[b_lams])
    S.op("dve", lambda e: e.tensor_tensor(lams[:, 2:3], lams[:, 0:1], lams[:, 1:2], ALU.subtract), r=[b_lams], w=[b_lams])
    S.op("dve", lambda e: e.tensor_scalar(lams[:, 3:4], lams[:, 2:3], lam_init, None, ALU.add), r=[b_lams], w=[b_lams])
    S.op("dve", lambda e: e.tensor_scalar(lams[:, 4:5], lams[:, 3:4], -1.0, None, ALU.mult), r=[b_lams], w=[b_lams])
    lam_ap = lams[:, 3:4]
    neglam_ap = lams[:, 4:5]

    Qd = [P.sb("Qd%d" % i, [128, 2, 512], BF16) for i in range(2)]
    Qg = [P.sb("Qg%d" % i, [128, 3, 512], BF16) for i in range(2)]
    pT = [P.sb("pT%d" % i, [128, 2, 512], BF16) for i in range(3)]
    Oall, b_Oall = P.sb("Oall", [128, 14, 512], F32)
    ps_s = [P.ps("ps_s%d" % i, [128, 2, 512]) for i in range(2)]
    ps_o = [P.ps("ps_o%d" % i, [128, 512]) for i in range(4)]
    ps_x = [(ps_s[i][0][:, 0, :], ps_s[i][1]) for i in range(2)]
    w1 = [P.sb("w1_%d" % i, [64, 512], F32) for i in range(2)]
    w2 = [P.sb("w2_%d" % i, [64, 512], F32) for i in range(2)]
    wsq = [P.sb("wsq%d" % i, [64, 512], BF16) for i in range(2)]
    wr = [P.sb("wr%d" % i, [64, 512], F32) for i in range(2)]
    obf = [P.sb("obf%d" % i, [64, 512], BF16) for i in range(4)]
    mixc = [P.sb("mixc%d" % i, [128, 512], BF16) for i in range(2)]
    rcp = [P.sb("rcp%d" % i, [64, 512], F32) for i in range(2)]
    cnt = {"s": 0, "o": 0, "x": 0, "w": 0, "obf": 0, "mix": 0, "rc": 0}

    def load_q(b):
        t0, nt = BLOCKS[b]
        qd, qdb = Qd[b % 2]
        qg, qgb = Qg[b % 2]
        S.dma(qd[:, :, :nt], k.QdT[:, :, t0:t0 + nt].rearrange("c p t -> p c t"), w=[qdb], sem=qdb)
        S.dma(qg[:, :, :nt], k.QgT[:, :, t0:t0 + nt].rearrange("c p t -> p c t"), w=[qgb], sem=qgb)

    blocks = list(range(0 if need_ctx else 1, len(BLOCKS)))
    load_q(blocks[0])
    for bi, b in enumerate(blocks):
        t0, nt = BLOCKS[b]
        if bi + 1 < len(blocks):
            load_q(blocks[bi + 1])
        qd, qdb = Qd[b % 2]
        qg, qgb = Qg[b % 2]
        kts = [0, 1] if b == 0 else list(range(34))
        pairs = []
        for c in range(2):
            for cc in range(2):
                pairs.append(("d", c, cc, [(2 * c + hi) * 2 + cc for hi in range(2)], [(2 * c + hi) * 65 for hi in range(2)], 32 ** -0.5))
        for c in range(3):
            kva, kvb = (2 * c) // 3, (2 * c + 1) // 3
            pairs.append(("g", c, kva + kvb, [8 + 2 * c, 8 + 2 * c + 1], [260 + kva * 65, 260 + kvb * 65], 64 ** -0.5))
        steps = []
        for pi, pr in enumerate(pairs):
            for ki, kt in enumerate(kts):
                steps.append((pi, pr, ki, kt))
        LAG = 1
        pend = []
        po_of = {}

        def emit_pv(st):
            (pi, pr, ki, kt, ptile, ptb) = st
            (kind, c, var, us, v0s, sc) = pr
            if ki == 0:
                po_of[pi] = [ps_o[(cnt["o"] + i) % 4] for i in range(2)]
                cnt["o"] += 2
            for hi in range(2):
                po, pob = po_of[pi][hi]
                S.op("pe", lambda e, hi=hi, po=po: e.matmul(po[0:65, :nt], lhsT=V[:, kt, v0s[hi]:v0s[hi] + 65], rhs=ptile[:, hi, :nt],
                                                        start=(ki == 0), stop=(ki == len(kts) - 1)), r=[b_V, ptb], w=[pob])
                if ki == len(kts) - 1:
                    S.op("dve", lambda e, hi=hi, po=po: e.tensor_copy(Oall[0:65, us[hi], :nt], po[0:65, :nt]), r=[pob], w=[b_Oall])

        for (pi, pr, ki, kt) in steps:
            (kind, c, var, us, v0s, sc) = pr
            pst, psb = ps_s[cnt["s"] % 2]
            ptile, ptb = pT[cnt["s"] % 3]
            cnt["s"] += 1
            qt, qb_ = (qd, qdb) if kind == "d" else (qg, qgb)
            Kt, Kb = (Kd, b_Kd) if kind == "d" else (Kg, b_Kg)

            def mmQK(e, pst=pst, Kt=Kt, qt=qt, kind=kind, c=c, var=var, kt=kt):
                last = None
                for hi in range(2):
                    rows = slice(hi * 64, hi * 64 + 64)
                    if kind == "d":
                        lhsT = Kt[rows, c, var, kt * 128:(kt + 1) * 128]
                    else:
                        lhsT = Kt[rows, var, kt * 128:(kt + 1) * 128]
                    last = e.matmul(pst[:, hi, :nt], lhsT=lhsT, rhs=qt[rows, c, :nt], start=True, stop=True)
                return last
            S.group("pe", mmQK, r=[Kb, qb_], w=[psb])
            S.op("act", lambda e, pst=pst, ptile=ptile, sc=sc: e.activation(out=ptile[:, :, :nt], in_=pst[:, :, :nt], func=AF.Exp, scale=sc),
                 r=[psb], w=[ptb])
            pend.append((pi, pr, ki, kt, ptile, ptb))
            if len(pend) > LAG:
                emit_pv(pend.pop(0))
        while pend:
            emit_pv(pend.pop(0))
        def bcast(u):
            px, pxb = ps_x[cnt["x"] % 2]
            cnt["x"] += 1
            S.op("pe", lambda e: e.matmul(px[0:64, :nt], lhsT=ones_r[64:65, 0:64], rhs=Oall[64:65, u, :nt], start=True, stop=True),
                 r=[b_Oall, b_ones_r], w=[pxb])
            rc, rcb = rcp[cnt["rc"] % 2]
            cnt["rc"] += 1
            S.op("dve", lambda e: e.reciprocal(rc[:, :nt], px[0:64, :nt]), r=[pxb], w=[rcb])
            return rc, rcb

        def place(chunk, parts):
            px, pxb = ps_x[cnt["x"] % 2]
            cnt["x"] += 1
            for i, (ot, otb, hi) in enumerate(parts):
                S.op("pe", lambda e, ot=ot, hi=hi, i=i: e.matmul(px[:, :nt], lhsT=k.place[0:64, hi, :], rhs=ot[:, :nt],
                                                                 start=(i == 0), stop=(i == len(parts) - 1)),
                     r=[otb, k.b_place], w=[pxb])
            mt, mtb = mixc[cnt["mix"] % 2]
            cnt["mix"] += 1
            S.op("act", lambda e: e.activation(out=mt[:, :nt], in_=px[:, :nt], func=AF.Copy), r=[pxb], w=[mtb])
            S.dma(k.mixT[chunk, :, t0:t0 + nt], mt[:, :nt], r=[mtb], sem=mtb)

        parts = []
        for hh in range(4):
            i = cnt["w"] % 2
            cnt["w"] += 1
            a1, a1b = w1[i]
            a2, a2b = w2[i]
            sqt, sqb = wsq[i]
            rt_, rtb = wr[i]
            px, pxb = bcast(2 * hh)
            S.op("dve", lambda e: e.tensor_tensor(a1[:, :nt], Oall[0:64, 2 * hh, :nt], px[:, :nt], ALU.mult),
                 r=[b_Oall, pxb], w=[a1b])
            px, pxb = bcast(2 * hh + 1)
            S.op("dve", lambda e: e.tensor_tensor(a2[:, :nt], Oall[0:64, 2 * hh + 1, :nt], px[:, :nt], ALU.mult),
                 r=[b_Oall, pxb], w=[a2b])
            S.op("dve", lambda e: e.scalar_tensor_tensor(out=a1[:, :nt], in0=a2[:, :nt], scalar=neglam_ap[0:64, :], in1=a1[:, :nt],
                                                         op0=ALU.mult, op1=ALU.add), r=[a1b, a2b, b_lams], w=[a1b])
            S.op("act", lambda e: e.activation(out=sqt[:, :nt], in_=a1[:, :nt], func=AF.Square), r=[a1b], w=[sqb])
            px, pxb = ps_x[cnt["x"] % 2]
            cnt["x"] += 1
            S.op("pe", lambda e: e.matmul(px[0:64, :nt], lhsT=k.blk64[0:64, 0:64], rhs=sqt[:, :nt], start=True, stop=True),
                 r=[sqb, k.b_blk64], w=[pxb])
            rsqrt_eps(k, rt_[:, :nt], rtb, px[0:64, :nt], pxb)
            ot, otb = obf[cnt["obf"] % 4]
            cnt["obf"] += 1
            S.op("dve", lambda e: e.scalar_tensor_tensor(out=ot[:, :nt], in0=a1[:, :nt], scalar=gd[0:64, :], in1=rt_[:, :nt],
                                                         op0=ALU.mult, op1=ALU.mult), r=[a1b, rtb, b_gd], w=[otb])
            parts.append((ot, otb, hh % 2))
            if hh % 2 == 1:
                place(hh // 2, parts)
                parts = []
        for h in range(6):
            u = 8 + h
            px, pxb = bcast(u)
            ot, otb = obf[cnt["obf"] % 4]
            cnt["obf"] += 1
            S.op("dve", lambda e: e.tensor_tensor(ot[:, :nt], Oall[0:64, u, :nt], px[:, :nt], ALU.mult),
                 r=[b_Oall, pxb], w=[otb])
            parts.append((ot, otb, h % 2))
            if h % 2 == 1:
                place(2 + h // 2, parts)
                parts = []
    S.barrier()
    P.close()


def ph_gdn_a(k, l):
    S, nc = k.S, k.nc
    P = Phase(k)
    W = T + 8
    segs = [(2, 0, 256), (262, 256, 4096)]
    xin = [P.sb("xin%d" % i, [128, W], F32) for i in range(2)]
    acc = [P.sb("acc%d" % i, [128, T], F32) for i in range(2)]
    sl, b_sl = P.sb("sl", [128, T], F32)
    sqb, b_sqb = P.sb("sqb", [128, T], BF16)
    rn, b_rn = P.sb("rn", [128, T], F32)
    ob = [P.sb("gob%d" % i, [128, T], BF16) for i in range(2)]
    cw, b_cw = P.sb("cw", [128, 9, 5], F32)
    tk = [P.sb("tk%d" % i, [128, 4, 128], BF16) for i in range(2)]
    ps_n = [P.ps("ps_n%d" % i, [128, 512]) for i in range(2)]
    ps_t = [P.ps("ps_t%d" % i, [128, 4, 128], BF16) for i in range(2)]
    S.dma(cw[:], k.conv_w[l], w=[b_cw], sem=b_cw)
    for i in range(2):
        S.op("pool", lambda e, i=i: e.memset(xin[i][0][:], 0.0), w=[xin[i][1]])
    cnt = {"n": 0, "t": 0}
    for c in range(9):
        xt, xb = xin[c % 2]
        at, ab = acc[c % 2]
        for (p0, t0, n) in segs:
            S.dma(xt[:, p0:p0 + n], k.gqkvT[c, :, t0:t0 + n], w=[xb], sem=xb)
        eng = "dve"
        for (p0, t0, n) in segs:
            for tap in range(5):
                src = xt[:, p0 + tap - 2:p0 + tap - 2 + n]
                if tap == 0:
                    S.op(eng, lambda e, src=src, t0=t0, n=n: e.tensor_scalar(at[:, t0:t0 + n], src, cw[:, c, 0:1], None, ALU.mult),
                         r=[xb, b_cw], w=[ab])
                else:
                    S.op(eng, lambda e, src=src, t0=t0, n=n, tap=tap: e.scalar_tensor_tensor(
                        out=at[:, t0:t0 + n], in0=src, scalar=cw[:, c, tap:tap + 1], in1=at[:, t0:t0 + n],
                        op0=ALU.mult, op1=ALU.add), r=[xb, b_cw, ab], w=[ab])
        o_t, o_b = ob[c % 2]
        if c >= 6:
            S.op("act", lambda e: e.activation(out=o_t[:], in_=at[:], func=AF.Silu), r=[ab], w=[o_b])
        else:
            S.op("act", lambda e: e.activation(out=sl[:], in_=at[:], func=AF.Silu), r=[ab], w=[b_sl])
            S.op("act", lambda e: e.activation(out=sqb[:], in_=sl[:], func=AF.Square), r=[b_sl], w=[b_sqb])
            for t0 in range(0, T, 512):
                n = min(512, T - t0)
                pn, pnb = ps_n[cnt["n"] % 2]
                cnt["n"] += 1
                S.op("pe", lambda e, t0=t0, n=n, pn=pn: e.matmul(pn[:, :n], lhsT=k.blk64s[:], rhs=sqb[:, t0:t0 + n], start=True, stop=True),
                     r=[b_sqb, k.b_blk64s], w=[pnb])
                S.op("act", lambda e, t0=t0, n=n, pn=pn: e.activation(out=rn[:, t0:t0 + n], in_=pn[:, :n], func=AF.Ln, bias=k.epsc[:, 0:1]),
                     r=[pnb, k.b_epsc], w=[b_rn])
            S.op("act", lambda e: e.activation(out=rn[:], in_=rn[:], func=AF.Exp, scale=-0.5), r=[b_rn], w=[b_rn])
            sc = 0.125 if c < 3 else 1.0
            S.op("dve", lambda e: e.scalar_tensor_tensor(out=o_t[:], in0=sl[:], scalar=sc, in1=rn[:], op0=ALU.mult, op1=ALU.mult),
                 r=[b_sl, b_rn], w=[o_b])
            dst = k.QnT[c] if c < 3 else k.KnT[c - 3]
            S.dma(dst, o_t[:], r=[o_b], sem=o_b)
        if c >= 3:
            dstT = k.Ktok if c < 6 else k.Vtok2
            cc = (c - 3) % 3
            for g0 in range(0, 34, 4):
                ng = min(4, 34 - g0)
                pt, ptb = ps_t[cnt["t"] % 2]
                tt, ttb = tk[cnt["t"] % 2]
                cnt["t"] += 1

                def tr(e, g0=g0, ng=ng, pt=pt):
                    last = None
                    for i in range(ng):
                        last = e.transpose(pt[:, i, :], o_t[:, (g0 + i) * 128:(g0 + i + 1) * 128], k.identb[:])
                    return last
                S.group("pe", tr, r=[o_b, k.b_identb], w=[ptb])
                S.op("act" if (g0 // 4) % 2 == 0 else "dve",
                     (lambda e, pt=pt, tt=tt, ng=ng: e.activation(out=tt[:, :ng, :], in_=pt[:, :ng, :], func=AF.Copy)) if (g0 // 4) % 2 == 0
                     else (lambda e, pt=pt, tt=tt, ng=ng: e.tensor_copy(tt[:, :ng, :], pt[:, :ng, :])), r=[ptb], w=[ttb])
                S.dma(dstT[g0 * 128:(g0 + ng) * 128, cc * 128:(cc + 1) * 128].rearrange("(n p) c -> p n c", p=128), tt[:, :ng, :],
                      r=[ttb], sem=ttb)
    S.barrier()
    P.close()


def ph_gdn_b(k, l, need_ctx):
    S, nc = k.S, k.nc
    P = Phase(k)
    NCH = 34
    MI, b_MI = P.sb("MI", [128, 2, 128], F32)
    SU, b_SU = P.sb("SU", [128, 2, 128], F32)
    EMK, b_EMK = P.sb("EMK", [128, 2, 256], F32)
    idf, b_idf = P.sb("idf", [128, 128], F32)
    on128, b_on128 = P.sb("on128", [128, 128], F32)
    gnb, b_gnb = P.sb("gnb", [128, 64], F32)
    nal, b_nal = P.sb("nal", [128, 12], F32)
    dtb, b_dtb = P.sb("dtb", [128, 12], F32)
    onec, b_onec = P.sb("onec", [128, 1], F32)
    S.dma(MI[:], k.c_MI, w=[b_MI], sem=b_MI)
    S.dma(SU[:], k.c_SU, w=[b_SU], sem=b_SU)
    S.dma(EMK[:], k.c_EMK, w=[b_EMK], sem=b_EMK)
    S.dma(idf[:], k.c_identf, w=[b_idf], sem=b_idf)
    mk_ = [P.sb("MK%d" % i, [128, 128], F32) for i in range(3)]
    MK = [x[0] for x in mk_]
    b_MK = Buf("MK")
    for i in range(3):
        S.dma(MK[i][:], k.c_MK[i], w=[b_MK], sem=mk_[i][1])
    S.op("pool", lambda e: e.memset(on128[:], 1.0), w=[b_on128])
    S.op("pool", lambda e: e.memset(onec[:], 1.0), w=[b_onec])
    S.dma(gnb[:], k.gdn_g[l:l + 1, :].partition_broadcast(128), w=[b_gnb], sem=b_gnb)
    S.dma(nal[:], k.gdn_alog[l:l + 1, :].partition_broadcast(128), w=[b_nal], sem=b_nal)
    S.dma(dtb[:], k.gdn_dtb[l:l + 1, :].partition_broadcast(128), w=[b_dtb], sem=b_dtb)
    S.op("act", lambda e: e.activation(out=nal[:], in_=nal[:], func=AF.Exp), r=[b_nal], w=[b_nal])
    S.op("dve", lambda e: e.tensor_scalar(nal[:], nal[:], -1.0, None, ALU.mult), r=[b_nal], w=[b_nal])
    ab, b_ab = P.sb("ab", [128, NCH, 24], F32)
    G, b_G = P.sb("G", [128, NCH, 12], F32)
    LB, b_LB = P.sb("LB", [128, NCH, 12], F32)
    BETA, b_BETA = P.sb("BETA", [128, NCH, 12], F32)
    KDS, b_KDS = P.sb("KDS", [128, NCH, 12], F32)
    GL, b_GL = P.sb("GL", [128, NCH, 12], F32)
    GLP, b_GLP = P.sb("GLP", [128, NCH, 2, 3], F32)
    S.dma(ab[:], k.zab[:, 384:408].rearrange("(n p) c -> p n c", p=128), w=[b_ab], sem=b_ab)
    S.op("dve", lambda e: e.tensor_tensor(G[:], ab[:, :, 0:12], dtb[:].unsqueeze(1).broadcast_to([128, NCH, 12]), ALU.add),
         r=[b_ab, b_dtb], w=[b_G])
    S.op("act", lambda e: e.activation(out=G[:], in_=G[:], func=AF.Exp), r=[b_G], w=[b_G])
    S.op("act", lambda e: e.activation(out=G[:], in_=G[:], func=AF.Ln, bias=onec[:, 0:1]), r=[b_G, b_onec], w=[b_G])
    S.op("dve", lambda e: e.tensor_tensor(G[:], G[:], nal[:].unsqueeze(1).broadcast_to([128, NCH, 12]), ALU.mult),
         r=[b_G, b_nal], w=[b_G])
    S.op("act", lambda e: e.activation(out=LB[:], in_=ab[:, :, 12:24], func=AF.Exp, scale=-1.0), r=[b_ab], w=[b_LB])
    S.op("act", lambda e: e.activation(out=LB[:], in_=LB[:], func=AF.Ln, bias=onec[:, 0:1]), r=[b_LB, b_onec], w=[b_LB])
    S.op("dve", lambda e: e.tensor_scalar(LB[:], LB[:], -1.0, None, ALU.mult), r=[b_LB], w=[b_LB])
    S.op("act", lambda e: e.activation(out=BETA[:], in_=LB[:], func=AF.Exp), r=[b_LB], w=[b_BETA])
    PG = []
    for g in range(2):
        t_, _ = P.ps("PG%d" % g, [128, 3, 512])
        PG.append((t_, [Buf("PG%d_%d" % (g, i), excl=True) for i in range(3)]))
    bA = [(PG[0][0][:, i, :], PG[0][1][i]) for i in range(3)]
    bM = [(PG[1][0][:, i, :], PG[1][1][i]) for i in range(2)]
    bX = (PG[1][0][:, 2, :], PG[1][1][2])
    PBK = [bA[0], bA[1], bA[2], bM[0], bM[1], bX]
    bS = [P.ps("bS%d" % i, [128, 512]) for i in range(2)]
    for half in range(2):
        pt, pb = bA[half]
        n0 = half * 17

        def mm(e, pt=pt, n0=n0):
            last = None
            for i in range(17):
                n = n0 + i
                for d in range(2):
                    last = e.matmul(pt[:, i * 24 + d * 6:i * 24 + d * 6 + 6], lhsT=SU[:, d, :], rhs=G[:, n, d * 6:d * 6 + 6],
                                    start=True, stop=True)
                last = e.matmul(pt[:, i * 24 + 12:i * 24 + 24], lhsT=on128[:], rhs=G[:, n, :], start=True, stop=True)
            return last
        S.group("pe", mm, r=[b_SU, b_G, b_on128], w=[pb])
        S.op("act", lambda e, pt=pt, n0=n0: e.activation(out=KDS[:, n0:n0 + 17, :],
                                                         in_=pt[:, 0:408].rearrange("p (n c) -> p n c", c=24)[:, :, 0:12], func=AF.Exp),
             r=[pb], w=[b_KDS])
        S.op("act", lambda e, pt=pt, n0=n0: e.activation(out=GL[:, n0:n0 + 17, :],
                                                         in_=pt[:, 0:408].rearrange("p (n c) -> p n c", c=24)[:, :, 12:24], func=AF.Exp),
             r=[pb], w=[b_GL])
    GLv = GL[:].rearrange("p n (d c h) -> p n d c h", d=2, c=3)
    S.op("dve", lambda e: e.tensor_copy(GLP[0:64], GLv[0:64, :, :, :, 0]), r=[b_GL], w=[b_GLP])
    S.op("dve", lambda e: e.tensor_copy(GLP[64:128], GLv[64:128, :, :, :, 1]), r=[b_GL], w=[b_GLP])

    GST = int(os.environ.get("GST", "99"))
    if GST <= 1:
        S.barrier(); P.close(); return
    Ofin, b_Ofin = P.sb("Ofin", [128, NCH, 384], F32)
    def mk(i):
        w = K()
        w.qk, w.b_qk = P.sb("cqk%d" % i, [128, 6, 128], BF16)
        w.kt, w.b_kt = P.sb("ckt%d" % i, [128, 384], BF16)
        w.vt, w.b_vt = P.sb("cvt%d" % i, [128, 384], BF16)
        w.rhsD, w.b_rhsD = P.sb("rhsD%d" % i, [128, 6, 2, 128], F32)
        w.Em, w.b_Em = P.sb("Em%d" % i, [128, 6, 256], F32)
        w.EB, w.b_EB = P.sb("EB%d" % i, [128, 6, 256], F32)
        w.qkTm, w.b_qkTm = P.sb("qkTm%d" % i, [128, 6, 128], BF16)
        def grp(nm, shape, dt):
            xs = [P.sb("%s%d_%d" % (nm, i, g), shape, dt) for g in range(2)]
            return [x[0] for x in xs], [x[1] for x in xs]
        w.Z, w.b_Zg = grp("Z", [128, 3, 3, 128], BF16)
        w.T32, w.b_T32g = grp("T32", [128, 3, 128], F32)
        w.O1T, w.b_O1Tg = grp("O1T", [128, 3, 128], BF16)
        w.O2T, w.b_O2Tg = grp("O2T", [128, 3, 128], BF16)
        w.NY, w.b_NYg = grp("NY", [128, 3, 128], BF16)
        w.QdT, w.b_QdT = P.sb("QdT%d" % i, [128, 3, 128], BF16)
        w.RwT, w.b_RwT = P.sb("RwT%d" % i, [128, 3, 128], BF16)
        w.KD, w.b_KD = P.sb("KD%d" % i, [128, 3, 2, 128], BF16)
        w.Ru, w.b_Ru = P.sb("Ru%d" % i, [128, 6, 64], F32)
        S.op("pool", lambda e: e.memset(w.KD[:], 0.0), w=[w.b_KD])
        return w
    WS = [mk(0), mk(1)]
    S32 = [P.sb("S32_%d" % i, [128, 64], F32) for i in range(6)]
    Sbf = [P.sb("Sbf%d" % i, [128, 128], BF16) for i in range(6)]
    Xb = [P.sb("Xb%d" % i, [128, 128], BF16) for i in range(3)]
    vnb = [P.sb("vnb%d" % i, [128, 128], BF16) for i in range(3)]
    for i in range(6):
        S.op("pool", lambda e, i=i: e.memset(S32[i][0][:], 0.0), w=[S32[i][1]])
        S.op("pool", lambda e, i=i: e.memset(Sbf[i][0][:], 0.0), w=[Sbf[i][1]])
    bSc = [bS[0], bS[1], bS[0]]

    def prep(w, n, d):
        t0 = n * 128
        S.dma(w.qk[:, 0:3, :], k.QnT[:, :, t0:t0 + 128].rearrange("c p t -> p c t"), w=[w.b_qk], sem=w.b_qk)
        S.dma(w.qk[:, 3:6, :], k.KnT[:, :, t0:t0 + 128].rearrange("c p t -> p c t"), w=[w.b_qk], sem=w.b_qk)
        S.dma(w.kt[:], k.Ktok[t0:t0 + 128, :], w=[w.b_kt], sem=w.b_kt)
        S.dma(w.vt[:], k.Vtok2[t0:t0 + 128, :], w=[w.b_vt], sem=w.b_vt)
        for h in range(6):
            u = d * 6 + h
            S.op("dve", lambda e, h=h, u=u: e.tensor_scalar(w.rhsD[:, h, 0, :], MI[:, d, :], G[:, n, u:u + 1], None, ALU.mult),
                 r=[b_MI, b_G], w=[w.b_rhsD])
            S.op("dve", lambda e, h=h, u=u: e.scalar_tensor_tensor(out=w.rhsD[:, h, 1, :], in0=idf[:], scalar=LB[:, n, u:u + 1],
                                                                    in1=w.rhsD[:, h, 0, :], op0=ALU.mult, op1=ALU.add),
                 r=[b_idf, b_LB, w.b_rhsD], w=[w.b_rhsD])
        for c in range(3):
            for (lhs, lhsb, (pt, pb), dst, dstb) in ((SU[:, d, :], b_SU, bA[c], w.Em, w.b_Em), (on128[:], b_on128, PBK[3 + c], w.EB, w.b_EB)):
                def mmD(e, c=c, pt=pt, lhs=lhs):
                    last = None
                    for hp in range(2):
                        last = e.matmul(pt[:, hp * 256:(hp + 1) * 256], lhsT=lhs,
                                        rhs=w.rhsD[:, 2 * c + hp, :, :].rearrange("p a b -> p (a b)"), start=True, stop=True)
                    return last
                S.group("pe", mmD, r=[lhsb, w.b_rhsD], w=[pb])
                S.op("act", lambda e, c=c, pt=pt, dst=dst: e.activation(out=dst[:, 2 * c:2 * c + 2, :].rearrange("p a b -> p (a b)"), in_=pt[:], func=AF.Exp),
                     r=[pb], w=[dstb])
        S.op("dve", lambda e: e.tensor_tensor(w.Em[:], w.Em[:], EMK[:, d, :].unsqueeze(1).broadcast_to([128, 6, 256]), ALU.mult),
             r=[w.b_Em, b_EMK], w=[w.b_Em])
        tick()
        kslots = [([0, 2], bA[0]), ([1, 3], bA[1]), ([4], bA[2]), ([5], bM[0])]
        for heads, (pt, pb) in kslots:
            def mmK(e, heads=heads, pt=pt):
                last = None
                for i, h in enumerate(heads):
                    ps_ = slice((h % 2) * 64, (h % 2) * 64 + 64)
                    cq = h // 2
                    e.matmul(pt[:, i * 256:i * 256 + 128], lhsT=w.qk[ps_, 3 + cq, :], rhs=w.qk[ps_, cq, :], start=True, stop=True)
                    last = e.matmul(pt[:, i * 256 + 128:i * 256 + 256], lhsT=w.qk[ps_, 3 + cq, :], rhs=w.qk[ps_, 3 + cq, :],
                                    start=True, stop=True)
                return last
            S.group("pe", mmK, r=[w.b_qk], w=[pb])
            nh = len(heads)
            hs = slice(heads[0], heads[-1] + 1, 2)
            pv = pt[:, 0:nh * 256].rearrange("p (h x c) -> p h x c", h=nh, x=2)
            S.op("dve", lambda e, hs=hs, pv=pv: e.tensor_tensor(w.qkTm[:, hs, :], pv[:, :, 0, :], w.Em[:, hs, 0:128], ALU.mult),
                 r=[pb, w.b_Em], w=[w.b_qkTm])
            for i, h in enumerate(heads):
                S.op("dve", lambda e, i=i, h=h, pv=pv: e.tensor_tensor(w.T32[h // 3][:, h % 3, :], pv[:, i, 1, :], w.Em[:, h, 128:256], ALU.mult),
                     r=[pb, w.b_Em], w=[w.b_T32g[h // 3]])
        tick()
        m3 = lambda t: t[:].unsqueeze(1).broadcast_to([128, 3, 128])
        for g in range(2):
            T32, T32b = w.T32[g], w.b_T32g[g]
            S.op("dve", lambda e, g=g, T32=T32: e.tensor_tensor(w.Z[g][:, :, 1, :], T32[:], m3(MK[2]), ALU.mult), r=[T32b, b_MK], w=[w.b_Zg[g]])
            S.op("pool", lambda e, g=g: e.tensor_tensor(w.Z[g][:, :, 2, :], w.Z[g][:, :, 1, :], m3(idf), ALU.add), r=[w.b_Zg[g], b_idf], w=[w.b_Zg[g]])
            S.op("pool", lambda e, g=g, T32=T32: e.tensor_tensor(w.O1T[g][:], T32[:], m3(MK[0]), ALU.mult), r=[T32b, b_MK], w=[w.b_O1Tg[g]])
            S.op("pool", lambda e, g=g, T32=T32: e.tensor_tensor(w.O2T[g][:], T32[:], m3(MK[1]), ALU.mult), r=[T32b, b_MK], w=[w.b_O2Tg[g]])
        tick()

        def pe3(g, fn, r):
            pG, pGb = PG[g]

            def f(e):
                last = None
                for i in range(3):
                    last = fn(e, pG, i)
                return last
            S.group("pe", f, r=r, w=pGb)
        for g in range(2):
            Z = w.Z[g]
            pe3(g, lambda e, pG, i, Z=Z: e.matmul(pG[:, i, 0:128], lhsT=Z[:, i, 1, :], rhs=k.identb[:], start=True, stop=True), [w.b_Zg[g], k.b_identb])
        for g in range(2):
            pG, pGb = PG[g]
            S.op("act", lambda e, g=g, pG=pG: e.activation(out=w.Z[g][:, :, 0, :], in_=pG[:, :, 0:128], func=AF.Copy), r=pGb, w=[w.b_Zg[g]])
        for j in range(5):
            tick()
            for g in range(2):
                Z = w.Z[g]
                if j < 4:
                    pe3(g, lambda e, pG, i, Z=Z: e.matmul(pG[:, i, 0:128], lhsT=Z[:, i, 1, :], rhs=Z[:, i, 0, :], start=True, stop=True), [w.b_Zg[g]])
                if j == 0:
                    pe3(g, lambda e, pG, i, Z=Z: e.matmul(pG[:, i, 128:256], lhsT=Z[:, i, 0, :], rhs=Z[:, i, 1, :], start=True, stop=True), [w.b_Zg[g]])
                elif j < 4:
                    pe3(g, lambda e, pG, i, Z=Z: e.matmul(pG[:, i, 128:384], lhsT=Z[:, i, 0, :], rhs=Z[:, i, 1:3, :].rearrange("p a b -> p (a b)"),
                                                          start=True, stop=True), [w.b_Zg[g]])
                else:
                    pe3(g, lambda e, pG, i, Z=Z: e.matmul(pG[:, i, 256:384], lhsT=Z[:, i, 0, :], rhs=Z[:, i, 2, :], start=True, stop=True), [w.b_Zg[g]])
            for g in range(2):
                pG, pGb = PG[g]
                Z = w.Z[g]
                if j >= 1:
                    S.op("dve", lambda e, Z=Z, pG=pG: e.tensor_tensor(Z[:, :, 2, :], Z[:, :, 2, :], pG[:, :, 256:384], ALU.add), r=pGb + [w.b_Zg[g]], w=[w.b_Zg[g]])
                if j < 4:
                    S.op("act", lambda e, Z=Z, pG=pG: e.activation(out=Z[:, :, 0:2, :].rearrange("p h a b -> p h (a b)"), in_=pG[:, :, 0:256], func=AF.Copy),
                         r=pGb, w=[w.b_Zg[g]])
        tick()
        for g in range(2):
            Z = w.Z[g]
            pe3(g, lambda e, pG, i, Z=Z: e.matmul(pG[:, i, 128:256], lhsT=Z[:, i, 2, :], rhs=k.identb[:], start=True, stop=True), [w.b_Zg[g], k.b_identb])
        for g in range(2):
            pG, pGb = PG[g]
            S.op("act", lambda e, g=g, pG=pG: e.activation(out=w.Z[g][:, :, 1, :], in_=pG[:, :, 128:256], func=AF.Copy), r=pGb, w=[w.b_Zg[g]])
        for mi in range(2):
            tick()
            OT = w.O1T if mi == 0 else w.O2T
            OTb = w.b_O1Tg if mi == 0 else w.b_O2Tg
            for g in range(2):
                Z = w.Z[g]
                pe3(g, lambda e, pG, i, Z=Z, OTg=OT[g]: e.matmul(pG[:, i, 0:128], lhsT=OTg[:, i, :], rhs=Z[:, i, 1, :], start=True, stop=True), [OTb[g], w.b_Zg[g]])
            for g in range(2):
                pG, pGb = PG[g]
                S.op("act", lambda e, g=g, pG=pG: e.activation(out=w.NY[g][:], in_=pG[:, :, 0:128], func=AF.Copy), r=pGb, w=[w.b_NYg[g]])
            for g in range(2):
                Z = w.Z[g]
                pe3(g, lambda e, pG, i, Z=Z, NYg=w.NY[g]: e.matmul(pG[:, i, 256:384], lhsT=NYg[:, i, :], rhs=Z[:, i, 2, :], start=True, stop=True), [w.b_NYg[g], w.b_Zg[g]])
                if mi == 0:
                    pe3(g, lambda e, pG, i, Z=Z, NYg=w.NY[g]: e.matmul(pG[:, i, 128:256], lhsT=Z[:, i, 2, :], rhs=NYg[:, i, :], start=True, stop=True), [w.b_NYg[g], w.b_Zg[g]])
            for g in range(2):
                pG, pGb = PG[g]
                Z = w.Z[g]
                if mi == 0:
                    S.op("dve", lambda e, Z=Z, pG=pG: e.tensor_tensor(Z[:, :, 1:3, :].rearrange("p h a b -> p h (a b)"), Z[:, :, 1:3, :].rearrange("p h a b -> p h (a b)"),
                                                                       pG[:, :, 128:384], ALU.add), r=pGb + [w.b_Zg[g]], w=[w.b_Zg[g]])
                else:
                    S.op("dve", lambda e, Z=Z, pG=pG: e.tensor_tensor(Z[:, :, 2, :], Z[:, :, 2, :], pG[:, :, 256:384], ALU.add), r=pGb + [w.b_Zg[g]], w=[w.b_Zg[g]])
        for half in range(2):
            ps_ = slice(half * 64, half * 64 + 64)
            S.op("pool", lambda e, ps_=ps_, half=half: e.tensor_tensor(w.QdT[ps_, :, :], w.qk[ps_, 0:3, :], w.EB[ps_, half:6:2, 0:128], ALU.mult),
                 r=[w.b_qk, w.b_EB], w=[w.b_QdT])
            S.op("pool", lambda e, ps_=ps_, half=half: e.tensor_tensor(w.RwT[ps_, :, :], w.qk[ps_, 3:6, :], w.EB[ps_, half:6:2, 128:256], ALU.mult),
                 r=[w.b_qk, w.b_EB], w=[w.b_RwT])
        KDv = w.KD[:].rearrange("p c a b -> p c (a b)").rearrange("p c (x y) -> p c x y", y=64)[:, :, 0:4:3, :]
        S.op("dve", lambda e: e.tensor_tensor(KDv, w.kt[:].rearrange("p (c a y) -> p c a y", c=3, a=2),
                                              KDS[:, n, d * 6:d * 6 + 6].rearrange("p (c a) -> p c a", a=2).unsqueeze(3).broadcast_to([128, 3, 2, 64]),
                                              ALU.mult), r=[w.b_kt, b_KDS], w=[w.b_KD])
        S.op("dve", lambda e: e.tensor_tensor(w.Ru[:], w.vt[:].rearrange("p (h y) -> p h y", y=64),
                                              BETA[:, n, d * 6:d * 6 + 6].unsqueeze(2).broadcast_to([128, 6, 64]), ALU.mult),
             r=[w.b_vt, b_BETA], w=[w.b_Ru])

    def scan(w, n, d, first_pass):
        prs = []
        for c in range(3):
            bank = bS[c % 2]
            prs.append((c, (bank[0][:, (c // 2) * 256:(c // 2) * 256 + 256], bank[1]), S32[d * 3 + c], Sbf[d * 3 + c], Xb[c], vnb[c]))
        for (c, (bk, bkb), (s32, s32b), (sbf, sbfb), (xb, xbb), (vb, vbb)) in prs:
            S.op("pe", lambda e, bk=bk, c=c, sbf=sbf: e.matmul(bk[:, 0:128], lhsT=w.RwT[:, c, :], rhs=sbf[:], start=True, stop=True),
                 r=[w.b_RwT, sbfb], w=[bkb])
            if c == 1:
                yield
        yield
        for (c, (bk, bkb), (s32, s32b), (sbf, sbfb), (xb, xbb), (vb, vbb)) in prs:
            S.op("dve", lambda e, bk=bk, c=c, xb=xb: e.tensor_tensor(xb[:], w.Ru[:, 2 * c:2 * c + 2, :].rearrange("p a b -> p (a b)"), bk[:, 0:128], ALU.subtract),
                 r=[w.b_Ru, bkb], w=[xbb])

            def mmV(e, bk=bk, c=c, xb=xb):
                last = None
                for hp in range(2):
                    last = e.matmul(bk[:, 128 + hp * 64:128 + hp * 64 + 64], lhsT=w.Z[(2 * c + hp) // 3][:, (2 * c + hp) % 3, 2, :], rhs=xb[:, hp * 64:hp * 64 + 64],
                                    start=True, stop=True)
                return last
            S.group("pe", mmV, r=[w.b_Zg[0], w.b_Zg[1], xbb], w=[bkb])
            if c == 1:
                yield
        yield
        for (c, (bk, bkb), (s32, s32b), (sbf, sbfb), (xb, xbb), (vb, vbb)) in prs:
            S.op("act", lambda e, bk=bk, vb=vb: e.activation(out=vb[:], in_=bk[:, 128:256], func=AF.Copy), r=[bkb], w=[vbb])

            def mmO(e, bk=bk, c=c, sbf=sbf, vb=vb):
                e.matmul(bk[:, 0:128], lhsT=w.QdT[:, c, :], rhs=sbf[:], start=True, stop=False)
                for hp in range(2):
                    e.matmul(bk[:, hp * 64:hp * 64 + 64], lhsT=w.qkTm[:, 2 * c + hp, :], rhs=vb[:, hp * 64:hp * 64 + 64],
                             start=False, stop=(hp == 1))
                e.matmul(bk[:, 128:192], lhsT=w.KD[:, c, 0, :], rhs=vb[:, 0:64], start=True, stop=False)
                last = e.matmul(bk[:, 128:192], lhsT=w.KD[:, c, 1, :], rhs=vb[:, 64:128], start=False, stop=True)
                return last
            S.group("pe", mmO, r=[w.b_QdT, sbfb, w.b_qkTm, vbb, w.b_KD], w=[bkb])
            if c == 1:
                yield
        yield
        for (c, (bk, bkb), (s32, s32b), (sbf, sbfb), (xb, xbb), (vb, vbb)) in prs:
            S.op("dve", lambda e, bk=bk, c=c, s32=s32: e.scalar_tensor_tensor(out=s32[:], in0=s32[:], scalar=GLP[:, n, d, c:c + 1], in1=bk[:, 128:192],
                                                                             op0=ALU.mult, op1=ALU.add), r=[s32b, b_GLP, bkb], w=[s32b])
            S.op("pool", lambda e, sbf=sbf, s32=s32: e.tensor_copy(sbf[0:64, 0:64], s32[0:64, :]), r=[s32b], w=[sbfb])
            S.op("pool", lambda e, sbf=sbf, s32=s32: e.tensor_copy(sbf[64:128, 64:128], s32[64:128, :]), r=[s32b], w=[sbfb])
            ov = Ofin[:, n, 2 * c * 64:(2 * c + 2) * 64]
            if first_pass:
                S.op("act", lambda e, ov=ov, bk=bk: e.activation(out=ov, in_=bk[:, 0:128], func=AF.Copy), r=[bkb], w=[b_Ofin])
            else:
                S.op("dve", lambda e, ov=ov, bk=bk: e.tensor_tensor(ov, ov, bk[:, 0:128], ALU.add), r=[bkb, b_Ofin], w=[b_Ofin])
            if c == 1:
                yield

    cur_scan = [None]

    def tick():
        g_ = cur_scan[0]
        if g_ is not None:
            try:
                next(g_)
            except StopIteration:
                cur_scan[0] = None

    def drain():
        while cur_scan[0] is not None:
            tick()

    orders = [list(range(NCH)), [1, 0] + list(range(NCH - 1, 1, -1))]
    step = 0
    if GST in (2, 3):
        w = WS[0]
        DN, DD = int(os.environ.get("DN", "0")), int(os.environ.get("DD", "0"))
        prep(w, DN, DD)
        if GST == 3:
            cur_scan[0] = scan(w, DN, DD, True)
            drain()
        S.barrier()
        def dump(name, ap, shape, dt=F32):
            o = nc.dram_tensor(name, list(shape), dt, kind="ExternalOutput").ap()
            S.dma(o, ap, sem=Buf(name))
        dump("d_G", G[:].rearrange("p n c -> p (n c)"), [128, NCH * 12])
        dump("d_LB", LB[:].rearrange("p n c -> p (n c)"), [128, NCH * 12])
        dump("d_KDS", KDS[:].rearrange("p n c -> p (n c)"), [128, NCH * 12])
        dump("d_GLP", GLP[:].rearrange("p n d c -> p (n d c)"), [128, NCH * 6])
        dump("d_Em", w.Em[:].rearrange("p a b -> p (a b)"), [128, 6 * 256])
        dump("d_EB", w.EB[:].rearrange("p a b -> p (a b)"), [128, 6 * 256])
        dump("d_qkTm", w.qkTm[:].rearrange("p a b -> p (a b)"), [128, 6 * 128], BF16)
        dump("d_PT32", w.T32[0][:], [128, 3, 128])
        dump("d_MP", w.Z[0][:, :, 2, :], [128, 3, 128], BF16)
        dump("d_QdT", w.QdT[:].rearrange("p a b -> p (a b)"), [128, 384], BF16)
        dump("d_RwT", w.RwT[:].rearrange("p a b -> p (a b)"), [128, 384], BF16)
        dump("d_KD", w.KD[:].rearrange("p a b c -> p (a b c)"), [128, 768], BF16)
        dump("d_Ru", w.Ru[:].rearrange("p a b -> p (a b)"), [128, 384])
        dump("d_Ofin", Ofin[:, DN, :], [128, 384])
        dump("d_S32", S32[DD * 3][0][:], [128, 64])
        S.barrier(); P.close(); return
    for d in range(2):
        order = orders[d]
        prep(WS[step % 2], order[0], d)
        for i, n in enumerate(order):
            w = WS[step % 2]
            step += 1
            cur_scan[0] = scan(w, n, d, d == 0)
            if i + 1 < len(order):
                prep(WS[step % 2], order[i + 1], d)
            drain()
    if GST == 4:
        S.barrier(); P.close(); return

    zt = [P.sb("zt%d" % i, [128, 384], F32) for i in range(2)]
    sqo, b_sqo = P.sb("sqo", [128, 384], F32)
    ssum, b_ssum = P.sb("ssum", [128, 6], F32)
    yb = [P.sb("yb%d" % i, [128, 384], F32) for i in range(2)]
    yT = [P.sb("yT%d" % i, [128, 3, 512], BF16) for i in range(2)]
    tiles = list(range(0 if need_ctx else 2, NCH))
    groups = []
    if need_ctx:
        groups.append([0, 1])
    for g0 in range(2, NCH, 4):
        groups.append(list(range(g0, g0 + 4)))
    cntr = 0
    for gi, grp in enumerate(groups):
        yt_, ytb = yT[gi % 2]
        for ti, n in enumerate(grp):
            z_, zb = zt[cntr % 2]
            y_, ybb = yb[cntr % 2]
            cntr += 1
            S.dma(z_[:], k.zab[n * 128:(n + 1) * 128, 0:384], w=[zb], sem=zb)
            S.op("act", lambda e, z_=z_: e.activation(out=z_[:], in_=z_[:], func=AF.Silu), r=[zb], w=[zb])
            o_ = Ofin[:, n, :]
            S.op("dve", lambda e, o_=o_: e.tensor_tensor(sqo[:], o_, o_, ALU.mult), r=[b_Ofin], w=[b_sqo])
            S.op("dve", lambda e: e.reduce_sum(ssum[:], sqo[:].rearrange("p (h y) -> p h y", y=64), axis=AX.X), r=[b_sqo], w=[b_ssum])
            S.op("act", lambda e: e.activation(out=ssum[:], in_=ssum[:], func=AF.Ln, scale=1.0 / 64, bias=k.epsc[:, 0:1]), r=[b_ssum, k.b_epsc], w=[b_ssum])
            S.op("act", lambda e: e.activation(out=ssum[:], in_=ssum[:], func=AF.Exp, scale=-0.5), r=[b_ssum], w=[b_ssum])
            S.op("dve", lambda e, o_=o_: e.tensor_tensor(sqo[:].rearrange("p (h y) -> p h y", y=64), o_.rearrange("p (h y) -> p h y", y=64),
                                                          ssum[:].unsqueeze(2).broadcast_to([128, 6, 64]), ALU.mult), r=[b_Ofin, b_ssum], w=[b_sqo])
            S.op("pool", lambda e: e.tensor_tensor(sqo[:].rearrange("p (h y) -> p h y", y=64), sqo[:].rearrange("p (h y) -> p h y", y=64),
                                                   gnb[:].unsqueeze(1).broadcast_to([128, 6, 64]), ALU.mult), r=[b_sqo, b_gnb], w=[b_sqo])
            S.op("pool", lambda e, z_=z_, y_=y_: e.tensor_tensor(y_[:], sqo[:], z_[:], ALU.mult), r=[b_sqo, zb], w=[ybb])

            def trY(e, y_=y_):
                last = None
                for c in range(3):
                    last = e.transpose(bX[0][:, c * 128:(c + 1) * 128], y_[:, c * 128:(c + 1) * 128], idf[:])
                return last
            S.group("pe", trY, r=[ybb, b_idf], w=[bX[1]])
            S.op("act", lambda e, ti=ti, yt_=yt_: e.activation(out=yt_[:, :, ti * 128:(ti + 1) * 128],
                                                               in_=bX[0][:, 0:384].rearrange("p (c t) -> p c t", c=3), func=AF.Copy), r=[bX[1]], w=[ytb])
        t0 = grp[0] * 128
        nt = len(grp) * 128
        S.dma(k.mixT[5:8, :, t0:t0 + nt].rearrange("c p t -> p c t"), yt_[:, :, :nt], r=[ytb], sem=ytb)
    S.barrier()
    P.close()


def ph_out(k, l, hsrc, hdst, need_ctx):
    S, nc = k.S, k.nc
    P = Phase(k)
    wbf, b_w = P.sb("w_out_bf", [128, 8, 1024], BF16)
    hb = [P.sb("ohb%d" % i, [128, 8, 512], F32) for i in range(2)]
    ho = [P.sb("oho%d" % i, [128, 8, 512], F32) for i in range(2)]
    mx = [P.sb("omx%d" % i, [128, 8, 512], BF16) for i in range(2)]
    pp = [P.ps("opp%d" % i, [128, 512]) for i in range(4)]
    wv = k.w_out[l].rearrange("(k p) n -> p k n", p=128)
    for ci in range(2):
        st, sbuf_ = ho[ci]
        S.dma(st[:], wv[:, :, ci * 512:(ci + 1) * 512], w=[sbuf_], sem=sbuf_)
        S.op("pool", lambda e, st=st, ci=ci: e.tensor_copy(wbf[:, :, ci * 512:(ci + 1) * 512], st[:]), r=[sbuf_], w=[b_w])
    hv = hsrc.rearrange("(k p) t -> p k t", p=128)
    ov = hdst.rearrange("(k p) t -> p k t", p=128)
    blocks = list(range(0 if need_ctx else 1, len(BLOCKS)))

    def load(b):
        t0, nt = BLOCKS[b]
        S.dma(hb[b % 2][0][:, :, :nt], hv[:, :, t0:t0 + nt], w=[hb[b % 2][1]], sem=hb[b % 2][1])
        S.dma(mx[b % 2][0][:, :, :nt], k.mixT[:, :, t0:t0 + nt].rearrange("c p t -> p c t"), w=[mx[b % 2][1]], sem=mx[b % 2][1])
    load(blocks[0])
    cnt = 0
    for bi, b in enumerate(blocks):
        t0, nt = BLOCKS[b]
        j = 1 if b == 0 else 0
        if bi + 1 < len(blocks):
            load(blocks[bi + 1])
        ht, hbb = hb[b % 2]
        mt, mtb = mx[b % 2]
        ot, otb = ho[b % 2]
        for n in range(8):
            pt, pb = pp[cnt % 4]
            cnt += 1

            def mm(e, n=n, pt=pt):
                last = None
                for kk in range(8):
                    last = e.matmul(pt[:, :nt], lhsT=wbf[:, kk, n * 128:(n + 1) * 128], rhs=mt[:, kk, :nt], start=(kk == 0), stop=(kk == 7))
                return last
            S.group("pe", mm, r=[b_w, mtb], w=[pb])
            S.op("dve", lambda e, n=n, pt=pt: e.scalar_tensor_tensor(out=ot[:, n, :nt], in0=pt[:, :nt], scalar=k.mod[:, 16 + n, j:j + 1],
                                                                     in1=ht[:, n, :nt], op0=ALU.mult, op1=ALU.add),
                 r=[pb, k.b_mod, hbb], w=[otb])
        S.dma(ov[:, :, t0:t0 + nt], ot[:, :, :nt], r=[otb], sem=otb)
    S.barrier()
    P.close()


def ph_ffn(k, l, hsrc, hdst, need_ctx, final):
    S, nc = k.S, k.nc
    P = Phase(k)
    NH = 22
    wgu, b_wgu = P.sb("wgu", [128, 8, 2 * 2816], BF16)
    wdn, b_wdn = P.sb("wdn", [128, NH, 1024], BF16)
    hb = [P.sb("fhb%d" % i, [128, 8, 256], F32) for i in range(2)]
    tmp, b_tmp = P.sb("ftmp", [128, 8, 256], F32)
    stg = [hb[0], hb[1], (tmp, b_tmp)]
    si = 0
    wv = k.w_gu[l].rearrange("(k p) n -> p k n", p=128)
    for ci in range(22):
        st, sbuf_ = stg[si % 3]
        si += 1
        S.dma(st[:], wv[:, :, ci * 256:(ci + 1) * 256], w=[sbuf_], sem=sbuf_)
        S.op("pool" if ci % 2 == 0 else "act",
             (lambda e, st=st, ci=ci: e.tensor_copy(wgu[:, :, ci * 256:(ci + 1) * 256], st[:])) if ci % 2 == 0
             else (lambda e, st=st, ci=ci: e.activation(out=wgu[:, :, ci * 256:(ci + 1) * 256], in_=st[:], func=AF.Copy)),
             r=[sbuf_], w=[b_wgu])
    wv = k.w_down[l].rearrange("(k p) n -> p k n", p=128)
    for m0 in (0, 8, 16):
        nm = min(8, NH - m0)
        for n0 in range(0, 1024, 256):
            st, sbuf_ = stg[si % 3]
            si += 1
            S.dma(st[:, :nm, :], wv[:, m0:m0 + nm, n0:n0 + 256], w=[sbuf_], sem=sbuf_)
            S.op("pool" if si % 2 == 0 else "act",
                 (lambda e, st=st, m0=m0, nm=nm, n0=n0: e.tensor_copy(wdn[:, m0:m0 + nm, n0:n0 + 256], st[:, :nm, :])) if si % 2 == 0
                 else (lambda e, st=st, m0=m0, nm=nm, n0=n0: e.activation(out=wdn[:, m0:m0 + nm, n0:n0 + 256], in_=st[:, :nm, :], func=AF.Copy)),
                 r=[sbuf_], w=[b_wdn])
    sq, b_sq = P.sb("fsq", [128, 8, 256], BF16)
    rstd, b_rstd = P.sb("frstd", [128, 256], F32)
    aT, b_aT = P.sb("faT", [128, 8, 256], BF16)
    hid, b_hid = P.sb("fhid", [128, NH, 256], BF16)
    sil = [P.sb("fsil%d" % i, [128, 256], F32) for i in range(2)]
    h2, b_h2 = P.sb("fh2", [128, 8, 256], F32)
    fg, b_fg = P.sb("ffg", [128, 8], F32)
    S.dma(fg[:], k.final_g, w=[b_fg], sem=b_fg)
    p_ss, b_pss = P.ps("fp_ss", [128, 512])
    p_gu = [P.ps("fp_gu%d" % i, [128, 512]) for i in range(4)]
    p_dn = [P.ps("fp_dn%d" % i, [128, 512]) for i in range(3)]
    hv = hsrc.rearrange("(k p) t -> p k t", p=128)
    ov = hdst.rearrange("(k p) t -> p k t", p=128) if hdst is not None else None
    NT = 256
    blocks = list(range(0 if need_ctx else 1, T // NT))

    def load(b):
        S.dma(hb[b % 2][0][:], hv[:, :, b * NT:(b + 1) * NT], w=[hb[b % 2][1]], sem=hb[b % 2][1])
    load(blocks[0])
    cg = 0
    cd = 0
    for bi, b in enumerate(blocks):
        t0 = b * NT
        j = 1 if b == 0 else 0
        if bi + 1 < len(blocks):
            load(blocks[bi + 1])
        ht, hbb = hb[b % 2]

        def norm(src, srcb, dst, dstb, gcol, bcol):
            S.op("act", lambda e: e.activation(out=sq[:], in_=src[:], func=AF.Square), r=[srcb], w=[b_sq])

            def mm_ss(e):
                last = None
                for kk in range(8):
                    last = e.matmul(p_ss[:, :NT], lhsT=k.ones_d[:], rhs=sq[:, kk, :], start=(kk == 0), stop=(kk == 7))
                return last
            S.group("pe", mm_ss, r=[b_sq, k.b_ones_d], w=[b_pss])
            rsqrt_eps(k, rstd[:], b_rstd, p_ss[:, :NT], b_pss)
            S.op("dve", lambda e: e.tensor_tensor(tmp[:], src[:], rstd[:].unsqueeze(1).broadcast_to([128, 8, NT]), ALU.mult),
                 r=[srcb, b_rstd], w=[b_tmp])
            for kk in range(8):
                if bcol is not None:
                    S.op("act", lambda e, kk=kk: e.activation(out=dst[:, kk, :], in_=tmp[:, kk, :], func=AF.Identity,
                                                               scale=gcol(kk), bias=bcol(kk)), r=[b_tmp, k.b_G2, k.b_mod, b_fg], w=[dstb])
                else:
                    S.op("act", lambda e, kk=kk: e.activation(out=dst[:, kk, :], in_=tmp[:, kk, :], func=AF.Copy, scale=gcol(kk)),
                         r=[b_tmp, b_fg], w=[dstb])
        norm(ht, hbb, aT, b_aT, lambda kk: k.G2[:, kk, j:j + 1], lambda kk: k.mod[:, 24 + kk, j:j + 1])
        for m in range(NH):
            pt, pb = p_gu[cg % 4]
            st_, sb_ = sil[cg % 2]
            cg += 1

            def mm(e, m=m, pt=pt):
                last = None
                for half in range(2):
                    c0 = half * 2816 + m * 128
                    for kk in range(8):
                        last = e.matmul(pt[:, half * 256:(half + 1) * 256], lhsT=wgu[:, kk, c0:c0 + 128], rhs=aT[:, kk, :],
                                        start=(kk == 0), stop=(kk == 7))
                return last
            S.group("pe", mm, r=[b_wgu, b_aT], w=[pb])
            S.op("act", lambda e, pt=pt, st_=st_: e.activation(out=st_[:], in_=pt[:, 0:256], func=AF.Silu), r=[pb], w=[sb_])
            S.op("dve", lambda e, pt=pt, st_=st_, m=m: e.tensor_tensor(hid[:, m, :], st_[:], pt[:, 256:512], ALU.mult), r=[pb, sb_], w=[b_hid])
        for n in range(8):
            pt, pb = p_dn[cd % 3]
            cd += 1

            def mm(e, n=n, pt=pt):
                last = None
                for m in range(NH):
                    last = e.matmul(pt[:, :NT], lhsT=wdn[:, m, n * 128:(n + 1) * 128], rhs=hid[:, m, :], start=(m == 0), stop=(m == NH - 1))
                return last
            S.group("pe", mm, r=[b_wdn, b_hid], w=[pb])
            S.op("dve", lambda e, n=n, pt=pt: e.scalar_tensor_tensor(out=h2[:, n, :], in0=pt[:, :NT], scalar=k.mod[:, 40 + n, j:j + 1],
                                                                     in1=ht[:, n, :], op0=ALU.mult, op1=ALU.add),
                 r=[pb, k.b_mod, hbb], w=[b_h2])
        if not final:
            S.dma(ov[:, :, t0:t0 + NT], h2[:], r=[b_h2], sem=b_h2)
        else:
            norm(h2, b_h2, h2, b_h2, lambda kk: fg[:, kk:kk + 1], None)
            S.dma(k.outT.rearrange("(k p) t -> p k t", p=128)[:, :, t0 - NCTX:t0 - NCTX + NT], h2[:], r=[b_h2], sem=b_h2)
    S.barrier()
    P.close()


_CONSTS = None


def kernel(**inputs):
    global _CONSTS
    inp = {k_: np.asarray(v) for k_, v in inputs.items()}
    nc = build(depth=2)
    if _CONSTS is None:
        _CONSTS = host_consts()
    in_maps = []
    for b in range(8):
        d = host_inputs(inp, b)
        d.update(_CONSTS)
        in_maps.append(d)
    res = run_bass_kernel_spmd(nc, in_maps, core_ids=list(range(8)))
    out = np.stack([np.ascontiguousarray(np.asarray(res.results[b]["outT"]).T) for b in range(8)])
    return out.astype(np.float32)
```

```python
import numpy as np, contextlib, math
import concourse.bass as bass
import concourse.mybir as mybir
from concourse.bass_utils import run_bass_kernel_spmd

F32 = mybir.dt.float32
BF16 = mybir.dt.bfloat16
AF = mybir.ActivationFunctionType
ALU = mybir.AluOpType
AX = mybir.AxisListType


class Buf:
    __slots__ = ("name", "w", "r", "dsem", "excl")

    def __init__(self, name, excl=False):
        self.name = name
        self.excl = excl
        self.w = None
        self.r = {}
        self.dsem = None


class Sched:
    ENG = ("pe", "act", "dve", "pool", "sp")

    def __init__(self, nc, es):
        self.nc = nc
        self.es = es
        self.eng = {"pe": nc.tensor, "act": nc.scalar, "dve": nc.vector,
                    "pool": nc.gpsimd, "sp": nc.sync}
        self.sems = {}
        self.cnt = {}
        for e in self.ENG:
            self.sems[e] = es.enter_context(nc.semaphore("s_" + e))
            self.cnt[e] = 0
        self.seen = {e: {} for e in self.ENG}
        self.free_dsems = []
        self.n_dsem = 0
        self.live_dsems = []

    def _dsem(self, buf):
        if buf.dsem is None:
            if self.free_dsems:
                key = self.free_dsems.pop()
            else:
                key = "d%d" % self.n_dsem
                self.n_dsem += 1
                self.sems[key] = self.es.enter_context(self.nc.semaphore("s_" + key))
                self.cnt[key] = 0
            buf.dsem = key
            self.live_dsems.append(buf)
        return buf.dsem

    def _deps(self, e, r, w):
        deps = {}

        def need(ev, raw):
            if ev is None:
                return
            key, val = ev
            if key == e and (e == "pe" or not raw):
                return
            if self.seen[e].get(key, 0) >= val:
                return
            if deps.get(key, 0) < val:
                deps[key] = val
        for b in r:
            need(b.w, True)
            if b.excl:
                for k, v in b.r.items():
                    need((k, v), False)
        for b in w:
            need(b.w, False)
            for k, v in b.r.items():
                need((k, v), False)
        return deps

    def _emit(self, e, deps, fn):
        eng = self.eng[e]
        items = list(deps.items())
        for key, val in items[:-1]:
            eng.wait_ge(self.sems[key], val)
        ins = fn(eng)
        if isinstance(ins, (list, tuple)):
            first, last = ins[0], ins[-1]
            multi = len(ins) > 1
        else:
            first = last = ins
            multi = False
        if items:
            key, val = items[-1]
            if multi:
                raise RuntimeError("multi-instruction op must use op_group")
            first._wait_ge(self.sems[key], val)
        for key, val in items:
            self.seen[e][key] = val
        return last

    def op(self, e, fn, r=(), w=()):
        deps = self._deps(e, r, w)
        last = self._emit(e, deps, fn)
        self.cnt[e] += 1
        last.then_inc(self.sems[e], 1)
        ev = (e, self.cnt[e])
        self._mark(ev, r, w)
        return ev

    def group(self, e, fn, r=(), w=()):
        deps = self._deps(e, r, w)
        eng = self.eng[e]
        for key, val in deps.items():
            eng.wait_ge(self.sems[key], val)
            self.seen[e][key] = val
        last = fn(eng)
        self.cnt[e] += 1
        last.then_inc(self.sems[e], 1)
        ev = (e, self.cnt[e])
        self._mark(ev, r, w)
        return ev

    def _mark(self, ev, r, w):
        key, val = ev
        for b in r:
            if b.excl:
                b.r = {key: val}
            elif b.r.get(key, 0) < val:
                b.r[key] = val
        for b in w:
            b.w = ev
            b.r = {}

    def dma(self, out, in_, r=(), w=(), sem=None, q="sp", **kw):
        deps = self._deps(q, r, w)
        key = self._dsem(sem)
        eng = self.eng[q]
        for k, v in deps.items():
            eng.wait_ge(self.sems[k], v)
            self.seen[q][k] = v
        ins = eng.dma_start(out=out, in_=in_, **kw)
        self.cnt[key] += 16
        ins.then_inc(self.sems[key], 16)
        ev = (key, self.cnt[key])
        self._mark(ev, r, w)
        return ev

    def barrier(self):
        sp = self.eng["sp"]
        for key in list(self.sems.keys()):
            if key == "sp":
                continue
            v = self.cnt[key]
            if v > 0 and self.seen["sp"].get(key, 0) < v:
                sp.wait_ge(self.sems[key], v)
                self.seen["sp"][key] = v
        self.cnt["sp"] += 1
        sp.sem_inc(self.sems["sp"], 1)
        for e in self.ENG:
            if e == "sp":
                continue
            self.eng[e].wait_ge(self.sems["sp"], self.cnt["sp"])
        for e in self.ENG:
            for key in self.sems:
                self.seen[e][key] = self.cnt[key]
        for b in self.live_dsems:
            self.free_dsems.append(b.dsem)
            b.dsem = None
        self.live_dsems = []

    def finish(self):
        self.barrier()


import os
import ml_dtypes

T = 4352
NCTX = 256
D = 1024
IN_DIM = 2968
EPS = 1e-6
BLOCKS = [(0, 256)] + [(256 + 512 * i, 512) for i in range(8)]


class K:
    pass


def build(depth=2, stop_after=None, debug=False):
    nc = bass.Bass("TRN2", target_bir_lowering=False)
    es = contextlib.ExitStack()
    S = Sched(nc, es)
    k = K()
    k.nc, k.S, k.es = nc, S, es
    k.debug = debug

    def din(name, shape, dt=F32):
        return nc.dram_tensor(name, list(shape), dt, kind="ExternalInput").ap()

    def dscr(name, shape, dt=F32):
        kind = "ExternalOutput" if debug else "Internal"
        return nc.dram_tensor(name, list(shape), dt, kind=kind).ap()

    k.xT = din("xT", [D, T])
    k.ccin = din("ccin", [128, 8, 2])
    k.ada_w = din("ada_w", [depth, D, 6 * D])
    k.ada_b = din("ada_b", [depth, 128, 48])
    k.norm1_g = din("norm1_g", [depth, 128, 8])
    k.norm2_g = din("norm2_g", [depth, 128, 8])
    k.final_g = din("final_g", [128, 8])
    k.w_in = din("w_in", [depth, D, IN_DIM])
    k.w_out = din("w_out", [depth, D, D])
    k.w_gu = din("w_gu", [depth, D, 2 * 2816])
    k.w_down = din("w_down", [depth, 2816, D])
    k.outT = nc.dram_tensor("outT", [D, 4096], F32, kind="ExternalOutput").ap()
    k.qk_g = din("qk_g", [depth, 128, 2])
    k.c_ones_d = din("c_ones_d", [128, 128], BF16)
    k.c_blk64 = din("c_blk64", [128, 128], BF16)
    k.c_rp_d = din("c_rp_d", [128, 128], BF16)
    k.c_rp_g = din("c_rp_g", [128, 128], BF16)
    k.c_rope = din("c_rope", [4, 128, 4096])
    k.c_place = din("c_place", [64, 2, 128], BF16)
    k.diff_lambda = din("diff_lambda", [depth, 128])
    k.diff_g = din("diff_g", [depth, 128, 1])
    k.conv_w = din("conv_w", [depth, 128, 9, 5])
    k.c_blk64s = din("c_blk64s", [128, 128], BF16)
    k.c_identb = din("c_identb", [128, 128], BF16)
    k.c_identf = din("c_identf", [128, 128])
    k.c_MI = din("c_MI", [128, 2, 128])
    k.c_SU = din("c_SU", [128, 2, 128])
    k.c_EMK = din("c_EMK", [128, 2, 256])
    k.c_MK = din("c_MK", [3, 128, 128])
    k.gdn_g = din("gdn_g", [depth, 64])
    k.gdn_alog = din("gdn_alog", [depth, 12])
    k.gdn_dtb = din("gdn_dtb", [depth, 12])
    k.hA = dscr("hA", [D, T])
    k.hB = dscr("hB", [D, T])
    k.QdT = dscr("QdT", [2, 128, T], BF16)
    k.KdT = dscr("KdT", [2, 128, T], BF16)
    k.QgT = dscr("QgT", [3, 128, T], BF16)
    k.KgT = dscr("KgT", [1, 128, T], BF16)
    k.Vtok = dscr("Vtok", [T, 390], BF16)
    k.gqkvT = dscr("gqkvT", [9, 128, T])
    k.zab = dscr("zab", [T, 408])
    k.mixT = dscr("mixT", [8, 128, T], BF16)
    k.QnT = dscr("QnT", [3, 128, T], BF16)
    k.KnT = dscr("KnT", [3, 128, T], BF16)
    k.Ktok = dscr("Ktok", [T, 384], BF16)
    k.Vtok2 = dscr("Vtok2", [T, 384], BF16)
    if debug:
        k.dbg_mod = nc.dram_tensor("dbg_mod", [128, 96], F32, kind="ExternalOutput").ap()

    def sbp(name, shape, dt):
        t = es.enter_context(nc.sbuf_tensor(name, list(shape), dt))
        return t, Buf(name)

    k.ones_d, k.b_ones_d = sbp("ones_d", [128, 128], BF16)
    k.blk64, k.b_blk64 = sbp("blk64", [128, 128], BF16)
    k.rp_d, k.b_rp_d = sbp("rp_d", [128, 128], BF16)
    k.rp_g, k.b_rp_g = sbp("rp_g", [128, 128], BF16)
    k.scc, k.b_scc = sbp("scc", [128, 8, 2], F32)
    k.mod, k.b_mod = sbp("mod", [128, 48, 2], F32)
    k.G1, k.b_G1 = sbp("G1", [128, 8, 2], F32)
    k.G2, k.b_G2 = sbp("G2", [128, 8, 2], F32)
    k.qkg, k.b_qkg = sbp("qkg", [128, 2], F32)
    k.epsc, k.b_epsc = sbp("epsc", [128, 2], F32)
    k.place, k.b_place = sbp("place", [64, 2, 128], BF16)
    k.blk64s, k.b_blk64s = sbp("blk64s", [128, 128], BF16)
    k.identb, k.b_identb = sbp("identb", [128, 128], BF16)
    S.dma(k.blk64s[:], k.c_blk64s, w=[k.b_blk64s], sem=k.b_blk64s)
    S.dma(k.identb[:], k.c_identb, w=[k.b_identb], sem=k.b_identb)
    S.dma(k.place[:], k.c_place, w=[k.b_place], sem=k.b_place)
    S.op("pool", lambda e: e.memset(k.epsc[:], EPS), w=[k.b_epsc])

    S.dma(k.ones_d[:], k.c_ones_d, w=[k.b_ones_d], sem=k.b_ones_d)
    S.dma(k.blk64[:], k.c_blk64, w=[k.b_blk64], sem=k.b_blk64)
    S.dma(k.rp_d[:], k.c_rp_d, w=[k.b_rp_d], sem=k.b_rp_d)
    S.dma(k.rp_g[:], k.c_rp_g, w=[k.b_rp_g], sem=k.b_rp_g)
    S.dma(k.scc[:], k.ccin, w=[k.b_scc], sem=k.b_scc)
    S.op("act", lambda e: e.activation(out=k.scc[:], in_=k.scc[:], func=AF.Silu), r=[k.b_scc], w=[k.b_scc])
    S.barrier()

    for l in range(depth):
        ph_mod(k, l)
        S.barrier()
        if stop_after == ("mod", l):
            break
        ph_proj(k, l, k.xT if l == 0 else k.hB)
        S.barrier()
        if stop_after == ("proj", l):
            break
        ph_att(k, l, need_ctx=(l < depth - 1))
        S.barrier()
        if stop_after == ("att", l):
            break
        ph_gdn_a(k, l)
        S.barrier()
        if stop_after == ("gdna", l):
            break
        ph_gdn_b(k, l, need_ctx=(l < depth - 1))
        S.barrier()
        if stop_after == ("gdnb", l):
            break
        last = (l == depth - 1)
        hsrc = k.xT if l == 0 else k.hB
        ph_out(k, l, hsrc, k.hA, need_ctx=not last)
        S.barrier()
        if stop_after == ("out", l):
            break
        ph_ffn(k, l, k.hA, None if last else k.hB, need_ctx=not last, final=last)
        S.barrier()
        if stop_after == ("ffn", l):
            break
    S.finish()
    es.close()
    return nc


def rsqrt_eps(k, dst, dst_b, src, src_b, eps=EPS):
    S = k.S
    npart = dst.shape[0]
    S.op("act", lambda e: e.activation(out=dst, in_=src, func=AF.Ln, bias=k.epsc[0:npart, 0:1]), r=[src_b, k.b_epsc], w=[dst_b])
    S.op("act", lambda e: e.activation(out=dst, in_=dst, func=AF.Exp, scale=-0.5), r=[dst_b], w=[dst_b])


class Phase:
    def __init__(self, k):
        self.k = k
        self.es = contextlib.ExitStack()
        self.nps = 0

    _uid = [0]

    def sb(self, name, shape, dt):
        Phase._uid[0] += 1
        name = "%s_u%d" % (name, Phase._uid[0])
        t = self.es.enter_context(self.k.nc.sbuf_tensor(name, list(shape), dt))
        return t, Buf(name)

    def ps(self, name, shape, dt=F32):
        Phase._uid[0] += 1
        name = "%s_u%d" % (name, Phase._uid[0])
        t = self.es.enter_context(self.k.nc.psum_tensor(name, list(shape), dt))
        return t, Buf(name, excl=True)

    def close(self):
        self.es.close()


def ph_mod(k, l):
    S, nc = k.S, k.nc
    P = Phase(k)
    wst = [P.sb("adaw%d" % i, [128, 8, 512], F32) for i in range(2)]
    adab, b_adab = P.sb("adab", [128, 48], F32)
    n1, b_n1 = P.sb("n1g", [128, 8], F32)
    n2, b_n2 = P.sb("n2g", [128, 8], F32)
    pm, b_pm = P.ps("pm", [128, 96])
    S.dma(adab[:], k.ada_b[l], w=[b_adab], sem=b_adab)
    S.dma(n1[:], k.norm1_g[l], w=[b_n1], sem=b_n1)
    S.dma(n2[:], k.norm2_g[l], w=[b_n2], sem=b_n2)
    S.dma(k.qkg[:], k.qk_g[l], w=[k.b_qkg], sem=k.b_qkg)
    wv = k.ada_w[l].rearrange("(k p) n -> p k n", p=128)
    for ci in range(12):
        wt, wb = wst[ci % 2]
        S.dma(wt[:], wv[:, :, ci * 512:(ci + 1) * 512], w=[wb], sem=wb)
        for j in range(4):
            c = ci * 4 + j

            def mm(e, c=c, j=j, wt=wt):
                last = None
                for kk in range(8):
                    last = e.matmul(pm[:, 2 * c:2 * c + 2], lhsT=wt[:, kk, j * 128:(j + 1) * 128],
                                    rhs=k.scc[:, kk, :], start=(kk == 0), stop=(kk == 7))
                return last
            S.group("pe", mm, r=[wb, k.b_scc], w=[b_pm])
    S.op("dve", lambda e: e.tensor_tensor(k.mod[:], pm[:].rearrange("p (c j) -> p c j", j=2),
                                          adab[:].unsqueeze(2).broadcast_to([128, 48, 2]), ALU.add),
         r=[b_pm, b_adab], w=[k.b_mod])
    S.op("dve", lambda e: e.scalar_tensor_tensor(out=k.G1[:], in0=k.mod[:, 8:16, :], scalar=1.0,
                                                 in1=n1[:].unsqueeze(2).broadcast_to([128, 8, 2]),
                                                 op0=ALU.add, op1=ALU.mult),
         r=[k.b_mod, b_n1], w=[k.b_G1])
    S.op("dve", lambda e: e.scalar_tensor_tensor(out=k.G2[:], in0=k.mod[:, 32:40, :], scalar=1.0,
                                                 in1=n2[:].unsqueeze(2).broadcast_to([128, 8, 2]),
                                                 op0=ALU.add, op1=ALU.mult),
         r=[k.b_mod, b_n2], w=[k.b_G2])
    if k.debug:
        S.dma(k.dbg_mod, k.mod[:].rearrange("p c j -> p (c j)"), r=[k.b_mod], sem=k.b_mod)
    S.barrier()
    P.close()


def ph_proj(k, l, hsrc):
    S, nc = k.S, k.nc
    P = Phase(k)
    wbf, b_w = P.sb("w_in_bf", [128, 8, IN_DIM], BF16)
    wv = k.w_in[l].rearrange("(k p) n -> p k n", p=128)
    hb = [P.sb("hblk%d" % i, [128, 8, 512], F32) for i in range(2)]
    tmp, b_tmp = P.sb("tmp", [128, 8, 512], F32)
    stg = [(tmp, b_tmp), hb[1]]
    for ci, c0 in enumerate(range(0, IN_DIM, 512)):
        n = min(512, IN_DIM - c0)
        st, sbuf_ = stg[ci % 2]
        S.dma(st[:, :, :n], wv[:, :, c0:c0 + n], w=[sbuf_], sem=sbuf_)
        S.op("pool", lambda e, st=st, c0=c0, n=n: e.tensor_copy(wbf[:, :, c0:c0 + n], st[:, :, :n]),
             r=[sbuf_], w=[b_w])

    sq, b_sq = P.sb("sq", [128, 8, 512], BF16)
    rstd, b_rstd = P.sb("rstd", [128, 512], F32)
    aT = [P.sb("aT%d" % i, [128, 8, 512], BF16) for i in range(2)]
    rope = [P.sb("rope%d" % i, [128, 4, 512], F32) for i in range(2)]
    p_ss, b_pss = P.ps("p_ss", [128, 512])
    p_fm = [P.ps("p_fm%d" % i, [128, 512]) for i in range(3)]
    p_aux = [P.ps("p_aux%d" % i, [128, 512]) for i in range(2)]
    p_tm = [P.ps("p_tm%d" % i, [128, 512]) for i in range(2)]
    qb = [P.sb("qb%d" % i, [128, 512], BF16) for i in range(2)]
    sqh = [P.sb("sqh%d" % i, [128, 512], BF16) for i in range(2)]
    t1 = [P.sb("t1_%d" % i, [128, 512], F32) for i in range(2)]
    t2 = [P.sb("t2_%d" % i, [128, 512], F32) for i in range(2)]
    rs2 = [P.sb("rs2_%d" % i, [128, 512], F32) for i in range(2)]
    ob = [P.sb("ob%d" % i, [128, 512], BF16) for i in range(3)]
    of = [P.sb("of%d" % i, [128, 512], F32) for i in range(3)]
    vt = [P.sb("vt%d" % i, [128, 4, 390], BF16) for i in range(2)]
    zt = [P.sb("zt%d" % i, [128, 4, 408], F32) for i in range(2)]
    for i in range(2):
        S.op("pool", lambda e, i=i: e.memset(vt[i][0][:], 1.0), w=[vt[i][1]])

    hv = hsrc.rearrange("(k p) t -> p k t", p=128)
    cnt = {"fm": 0, "aux": 0, "tm": 0, "w": 0, "ob": 0, "of": 0}

    def load_block(b):
        t0, nt = BLOCKS[b]
        ht, hbuf = hb[b % 2]
        S.dma(ht[:, :, :nt], hv[:, :, t0:t0 + nt], w=[hbuf], sem=hbuf)
        if b > 0:
            rt, rbuf = rope[b % 2]
            S.dma(rt[:, :, :nt], k.c_rope[:, :, t0 - NCTX:t0 - NCTX + nt].rearrange("c p t -> p c t"),
                  w=[rbuf], sem=rbuf)

    load_block(0)
    for b in range(len(BLOCKS)):
        t0, nt = BLOCKS[b]
        j = 1 if b == 0 else 0
        if b + 1 < len(BLOCKS):
            load_block(b + 1)
        ht, hbuf = hb[b % 2]
        rt, rbuf = rope[b % 2]
        at, abuf = aT[b % 2]
        S.op("act", lambda e: e.activation(out=sq[:, :, :nt], in_=ht[:, :, :nt], func=AF.Square),
             r=[hbuf], w=[b_sq])

        def mm_ss(e):
            last = None
            for kk in range(8):
                last = e.matmul(p_ss[:, :nt], lhsT=k.ones_d[:], rhs=sq[:, kk, :nt], start=(kk == 0), stop=(kk == 7))
            return last
        S.group("pe", mm_ss, r=[b_sq, k.b_ones_d], w=[b_pss])
        rsqrt_eps(k, rstd[:, :nt], b_rstd, p_ss[:, :nt], b_pss)
        S.op("dve", lambda e: e.tensor_tensor(tmp[:, :, :nt], ht[:, :, :nt],
                                              rstd[:, :nt].unsqueeze(1).broadcast_to([128, 8, nt]), ALU.mult),
             r=[hbuf, b_rstd], w=[b_tmp])
        for kk in range(8):
            S.op("act", lambda e, kk=kk: e.activation(out=at[:, kk, :nt], in_=tmp[:, kk, :nt], func=AF.Identity,
                                                       scale=k.G1[:, kk, j:j + 1], bias=k.mod[:, kk, j:j + 1]),
                 r=[b_tmp, k.b_G1, k.b_mod], w=[abuf])

        if os.environ.get("BISECT") == "1":
            continue
        def fm_matmul(c0):
            pt, pb = p_fm[cnt["fm"] % 3]
            cnt["fm"] += 1

            def mm(e):
                last = None
                for kk in range(8):
                    last = e.matmul(pt[:, :nt], lhsT=wbf[:, kk, c0:c0 + 128], rhs=at[:, kk, :nt],
                                    start=(kk == 0), stop=(kk == 7))
                return last
            S.group("pe", mm, r=[b_w, abuf], w=[pb])
            return pt, pb

        def store(dst, src_t, src_b):
            S.dma(dst, src_t, r=[src_b], sem=src_b)

        def rope_chunk(pt, pb, kind, dst, gcol=None):
            i = cnt["w"] % 2
            cnt["w"] += 1
            qt, qbuf = qb[i]
            o_t, o_b = ob[cnt["ob"] % 3]
            cnt["ob"] += 1
            rp = k.rp_d if kind == "d" else k.rp_g
            rpb = k.b_rp_d if kind == "d" else k.b_rp_g
            ci, si = (0, 1) if kind == "d" else (2, 3)
            if kind == "d":
                S.op("act", lambda e: e.activation(out=qt[:, :nt], in_=pt[:, :nt], func=AF.Copy), r=[pb], w=[qbuf])
            else:
                S.op("act", lambda e: e.activation(out=qt[:, :nt], in_=pt[:, :nt], func=AF.Copy, scale=gcol),
                     r=[pb, k.b_qkg], w=[qbuf])
                st_, sb_ = sqh[i]
                S.op("act", lambda e: e.activation(out=st_[:, :nt], in_=pt[:, :nt], func=AF.Square), r=[pb], w=[sb_])
                pa, pab = p_aux[cnt["aux"] % 2]
                cnt["aux"] += 1
                S.op("pe", lambda e: e.matmul(pa[:, :nt], lhsT=k.blk64[:], rhs=st_[:, :nt], start=True, stop=True),
                     r=[sb_, k.b_blk64], w=[pab])
                r2, r2b = rs2[i]
                rsqrt_eps(k, r2[:, :nt], r2b, pa[:, :nt], pab)
            if b == 0:
                if kind == "d":
                    store(dst, qt[:, :nt], qbuf)
                else:
                    S.op("pool", lambda e: e.tensor_tensor(o_t[:, :nt], qt[:, :nt], r2[:, :nt], ALU.mult),
                         r=[qbuf, r2b], w=[o_b])
                    store(dst, o_t[:, :nt], o_b)
                return
            pa, pab = p_aux[cnt["aux"] % 2]
            cnt["aux"] += 1
            S.op("pe", lambda e: e.matmul(pa[:, :nt], lhsT=rp[:], rhs=qt[:, :nt], start=True, stop=True),
                 r=[qbuf, rpb], w=[pab])
            a1, a1b = t1[i]
            a2, a2b = t2[i]
            if kind == "d":
                S.op("dve", lambda e: e.tensor_tensor(a1[:, :nt], pt[:, :nt], rt[:, ci, :nt], ALU.mult),
                     r=[pb, rbuf], w=[a1b])
            else:
                S.op("pool", lambda e: e.tensor_tensor(a1[:, :nt], qt[:, :nt], rt[:, ci, :nt], ALU.mult),
                     r=[qbuf, rbuf], w=[a1b])
            S.op("dve", lambda e: e.tensor_tensor(a2[:, :nt], pa[:, :nt], rt[:, si, :nt], ALU.mult),
                 r=[pab, rbuf], w=[a2b])
            if kind == "d":
                S.op("pool", lambda e: e.tensor_tensor(o_t[:, :nt], a1[:, :nt], a2[:, :nt], ALU.add),
                     r=[a1b, a2b], w=[o_b])
            else:
                S.op("pool", lambda e: e.tensor_tensor(a1[:, :nt], a1[:, :nt], a2[:, :nt], ALU.add),
                     r=[a1b, a2b], w=[a1b])
                S.op("pool", lambda e: e.tensor_tensor(o_t[:, :nt], a1[:, :nt], r2[:, :nt], ALU.mult),
                     r=[a1b, r2b], w=[o_b])
            store(dst, o_t[:, :nt], o_b)

        B2 = os.environ.get("BISECT2", "dgn")
        for c in range(2 if "d" in B2 else 0):
            pt, pb = fm_matmul(c * 128)
            rope_chunk(pt, pb, "d", k.QdT[c, :, t0:t0 + nt])
        for c in range(2 if "d" in B2 else 0):
            pt, pb = fm_matmul(256 + c * 128)
            rope_chunk(pt, pb, "d", k.KdT[c, :, t0:t0 + nt])
        for c in range(3 if "g" in B2 else 0):
            pt, pb = fm_matmul(768 + c * 128)
            rope_chunk(pt, pb, "g", k.QgT[c, :, t0:t0 + nt], gcol=k.qkg[:, 0:1])
        if "g" in B2:
            pt, pb = fm_matmul(1152)
            rope_chunk(pt, pb, "g", k.KgT[0, :, t0:t0 + nt], gcol=k.qkg[:, 1:2])
        for c in range(9 if "n" in B2 else 0):
            pt, pb = fm_matmul(1408 + c * 128)
            o_t, o_b = of[cnt["of"] % 3]
            cnt["of"] += 1
            if c % 2 == 0:
                S.op("dve", lambda e: e.tensor_copy(o_t[:, :nt], pt[:, :nt]), r=[pb], w=[o_b])
            else:
                S.op("act", lambda e: e.activation(out=o_t[:, :nt], in_=pt[:, :nt], func=AF.Copy), r=[pb], w=[o_b])
            store(k.gqkvT[c, :, t0:t0 + nt], o_t[:, :nt], o_b)

        if os.environ.get("BISECT") == "2":
            continue
        v_t, v_b = vt[b % 2]
        z_t, z_b = zt[b % 2]
        ntile = nt // 128
        for tt in range(ntile):
            for (c0, n, kind) in ((512, 256, "dv"), (1280, 128, "gv"), (2560, 408, "zab")):
                pt, pb = p_tm[cnt["tm"] % 2]
                cnt["tm"] += 1

                def mm(e, c0=c0, n=n, pt=pt):
                    last = None
                    for kk in range(8):
                        last = e.matmul(pt[:, :n], lhsT=at[:, kk, tt * 128:(tt + 1) * 128], rhs=wbf[:, kk, c0:c0 + n],
                                        start=(kk == 0), stop=(kk == 7))
                    return last
                S.group("pe", mm, r=[b_w, abuf], w=[pb])
                if kind == "dv":
                    S.op("act", lambda e, pt=pt: e.activation(
                        out=v_t[:, tt, 0:260].rearrange("p (h d) -> p h d", d=65)[:, :, 0:64],
                        in_=pt[:, 0:256].rearrange("p (h d) -> p h d", d=64), func=AF.Copy), r=[pb], w=[v_b])
                elif kind == "gv":
                    S.op("act", lambda e, pt=pt: e.activation(
                        out=v_t[:, tt, 260:390].rearrange("p (h d) -> p h d", d=65)[:, :, 0:64],
                        in_=pt[:, 0:128].rearrange("p (h d) -> p h d", d=64), func=AF.Copy), r=[pb], w=[v_b])
                else:
                    S.op("dve", lambda e, pt=pt: e.tensor_copy(z_t[:, tt, :], pt[:, 0:408]), r=[pb], w=[z_b])
        S.dma(k.Vtok[t0:t0 + nt, :].rearrange("(n p) c -> p n c", p=128), v_t[:, :ntile, :], r=[v_b], sem=v_b)
        S.dma(k.zab[t0:t0 + nt, :].rearrange("(n p) c -> p n c", p=128), z_t[:, :ntile, :], r=[z_b], sem=z_b)
    S.barrier()
    P.close()


def host_consts():
    bf = ml_dtypes.bfloat16
    c = {}
    c["c_ones_d"] = np.full((128, 128), 1.0 / 1024, np.float32).astype(bf)
    blk = np.zeros((128, 128), np.float32)
    blk[:64, :64] = 1.0 / 64
    blk[64:, 64:] = 1.0 / 64
    c["c_blk64"] = blk.astype(bf)
    c["c_blk64s"] = (blk * 64).astype(bf)
    c["c_identb"] = np.eye(128, dtype=np.float32).astype(bf)
    c["c_identf"] = np.eye(128, dtype=np.float32)
    ti = np.arange(128)
    le = (ti[:, None] <= ti[None, :]).astype(np.float32)
    gt = (ti[:, None] > ti[None, :]).astype(np.float32)
    c["c_MI"] = np.ascontiguousarray(np.stack([le, le.T], axis=1))
    c["c_SU"] = np.ascontiguousarray(np.stack([gt, gt.T], axis=1))
    incl_f = (ti[None, :] >= ti[:, None]).astype(np.float32); strict_f = (ti[None, :] > ti[:, None]).astype(np.float32)
    emk_f = np.concatenate([incl_f, -strict_f], axis=1)
    emk_b = np.concatenate([incl_f.T, -strict_f.T], axis=1)
    c["c_EMK"] = np.ascontiguousarray(np.stack([emk_f, emk_b], axis=1))
    blk = lambda b_: (ti[:, None] // b_) == (ti[None, :] // b_)
    c["c_MK"] = np.stack([blk(64) & ~blk(32), blk(128) & ~blk(64), blk(32)]).astype(np.float32)

    def rp(group, nf):
        Rp = np.zeros((128, 128), np.float32)
        for g0 in range(0, 128, group):
            for a in range(2):
                for f in range(nf):
                    i0 = g0 + a * 2 * nf + f
                    i1 = g0 + a * 2 * nf + nf + f
                    Rp[i0, i1] = -1.0
                    Rp[i1, i0] = 1.0
        return np.ascontiguousarray(Rp.T).astype(bf)
    c["c_rp_d"] = rp(32, 8)
    c["c_rp_g"] = rp(64, 16)
    t = np.arange(4096)
    row = (t // 64).astype(np.float32)
    col = (t % 64).astype(np.float32)

    def tables(group, nf):
        inv = (np.float32(10000.0) ** (-np.arange(nf, dtype=np.float32) / np.float32(nf))).astype(np.float32)
        cos = np.zeros((128, 4096), np.float32)
        sin = np.zeros((128, 4096), np.float32)
        for p in range(128):
            w = p % group
            a = w // (2 * nf)
            f = w % nf
            pos = row if a == 0 else col
            ang = (pos * inv[f]).astype(np.float32)
            cos[p] = np.cos(ang)
            sin[p] = np.sin(ang)
        return cos, sin
    cd, sd = tables(32, 8)
    cg, sg = tables(64, 16)
    c["c_rope"] = np.stack([cd, sd, cg, sg]).astype(np.float32)
    pl = np.zeros((64, 2, 128), np.float32)
    for i in range(64):
        pl[i, 0, i] = 1.0
        pl[i, 1, 64 + i] = 1.0
    c["c_place"] = pl.astype(bf)
    return c


def col_layout(v, nch):
    v = np.asarray(v)
    return np.ascontiguousarray(np.swapaxes(v.reshape(v.shape[:-1] + (nch, 128)), -1, -2))


def host_inputs(inp, b):
    d = {}
    xT = np.concatenate([inp["ctx"][b], inp["x"][b]], axis=0).T
    d["xT"] = np.ascontiguousarray(xT)
    cc = np.stack([col_layout(inp["c"][b], 8), col_layout(inp["c_ctx"], 8)], axis=-1)
    d["ccin"] = np.ascontiguousarray(cc.astype(np.float32))
    d["ada_w"] = inp["ada_w"]
    d["ada_b"] = col_layout(inp["ada_b"], 48)
    d["norm1_g"] = col_layout(inp["norm1_g"], 8)
    d["norm2_g"] = col_layout(inp["norm2_g"], 8)
    d["final_g"] = col_layout(inp["final_norm_g"], 8)
    d["w_in"] = inp["w_in"]
    d["w_out"] = inp["w_out"]
    d["w_gu"] = inp["ffn_w_gu"]
    d["w_down"] = inp["ffn_w_down"]
    d["gdn_g"] = inp["gdn_norm_g"]
    d["gdn_alog"] = np.ascontiguousarray(inp["gdn_a_log"].reshape(-1, 12))
    d["gdn_dtb"] = np.ascontiguousarray(inp["gdn_dt_bias"].reshape(-1, 12))
    cwl = inp["gdn_conv_w"]
    d["conv_w"] = np.ascontiguousarray(cwl.reshape(cwl.shape[0], 5, 9, 128).transpose(0, 3, 2, 1))
    d["diff_lambda"] = np.ascontiguousarray(inp["diff_lambda"].reshape(-1, 128))
    d["diff_g"] = np.ascontiguousarray(np.tile(inp["diff_norm_g"], (1, 2))[:, :, None].astype(np.float32))
    qg = np.tile(inp["q_norm_g"], (1, 2))
    kg = np.tile(inp["k_norm_g"], (1, 2))
    d["qk_g"] = np.ascontiguousarray(np.stack([qg, kg], axis=-1).astype(np.float32))
    return d


def ph_att(k, l, need_ctx):
    S, nc = k.S, k.nc
    P = Phase(k)
    lam_init = 0.8 - 0.6 * math.exp(-0.3 * l)
    Kd, b_Kd = P.sb("Kd", [128, 2, 2, T], BF16)
    Kg, b_Kg = P.sb("Kg", [128, 3, T], BF16)
    V, b_V = P.sb("V", [128, 34, 390], BF16)
    S.op("pool", lambda e: e.memset(Kd[:], 0.0), w=[b_Kd])
    for c in range(2):
        for hi in range(2):
            for cc in range(2):
                r0 = hi * 64 + cc * 32
                S.dma(Kd[r0:r0 + 32, c, cc, :], k.KdT[c, r0:r0 + 32, :], w=[b_Kd], sem=b_Kd)
    for vi, (ka, kb) in enumerate(((0, 0), (0, 1), (1, 1))):
        S.dma(Kg[0:64, vi, :], k.KgT[0, ka * 64:ka * 64 + 64, :], w=[b_Kg], sem=b_Kg)
        S.dma(Kg[64:128, vi, :], k.KgT[0, kb * 64:kb * 64 + 64, :], w=[b_Kg], sem=b_Kg)
    S.dma(V[:], k.Vtok.rearrange("(n p) c -> p n c", p=128), w=[b_V], sem=b_V)
    lamt, b_lamt = P.sb("lamt", [128, 4, 32], F32)
    lamp, b_lamp = P.sb("lamp", [128, 2, 32], F32)
    lams, b_lams = P.sb("lams", [128, 5], F32)
    gd, b_gd = P.sb("gd", [128, 1], F32)
    ones_r, b_ones_r = P.sb("ones_r", [128, 64], F32)
    S.op("pool", lambda e: e.memset(ones_r[:], 1.0), w=[b_ones_r])
    S.dma(lamt[:].rearrange("p a b -> p (a b)"), k.diff_lambda[l:l + 1, :].partition_broadcast(128), w=[b_lamt], sem=b_lamt)
    S.dma(gd[:], k.diff_g[l], w=[b_gd], sem=b_gd)
    S.op("act", lambda e: e.mul(gd[:], gd[:], 1.0 - lam_init), r=[b_gd], w=[b_gd])
    S.op("dve", lambda e: e.tensor_tensor(lamp[:], lamt[:, 0:4:2, :], lamt[:, 1:4:2, :], ALU.mult), r=[b_lamt], w=[b_lamp])
    S.op("dve", lambda e: e.reduce_sum(lams[:, 0:2], lamp[:], axis=AX.X), r=[b_lamp], w=[b_lams])
    S.op("act", lambda e: e.activation(out=lams[:, 0:2], in_=lams[:, 0:2], func=AF.Exp), r=[b_lams], w=[b_lams])
    S.op("dve", lambda e: e.tensor_tensor(lams[:, 2:3], lams[:, 0:1], lams[:, 1:2], ALU.subtract), r=[b_lams], w=[b_lams])
    S.op("dve", lambda e: e.tensor_scalar(lams[:, 3:4], lams[:, 2:3], lam_init, None, ALU.add), r=[b_lams], w=[b_lams])
    S.op("dve", lambda e: e.tensor_scalar(lams[:, 4:5], lams[:, 3:4], -1.0, None, ALU.mult), r=[b_lams], w=[b_lams])
    lam_ap = lams[:, 3:4]
    neglam_ap = lams[:, 4:5]

    Qd = [P.sb("Qd%d" % i, [128, 2, 512], BF16) for i in range(2)]
    Qg = [P.sb("Qg%d" % i, [128, 3, 512], BF16) for i in range(2)]
    pT = [P.sb("pT%d" % i, [128, 2, 512], BF16) for i in range(3)]
    Oalls = [P.sb("Oall%d" % i, [128, 14, 512], F32) for i in range(2)]
    ps_s = [P.ps("ps_s%d" % i, [128, 2, 512]) for i in range(2)]
    ps_o = [P.ps("ps_o%d" % i, [128, 512]) for i in range(2)]
    ps_x = [P.ps("ps_x%d" % i, [128, 512]) for i in range(2)]
    w1 = [P.sb("w1_%d" % i, [64, 512], F32) for i in range(2)]
    w2 = [P.sb("w2_%d" % i, [64, 512], F32) for i in range(2)]
    wsq = [P.sb("wsq%d" % i, [64, 512], BF16) for i in range(2)]
    wr = [P.sb("wr%d" % i, [64, 512], F32) for i in range(2)]
    obf = [P.sb("obf%d" % i, [64, 512], BF16) for i in range(4)]
    mixc = [P.sb("mixc%d" % i, [128, 512], BF16) for i in range(2)]
    rcp = [P.sb("rcp%d" % i, [64, 512], F32) for i in range(2)]
    cnt = {"s": 0, "o": 0, "x": 0, "w": 0, "obf": 0, "mix": 0, "rc": 0}

    def load_q(b):
        t0, nt = BLOCKS[b]
        qd, qdb = Qd[b % 2]
        qg, qgb = Qg[b % 2]
        S.dma(qd[:, :, :nt], k.QdT[:, :, t0:t0 + nt].rearrange("c p t -> p c t"), w=[qdb], sem=qdb)
        S.dma(qg[:, :, :nt], k.QgT[:, :, t0:t0 + nt].rearrange("c p t -> p c t"), w=[qgb], sem=qgb)

    post_gen = [None]

    def tick_post():
        g_ = post_gen[0]
        if g_ is not None:
            try:
                next(g_)
            except StopIteration:
                post_gen[0] = None

    def drain_post():
        while post_gen[0] is not None:
            tick_post()

    def post(t0, nt, Oall, b_Oall):
            def bcast(u):
                px, pxb = ps_x[cnt["x"] % 2]
                cnt["x"] += 1
                S.op("pe", lambda e: e.matmul(px[0:64, :nt], lhsT=ones_r[64:65, 0:64], rhs=Oall[64:65, u, :nt], start=True, stop=True),
                     r=[b_Oall, b_ones_r], w=[pxb])
                rc, rcb = rcp[cnt["rc"] % 2]
                cnt["rc"] += 1
                S.op("dve", lambda e: e.reciprocal(rc[:, :nt], px[0:64, :nt]), r=[pxb], w=[rcb])
                return rc, rcb

            def place(chunk, parts):
                px, pxb = ps_x[cnt["x"] % 2]
                cnt["x"] += 1
                for i, (ot, otb, hi) in enumerate(parts):
                    S.op("pe", lambda e, ot=ot, hi=hi, i=i: e.matmul(px[:, :nt], lhsT=k.place[0:64, hi, :], rhs=ot[:, :nt],
                                                                     start=(i == 0), stop=(i == len(parts) - 1)),
                         r=[otb, k.b_place], w=[pxb])
                mt, mtb = mixc[cnt["mix"] % 2]
                cnt["mix"] += 1
                S.op("act", lambda e: e.activation(out=mt[:, :nt], in_=px[:, :nt], func=AF.Copy), r=[pxb], w=[mtb])
                S.dma(k.mixT[chunk, :, t0:t0 + nt], mt[:, :nt], r=[mtb], sem=mtb)

            parts = []
            for hh in range(4):
                i = cnt["w"] % 2
                cnt["w"] += 1
                a1, a1b = w1[i]
                a2, a2b = w2[i]
                sqt, sqb = wsq[i]
                rt_, rtb = wr[i]
                px, pxb = bcast(2 * hh)
                S.op("dve", lambda e: e.tensor_tensor(a1[:, :nt], Oall[0:64, 2 * hh, :nt], px[:, :nt], ALU.mult),
                     r=[b_Oall, pxb], w=[a1b])
                yield
                px, pxb = bcast(2 * hh + 1)
                S.op("dve", lambda e: e.tensor_tensor(a2[:, :nt], Oall[0:64, 2 * hh + 1, :nt], px[:, :nt], ALU.mult),
                     r=[b_Oall, pxb], w=[a2b])
                S.op("dve", lambda e: e.scalar_tensor_tensor(out=a1[:, :nt], in0=a2[:, :nt], scalar=neglam_ap[0:64, :], in1=a1[:, :nt],
                                                             op0=ALU.mult, op1=ALU.add), r=[a1b, a2b, b_lams], w=[a1b])
                yield
                S.op("act", lambda e: e.activation(out=sqt[:, :nt], in_=a1[:, :nt], func=AF.Square), r=[a1b], w=[sqb])
                px, pxb = ps_x[cnt["x"] % 2]
                cnt["x"] += 1
                S.op("pe", lambda e: e.matmul(px[0:64, :nt], lhsT=k.blk64[0:64, 0:64], rhs=sqt[:, :nt], start=True, stop=True),
                     r=[sqb, k.b_blk64], w=[pxb])
                yield
                rsqrt_eps(k, rt_[:, :nt], rtb, px[0:64, :nt], pxb)
                yield
                ot, otb = obf[cnt["obf"] % 4]
                cnt["obf"] += 1
                S.op("dve", lambda e: e.scalar_tensor_tensor(out=ot[:, :nt], in0=a1[:, :nt], scalar=gd[0:64, :], in1=rt_[:, :nt],
                                                             op0=ALU.mult, op1=ALU.mult), r=[a1b, rtb, b_gd], w=[otb])
                parts.append((ot, otb, hh % 2))
                yield
                if hh % 2 == 1:
                    place(hh // 2, parts)
                    parts = []
                    yield
            for h in range(6):
                u = 8 + h
                px, pxb = bcast(u)
                ot, otb = obf[cnt["obf"] % 4]
                cnt["obf"] += 1
                S.op("dve", lambda e: e.tensor_tensor(ot[:, :nt], Oall[0:64, u, :nt], px[:, :nt], ALU.mult),
                     r=[b_Oall, pxb], w=[otb])
                parts.append((ot, otb, h % 2))
                yield
                if h % 2 == 1:
                    place(2 + h // 2, parts)
                    parts = []
                    yield

    blocks = list(range(0 if need_ctx else 1, len(BLOCKS)))
    load_q(blocks[0])
    for bi, b in enumerate(blocks):
        t0, nt = BLOCKS[b]
        if bi + 1 < len(blocks):
            load_q(blocks[bi + 1])
        qd, qdb = Qd[b % 2]
        qg, qgb = Qg[b % 2]
        Oall, b_Oall = Oalls[bi % 2]
        kts = [0, 1] if b == 0 else list(range(34))
        pairs = []
        for c in range(2):
            for cc in range(2):
                pairs.append(("d", c, cc, [(2 * c + hi) * 2 + cc for hi in range(2)], [(2 * c + hi) * 65 for hi in range(2)], 32 ** -0.5))
        for c in range(3):
            kva, kvb = (2 * c) // 3, (2 * c + 1) // 3
            pairs.append(("g", c, kva + kvb, [8 + 2 * c, 8 + 2 * c + 1], [260 + kva * 65, 260 + kvb * 65], 64 ** -0.5))
        steps = []
        for pi, pr in enumerate(pairs):
            for ki, kt in enumerate(kts):
                steps.append((pi, pr, ki, kt))
        LAG = 1
        pend = []
        po_of = {}

        def emit_pv(st):
            (pi, pr, ki, kt, ptile, ptb) = st
            (kind, c, var, us, v0s, sc) = pr
            if ki == 0:
                po_of[pi] = [ps_o[(cnt["o"] + i) % 2] for i in range(2)]
                cnt["o"] += 2
            for hi in range(2):
                po, pob = po_of[pi][hi]
                S.op("pe", lambda e, hi=hi, po=po: e.matmul(po[0:65, :nt], lhsT=V[:, kt, v0s[hi]:v0s[hi] + 65], rhs=ptile[:, hi, :nt],
                                                        start=(ki == 0), stop=(ki == len(kts) - 1)), r=[b_V, ptb], w=[pob])
                if ki == len(kts) - 1:
                    S.op("dve", lambda e, hi=hi, po=po: e.tensor_copy(Oall[0:65, us[hi], :nt], po[0:65, :nt]), r=[pob], w=[b_Oall])

        for si_, (pi, pr, ki, kt) in enumerate(steps):
            if si_ % 5 == 4:
                tick_post()
            (kind, c, var, us, v0s, sc) = pr
            pst, psb = ps_s[cnt["s"] % 2]
            ptile, ptb = pT[cnt["s"] % 3]
            cnt["s"] += 1
            qt, qb_ = (qd, qdb) if kind == "d" else (qg, qgb)
            Kt, Kb = (Kd, b_Kd) if kind == "d" else (Kg, b_Kg)

            def mmQK(e, pst=pst, Kt=Kt, qt=qt, kind=kind, c=c, var=var, kt=kt):
                last = None
                for hi in range(2):
                    rows = slice(hi * 64, hi * 64 + 64)
                    if kind == "d":
                        lhsT = Kt[rows, c, var, kt * 128:(kt + 1) * 128]
                    else:
                        lhsT = Kt[rows, var, kt * 128:(kt + 1) * 128]
                    last = e.matmul(pst[:, hi, :nt], lhsT=lhsT, rhs=qt[rows, c, :nt], start=True, stop=True)
                return last
            S.group("pe", mmQK, r=[Kb, qb_], w=[psb])
            S.op("act", lambda e, pst=pst, ptile=ptile, sc=sc: e.activation(out=ptile[:, :, :nt], in_=pst[:, :, :nt], func=AF.Exp, scale=sc),
                 r=[psb], w=[ptb])
            pend.append((pi, pr, ki, kt, ptile, ptb))
            if len(pend) > LAG:
                emit_pv(pend.pop(0))
        while pend:
            emit_pv(pend.pop(0))
        drain_post()
        post_gen[0] = post(t0, nt, Oall, b_Oall)
    drain_post()
    S.barrier()
    P.close()


def ph_gdn_a(k, l):
    S, nc = k.S, k.nc
    P = Phase(k)
    W = T + 8
    segs = [(2, 0, 256), (262, 256, 4096)]
    xin = [P.sb("xin%d" % i, [128, W], F32) for i in range(2)]
    acc = [P.sb("acc%d" % i, [128, T], F32) for i in range(2)]
    sl, b_sl = P.sb("sl", [128, T], F32)
    sqb, b_sqb = P.sb("sqb", [128, T], BF16)
    rn, b_rn = P.sb("rn", [128, T], F32)
    ob = [P.sb("gob%d" % i, [128, T], BF16) for i in range(2)]
    cw, b_cw = P.sb("cw", [128, 9, 5], F32)
    tk = [P.sb("tk%d" % i, [128, 4, 128], BF16) for i in range(2)]
    ps_n = [P.ps("ps_n%d" % i, [128, 512]) for i in range(2)]
    ps_t = [P.ps("ps_t%d" % i, [128, 4, 128], BF16) for i in range(2)]
    S.dma(cw[:], k.conv_w[l], w=[b_cw], sem=b_cw)
    for i in range(2):
        S.op("pool", lambda e, i=i: e.memset(xin[i][0][:], 0.0), w=[xin[i][1]])
    cnt = {"n": 0, "t": 0}
    for c in range(9):
        xt, xb = xin[c % 2]
        at, ab = acc[c % 2]
        for (p0, t0, n) in segs:
            S.dma(xt[:, p0:p0 + n], k.gqkvT[c, :, t0:t0 + n], w=[xb], sem=xb)
        eng = "dve"
        for (p0, t0, n) in segs:
            for tap in range(5):
                src = xt[:, p0 + tap - 2:p0 + tap - 2 + n]
                if tap == 0:
                    S.op(eng, lambda e, src=src, t0=t0, n=n: e.tensor_scalar(at[:, t0:t0 + n], src, cw[:, c, 0:1], None, ALU.mult),
                         r=[xb, b_cw], w=[ab])
                else:
                    S.op(eng, lambda e, src=src, t0=t0, n=n, tap=tap: e.scalar_tensor_tensor(
                        out=at[:, t0:t0 + n], in0=src, scalar=cw[:, c, tap:tap + 1], in1=at[:, t0:t0 + n],
                        op0=ALU.mult, op1=ALU.add), r=[xb, b_cw, ab], w=[ab])
        o_t, o_b = ob[c % 2]
        if c >= 6:
            S.op("act", lambda e: e.activation(out=o_t[:], in_=at[:], func=AF.Silu), r=[ab], w=[o_b])
        else:
            S.op("act", lambda e: e.activation(out=sl[:], in_=at[:], func=AF.Silu), r=[ab], w=[b_sl])
            S.op("act", lambda e: e.activation(out=sqb[:], in_=sl[:], func=AF.Square), r=[b_sl], w=[b_sqb])
            for t0 in range(0, T, 512):
                n = min(512, T - t0)
                pn, pnb = ps_n[cnt["n"] % 2]
                cnt["n"] += 1
                S.op("pe", lambda e, t0=t0, n=n, pn=pn: e.matmul(pn[:, :n], lhsT=k.blk64s[:], rhs=sqb[:, t0:t0 + n], start=True, stop=True),
                     r=[b_sqb, k.b_blk64s], w=[pnb])
                S.op("act", lambda e, t0=t0, n=n, pn=pn: e.activation(out=rn[:, t0:t0 + n], in_=pn[:, :n], func=AF.Ln, bias=k.epsc[:, 0:1]),
                     r=[pnb, k.b_epsc], w=[b_rn])
            S.op("act", lambda e: e.activation(out=rn[:], in_=rn[:], func=AF.Exp, scale=-0.5), r=[b_rn], w=[b_rn])
            sc = 0.125 if c < 3 else 1.0
            S.op("dve", lambda e: e.scalar_tensor_tensor(out=o_t[:], in0=sl[:], scalar=sc, in1=rn[:], op0=ALU.mult, op1=ALU.mult),
                 r=[b_sl, b_rn], w=[o_b])
            dst = k.QnT[c] if c < 3 else k.KnT[c - 3]
            S.dma(dst, o_t[:], r=[o_b], sem=o_b)
        if c >= 3:
            dstT = k.Ktok if c < 6 else k.Vtok2
            cc = (c - 3) % 3
            for g0 in range(0, 34, 4):
                ng = min(4, 34 - g0)
                pt, ptb = ps_t[cnt["t"] % 2]
                tt, ttb = tk[cnt["t"] % 2]
                cnt["t"] += 1

                def tr(e, g0=g0, ng=ng, pt=pt):
                    last = None
                    for i in range(ng):
                        last = e.transpose(pt[:, i, :], o_t[:, (g0 + i) * 128:(g0 + i + 1) * 128], k.identb[:])
                    return last
                S.group("pe", tr, r=[o_b, k.b_identb], w=[ptb])
                S.op("act" if (g0 // 4) % 2 == 0 else "dve",
                     (lambda e, pt=pt, tt=tt, ng=ng: e.activation(out=tt[:, :ng, :], in_=pt[:, :ng, :], func=AF.Copy)) if (g0 // 4) % 2 == 0
                     else (lambda e, pt=pt, tt=tt, ng=ng: e.tensor_copy(tt[:, :ng, :], pt[:, :ng, :])), r=[ptb], w=[ttb])
                S.dma(dstT[g0 * 128:(g0 + ng) * 128, cc * 128:(cc + 1) * 128].rearrange("(n p) c -> p n c", p=128), tt[:, :ng, :],
                      r=[ttb], sem=ttb)
    S.barrier()
    P.close()


def ph_gdn_b(k, l, need_ctx):
    S, nc = k.S, k.nc
    P = Phase(k)
    NCH = 34
    MI, b_MI = P.sb("MI", [128, 2, 128], F32)
    SU, b_SU = P.sb("SU", [128, 2, 128], F32)
    EMK, b_EMK = P.sb("EMK", [128, 2, 256], F32)
    idf, b_idf = P.sb("idf", [128, 128], F32)
    on128, b_on128 = P.sb("on128", [128, 128], F32)
    gnb, b_gnb = P.sb("gnb", [128, 64], F32)
    nal, b_nal = P.sb("nal", [128, 12], F32)
    dtb, b_dtb = P.sb("dtb", [128, 12], F32)
    onec, b_onec = P.sb("onec", [128, 1], F32)
    S.dma(MI[:], k.c_MI, w=[b_MI], sem=b_MI)
    S.dma(SU[:], k.c_SU, w=[b_SU], sem=b_SU)
    S.dma(EMK[:], k.c_EMK, w=[b_EMK], sem=b_EMK)
    S.dma(idf[:], k.c_identf, w=[b_idf], sem=b_idf)
    mk_ = [P.sb("MK%d" % i, [128, 128], F32) for i in range(3)]
    MK = [x[0] for x in mk_]
    b_MK = Buf("MK")
    for i in range(3):
        S.dma(MK[i][:], k.c_MK[i], w=[b_MK], sem=mk_[i][1])
    S.op("pool", lambda e: e.memset(on128[:], 1.0), w=[b_on128])
    S.op("pool", lambda e: e.memset(onec[:], 1.0), w=[b_onec])
    S.dma(gnb[:], k.gdn_g[l:l + 1, :].partition_broadcast(128), w=[b_gnb], sem=b_gnb)
    S.dma(nal[:], k.gdn_alog[l:l + 1, :].partition_broadcast(128), w=[b_nal], sem=b_nal)
    S.dma(dtb[:], k.gdn_dtb[l:l + 1, :].partition_broadcast(128), w=[b_dtb], sem=b_dtb)
    S.op("act", lambda e: e.activation(out=nal[:], in_=nal[:], func=AF.Exp), r=[b_nal], w=[b_nal])
    S.op("dve", lambda e: e.tensor_scalar(nal[:], nal[:], -1.0, None, ALU.mult), r=[b_nal], w=[b_nal])
    ab, b_ab = P.sb("ab", [128, NCH, 24], F32)
    G, b_G = P.sb("G", [128, NCH, 12], F32)
    LB, b_LB = P.sb("LB", [128, NCH, 12], F32)
    BETA, b_BETA = P.sb("BETA", [128, NCH, 12], F32)
    KDS, b_KDS = P.sb("KDS", [128, NCH, 12], F32)
    GL, b_GL = P.sb("GL", [128, NCH, 12], F32)
    GLP, b_GLP = P.sb("GLP", [128, NCH, 2, 3], F32)
    S.dma(ab[:], k.zab[:, 384:408].rearrange("(n p) c -> p n c", p=128), w=[b_ab], sem=b_ab)
    S.op("dve", lambda e: e.tensor_tensor(G[:], ab[:, :, 0:12], dtb[:].unsqueeze(1).broadcast_to([128, NCH, 12]), ALU.add),
         r=[b_ab, b_dtb], w=[b_G])
    S.op("act", lambda e: e.activation(out=G[:], in_=G[:], func=AF.Exp), r=[b_G], w=[b_G])
    S.op("act", lambda e: e.activation(out=G[:], in_=G[:], func=AF.Ln, bias=onec[:, 0:1]), r=[b_G, b_onec], w=[b_G])
    S.op("dve", lambda e: e.tensor_tensor(G[:], G[:], nal[:].unsqueeze(1).broadcast_to([128, NCH, 12]), ALU.mult),
         r=[b_G, b_nal], w=[b_G])
    S.op("act", lambda e: e.activation(out=LB[:], in_=ab[:, :, 12:24], func=AF.Exp, scale=-1.0), r=[b_ab], w=[b_LB])
    S.op("act", lambda e: e.activation(out=LB[:], in_=LB[:], func=AF.Ln, bias=onec[:, 0:1]), r=[b_LB, b_onec], w=[b_LB])
    S.op("dve", lambda e: e.tensor_scalar(LB[:], LB[:], -1.0, None, ALU.mult), r=[b_LB], w=[b_LB])
    S.op("act", lambda e: e.activation(out=BETA[:], in_=LB[:], func=AF.Exp), r=[b_LB], w=[b_BETA])
    PG = []
    for g in range(2):
        t_, _ = P.ps("PG%d" % g, [128, 3, 512])
        PG.append((t_, [Buf("PG%d_%d" % (g, i), excl=True) for i in range(3)]))
    bA = [(PG[0][0][:, i, :], PG[0][1][i]) for i in range(3)]
    bM = [(PG[1][0][:, i, :], PG[1][1][i]) for i in range(2)]
    bX = (PG[1][0][:, 2, :], PG[1][1][2])
    PBK = [bA[0], bA[1], bA[2], bM[0], bM[1], bX]
    bS = [P.ps("bS%d" % i, [128, 512]) for i in range(2)]
    for half in range(2):
        pt, pb = bA[half]
        n0 = half * 17

        def mm(e, pt=pt, n0=n0):
            last = None
            for i in range(17):
                n = n0 + i
                for d in range(2):
                    last = e.matmul(pt[:, i * 24 + d * 6:i * 24 + d * 6 + 6], lhsT=SU[:, d, :], rhs=G[:, n, d * 6:d * 6 + 6],
                                    start=True, stop=True)
                last = e.matmul(pt[:, i * 24 + 12:i * 24 + 24], lhsT=on128[:], rhs=G[:, n, :], start=True, stop=True)
            return last
        S.group("pe", mm, r=[b_SU, b_G, b_on128], w=[pb])
        S.op("act", lambda e, pt=pt, n0=n0: e.activation(out=KDS[:, n0:n0 + 17, :],
                                                         in_=pt[:, 0:408].rearrange("p (n c) -> p n c", c=24)[:, :, 0:12], func=AF.Exp),
             r=[pb], w=[b_KDS])
        S.op("act", lambda e, pt=pt, n0=n0: e.activation(out=GL[:, n0:n0 + 17, :],
                                                         in_=pt[:, 0:408].rearrange("p (n c) -> p n c", c=24)[:, :, 12:24], func=AF.Exp),
             r=[pb], w=[b_GL])
    GLv = GL[:].rearrange("p n (d c h) -> p n d c h", d=2, c=3)
    S.op("dve", lambda e: e.tensor_copy(GLP[0:64], GLv[0:64, :, :, :, 0]), r=[b_GL], w=[b_GLP])
    S.op("dve", lambda e: e.tensor_copy(GLP[64:128], GLv[64:128, :, :, :, 1]), r=[b_GL], w=[b_GLP])

    GST = int(os.environ.get("GST", "99"))
    if GST <= 1:
        S.barrier(); P.close(); return
    Ofin, b_Ofin = P.sb("Ofin", [128, NCH, 384], F32)
    def mk(i):
        w = K()
        w.qk, w.b_qk = P.sb("cqk%d" % i, [128, 6, 128], BF16)
        w.kt, w.b_kt = P.sb("ckt%d" % i, [128, 384], BF16)
        w.vt, w.b_vt = P.sb("cvt%d" % i, [128, 384], BF16)
        w.rhsD, w.b_rhsD = P.sb("rhsD%d" % i, [128, 6, 2, 128], F32)
        w.Em, w.b_Em = P.sb("Em%d" % i, [128, 6, 256], F32)
        w.EB, w.b_EB = P.sb("EB%d" % i, [128, 6, 256], F32)
        w.qkTm, w.b_qkTm = P.sb("qkTm%d" % i, [128, 6, 128], BF16)
        def grp(nm, shape, dt):
            xs = [P.sb("%s%d_%d" % (nm, i, g), shape, dt) for g in range(2)]
            return [x[0] for x in xs], [x[1] for x in xs]
        w.Z, w.b_Zg = grp("Z", [128, 3, 3, 128], BF16)
        w.T32, w.b_T32g = grp("T32", [128, 3, 128], F32)
        w.O1T, w.b_O1Tg = grp("O1T", [128, 3, 128], BF16)
        w.O2T, w.b_O2Tg = grp("O2T", [128, 3, 128], BF16)
        w.NY, w.b_NYg = grp("NY", [128, 3, 128], BF16)
        w.QdT, w.b_QdT = P.sb("QdT%d" % i, [128, 3, 128], BF16)
        w.RwT, w.b_RwT = P.sb("RwT%d" % i, [128, 3, 128], BF16)
        w.KD, w.b_KD = P.sb("KD%d" % i, [128, 3, 2, 128], BF16)
        w.Ru, w.b_Ru = P.sb("Ru%d" % i, [128, 6, 64], F32)
        S.op("pool", lambda e: e.memset(w.KD[:], 0.0), w=[w.b_KD])
        return w
    WS = [mk(0), mk(1)]
    S32 = [P.sb("S32_%d" % i, [128, 64], F32) for i in range(6)]
    Sbf = [P.sb("Sbf%d" % i, [128, 128], BF16) for i in range(6)]
    Xb = [P.sb("Xb%d" % i, [128, 128], BF16) for i in range(3)]
    vnb = [P.sb("vnb%d" % i, [128, 128], BF16) for i in range(3)]
    for i in range(6):
        S.op("pool", lambda e, i=i: e.memset(S32[i][0][:], 0.0), w=[S32[i][1]])
        S.op("pool", lambda e, i=i: e.memset(Sbf[i][0][:], 0.0), w=[Sbf[i][1]])
    bSc = [bS[0], bS[1], bS[0]]

    def prep(w, n, d):
        t0 = n * 128
        S.dma(w.qk[:, 0:3, :], k.QnT[:, :, t0:t0 + 128].rearrange("c p t -> p c t"), w=[w.b_qk], sem=w.b_qk)
        S.dma(w.qk[:, 3:6, :], k.KnT[:, :, t0:t0 + 128].rearrange("c p t -> p c t"), w=[w.b_qk], sem=w.b_qk)
        S.dma(w.kt[:], k.Ktok[t0:t0 + 128, :], w=[w.b_kt], sem=w.b_kt)
        S.dma(w.vt[:], k.Vtok2[t0:t0 + 128, :], w=[w.b_vt], sem=w.b_vt)
        for h in range(6):
            u = d * 6 + h
            S.op("dve", lambda e, h=h, u=u: e.tensor_scalar(w.rhsD[:, h, 0, :], MI[:, d, :], G[:, n, u:u + 1], None, ALU.mult),
                 r=[b_MI, b_G], w=[w.b_rhsD])
            S.op("dve", lambda e, h=h, u=u: e.scalar_tensor_tensor(out=w.rhsD[:, h, 1, :], in0=idf[:], scalar=LB[:, n, u:u + 1],
                                                                    in1=w.rhsD[:, h, 0, :], op0=ALU.mult, op1=ALU.add),
                 r=[b_idf, b_LB, w.b_rhsD], w=[w.b_rhsD])
        for c in range(3):
            for (lhs, lhsb, (pt, pb), dst, dstb) in ((SU[:, d, :], b_SU, bA[c], w.Em, w.b_Em), (on128[:], b_on128, PBK[3 + c], w.EB, w.b_EB)):
                def mmD(e, c=c, pt=pt, lhs=lhs):
                    last = None
                    for hp in range(2):
                        last = e.matmul(pt[:, hp * 256:(hp + 1) * 256], lhsT=lhs,
                                        rhs=w.rhsD[:, 2 * c + hp, :, :].rearrange("p a b -> p (a b)"), start=True, stop=True)
                    return last
                S.group("pe", mmD, r=[lhsb, w.b_rhsD], w=[pb])
                S.op("act", lambda e, c=c, pt=pt, dst=dst: e.activation(out=dst[:, 2 * c:2 * c + 2, :].rearrange("p a b -> p (a b)"), in_=pt[:], func=AF.Exp),
                     r=[pb], w=[dstb])
        S.op("dve", lambda e: e.tensor_tensor(w.Em[:], w.Em[:], EMK[:, d, :].unsqueeze(1).broadcast_to([128, 6, 256]), ALU.mult),
             r=[w.b_Em, b_EMK], w=[w.b_Em])
        tick()
        kslots = [([0, 2], bA[0]), ([1, 3], bA[1]), ([4], bA[2]), ([5], bM[0])]
        for heads, (pt, pb) in kslots:
            def mmK(e, heads=heads, pt=pt):
                last = None
                for i, h in enumerate(heads):
                    ps_ = slice((h % 2) * 64, (h % 2) * 64 + 64)
                    cq = h // 2
                    e.matmul(pt[:, i * 256:i * 256 + 128], lhsT=w.qk[ps_, 3 + cq, :], rhs=w.qk[ps_, cq, :], start=True, stop=True)
                    last = e.matmul(pt[:, i * 256 + 128:i * 256 + 256], lhsT=w.qk[ps_, 3 + cq, :], rhs=w.qk[ps_, 3 + cq, :],
                                    start=True, stop=True)
                return last
            S.group("pe", mmK, r=[w.b_qk], w=[pb])
            nh = len(heads)
            hs = slice(heads[0], heads[-1] + 1, 2)
            pv = pt[:, 0:nh * 256].rearrange("p (h x c) -> p h x c", h=nh, x=2)
            S.op("dve", lambda e, hs=hs, pv=pv: e.tensor_tensor(w.qkTm[:, hs, :], pv[:, :, 0, :], w.Em[:, hs, 0:128], ALU.mult),
                 r=[pb, w.b_Em], w=[w.b_qkTm])
            for i, h in enumerate(heads):
                S.op("dve", lambda e, i=i, h=h, pv=pv: e.tensor_tensor(w.T32[h // 3][:, h % 3, :], pv[:, i, 1, :], w.Em[:, h, 128:256], ALU.mult),
                     r=[pb, w.b_Em], w=[w.b_T32g[h // 3]])
        tick()
        m3 = lambda t: t[:].unsqueeze(1).broadcast_to([128, 3, 128])
        for g in range(2):
            T32, T32b = w.T32[g], w.b_T32g[g]
            S.op("dve", lambda e, g=g, T32=T32: e.tensor_tensor(w.Z[g][:, :, 1, :], T32[:], m3(MK[2]), ALU.mult), r=[T32b, b_MK], w=[w.b_Zg[g]])
            S.op("pool", lambda e, g=g: e.tensor_tensor(w.Z[g][:, :, 2, :], w.Z[g][:, :, 1, :], m3(idf), ALU.add), r=[w.b_Zg[g], b_idf], w=[w.b_Zg[g]])
            S.op("pool", lambda e, g=g, T32=T32: e.tensor_tensor(w.O1T[g][:], T32[:], m3(MK[0]), ALU.mult), r=[T32b, b_MK], w=[w.b_O1Tg[g]])
            S.op("pool", lambda e, g=g, T32=T32: e.tensor_tensor(w.O2T[g][:], T32[:], m3(MK[1]), ALU.mult), r=[T32b, b_MK], w=[w.b_O2Tg[g]])
        tick()

        def pe3(g, fn, r):
            pG, pGb = PG[g]

            def f(e):
                last = None
                for i in range(3):
                    last = fn(e, pG, i)
                return last
            S.group("pe", f, r=r, w=pGb)
        for g in range(2):
            Z = w.Z[g]
            pe3(g, lambda e, pG, i, Z=Z: e.matmul(pG[:, i, 0:128], lhsT=Z[:, i, 1, :], rhs=k.identb[:], start=True, stop=True), [w.b_Zg[g], k.b_identb])
        for g in range(2):
            pG, pGb = PG[g]
            S.op("act", lambda e, g=g, pG=pG: e.activation(out=w.Z[g][:, :, 0, :], in_=pG[:, :, 0:128], func=AF.Copy), r=pGb, w=[w.b_Zg[g]])
        for j in range(5):
            tick()
            for g in range(2):
                Z = w.Z[g]
                if j < 4:
                    pe3(g, lambda e, pG, i, Z=Z: e.matmul(pG[:, i, 0:128], lhsT=Z[:, i, 1, :], rhs=Z[:, i, 0, :], start=True, stop=True), [w.b_Zg[g]])
                if j == 0:
                    pe3(g, lambda e, pG, i, Z=Z: e.matmul(pG[:, i, 128:256], lhsT=Z[:, i, 0, :], rhs=Z[:, i, 1, :], start=True, stop=True), [w.b_Zg[g]])
                elif j < 4:
                    pe3(g, lambda e, pG, i, Z=Z: e.matmul(pG[:, i, 128:384], lhsT=Z[:, i, 0, :], rhs=Z[:, i, 1:3, :].rearrange("p a b -> p (a b)"),
                                                          start=True, stop=True), [w.b_Zg[g]])
                else:
                    pe3(g, lambda e, pG, i, Z=Z: e.matmul(pG[:, i, 256:384], lhsT=Z[:, i, 0, :], rhs=Z[:, i, 2, :], start=True, stop=True), [w.b_Zg[g]])
            for g in range(2):
                pG, pGb = PG[g]
                Z = w.Z[g]
                if j >= 1:
                    S.op("dve", lambda e, Z=Z, pG=pG: e.tensor_tensor(Z[:, :, 2, :], Z[:, :, 2, :], pG[:, :, 256:384], ALU.add), r=pGb + [w.b_Zg[g]], w=[w.b_Zg[g]])
                if j < 4:
                    S.op("act", lambda e, Z=Z, pG=pG: e.activation(out=Z[:, :, 0:2, :].rearrange("p h a b -> p h (a b)"), in_=pG[:, :, 0:256], func=AF.Copy),
                         r=pGb, w=[w.b_Zg[g]])
        tick()
        for g in range(2):
            Z = w.Z[g]
            pe3(g, lambda e, pG, i, Z=Z: e.matmul(pG[:, i, 128:256], lhsT=Z[:, i, 2, :], rhs=k.identb[:], start=True, stop=True), [w.b_Zg[g], k.b_identb])
        for g in range(2):
            pG, pGb = PG[g]
            S.op("act", lambda e, g=g, pG=pG: e.activation(out=w.Z[g][:, :, 1, :], in_=pG[:, :, 128:256], func=AF.Copy), r=pGb, w=[w.b_Zg[g]])
        for mi in range(2):
            tick()
            OT = w.O1T if mi == 0 else w.O2T
            OTb = w.b_O1Tg if mi == 0 else w.b_O2Tg
            for g in range(2):
                Z = w.Z[g]
                pe3(g, lambda e, pG, i, Z=Z, OTg=OT[g]: e.matmul(pG[:, i, 0:128], lhsT=OTg[:, i, :], rhs=Z[:, i, 1, :], start=True, stop=True), [OTb[g], w.b_Zg[g]])
            for g in range(2):
                pG, pGb = PG[g]
                S.op("act", lambda e, g=g, pG=pG: e.activation(out=w.NY[g][:], in_=pG[:, :, 0:128], func=AF.Copy), r=pGb, w=[w.b_NYg[g]])
            for g in range(2):
                Z = w.Z[g]
                pe3(g, lambda e, pG, i, Z=Z, NYg=w.NY[g]: e.matmul(pG[:, i, 256:384], lhsT=NYg[:, i, :], rhs=Z[:, i, 2, :], start=True, stop=True), [w.b_NYg[g], w.b_Zg[g]])
                if mi == 0:
                    pe3(g, lambda e, pG, i, Z=Z, NYg=w.NY[g]: e.matmul(pG[:, i, 128:256], lhsT=Z[:, i, 2, :], rhs=NYg[:, i, :], start=True, stop=True), [w.b_NYg[g], w.b_Zg[g]])
            for g in range(2):
                pG, pGb = PG[g]
                Z = w.Z[g]
                if mi == 0:
                    S.op("dve", lambda e, Z=Z, pG=pG: e.tensor_tensor(Z[:, :, 1:3, :].rearrange("p h a b -> p h (a b)"), Z[:, :, 1:3, :].rearrange("p h a b -> p h (a b)"),
                                                                       pG[:, :, 128:384], ALU.add), r=pGb + [w.b_Zg[g]], w=[w.b_Zg[g]])
                else:
                    S.op("dve", lambda e, Z=Z, pG=pG: e.tensor_tensor(Z[:, :, 2, :], Z[:, :, 2, :], pG[:, :, 256:384], ALU.add), r=pGb + [w.b_Zg[g]], w=[w.b_Zg[g]])
        for half in range(2):
            ps_ = slice(half * 64, half * 64 + 64)
            S.op("pool", lambda e, ps_=ps_, half=half: e.tensor_tensor(w.QdT[ps_, :, :], w.qk[ps_, 0:3, :], w.EB[ps_, half:6:2, 0:128], ALU.mult),
                 r=[w.b_qk, w.b_EB], w=[w.b_QdT])
            S.op("pool", lambda e, ps_=ps_, half=half: e.tensor_tensor(w.RwT[ps_, :, :], w.qk[ps_, 3:6, :], w.EB[ps_, half:6:2, 128:256], ALU.mult),
                 r=[w.b_qk, w.b_EB], w=[w.b_RwT])
        KDv = w.KD[:].rearrange("p c a b -> p c (a b)").rearrange("p c (x y) -> p c x y", y=64)[:, :, 0:4:3, :]
        S.op("dve", lambda e: e.tensor_tensor(KDv, w.kt[:].rearrange("p (c a y) -> p c a y", c=3, a=2),
                                              KDS[:, n, d * 6:d * 6 + 6].rearrange("p (c a) -> p c a", a=2).unsqueeze(3).broadcast_to([128, 3, 2, 64]),
                                              ALU.mult), r=[w.b_kt, b_KDS], w=[w.b_KD])
        S.op("dve", lambda e: e.tensor_tensor(w.Ru[:], w.vt[:].rearrange("p (h y) -> p h y", y=64),
                                              BETA[:, n, d * 6:d * 6 + 6].unsqueeze(2).broadcast_to([128, 6, 64]), ALU.mult),
             r=[w.b_vt, b_BETA], w=[w.b_Ru])

    def scan(w, n, d, first_pass):
        prs = []
        for c in range(3):
            bank = bS[c % 2]
            prs.append((c, (bank[0][:, (c // 2) * 256:(c // 2) * 256 + 256], bank[1]), S32[d * 3 + c], Sbf[d * 3 + c], Xb[c], vnb[c]))
        for (c, (bk, bkb), (s32, s32b), (sbf, sbfb), (xb, xbb), (vb, vbb)) in prs:
            S.op("pe", lambda e, bk=bk, c=c, sbf=sbf: e.matmul(bk[:, 0:128], lhsT=w.RwT[:, c, :], rhs=sbf[:], start=True, stop=True),
                 r=[w.b_RwT, sbfb], w=[bkb])
            if c == 1:
                yield
        yield
        for (c, (bk, bkb), (s32, s32b), (sbf, sbfb), (xb, xbb), (vb, vbb)) in prs:
            S.op("dve", lambda e, bk=bk, c=c, xb=xb: e.tensor_tensor(xb[:], w.Ru[:, 2 * c:2 * c + 2, :].rearrange("p a b -> p (a b)"), bk[:, 0:128], ALU.subtract),
                 r=[w.b_Ru, bkb], w=[xbb])

            def mmV(e, bk=bk, c=c, xb=xb):
                last = None
                for hp in range(2):
                    last = e.matmul(bk[:, 128 + hp * 64:128 + hp * 64 + 64], lhsT=w.Z[(2 * c + hp) // 3][:, (2 * c + hp) % 3, 2, :], rhs=xb[:, hp * 64:hp * 64 + 64],
                                    start=True, stop=True)
                return last
            S.group("pe", mmV, r=[w.b_Zg[0], w.b_Zg[1], xbb], w=[bkb])
            if c == 1:
                yield
        yield
        for (c, (bk, bkb), (s32, s32b), (sbf, sbfb), (xb, xbb), (vb, vbb)) in prs:
            S.op("act", lambda e, bk=bk, vb=vb: e.activation(out=vb[:], in_=bk[:, 128:256], func=AF.Copy), r=[bkb], w=[vbb])

            def mmO(e, bk=bk, c=c, sbf=sbf, vb=vb):
                e.matmul(bk[:, 0:128], lhsT=w.QdT[:, c, :], rhs=sbf[:], start=True, stop=False)
                for hp in range(2):
                    e.matmul(bk[:, hp * 64:hp * 64 + 64], lhsT=w.qkTm[:, 2 * c + hp, :], rhs=vb[:, hp * 64:hp * 64 + 64],
                             start=False, stop=(hp == 1))
                e.matmul(bk[:, 128:192], lhsT=w.KD[:, c, 0, :], rhs=vb[:, 0:64], start=True, stop=False)
                last = e.matmul(bk[:, 128:192], lhsT=w.KD[:, c, 1, :], rhs=vb[:, 64:128], start=False, stop=True)
                return last
            S.group("pe", mmO, r=[w.b_QdT, sbfb, w.b_qkTm, vbb, w.b_KD], w=[bkb])
            if c == 1:
                yield
        yield
        for (c, (bk, bkb), (s32, s32b), (sbf, sbfb), (xb, xbb), (vb, vbb)) in prs:
            S.op("dve", lambda e, bk=bk, c=c, s32=s32: e.scalar_tensor_tensor(out=s32[:], in0=s32[:], scalar=GLP[:, n, d, c:c + 1], in1=bk[:, 128:192],
                                                                             op0=ALU.mult, op1=ALU.add), r=[s32b, b_GLP, bkb], w=[s32b])
            S.op("pool", lambda e, sbf=sbf, s32=s32: e.tensor_copy(sbf[0:64, 0:64], s32[0:64, :]), r=[s32b], w=[sbfb])
            S.op("pool", lambda e, sbf=sbf, s32=s32: e.tensor_copy(sbf[64:128, 64:128], s32[64:128, :]), r=[s32b], w=[sbfb])
            ov = Ofin[:, n, 2 * c * 64:(2 * c + 2) * 64]
            if first_pass:
                S.op("act", lambda e, ov=ov, bk=bk: e.activation(out=ov, in_=bk[:, 0:128], func=AF.Copy), r=[bkb], w=[b_Ofin])
            else:
                S.op("dve", lambda e, ov=ov, bk=bk: e.tensor_tensor(ov, ov, bk[:, 0:128], ALU.add), r=[bkb, b_Ofin], w=[b_Ofin])
            if c == 1:
                yield

    cur_scan = [None]

    def tick():
        g_ = cur_scan[0]
        if g_ is not None:
            try:
                next(g_)
            except StopIteration:
                cur_scan[0] = None

    def drain():
        while cur_scan[0] is not None:
            tick()

    orders = [list(range(NCH)), [1, 0] + list(range(NCH - 1, 1, -1))]
    step = 0
    if GST in (2, 3):
        w = WS[0]
        DN, DD = int(os.environ.get("DN", "0")), int(os.environ.get("DD", "0"))
        prep(w, DN, DD)
        if GST == 3:
            cur_scan[0] = scan(w, DN, DD, True)
            drain()
        S.barrier()
        def dump(name, ap, shape, dt=F32):
            o = nc.dram_tensor(name, list(shape), dt, kind="ExternalOutput").ap()
            S.dma(o, ap, sem=Buf(name))
        dump("d_G", G[:].rearrange("p n c -> p (n c)"), [128, NCH * 12])
        dump("d_LB", LB[:].rearrange("p n c -> p (n c)"), [128, NCH * 12])
        dump("d_KDS", KDS[:].rearrange("p n c -> p (n c)"), [128, NCH * 12])
        dump("d_GLP", GLP[:].rearrange("p n d c -> p (n d c)"), [128, NCH * 6])
        dump("d_Em", w.Em[:].rearrange("p a b -> p (a b)"), [128, 6 * 256])
        dump("d_EB", w.EB[:].rearrange("p a b -> p (a b)"), [128, 6 * 256])
        dump("d_qkTm", w.qkTm[:].rearrange("p a b -> p (a b)"), [128, 6 * 128], BF16)
        dump("d_PT32", w.T32[0][:], [128, 3, 128])
        dump("d_MP", w.Z[0][:, :, 2, :], [128, 3, 128], BF16)
        dump("d_QdT", w.QdT[:].rearrange("p a b -> p (a b)"), [128, 384], BF16)
        dump("d_RwT", w.RwT[:].rearrange("p a b -> p (a b)"), [128, 384], BF16)
        dump("d_KD", w.KD[:].rearrange("p a b c -> p (a b c)"), [128, 768], BF16)
        dump("d_Ru", w.Ru[:].rearrange("p a b -> p (a b)"), [128, 384])
        dump("d_Ofin", Ofin[:, DN, :], [128, 384])
        dump("d_S32", S32[DD * 3][0][:], [128, 64])
        S.barrier(); P.close(); return
    for d in range(2):
        order = orders[d]
        prep(WS[step % 2], order[0], d)
        for i, n in enumerate(order):
            w = WS[step % 2]
            step += 1
            cur_scan[0] = scan(w, n, d, d == 0)
            if i + 1 < len(order):
                prep(WS[step % 2], order[i + 1], d)
            drain()
    if GST == 4:
        S.barrier(); P.close(); return

    zt = [P.sb("zt%d" % i, [128, 384], F32) for i in range(2)]
    sqo, b_sqo = P.sb("sqo", [128, 384], F32)
    ssum, b_ssum = P.sb("ssum", [128, 6], F32)
    yb = [P.sb("yb%d" % i, [128, 384], F32) for i in range(2)]
    yT = [P.sb("yT%d" % i, [128, 3, 512], BF16) for i in range(2)]
    tiles = list(range(0 if need_ctx else 2, NCH))
    groups = []
    if need_ctx:
        groups.append([0, 1])
    for g0 in range(2, NCH, 4):
        groups.append(list(range(g0, g0 + 4)))
    cntr = 0
    for gi, grp in enumerate(groups):
        yt_, ytb = yT[gi % 2]
        for ti, n in enumerate(grp):
            z_, zb = zt[cntr % 2]
            y_, ybb = yb[cntr % 2]
            cntr += 1
            S.dma(z_[:], k.zab[n * 128:(n + 1) * 128, 0:384], w=[zb], sem=zb)
            S.op("act", lambda e, z_=z_: e.activation(out=z_[:], in_=z_[:], func=AF.Silu), r=[zb], w=[zb])
            o_ = Ofin[:, n, :]
            S.op("dve", lambda e, o_=o_: e.tensor_tensor(sqo[:], o_, o_, ALU.mult), r=[b_Ofin], w=[b_sqo])
            S.op("dve", lambda e: e.reduce_sum(ssum[:], sqo[:].rearrange("p (h y) -> p h y", y=64), axis=AX.X), r=[b_sqo], w=[b_ssum])
            S.op("act", lambda e: e.activation(out=ssum[:], in_=ssum[:], func=AF.Ln, scale=1.0 / 64, bias=k.epsc[:, 0:1]), r=[b_ssum, k.b_epsc], w=[b_ssum])
            S.op("act", lambda e: e.activation(out=ssum[:], in_=ssum[:], func=AF.Exp, scale=-0.5), r=[b_ssum], w=[b_ssum])
            S.op("dve", lambda e, o_=o_: e.tensor_tensor(sqo[:].rearrange("p (h y) -> p h y", y=64), o_.rearrange("p (h y) -> p h y", y=64),
                                                          ssum[:].unsqueeze(2).broadcast_to([128, 6, 64]), ALU.mult), r=[b_Ofin, b_ssum], w=[b_sqo])
            S.op("pool", lambda e: e.tensor_tensor(sqo[:].rearrange("p (h y) -> p h y", y=64), sqo[:].rearrange("p (h y) -> p h y", y=64),
                                                   gnb[:].unsqueeze(1).broadcast_to([128, 6, 64]), ALU.mult), r=[b_sqo, b_gnb], w=[b_sqo])
            S.op("pool", lambda e, z_=z_, y_=y_: e.tensor_tensor(y_[:], sqo[:], z_[:], ALU.mult), r=[b_sqo, zb], w=[ybb])

            def trY(e, y_=y_):
                last = None
                for c in range(3):
                    last = e.transpose(bX[0][:, c * 128:(c + 1) * 128], y_[:, c * 128:(c + 1) * 128], idf[:])
                return last
            S.group("pe", trY, r=[ybb, b_idf], w=[bX[1]])
            S.op("act", lambda e, ti=ti, yt_=yt_: e.activation(out=yt_[:, :, ti * 128:(ti + 1) * 128],
                                                               in_=bX[0][:, 0:384].rearrange("p (c t) -> p c t", c=3), func=AF.Copy), r=[bX[1]], w=[ytb])
        t0 = grp[0] * 128
        nt = len(grp) * 128
        S.dma(k.mixT[5:8, :, t0:t0 + nt].rearrange("c p t -> p c t"), yt_[:, :, :nt], r=[ytb], sem=ytb)
    S.barrier()
    P.close()


def ph_out(k, l, hsrc, hdst, need_ctx):
    S, nc = k.S, k.nc
    P = Phase(k)
    wbf, b_w = P.sb("w_out_bf", [128, 8, 1024], BF16)
    hb = [P.sb("ohb%d" % i, [128, 8, 512], F32) for i in range(2)]
    ho = [P.sb("oho%d" % i, [128, 8, 512], F32) for i in range(2)]
    mx = [P.sb("omx%d" % i, [128, 8, 512], BF16) for i in range(2)]
    pp = [P.ps("opp%d" % i, [128, 512]) for i in range(4)]
    wv = k.w_out[l].rearrange("(k p) n -> p k n", p=128)
    for ci in range(2):
        st, sbuf_ = ho[ci]
        S.dma(st[:], wv[:, :, ci * 512:(ci + 1) * 512], w=[sbuf_], sem=sbuf_)
        S.op("pool", lambda e, st=st, ci=ci: e.tensor_copy(wbf[:, :, ci * 512:(ci + 1) * 512], st[:]), r=[sbuf_], w=[b_w])
    hv = hsrc.rearrange("(k p) t -> p k t", p=128)
    ov = hdst.rearrange("(k p) t -> p k t", p=128)
    blocks = list(range(0 if need_ctx else 1, len(BLOCKS)))

    def load(b):
        t0, nt = BLOCKS[b]
        S.dma(hb[b % 2][0][:, :, :nt], hv[:, :, t0:t0 + nt], w=[hb[b % 2][1]], sem=hb[b % 2][1])
        S.dma(mx[b % 2][0][:, :, :nt], k.mixT[:, :, t0:t0 + nt].rearrange("c p t -> p c t"), w=[mx[b % 2][1]], sem=mx[b % 2][1])
    load(blocks[0])
    cnt = 0
    for bi, b in enumerate(blocks):
        t0, nt = BLOCKS[b]
        j = 1 if b == 0 else 0
        if bi + 1 < len(blocks):
            load(blocks[bi + 1])
        ht, hbb = hb[b % 2]
        mt, mtb = mx[b % 2]
        ot, otb = ho[b % 2]
        for n in range(8):
            pt, pb = pp[cnt % 4]
            cnt += 1

            def mm(e, n=n, pt=pt):
                last = None
                for kk in range(8):
                    last = e.matmul(pt[:, :nt], lhsT=wbf[:, kk, n * 128:(n + 1) * 128], rhs=mt[:, kk, :nt], start=(kk == 0), stop=(kk == 7))
                return last
            S.group("pe", mm, r=[b_w, mtb], w=[pb])
            S.op("dve", lambda e, n=n, pt=pt: e.scalar_tensor_tensor(out=ot[:, n, :nt], in0=pt[:, :nt], scalar=k.mod[:, 16 + n, j:j + 1],
                                                                     in1=ht[:, n, :nt], op0=ALU.mult, op1=ALU.add),
                 r=[pb, k.b_mod, hbb], w=[otb])
        S.dma(ov[:, :, t0:t0 + nt], ot[:, :, :nt], r=[otb], sem=otb)
    S.barrier()
    P.close()


def ph_ffn(k, l, hsrc, hdst, need_ctx, final):
    S, nc = k.S, k.nc
    P = Phase(k)
    NH = 22
    wgu, b_wgu = P.sb("wgu", [128, 8, 2 * 2816], BF16)
    wdn, b_wdn = P.sb("wdn", [128, NH, 1024], BF16)
    hb = [P.sb("fhb%d" % i, [128, 8, 256], F32) for i in range(2)]
    tmp, b_tmp = P.sb("ftmp", [128, 8, 256], F32)
    stg = [hb[0], hb[1], (tmp, b_tmp)]
    si = 0
    wv = k.w_gu[l].rearrange("(k p) n -> p k n", p=128)
    for ci in range(22):
        st, sbuf_ = stg[si % 3]
        si += 1
        S.dma(st[:], wv[:, :, ci * 256:(ci + 1) * 256], w=[sbuf_], sem=sbuf_)
        S.op("pool" if ci % 2 == 0 else "act",
             (lambda e, st=st, ci=ci: e.tensor_copy(wgu[:, :, ci * 256:(ci + 1) * 256], st[:])) if ci % 2 == 0
             else (lambda e, st=st, ci=ci: e.activation(out=wgu[:, :, ci * 256:(ci + 1) * 256], in_=st[:], func=AF.Copy)),
             r=[sbuf_], w=[b_wgu])
    wv = k.w_down[l].rearrange("(k p) n -> p k n", p=128)
    for m0 in (0, 8, 16):
        nm = min(8, NH - m0)
        for n0 in range(0, 1024, 256):
            st, sbuf_ = stg[si % 3]
            si += 1
            S.dma(st[:, :nm, :], wv[:, m0:m0 + nm, n0:n0 + 256], w=[sbuf_], sem=sbuf_)
            S.op("pool" if si % 2 == 0 else "act",
                 (lambda e, st=st, m0=m0, nm=nm, n0=n0: e.tensor_copy(wdn[:, m0:m0 + nm, n0:n0 + 256], st[:, :nm, :])) if si % 2 == 0
                 else (lambda e, st=st, m0=m0, nm=nm, n0=n0: e.activation(out=wdn[:, m0:m0 + nm, n0:n0 + 256], in_=st[:, :nm, :], func=AF.Copy)),
                 r=[sbuf_], w=[b_wdn])
    sq, b_sq = P.sb("fsq", [128, 8, 256], BF16)
    rstd, b_rstd = P.sb("frstd", [128, 256], F32)
    aT, b_aT = P.sb("faT", [128, 8, 256], BF16)
    hid, b_hid = P.sb("fhid", [128, NH, 256], BF16)
    sil = [P.sb("fsil%d" % i, [128, 256], F32) for i in range(2)]
    h2, b_h2 = P.sb("fh2", [128, 8, 256], F32)
    fg, b_fg = P.sb("ffg", [128, 8], F32)
    S.dma(fg[:], k.final_g, w=[b_fg], sem=b_fg)
    p_ss, b_pss = P.ps("fp_ss", [128, 512])
    p_gu = [P.ps("fp_gu%d" % i, [128, 512]) for i in range(4)]
    p_dn = [P.ps("fp_dn%d" % i, [128, 512]) for i in range(3)]
    hv = hsrc.rearrange("(k p) t -> p k t", p=128)
    ov = hdst.rearrange("(k p) t -> p k t", p=128) if hdst is not None else None
    NT = 256
    blocks = list(range(0 if need_ctx else 1, T // NT))

    def load(b):
        S.dma(hb[b % 2][0][:], hv[:, :, b * NT:(b + 1) * NT], w=[hb[b % 2][1]], sem=hb[b % 2][1])
    load(blocks[0])
    cg = 0
    cd = 0
    for bi, b in enumerate(blocks):
        t0 = b * NT
        j = 1 if b == 0 else 0
        if bi + 1 < len(blocks):
            load(blocks[bi + 1])
        ht, hbb = hb[b % 2]

        def norm(src, srcb, dst, dstb, gcol, bcol):
            S.op("act", lambda e: e.activation(out=sq[:], in_=src[:], func=AF.Square), r=[srcb], w=[b_sq])

            def mm_ss(e):
                last = None
                for kk in range(8):
                    last = e.matmul(p_ss[:, :NT], lhsT=k.ones_d[:], rhs=sq[:, kk, :], start=(kk == 0), stop=(kk == 7))
                return last
            S.group("pe", mm_ss, r=[b_sq, k.b_ones_d], w=[b_pss])
            rsqrt_eps(k, rstd[:], b_rstd, p_ss[:, :NT], b_pss)
            S.op("dve", lambda e: e.tensor_tensor(tmp[:], src[:], rstd[:].unsqueeze(1).broadcast_to([128, 8, NT]), ALU.mult),
                 r=[srcb, b_rstd], w=[b_tmp])
            for kk in range(8):
                if bcol is not None:
                    S.op("act", lambda e, kk=kk: e.activation(out=dst[:, kk, :], in_=tmp[:, kk, :], func=AF.Identity,
                                                               scale=gcol(kk), bias=bcol(kk)), r=[b_tmp, k.b_G2, k.b_mod, b_fg], w=[dstb])
                else:
                    S.op("act", lambda e, kk=kk: e.activation(out=dst[:, kk, :], in_=tmp[:, kk, :], func=AF.Copy, scale=gcol(kk)),
                         r=[b_tmp, b_fg], w=[dstb])
        if bi == 0 or final:
            norm(ht, hbb, aT, b_aT, lambda kk: k.G2[:, kk, j:j + 1], lambda kk: k.mod[:, 24 + kk, j:j + 1])
        for m in range(NH):
            pt, pb = p_gu[cg % 4]
            st_, sb_ = sil[cg % 2]
            cg += 1

            def mm(e, m=m, pt=pt):
                last = None
                for half in range(2):
                    c0 = half * 2816 + m * 128
                    for kk in range(8):
                        last = e.matmul(pt[:, half * 256:(half + 1) * 256], lhsT=wgu[:, kk, c0:c0 + 128], rhs=aT[:, kk, :],
                                        start=(kk == 0), stop=(kk == 7))
                return last
            S.group("pe", mm, r=[b_wgu, b_aT], w=[pb])
            S.op("act", lambda e, pt=pt, st_=st_: e.activation(out=st_[:], in_=pt[:, 0:256], func=AF.Silu), r=[pb], w=[sb_])
            S.op("dve", lambda e, pt=pt, st_=st_, m=m: e.tensor_tensor(hid[:, m, :], st_[:], pt[:, 256:512], ALU.mult), r=[pb, sb_], w=[b_hid])
        if bi + 1 < len(blocks) and not final:
            nb_ = blocks[bi + 1]
            jn = 1 if nb_ == 0 else 0
            norm(hb[nb_ % 2][0], hb[nb_ % 2][1], aT, b_aT, lambda kk: k.G2[:, kk, jn:jn + 1], lambda kk: k.mod[:, 24 + kk, jn:jn + 1])
        for n in range(8):
            pt, pb = p_dn[cd % 3]
            cd += 1

            def mm(e, n=n, pt=pt):
                last = None
                for m in range(NH):
                    last = e.matmul(pt[:, :NT], lhsT=wdn[:, m, n * 128:(n + 1) * 128], rhs=hid[:, m, :], start=(m == 0), stop=(m == NH - 1))
                return last
            S.group("pe", mm, r=[b_wdn, b_hid], w=[pb])
            S.op("dve", lambda e, n=n, pt=pt: e.scalar_tensor_tensor(out=h2[:, n, :], in0=pt[:, :NT], scalar=k.mod[:, 40 + n, j:j + 1],
                                                                     in1=ht[:, n, :], op0=ALU.mult, op1=ALU.add),
                 r=[pb, k.b_mod, hbb], w=[b_h2])
        if not final:
            S.dma(ov[:, :, t0:t0 + NT], h2[:], r=[b_h2], sem=b_h2)
        else:
            norm(h2, b_h2, h2, b_h2, lambda kk: fg[:, kk:kk + 1], None)
            S.dma(k.outT.rearrange("(k p) t -> p k t", p=128)[:, :, t0 - NCTX:t0 - NCTX + NT], h2[:], r=[b_h2], sem=b_h2)
    S.barrier()
    P.close()


_CONSTS = None


def kernel(**inputs):
    global _CONSTS
    inp = {k_: np.asarray(v) for k_, v in inputs.items()}
    nc = build(depth=2)
    if _CONSTS is None:
        _CONSTS = host_consts()
    in_maps = []
    for b in range(8):
        d = host_inputs(inp, b)
        d.update(_CONSTS)
        in_maps.append(d)
    res = run_bass_kernel_spmd(nc, in_maps, core_ids=list(range(8)))
    out = np.stack([np.ascontiguousarray(np.asarray(res.results[b]["outT"]).T) for b in range(8)])
    return out.astype(np.float32)
```

```python
import numpy as np, contextlib, math
import concourse.bass as bass
import concourse.mybir as mybir
from concourse.bass_utils import run_bass_kernel_spmd

F32 = mybir.dt.float32
BF16 = mybir.dt.bfloat16
AF = mybir.ActivationFunctionType
ALU = mybir.AluOpType
AX = mybir.AxisListType


class Buf:
    __slots__ = ("name", "w", "r", "dsem", "excl")

    def __init__(self, name, excl=False):
        self.name = name
        self.excl = excl
        self.w = None
        self.r = {}
        self.dsem = None


class Sched:
    ENG = ("pe", "act", "dve", "pool", "sp")

    def __init__(self, nc, es):
        self.nc = nc
        self.es = es
        self.eng = {"pe": nc.tensor, "act": nc.scalar, "dve": nc.vector,
                    "pool": nc.gpsimd, "sp": nc.sync}
        self.sems = {}
        self.cnt = {}
        for e in self.ENG:
            self.sems[e] = es.enter_context(nc.semaphore("s_" + e))
            self.cnt[e] = 0
        self.seen = {e: {} for e in self.ENG}
        self.free_dsems = []
        self.n_dsem = 0
        self.live_dsems = []

    def _dsem(self, buf):
        if buf.dsem is None:
            if self.free_dsems:
                key = self.free_dsems.pop()
            else:
                key = "d%d" % self.n_dsem
                self.n_dsem += 1
                self.sems[key] = self.es.enter_context(self.nc.semaphore("s_" + key))
                self.cnt[key] = 0
            buf.dsem = key
            self.live_dsems.append(buf)
        return buf.dsem

    def _deps(self, e, r, w):
        deps = {}

        def need(ev, raw):
            if ev is None:
                return
            key, val = ev
            if key == e and (e == "pe" or not raw):
                return
            if self.seen[e].get(key, 0) >= val:
                return
            if deps.get(key, 0) < val:
                deps[key] = val
        for b in r:
            need(b.w, True)
            if b.excl:
                for k, v in b.r.items():
                    need((k, v), False)
        for b in w:
            need(b.w, False)
            for k, v in b.r.items():
                need((k, v), False)
        return deps

    def _emit(self, e, deps, fn):
        eng = self.eng[e]
        items = list(deps.items())
        for key, val in items[:-1]:
            eng.wait_ge(self.sems[key], val)
        ins = fn(eng)
        if isinstance(ins, (list, tuple)):
            first, last = ins[0], ins[-1]
            multi = len(ins) > 1
        else:
            first = last = ins
            multi = False
        if items:
            key, val = items[-1]
            if multi:
                raise RuntimeError("multi-instruction op must use op_group")
            first._wait_ge(self.sems[key], val)
        for key, val in items:
            self.seen[e][key] = val
        return last

    def op(self, e, fn, r=(), w=()):
        deps = self._deps(e, r, w)
        last = self._emit(e, deps, fn)
        self.cnt[e] += 1
        last.then_inc(self.sems[e], 1)
        ev = (e, self.cnt[e])
        self._mark(ev, r, w)
        return ev

    def group(self, e, fn, r=(), w=()):
        deps = self._deps(e, r, w)
        eng = self.eng[e]
        for key, val in deps.items():
            eng.wait_ge(self.sems[key], val)
            self.seen[e][key] = val
        last = fn(eng)
        self.cnt[e] += 1
        last.then_inc(self.sems[e], 1)
        ev = (e, self.cnt[e])
        self._mark(ev, r, w)
        return ev

    def _mark(self, ev, r, w):
        key, val = ev
        for b in r:
            if b.excl:
                b.r = {key: val}
            elif b.r.get(key, 0) < val:
                b.r[key] = val
        for b in w:
            b.w = ev
            b.r = {}

    def dma(self, out, in_, r=(), w=(), sem=None, q="sp", **kw):
        deps = self._deps(q, r, w)
        key = self._dsem(sem)
        eng = self.eng[q]
        for k, v in deps.items():
            eng.wait_ge(self.sems[k], v)
            self.seen[q][k] = v
        ins = eng.dma_start(out=out, in_=in_, **kw)
        self.cnt[key] += 16
        ins.then_inc(self.sems[key], 16)
        ev = (key, self.cnt[key])
        self._mark(ev, r, w)
        return ev

    def barrier(self):
        sp = self.eng["sp"]
        for key in list(self.sems.keys()):
            if key == "sp":
                continue
            v = self.cnt[key]
            if v > 0 and self.seen["sp"].get(key, 0) < v:
                sp.wait_ge(self.sems[key], v)
                self.seen["sp"][key] = v
        self.cnt["sp"] += 1
        sp.sem_inc(self.sems["sp"], 1)
        for e in self.ENG:
            if e == "sp":
                continue
            self.eng[e].wait_ge(self.sems["sp"], self.cnt["sp"])
        for e in self.ENG:
            for key in self.sems:
                self.seen[e][key] = self.cnt[key]
        for b in self.live_dsems:
            self.free_dsems.append(b.dsem)
            b.dsem = None
        self.live_dsems = []

    def finish(self):
        self.barrier()


import os
import ml_dtypes

T = 4352
NCTX = 256
D = 1024
IN_DIM = 2968
EPS = 1e-6
BLOCKS = [(0, 256)] + [(256 + 512 * i, 512) for i in range(8)]


class K:
    pass


def build(depth=2, stop_after=None, debug=False):
    nc = bass.Bass("TRN2", target_bir_lowering=False)
    es = contextlib.ExitStack()
    S = Sched(nc, es)
    k = K()
    k.nc, k.S, k.es = nc, S, es
    k.debug = debug

    def din(name, shape, dt=F32):
        return nc.dram_tensor(name, list(shape), dt, kind="ExternalInput").ap()

    def dscr(name, shape, dt=F32):
        kind = "ExternalOutput" if debug else "Internal"
        return nc.dram_tensor(name, list(shape), dt, kind=kind).ap()

    k.xT = din("xT", [D, T])
    k.ccin = din("ccin", [128, 8, 2])
    k.ada_w = din("ada_w", [depth, D, 6 * D])
    k.ada_b = din("ada_b", [depth, 128, 48])
    k.norm1_g = din("norm1_g", [depth, 128, 8])
    k.norm2_g = din("norm2_g", [depth, 128, 8])
    k.final_g = din("final_g", [128, 8])
    k.w_in = din("w_in", [depth, D, IN_DIM])
    k.w_out = din("w_out", [depth, D, D])
    k.w_gu = din("w_gu", [depth, D, 2 * 2816])
    k.w_down = din("w_down", [depth, 2816, D])
    k.outT = nc.dram_tensor("outT", [D, 4096], F32, kind="ExternalOutput").ap()
    k.qk_g = din("qk_g", [depth, 128, 2])
    k.c_ones_d = din("c_ones_d", [128, 128], BF16)
    k.c_blk64 = din("c_blk64", [128, 128], BF16)
    k.c_rp_d = din("c_rp_d", [128, 128], BF16)
    k.c_rp_g = din("c_rp_g", [128, 128], BF16)
    k.c_rope = din("c_rope", [4, 128, 4096])
    k.c_place = din("c_place", [64, 2, 128], BF16)
    k.diff_lambda = din("diff_lambda", [depth, 128])
    k.diff_g = din("diff_g", [depth, 128, 1])
    k.conv_w = din("conv_w", [depth, 128, 9, 5])
    k.c_blk64s = din("c_blk64s", [128, 128], BF16)
    k.c_identb = din("c_identb", [128, 128], BF16)
    k.c_identf = din("c_identf", [128, 128])
    k.c_MI = din("c_MI", [128, 2, 128])
    k.c_SU = din("c_SU", [128, 2, 128])
    k.c_EMK = din("c_EMK", [128, 2, 256])
    k.c_MK = din("c_MK", [3, 128, 128])
    k.gdn_g = din("gdn_g", [depth, 64])
    k.gdn_alog = din("gdn_alog", [depth, 12])
    k.gdn_dtb = din("gdn_dtb", [depth, 12])
    k.hA = dscr("hA", [D, T])
    k.hB = dscr("hB", [D, T])
    k.QdT = dscr("QdT", [2, 128, T], BF16)
    k.KdT = dscr("KdT", [2, 128, T], BF16)
    k.QgT = dscr("QgT", [3, 128, T], BF16)
    k.KgT = dscr("KgT", [1, 128, T], BF16)
    k.Vtok = dscr("Vtok", [T, 390], BF16)
    k.gqkvT = dscr("gqkvT", [9, 128, T])
    k.zab = dscr("zab", [T, 408])
    k.mixT = dscr("mixT", [8, 128, T], BF16)
    k.QnT = dscr("QnT", [3, 128, T], BF16)
    k.KnT = dscr("KnT", [3, 128, T], BF16)
    k.Ktok = dscr("Ktok", [T, 384], BF16)
    k.Vtok2 = dscr("Vtok2", [T, 384], BF16)
    if debug:
        k.dbg_mod = nc.dram_tensor("dbg_mod", [128, 96], F32, kind="ExternalOutput").ap()

    def sbp(name, shape, dt):
        t = es.enter_context(nc.sbuf_tensor(name, list(shape), dt))
        return t, Buf(name)

    k.ones_d, k.b_ones_d = sbp("ones_d", [128, 128], BF16)
    k.blk64, k.b_blk64 = sbp("blk64", [128, 128], BF16)
    k.rp_d, k.b_rp_d = sbp("rp_d", [128, 128], BF16)
    k.rp_g, k.b_rp_g = sbp("rp_g", [128, 128], BF16)
    k.scc, k.b_scc = sbp("scc", [128, 8, 2], F32)
    k.mod, k.b_mod = sbp("mod", [128, 48, 2], F32)
    k.G1, k.b_G1 = sbp("G1", [128, 8, 2], F32)
    k.G2, k.b_G2 = sbp("G2", [128, 8, 2], F32)
    k.qkg, k.b_qkg = sbp("qkg", [128, 2], F32)
    k.epsc, k.b_epsc = sbp("epsc", [128, 2], F32)
    k.place, k.b_place = sbp("place", [64, 2, 128], BF16)
    k.blk64s, k.b_blk64s = sbp("blk64s", [128, 128], BF16)
    k.identb, k.b_identb = sbp("identb", [128, 128], BF16)
    S.dma(k.blk64s[:], k.c_blk64s, w=[k.b_blk64s], sem=k.b_blk64s)
    S.dma(k.identb[:], k.c_identb, w=[k.b_identb], sem=k.b_identb)
    S.dma(k.place[:], k.c_place, w=[k.b_place], sem=k.b_place)
    S.op("pool", lambda e: e.memset(k.epsc[:], EPS), w=[k.b_epsc])

    S.dma(k.ones_d[:], k.c_ones_d, w=[k.b_ones_d], sem=k.b_ones_d)
    S.dma(k.blk64[:], k.c_blk64, w=[k.b_blk64], sem=k.b_blk64)
    S.dma(k.rp_d[:], k.c_rp_d, w=[k.b_rp_d], sem=k.b_rp_d)
    S.dma(k.rp_g[:], k.c_rp_g, w=[k.b_rp_g], sem=k.b_rp_g)
    S.dma(k.scc[:], k.ccin, w=[k.b_scc], sem=k.b_scc)
    S.op("act", lambda e: e.activation(out=k.scc[:], in_=k.scc[:], func=AF.Silu), r=[k.b_scc], w=[k.b_scc])
    S.barrier()

    for l in range(depth):
        ph_mod(k, l)
        S.barrier()
        if stop_after == ("mod", l):
            break
        ph_proj(k, l, k.xT if l == 0 else k.hB)
        S.barrier()
        if stop_after == ("proj", l):
            break
        ph_att(k, l, need_ctx=(l < depth - 1))
        S.barrier()
        if stop_after == ("att", l):
            break
        ph_gdn_a(k, l)
        S.barrier()
        if stop_after == ("gdna", l):
            break
        ph_gdn_b(k, l, need_ctx=(l < depth - 1))
        S.barrier()
        if stop_after == ("gdnb", l):
            break
        last = (l == depth - 1)
        hsrc = k.xT if l == 0 else k.hB
        ph_out(k, l, hsrc, k.hA, need_ctx=not last)
        S.barrier()
        if stop_after == ("out", l):
            break
        ph_ffn(k, l, k.hA, None if last else k.hB, need_ctx=not last, final=last)
        S.barrier()
        if stop_after == ("ffn", l):
            break
    S.finish()
    es.close()
    return nc


def rsqrt_eps(k, dst, dst_b, src, src_b, eps=EPS):
    S = k.S
    npart = dst.shape[0]
    S.op("act", lambda e: e.activation(out=dst, in_=src, func=AF.Ln, bias=k.epsc[0:npart, 0:1]), r=[src_b, k.b_epsc], w=[dst_b])
    S.op("act", lambda e: e.activation(out=dst, in_=dst, func=AF.Exp, scale=-0.5), r=[dst_b], w=[dst_b])


class Phase:
    def __init__(self, k):
        self.k = k
        self.es = contextlib.ExitStack()
        self.nps = 0

    _uid = [0]

    def sb(self, name, shape, dt):
        Phase._uid[0] += 1
        name = "%s_u%d" % (name, Phase._uid[0])
        t = self.es.enter_context(self.k.nc.sbuf_tensor(name, list(shape), dt))
        return t, Buf(name)

    def ps(self, name, shape, dt=F32):
        Phase._uid[0] += 1
        name = "%s_u%d" % (name, Phase._uid[0])
        t = self.es.enter_context(self.k.nc.psum_tensor(name, list(shape), dt))
        return t, Buf(name, excl=True)

    def close(self):
        self.es.close()


def ph_mod(k, l):
    S, nc = k.S, k.nc
    P = Phase(k)
    wst = [P.sb("adaw%d" % i, [128, 8, 512], F32) for i in range(2)]
    adab, b_adab = P.sb("adab", [128, 48], F32)
    n1, b_n1 = P.sb("n1g", [128, 8], F32)
    n2, b_n2 = P.sb("n2g", [128, 8], F32)
    pm, b_pm = P.ps("pm", [128, 96])
    S.dma(adab[:], k.ada_b[l], w=[b_adab], sem=b_adab)
    S.dma(n1[:], k.norm1_g[l], w=[b_n1], sem=b_n1)
    S.dma(n2[:], k.norm2_g[l], w=[b_n2], sem=b_n2)
    S.dma(k.qkg[:], k.qk_g[l], w=[k.b_qkg], sem=k.b_qkg)
    wv = k.ada_w[l].rearrange("(k p) n -> p k n", p=128)
    for ci in range(12):
        wt, wb = wst[ci % 2]
        S.dma(wt[:], wv[:, :, ci * 512:(ci + 1) * 512], w=[wb], sem=wb)
        for j in range(4):
            c = ci * 4 + j

            def mm(e, c=c, j=j, wt=wt):
                last = None
                for kk in range(8):
                    last = e.matmul(pm[:, 2 * c:2 * c + 2], lhsT=wt[:, kk, j * 128:(j + 1) * 128],
                                    rhs=k.scc[:, kk, :], start=(kk == 0), stop=(kk == 7))
                return last
            S.group("pe", mm, r=[wb, k.b_scc], w=[b_pm])
    S.op("dve", lambda e: e.tensor_tensor(k.mod[:], pm[:].rearrange("p (c j) -> p c j", j=2),
                                          adab[:].unsqueeze(2).broadcast_to([128, 48, 2]), ALU.add),
         r=[b_pm, b_adab], w=[k.b_mod])
    S.op("dve", lambda e: e.scalar_tensor_tensor(out=k.G1[:], in0=k.mod[:, 8:16, :], scalar=1.0,
                                                 in1=n1[:].unsqueeze(2).broadcast_to([128, 8, 2]),
                                                 op0=ALU.add, op1=ALU.mult),
         r=[k.b_mod, b_n1], w=[k.b_G1])
    S.op("dve", lambda e: e.scalar_tensor_tensor(out=k.G2[:], in0=k.mod[:, 32:40, :], scalar=1.0,
                                                 in1=n2[:].unsqueeze(2).broadcast_to([128, 8, 2]),
                                                 op0=ALU.add, op1=ALU.mult),
         r=[k.b_mod, b_n2], w=[k.b_G2])
    if k.debug:
        S.dma(k.dbg_mod, k.mod[:].rearrange("p c j -> p (c j)"), r=[k.b_mod], sem=k.b_mod)
    S.barrier()
    P.close()


def ph_proj(k, l, hsrc):
    S, nc = k.S, k.nc
    P = Phase(k)
    wbf, b_w = P.sb("w_in_bf", [128, 8, IN_DIM], BF16)
    wv = k.w_in[l].rearrange("(k p) n -> p k n", p=128)
    hb = [P.sb("hblk%d" % i, [128, 8, 512], F32) for i in range(2)]
    tmp, b_tmp = P.sb("tmp", [128, 8, 512], F32)
    stg = [(tmp, b_tmp), hb[1]]
    for ci, c0 in enumerate(range(0, IN_DIM, 512)):
        n = min(512, IN_DIM - c0)
        st, sbuf_ = stg[ci % 2]
        S.dma(st[:, :, :n], wv[:, :, c0:c0 + n], w=[sbuf_], sem=sbuf_)
        S.op("pool", lambda e, st=st, c0=c0, n=n: e.tensor_copy(wbf[:, :, c0:c0 + n], st[:, :, :n]),
             r=[sbuf_], w=[b_w])

    sq, b_sq = P.sb("sq", [128, 8, 512], BF16)
    rstd, b_rstd = P.sb("rstd", [128, 512], F32)
    aT = [P.sb("aT%d" % i, [128, 8, 512], BF16) for i in range(2)]
    rope = [P.sb("rope%d" % i, [128, 4, 512], F32) for i in range(2)]
    p_ss, b_pss = P.ps("p_ss", [128, 512])
    p_fm = [P.ps("p_fm%d" % i, [128, 512]) for i in range(3)]
    p_aux = [P.ps("p_aux%d" % i, [128, 512]) for i in range(2)]
    p_tm = [P.ps("p_tm%d" % i, [128, 512]) for i in range(2)]
    qb = [P.sb("qb%d" % i, [128, 512], BF16) for i in range(2)]
    sqh = [P.sb("sqh%d" % i, [128, 512], BF16) for i in range(2)]
    t1 = [P.sb("t1_%d" % i, [128, 512], F32) for i in range(2)]
    t2 = [P.sb("t2_%d" % i, [128, 512], F32) for i in range(2)]
    rs2 = [P.sb("rs2_%d" % i, [128, 512], F32) for i in range(2)]
    ob = [P.sb("ob%d" % i, [128, 512], BF16) for i in range(3)]
    of = [P.sb("of%d" % i, [128, 512], F32) for i in range(3)]
    vt = [P.sb("vt%d" % i, [128, 4, 390], BF16) for i in range(2)]
    zt = [P.sb("zt%d" % i, [128, 4, 408], F32) for i in range(2)]
    for i in range(2):
        S.op("pool", lambda e, i=i: e.memset(vt[i][0][:], 1.0), w=[vt[i][1]])

    hv = hsrc.rearrange("(k p) t -> p k t", p=128)
    cnt = {"fm": 0, "aux": 0, "tm": 0, "w": 0, "ob": 0, "of": 0}

    def load_block(b):
        t0, nt = BLOCKS[b]
        ht, hbuf = hb[b % 2]
        S.dma(ht[:, :, :nt], hv[:, :, t0:t0 + nt], w=[hbuf], sem=hbuf)
        if b > 0:
            rt, rbuf = rope[b % 2]
            S.dma(rt[:, :, :nt], k.c_rope[:, :, t0 - NCTX:t0 - NCTX + nt].rearrange("c p t -> p c t"),
                  w=[rbuf], sem=rbuf)

    load_block(0)
    for b in range(len(BLOCKS)):
        t0, nt = BLOCKS[b]
        j = 1 if b == 0 else 0
        if b + 1 < len(BLOCKS):
            load_block(b + 1)
        ht, hbuf = hb[b % 2]
        rt, rbuf = rope[b % 2]
        at, abuf = aT[b % 2]
        S.op("act", lambda e: e.activation(out=sq[:, :, :nt], in_=ht[:, :, :nt], func=AF.Square),
             r=[hbuf], w=[b_sq])

        def mm_ss(e):
            last = None
            for kk in range(8):
                last = e.matmul(p_ss[:, :nt], lhsT=k.ones_d[:], rhs=sq[:, kk, :nt], start=(kk == 0), stop=(kk == 7))
            return last
        S.group("pe", mm_ss, r=[b_sq, k.b_ones_d], w=[b_pss])
        rsqrt_eps(k, rstd[:, :nt], b_rstd, p_ss[:, :nt], b_pss)
        S.op("dve", lambda e: e.tensor_tensor(tmp[:, :, :nt], ht[:, :, :nt],
                                              rstd[:, :nt].unsqueeze(1).broadcast_to([128, 8, nt]), ALU.mult),
             r=[hbuf, b_rstd], w=[b_tmp])
        for kk in range(8):
            S.op("act", lambda e, kk=kk: e.activation(out=at[:, kk, :nt], in_=tmp[:, kk, :nt], func=AF.Identity,
                                                       scale=k.G1[:, kk, j:j + 1], bias=k.mod[:, kk, j:j + 1]),
                 r=[b_tmp, k.b_G1, k.b_mod], w=[abuf])

        if os.environ.get("BISECT") == "1":
            continue
        def fm_matmul(c0):
            pt, pb = p_fm[cnt["fm"] % 3]
            cnt["fm"] += 1

            def mm(e):
                last = None
                for kk in range(8):
                    last = e.matmul(pt[:, :nt], lhsT=wbf[:, kk, c0:c0 + 128], rhs=at[:, kk, :nt],
                                    start=(kk == 0), stop=(kk == 7))
                return last
            S.group("pe", mm, r=[b_w, abuf], w=[pb])
            return pt, pb

        def store(dst, src_t, src_b):
            S.dma(dst, src_t, r=[src_b], sem=src_b)

        def rope_chunk(pt, pb, kind, dst, gcol=None):
            i = cnt["w"] % 2
            cnt["w"] += 1
            qt, qbuf = qb[i]
            o_t, o_b = ob[cnt["ob"] % 3]
            cnt["ob"] += 1
            rp = k.rp_d if kind == "d" else k.rp_g
            rpb = k.b_rp_d if kind == "d" else k.b_rp_g
            ci, si = (0, 1) if kind == "d" else (2, 3)
            if kind == "d":
                S.op("act", lambda e: e.activation(out=qt[:, :nt], in_=pt[:, :nt], func=AF.Copy), r=[pb], w=[qbuf])
            else:
                S.op("act", lambda e: e.activation(out=qt[:, :nt], in_=pt[:, :nt], func=AF.Copy, scale=gcol),
                     r=[pb, k.b_qkg], w=[qbuf])
                st_, sb_ = sqh[i]
                S.op("act", lambda e: e.activation(out=st_[:, :nt], in_=pt[:, :nt], func=AF.Square), r=[pb], w=[sb_])
                pa, pab = p_aux[cnt["aux"] % 2]
                cnt["aux"] += 1
                S.op("pe", lambda e: e.matmul(pa[:, :nt], lhsT=k.blk64[:], rhs=st_[:, :nt], start=True, stop=True),
                     r=[sb_, k.b_blk64], w=[pab])
                r2, r2b = rs2[i]
                rsqrt_eps(k, r2[:, :nt], r2b, pa[:, :nt], pab)
            if b == 0:
                if kind == "d":
                    store(dst, qt[:, :nt], qbuf)
                else:
                    S.op("pool", lambda e: e.tensor_tensor(o_t[:, :nt], qt[:, :nt], r2[:, :nt], ALU.mult),
                         r=[qbuf, r2b], w=[o_b])
                    store(dst, o_t[:, :nt], o_b)
                return
            pa, pab = p_aux[cnt["aux"] % 2]
            cnt["aux"] += 1
            S.op("pe", lambda e: e.matmul(pa[:, :nt], lhsT=rp[:], rhs=qt[:, :nt], start=True, stop=True),
                 r=[qbuf, rpb], w=[pab])
            a1, a1b = t1[i]
            a2, a2b = t2[i]
            if kind == "d":
                S.op("dve", lambda e: e.tensor_tensor(a1[:, :nt], pt[:, :nt], rt[:, ci, :nt], ALU.mult),
                     r=[pb, rbuf], w=[a1b])
            else:
                S.op("pool", lambda e: e.tensor_tensor(a1[:, :nt], qt[:, :nt], rt[:, ci, :nt], ALU.mult),
                     r=[qbuf, rbuf], w=[a1b])
            S.op("dve", lambda e: e.tensor_tensor(a2[:, :nt], pa[:, :nt], rt[:, si, :nt], ALU.mult),
                 r=[pab, rbuf], w=[a2b])
            if kind == "d":
                S.op("pool", lambda e: e.tensor_tensor(o_t[:, :nt], a1[:, :nt], a2[:, :nt], ALU.add),
                     r=[a1b, a2b], w=[o_b])
            else:
                S.op("pool", lambda e: e.tensor_tensor(a1[:, :nt], a1[:, :nt], a2[:, :nt], ALU.add),
                     r=[a1b, a2b], w=[a1b])
                S.op("pool", lambda e: e.tensor_tensor(o_t[:, :nt], a1[:, :nt], r2[:, :nt], ALU.mult),
                     r=[a1b, r2b], w=[o_b])
            store(dst, o_t[:, :nt], o_b)

        B2 = os.environ.get("BISECT2", "dgn")
        for c in range(2 if "d" in B2 else 0):
            pt, pb = fm_matmul(c * 128)
            rope_chunk(pt, pb, "d", k.QdT[c, :, t0:t0 + nt])
        for c in range(2 if "d" in B2 else 0):
            pt, pb = fm_matmul(256 + c * 128)
            rope_chunk(pt, pb, "d", k.KdT[c, :, t0:t0 + nt])
        for c in range(3 if "g" in B2 else 0):
            pt, pb = fm_matmul(768 + c * 128)
            rope_chunk(pt, pb, "g", k.QgT[c, :, t0:t0 + nt], gcol=k.qkg[:, 0:1])
        if "g" in B2:
            pt, pb = fm_matmul(1152)
            rope_chunk(pt, pb, "g", k.KgT[0, :, t0:t0 + nt], gcol=k.qkg[:, 1:2])
        for c in range(9 if "n" in B2 else 0):
            pt, pb = fm_matmul(1408 + c * 128)
            o_t, o_b = of[cnt["of"] % 3]
            cnt["of"] += 1
            if c % 2 == 0:
                S.op("dve", lambda e: e.tensor_copy(o_t[:, :nt], pt[:, :nt]), r=[pb], w=[o_b])
            else:
                S.op("act", lambda e: e.activation(out=o_t[:, :nt], in_=pt[:, :nt], func=AF.Copy), r=[pb], w=[o_b])
            store(k.gqkvT[c, :, t0:t0 + nt], o_t[:, :nt], o_b)

        if os.environ.get("BISECT") == "2":
            continue
        v_t, v_b = vt[b % 2]
        z_t, z_b = zt[b % 2]
        ntile = nt // 128
        for tt in range(ntile):
            for (c0, n, kind) in ((512, 256, "dv"), (1280, 128, "gv"), (2560, 408, "zab")):
                pt, pb = p_tm[cnt["tm"] % 2]
                cnt["tm"] += 1

                def mm(e, c0=c0, n=n, pt=pt):
                    last = None
                    for kk in range(8):
                        last = e.matmul(pt[:, :n], lhsT=at[:, kk, tt * 128:(tt + 1) * 128], rhs=wbf[:, kk, c0:c0 + n],
                                        start=(kk == 0), stop=(kk == 7))
                    return last
                S.group("pe", mm, r=[b_w, abuf], w=[pb])
                if kind == "dv":
                    S.op("act", lambda e, pt=pt: e.activation(
                        out=v_t[:, tt, 0:260].rearrange("p (h d) -> p h d", d=65)[:, :, 0:64],
                        in_=pt[:, 0:256].rearrange("p (h d) -> p h d", d=64), func=AF.Copy), r=[pb], w=[v_b])
                elif kind == "gv":
                    S.op("act", lambda e, pt=pt: e.activation(
                        out=v_t[:, tt, 260:390].rearrange("p (h d) -> p h d", d=65)[:, :, 0:64],
                        in_=pt[:, 0:128].rearrange("p (h d) -> p h d", d=64), func=AF.Copy), r=[pb], w=[v_b])
                else:
                    S.op("dve", lambda e, pt=pt: e.tensor_copy(z_t[:, tt, :], pt[:, 0:408]), r=[pb], w=[z_b])
        S.dma(k.Vtok[t0:t0 + nt, :].rearrange("(n p) c -> p n c", p=128), v_t[:, :ntile, :], r=[v_b], sem=v_b)
        S.dma(k.zab[t0:t0 + nt, :].rearrange("(n p) c -> p n c", p=128), z_t[:, :ntile, :], r=[z_b], sem=z_b)
    S.barrier()
    P.close()


def host_consts():
    bf = ml_dtypes.bfloat16
    c = {}
    c["c_ones_d"] = np.full((128, 128), 1.0 / 1024, np.float32).astype(bf)
    blk = np.zeros((128, 128), np.float32)
    blk[:64, :64] = 1.0 / 64
    blk[64:, 64:] = 1.0 / 64
    c["c_blk64"] = blk.astype(bf)
    c["c_blk64s"] = (blk * 64).astype(bf)
    c["c_identb"] = np.eye(128, dtype=np.float32).astype(bf)
    c["c_identf"] = np.eye(128, dtype=np.float32)
    ti = np.arange(128)
    le = (ti[:, None] <= ti[None, :]).astype(np.float32)
    gt = (ti[:, None] > ti[None, :]).astype(np.float32)
    c["c_MI"] = np.ascontiguousarray(np.stack([le, le.T], axis=1))
    c["c_SU"] = np.ascontiguousarray(np.stack([gt, gt.T], axis=1))
    incl_f = (ti[None, :] >= ti[:, None]).astype(np.float32); strict_f = (ti[None, :] > ti[:, None]).astype(np.float32)
    emk_f = np.concatenate([incl_f, -strict_f], axis=1)
    emk_b = np.concatenate([incl_f.T, -strict_f.T], axis=1)
    c["c_EMK"] = np.ascontiguousarray(np.stack([emk_f, emk_b], axis=1))
    blk = lambda b_: (ti[:, None] // b_) == (ti[None, :] // b_)
    c["c_MK"] = np.stack([blk(64) & ~blk(32), blk(128) & ~blk(64), blk(32)]).astype(np.float32)

    def rp(group, nf):
        Rp = np.zeros((128, 128), np.float32)
        for g0 in range(0, 128, group):
            for a in range(2):
                for f in range(nf):
                    i0 = g0 + a * 2 * nf + f
                    i1 = g0 + a * 2 * nf + nf + f
                    Rp[i0, i1] = -1.0
                    Rp[i1, i0] = 1.0
        return np.ascontiguousarray(Rp.T).astype(bf)
    c["c_rp_d"] = rp(32, 8)
    c["c_rp_g"] = rp(64, 16)
    t = np.arange(4096)
    row = (t // 64).astype(np.float32)
    col = (t % 64).astype(np.float32)

    def tables(group, nf):
        inv = (np.float32(10000.0) ** (-np.arange(nf, dtype=np.float32) / np.float32(nf))).astype(np.float32)
        cos = np.zeros((128, 4096), np.float32)
        sin = np.zeros((128, 4096), np.float32)
        for p in range(128):
            w = p % group
            a = w // (2 * nf)
            f = w % nf
            pos = row if a == 0 else col
            ang = (pos * inv[f]).astype(np.float32)
            cos[p] = np.cos(ang)
            sin[p] = np.sin(ang)
        return cos, sin
    cd, sd = tables(32, 8)
    cg, sg = tables(64, 16)
    c["c_rope"] = np.stack([cd, sd, cg, sg]).astype(np.float32)
    pl = np.zeros((64, 2, 128), np.float32)
    for i in range(64):
        pl[i, 0, i] = 1.0
        pl[i, 1, 64 + i] = 1.0
    c["c_place"] = pl.astype(bf)
    return c


def col_layout(v, nch):
    v = np.asarray(v)
    return np.ascontiguousarray(np.swapaxes(v.reshape(v.shape[:-1] + (nch, 128)), -1, -2))


def host_inputs(inp, b):
    d = {}
    xT = np.concatenate([inp["ctx"][b], inp["x"][b]], axis=0).T
    d["xT"] = np.ascontiguousarray(xT)
    cc = np.stack([col_layout(inp["c"][b], 8), col_layout(inp["c_ctx"], 8)], axis=-1)
    d["ccin"] = np.ascontiguousarray(cc.astype(np.float32))
    d["ada_w"] = inp["ada_w"]
    d["ada_b"] = col_layout(inp["ada_b"], 48)
    d["norm1_g"] = col_layout(inp["norm1_g"], 8)
    d["norm2_g"] = col_layout(inp["norm2_g"], 8)
    d["final_g"] = col_layout(inp["final_norm_g"], 8)
    d["w_in"] = inp["w_in"]
    d["w_out"] = inp["w_out"]
    d["w_gu"] = inp["ffn_w_gu"]
    d["w_down"] = inp["ffn_w_down"]
    d["gdn_g"] = inp["gdn_norm_g"]
    d["gdn_alog"] = np.ascontiguousarray(inp["gdn_a_log"].reshape(-1, 12))
    d["gdn_dtb"] = np.ascontiguousarray(inp["gdn_dt_bias"].reshape(-1, 12))
    cwl = inp["gdn_conv_w"]
    d["conv_w"] = np.ascontiguousarray(cwl.reshape(cwl.shape[0], 5, 9, 128).transpose(0, 3, 2, 1))
    d["diff_lambda"] = np.ascontiguousarray(inp["diff_lambda"].reshape(-1, 128))
    d["diff_g"] = np.ascontiguousarray(np.tile(inp["diff_norm_g"], (1, 2))[:, :, None].astype(np.float32))
    qg = np.tile(inp["q_norm_g"], (1, 2))
    kg = np.tile(inp["k_norm_g"], (1, 2))
    d["qk_g"] = np.ascontiguousarray(np.stack([qg, kg], axis=-1).astype(np.float32))
    return d


def ph_att(k, l, need_ctx):
    S, nc = k.S, k.nc
    P = Phase(k)
    lam_init = 0.8 - 0.6 * math.exp(-0.3 * l)
    Kd, b_Kd = P.sb("Kd", [128, 2, 2, T], BF16)
    Kg, b_Kg = P.sb("Kg", [128, 3, T], BF16)
    V, b_V = P.sb("V", [128, 34, 390], BF16)
    S.op("pool", lambda e: e.memset(Kd[:], 0.0), w=[b_Kd])
    for c in range(2):
        for hi in range(2):
            for cc in range(2):
                r0 = hi * 64 + cc * 32
                S.dma(Kd[r0:r0 + 32, c, cc, :], k.KdT[c, r0:r0 + 32, :], w=[b_Kd], sem=b_Kd)
    for vi, (ka, kb) in enumerate(((0, 0), (0, 1), (1, 1))):
        S.dma(Kg[0:64, vi, :], k.KgT[0, ka * 64:ka * 64 + 64, :], w=[b_Kg], sem=b_Kg)
        S.dma(Kg[64:128, vi, :], k.KgT[0, kb * 64:kb * 64 + 64, :], w=[b_Kg], sem=b_Kg)
    S.dma(V[:], k.Vtok.rearrange("(n p) c -> p n c", p=128), w=[b_V], sem=b_V)
    lamt, b_lamt = P.sb("lamt", [128, 4, 32], F32)
    lamp, b_lamp = P.sb("lamp", [128, 2, 32], F32)
    lams, b_lams = P.sb("lams", [128, 5], F32)
    gd, b_gd = P.sb("gd", [128, 1], F32)
    ones_r, b_ones_r = P.sb("ones_r", [128, 64], F32)
    S.op("pool", lambda e: e.memset(ones_r[:], 1.0), w=[b_ones_r])
    S.dma(lamt[:].rearrange("p a b -> p (a b)"), k.diff_lambda[l:l + 1, :].partition_broadcast(128), w=[b_lamt], sem=b_lamt)
    S.dma(gd[:], k.diff_g[l], w=[b_gd], sem=b_gd)
    S.op("act", lambda e: e.mul(gd[:], gd[:], 1.0 - lam_init), r=[b_gd], w=[b_gd])
    S.op("dve", lambda e: e.tensor_tensor(lamp[:], lamt[:, 0:4:2, :], lamt[:, 1:4:2, :], ALU.mult), r=[b_lamt], w=[b_lamp])
    S.op("dve", lambda e: e.reduce_sum(lams[:, 0:2], lamp[:], axis=AX.X), r=[b_lamp], w=[b_lams])
    S.op("act", lambda e: e.activation(out=lams[:, 0:2], in_=lams[:, 0:2], func=AF.Exp), r=[b_lams], w=[b_lams])
    S.op("dve", lambda e: e.tensor_tensor(lams[:, 2:3], lams[:, 0:1], lams[:, 1:2], ALU.subtract), r=[b_lams], w=[b_lams])
    S.op("dve", lambda e: e.tensor_scalar(lams[:, 3:4], lams[:, 2:3], lam_init, None, ALU.add), r=[b_lams], w=[b_lams])
    S.op("dve", lambda e: e.tensor_scalar(lams[:, 4:5], lams[:, 3:4], -1.0, None, ALU.mult), r=[b_lams], w=[b_lams])
    lam_ap = lams[:, 3:4]
    neglam_ap = lams[:, 4:5]

    Qd = [P.sb("Qd%d" % i, [128, 2, 512], BF16) for i in range(2)]
    Qg = [P.sb("Qg%d" % i, [128, 3, 512], BF16) for i in range(2)]
    pT = [P.sb("pT%d" % i, [128, 2, 512], BF16) for i in range(3)]
    Oalls = [P.sb("Oall%d" % i, [128, 14, 512], F32) for i in range(2)]
    ps_s = [P.ps("ps_s%d" % i, [128, 2, 512]) for i in range(2)]
    ps_o = [P.ps("ps_o%d" % i, [128, 512]) for i in range(2)]
    ps_x = [P.ps("ps_x%d" % i, [128, 512]) for i in range(2)]
    w1 = [P.sb("w1_%d" % i, [64, 512], F32) for i in range(2)]
    w2 = [P.sb("w2_%d" % i, [64, 512], F32) for i in range(2)]
    wsq = [P.sb("wsq%d" % i, [64, 512], BF16) for i in range(2)]
    wr = [P.sb("wr%d" % i, [64, 512], F32) for i in range(2)]
    obf = [P.sb("obf%d" % i, [64, 512], BF16) for i in range(4)]
    mixc = [P.sb("mixc%d" % i, [128, 512], BF16) for i in range(2)]
    rcp = [P.sb("rcp%d" % i, [64, 512], F32) for i in range(2)]
    cnt = {"s": 0, "o": 0, "x": 0, "w": 0, "obf": 0, "mix": 0, "rc": 0}

    def load_q(b):
        t0, nt = BLOCKS[b]
        qd, qdb = Qd[b % 2]
        qg, qgb = Qg[b % 2]
        S.dma(qd[:, :, :nt], k.QdT[:, :, t0:t0 + nt].rearrange("c p t -> p c t"), w=[qdb], sem=qdb)
        S.dma(qg[:, :, :nt], k.QgT[:, :, t0:t0 + nt].rearrange("c p t -> p c t"), w=[qgb], sem=qgb)

    post_gen = [None]

    def tick_post():
        g_ = post_gen[0]
        if g_ is not None:
            try:
                next(g_)
            except StopIteration:
                post_gen[0] = None

    def drain_post():
        while post_gen[0] is not None:
            tick_post()

    def post(t0, nt, Oall, b_Oall):
            def bcast(u):
                px, pxb = ps_x[cnt["x"] % 2]
                cnt["x"] += 1
                S.op("pe", lambda e: e.matmul(px[0:64, :nt], lhsT=ones_r[64:65, 0:64], rhs=Oall[64:65, u, :nt], start=True, stop=True),
                     r=[b_Oall, b_ones_r], w=[pxb])
                rc, rcb = rcp[cnt["rc"] % 2]
                cnt["rc"] += 1
                S.op("dve", lambda e: e.reciprocal(rc[:, :nt], px[0:64, :nt]), r=[pxb], w=[rcb])
                return rc, rcb

            def place(chunk, parts):
                px, pxb = ps_x[cnt["x"] % 2]
                cnt["x"] += 1
                for i, (ot, otb, hi) in enumerate(parts):
                    S.op("pe", lambda e, ot=ot, hi=hi, i=i: e.matmul(px[:, :nt], lhsT=k.place[0:64, hi, :], rhs=ot[:, :nt],
                                                                     start=(i == 0), stop=(i == len(parts) - 1)),
                         r=[otb, k.b_place], w=[pxb])
                mt, mtb = mixc[cnt["mix"] % 2]
                cnt["mix"] += 1
                S.op("act", lambda e: e.activation(out=mt[:, :nt], in_=px[:, :nt], func=AF.Copy), r=[pxb], w=[mtb])
                S.dma(k.mixT[chunk, :, t0:t0 + nt], mt[:, :nt], r=[mtb], sem=mtb)

            parts = []
            for hh in range(4):
                i = cnt["w"] % 2
                cnt["w"] += 1
                a1, a1b = w1[i]
                a2, a2b = w2[i]
                sqt, sqb = wsq[i]
                rt_, rtb = wr[i]
                px, pxb = bcast(2 * hh)
                S.op("dve", lambda e: e.tensor_tensor(a1[:, :nt], Oall[0:64, 2 * hh, :nt], px[:, :nt], ALU.mult),
                     r=[b_Oall, pxb], w=[a1b])
                yield
                px, pxb = bcast(2 * hh + 1)
                S.op("dve", lambda e: e.tensor_tensor(a2[:, :nt], Oall[0:64, 2 * hh + 1, :nt], px[:, :nt], ALU.mult),
                     r=[b_Oall, pxb], w=[a2b])
                S.op("dve", lambda e: e.scalar_tensor_tensor(out=a1[:, :nt], in0=a2[:, :nt], scalar=neglam_ap[0:64, :], in1=a1[:, :nt],
                                                             op0=ALU.mult, op1=ALU.add), r=[a1b, a2b, b_lams], w=[a1b])
                yield
                S.op("act", lambda e: e.activation(out=sqt[:, :nt], in_=a1[:, :nt], func=AF.Square), r=[a1b], w=[sqb])
                px, pxb = ps_x[cnt["x"] % 2]
                cnt["x"] += 1
                S.op("pe", lambda e: e.matmul(px[0:64, :nt], lhsT=k.blk64[0:64, 0:64], rhs=sqt[:, :nt], start=True, stop=True),
                     r=[sqb, k.b_blk64], w=[pxb])
                yield
                rsqrt_eps(k, rt_[:, :nt], rtb, px[0:64, :nt], pxb)
                yield
                ot, otb = obf[cnt["obf"] % 4]
                cnt["obf"] += 1
                S.op("dve", lambda e: e.scalar_tensor_tensor(out=ot[:, :nt], in0=a1[:, :nt], scalar=gd[0:64, :], in1=rt_[:, :nt],
                                                             op0=ALU.mult, op1=ALU.mult), r=[a1b, rtb, b_gd], w=[otb])
                parts.append((ot, otb, hh % 2))
                yield
                if hh % 2 == 1:
                    place(hh // 2, parts)
                    parts = []
                    yield
            for h in range(6):
                u = 8 + h
                px, pxb = bcast(u)
                ot, otb = obf[cnt["obf"] % 4]
                cnt["obf"] += 1
                S.op("dve", lambda e: e.tensor_tensor(ot[:, :nt], Oall[0:64, u, :nt], px[:, :nt], ALU.mult),
                     r=[b_Oall, pxb], w=[otb])
                parts.append((ot, otb, h % 2))
                yield
                if h % 2 == 1:
                    place(2 + h // 2, parts)
                    parts = []
                    yield

    blocks = list(range(0 if need_ctx else 1, len(BLOCKS)))
    load_q(blocks[0])
    for bi, b in enumerate(blocks):
        t0, nt = BLOCKS[b]
        if bi + 1 < len(blocks):
            load_q(blocks[bi + 1])
        qd, qdb = Qd[b % 2]
        qg, qgb = Qg[b % 2]
        Oall, b_Oall = Oalls[bi % 2]
        kts = [0, 1] if b == 0 else list(range(34))
        pairs = []
        for c in range(2):
            for cc in range(2):
                pairs.append(("d", c, cc, [(2 * c + hi) * 2 + cc for hi in range(2)], [(2 * c + hi) * 65 for hi in range(2)], 32 ** -0.5))
        for c in range(3):
            kva, kvb = (2 * c) // 3, (2 * c + 1) // 3
            pairs.append(("g", c, kva + kvb, [8 + 2 * c, 8 + 2 * c + 1], [260 + kva * 65, 260 + kvb * 65], 64 ** -0.5))
        steps = []
        for pi, pr in enumerate(pairs):
            for ki, kt in enumerate(kts):
                steps.append((pi, pr, ki, kt))
        LAG = 1
        pend = []
        po_of = {}

        def emit_pv(st):
            (pi, pr, ki, kt, ptile, ptb) = st
            (kind, c, var, us, v0s, sc) = pr
            if ki == 0:
                po_of[pi] = [ps_o[(cnt["o"] + i) % 2] for i in range(2)]
                cnt["o"] += 2
            for hi in range(2):
                po, pob = po_of[pi][hi]
                S.op("pe", lambda e, hi=hi, po=po: e.matmul(po[0:65, :nt], lhsT=V[:, kt, v0s[hi]:v0s[hi] + 65], rhs=ptile[:, hi, :nt],
                                                        start=(ki == 0), stop=(ki == len(kts) - 1)), r=[b_V, ptb], w=[pob])
                if ki == len(kts) - 1:
                    S.op("dve", lambda e, hi=hi, po=po: e.tensor_copy(Oall[0:65, us[hi], :nt], po[0:65, :nt]), r=[pob], w=[b_Oall])

        for si_, (pi, pr, ki, kt) in enumerate(steps):
            if si_ % 5 == 4:
                tick_post()
            (kind, c, var, us, v0s, sc) = pr
            pst, psb = ps_s[cnt["s"] % 2]
            ptile, ptb = pT[cnt["s"] % 3]
            cnt["s"] += 1
            qt, qb_ = (qd, qdb) if kind == "d" else (qg, qgb)
            Kt, Kb = (Kd, b_Kd) if kind == "d" else (Kg, b_Kg)

            def mmQK(e, pst=pst, Kt=Kt, qt=qt, kind=kind, c=c, var=var, kt=kt):
                last = None
                for hi in range(2):
                    rows = slice(hi * 64, hi * 64 + 64)
                    if kind == "d":
                        lhsT = Kt[rows, c, var, kt * 128:(kt + 1) * 128]
                    else:
                        lhsT = Kt[rows, var, kt * 128:(kt + 1) * 128]
                    last = e.matmul(pst[:, hi, :nt], lhsT=lhsT, rhs=qt[rows, c, :nt], start=True, stop=True)
                return last
            S.group("pe", mmQK, r=[Kb, qb_], w=[psb])
            S.op("act", lambda e, pst=pst, ptile=ptile, sc=sc: e.activation(out=ptile[:, :, :nt], in_=pst[:, :, :nt], func=AF.Exp, scale=sc),
                 r=[psb], w=[ptb])
            pend.append((pi, pr, ki, kt, ptile, ptb))
            if len(pend) > LAG:
                emit_pv(pend.pop(0))
        while pend:
            emit_pv(pend.pop(0))
        drain_post()
        post_gen[0] = post(t0, nt, Oall, b_Oall)
    drain_post()
    S.barrier()
    P.close()


def ph_gdn_a(k, l):
    S, nc = k.S, k.nc
    P = Phase(k)
    W = T + 8
    segs = [(2, 0, 256), (262, 256, 4096)]
    xin = [P.sb("xin%d" % i, [128, W], F32) for i in range(2)]
    acc = [P.sb("acc%d" % i, [128, T], F32) for i in range(2)]
    sl, b_sl = P.sb("sl", [128, T], F32)
    sqb, b_sqb = P.sb("sqb", [128, T], BF16)
    rn, b_rn = P.sb("rn", [128, T], F32)
    ob = [P.sb("gob%d" % i, [128, T], BF16) for i in range(2)]
    cw, b_cw = P.sb("cw", [128, 9, 5], F32)
    tk = [P.sb("tk%d" % i, [128, 4, 128], BF16) for i in range(2)]
    ps_n = [P.ps("ps_n%d" % i, [128, 512]) for i in range(2)]
    ps_t = [P.ps("ps_t%d" % i, [128, 4, 128], BF16) for i in range(2)]
    S.dma(cw[:], k.conv_w[l], w=[b_cw], sem=b_cw)
    for i in range(2):
        S.op("pool", lambda e, i=i: e.memset(xin[i][0][:], 0.0), w=[xin[i][1]])
    cnt = {"n": 0, "t": 0}

    def conv(c):
        xt, xb = xin[c % 2]
        at, ab = acc[c % 2]
        for (p0, t0, n) in segs:
            S.dma(xt[:, p0:p0 + n], k.gqkvT[c, :, t0:t0 + n], w=[xb], sem=xb)
        for (p0, t0, n) in segs:
            for tap in range(5):
                src = xt[:, p0 + tap - 2:p0 + tap - 2 + n]
                if tap == 0:
                    S.op("dve", lambda e, src=src, t0=t0, n=n: e.tensor_scalar(at[:, t0:t0 + n], src, cw[:, c, 0:1], None, ALU.mult),
                         r=[xb, b_cw], w=[ab])
                else:
                    S.op("dve", lambda e, src=src, t0=t0, n=n, tap=tap: e.scalar_tensor_tensor(
                        out=at[:, t0:t0 + n], in0=src, scalar=cw[:, c, tap:tap + 1], in1=at[:, t0:t0 + n],
                        op0=ALU.mult, op1=ALU.add), r=[xb, b_cw, ab], w=[ab])

    conv(0)
    for c in range(9):
        at, ab = acc[c % 2]
        if c + 1 < 9:
            conv(c + 1)
        o_t, o_b = ob[c % 2]
        if c >= 6:
            S.op("act", lambda e: e.activation(out=o_t[:], in_=at[:], func=AF.Silu), r=[ab], w=[o_b])
        else:
            S.op("act", lambda e: e.activation(out=sl[:], in_=at[:], func=AF.Silu), r=[ab], w=[b_sl])
            S.op("act", lambda e: e.activation(out=sqb[:], in_=sl[:], func=AF.Square), r=[b_sl], w=[b_sqb])
            for t0 in range(0, T, 512):
                n = min(512, T - t0)
                pn, pnb = ps_n[cnt["n"] % 2]
                cnt["n"] += 1
                S.op("pe", lambda e, t0=t0, n=n, pn=pn: e.matmul(pn[:, :n], lhsT=k.blk64s[:], rhs=sqb[:, t0:t0 + n], start=True, stop=True),
                     r=[b_sqb, k.b_blk64s], w=[pnb])
                S.op("act", lambda e, t0=t0, n=n, pn=pn: e.activation(out=rn[:, t0:t0 + n], in_=pn[:, :n], func=AF.Ln, bias=k.epsc[:, 0:1]),
                     r=[pnb, k.b_epsc], w=[b_rn])
            S.op("act", lambda e: e.activation(out=rn[:], in_=rn[:], func=AF.Exp, scale=-0.5), r=[b_rn], w=[b_rn])
            sc = 0.125 if c < 3 else 1.0
            S.op("dve", lambda e: e.scalar_tensor_tensor(out=o_t[:], in0=sl[:], scalar=sc, in1=rn[:], op0=ALU.mult, op1=ALU.mult),
                 r=[b_sl, b_rn], w=[o_b])
            dst = k.QnT[c] if c < 3 else k.KnT[c - 3]
            S.dma(dst, o_t[:], r=[o_b], sem=o_b)
        if c >= 3:
            dstT = k.Ktok if c < 6 else k.Vtok2
            cc = (c - 3) % 3
            for g0 in range(0, 34, 4):
                ng = min(4, 34 - g0)
                pt, ptb = ps_t[cnt["t"] % 2]
                tt, ttb = tk[cnt["t"] % 2]
                cnt["t"] += 1

                def tr(e, g0=g0, ng=ng, pt=pt):
                    last = None
                    for i in range(ng):
                        last = e.transpose(pt[:, i, :], o_t[:, (g0 + i) * 128:(g0 + i + 1) * 128], k.identb[:])
                    return last
                S.group("pe", tr, r=[o_b, k.b_identb], w=[ptb])
                S.op("act" if (g0 // 4) % 2 == 0 else "dve",
                     (lambda e, pt=pt, tt=tt, ng=ng: e.activation(out=tt[:, :ng, :], in_=pt[:, :ng, :], func=AF.Copy)) if (g0 // 4) % 2 == 0
                     else (lambda e, pt=pt, tt=tt, ng=ng: e.tensor_copy(tt[:, :ng, :], pt[:, :ng, :])), r=[ptb], w=[ttb])
                S.dma(dstT[g0 * 128:(g0 + ng) * 128, cc * 128:(cc + 1) * 128].rearrange("(n p) c -> p n c", p=128), tt[:, :ng, :],
                      r=[ttb], sem=ttb)
    S.barrier()
    P.close()


def ph_gdn_b(k, l, need_ctx):
    S, nc = k.S, k.nc
    P = Phase(k)
    NCH = 34
    MI, b_MI = P.sb("MI", [128, 2, 128], F32)
    SU, b_SU = P.sb("SU", [128, 2, 128], F32)
    EMK, b_EMK = P.sb("EMK", [128, 2, 256], F32)
    idf, b_idf = P.sb("idf", [128, 128], F32)
    on128, b_on128 = P.sb("on128", [128, 128], F32)
    gnb, b_gnb = P.sb("gnb", [128, 64], F32)
    nal, b_nal = P.sb("nal", [128, 12], F32)
    dtb, b_dtb = P.sb("dtb", [128, 12], F32)
    onec, b_onec = P.sb("onec", [128, 1], F32)
    S.dma(MI[:], k.c_MI, w=[b_MI], sem=b_MI)
    S.dma(SU[:], k.c_SU, w=[b_SU], sem=b_SU)
    S.dma(EMK[:], k.c_EMK, w=[b_EMK], sem=b_EMK)
    S.dma(idf[:], k.c_identf, w=[b_idf], sem=b_idf)
    mk_ = [P.sb("MK%d" % i, [128, 128], F32) for i in range(3)]
    MK = [x[0] for x in mk_]
    b_MK = Buf("MK")
    for i in range(3):
        S.dma(MK[i][:], k.c_MK[i], w=[b_MK], sem=b_MK)
    S.op("pool", lambda e: e.memset(on128[:], 1.0), w=[b_on128])
    S.op("pool", lambda e: e.memset(onec[:], 1.0), w=[b_onec])
    S.dma(gnb[:], k.gdn_g[l:l + 1, :].partition_broadcast(128), w=[b_gnb], sem=b_gnb)
    S.dma(nal[:], k.gdn_alog[l:l + 1, :].partition_broadcast(128), w=[b_nal], sem=b_nal)
    S.dma(dtb[:], k.gdn_dtb[l:l + 1, :].partition_broadcast(128), w=[b_dtb], sem=b_dtb)
    S.op("act", lambda e: e.activation(out=nal[:], in_=nal[:], func=AF.Exp), r=[b_nal], w=[b_nal])
    S.op("dve", lambda e: e.tensor_scalar(nal[:], nal[:], -1.0, None, ALU.mult), r=[b_nal], w=[b_nal])
    ab, b_ab = P.sb("ab", [128, NCH, 24], F32)
    G, b_G = P.sb("G", [128, NCH, 12], F32)
    LB, b_LB = P.sb("LB", [128, NCH, 12], F32)
    BETA, b_BETA = P.sb("BETA", [128, NCH, 12], F32)
    KDS, b_KDS = P.sb("KDS", [128, NCH, 12], F32)
    GL, b_GL = P.sb("GL", [128, NCH, 12], F32)
    GLP, b_GLP = P.sb("GLP", [128, NCH, 2, 3], F32)
    S.dma(ab[:], k.zab[:, 384:408].rearrange("(n p) c -> p n c", p=128), w=[b_ab], sem=b_ab)
    S.op("dve", lambda e: e.tensor_tensor(G[:], ab[:, :, 0:12], dtb[:].unsqueeze(1).broadcast_to([128, NCH, 12]), ALU.add),
         r=[b_ab, b_dtb], w=[b_G])
    S.op("act", lambda e: e.activation(out=G[:], in_=G[:], func=AF.Exp), r=[b_G], w=[b_G])
    S.op("act", lambda e: e.activation(out=G[:], in_=G[:], func=AF.Ln, bias=onec[:, 0:1]), r=[b_G, b_onec], w=[b_G])
    S.op("dve", lambda e: e.tensor_tensor(G[:], G[:], nal[:].unsqueeze(1).broadcast_to([128, NCH, 12]), ALU.mult),
         r=[b_G, b_nal], w=[b_G])
    S.op("act", lambda e: e.activation(out=LB[:], in_=ab[:, :, 12:24], func=AF.Exp, scale=-1.0), r=[b_ab], w=[b_LB])
    S.op("act", lambda e: e.activation(out=LB[:], in_=LB[:], func=AF.Ln, bias=onec[:, 0:1]), r=[b_LB, b_onec], w=[b_LB])
    S.op("dve", lambda e: e.tensor_scalar(LB[:], LB[:], -1.0, None, ALU.mult), r=[b_LB], w=[b_LB])
    S.op("act", lambda e: e.activation(out=BETA[:], in_=LB[:], func=AF.Exp), r=[b_LB], w=[b_BETA])
    PG = []
    for g in range(2):
        t_, _ = P.ps("PG%d" % g, [128, 3, 512])
        PG.append((t_, [Buf("PG%d_%d" % (g, i), excl=True) for i in range(3)]))
    bA = [(PG[0][0][:, i, :], PG[0][1][i]) for i in range(3)]
    bM = [(PG[1][0][:, i, :], PG[1][1][i]) for i in range(2)]
    bX = (PG[1][0][:, 2, :], PG[1][1][2])
    PBK = [bA[0], bA[1], bA[2], bM[0], bM[1], bX]
    bS = [P.ps("bS%d" % i, [128, 512]) for i in range(2)]
    for half in range(2):
        pt, pb = bA[half]
        n0 = half * 17

        def mm(e, pt=pt, n0=n0):
            last = None
            for i in range(17):
                n = n0 + i
                for d in range(2):
                    last = e.matmul(pt[:, i * 24 + d * 6:i * 24 + d * 6 + 6], lhsT=SU[:, d, :], rhs=G[:, n, d * 6:d * 6 + 6],
                                    start=True, stop=True)
                last = e.matmul(pt[:, i * 24 + 12:i * 24 + 24], lhsT=on128[:], rhs=G[:, n, :], start=True, stop=True)
            return last
        S.group("pe", mm, r=[b_SU, b_G, b_on128], w=[pb])
        S.op("act", lambda e, pt=pt, n0=n0: e.activation(out=KDS[:, n0:n0 + 17, :],
                                                         in_=pt[:, 0:408].rearrange("p (n c) -> p n c", c=24)[:, :, 0:12], func=AF.Exp),
             r=[pb], w=[b_KDS])
        S.op("act", lambda e, pt=pt, n0=n0: e.activation(out=GL[:, n0:n0 + 17, :],
                                                         in_=pt[:, 0:408].rearrange("p (n c) -> p n c", c=24)[:, :, 12:24], func=AF.Exp),
             r=[pb], w=[b_GL])
    GLv = GL[:].rearrange("p n (d c h) -> p n d c h", d=2, c=3)
    S.op("dve", lambda e: e.tensor_copy(GLP[0:64], GLv[0:64, :, :, :, 0]), r=[b_GL], w=[b_GLP])
    S.op("dve", lambda e: e.tensor_copy(GLP[64:128], GLv[64:128, :, :, :, 1]), r=[b_GL], w=[b_GLP])

    GST = int(os.environ.get("GST", "99"))
    if GST <= 1:
        S.barrier(); P.close(); return
    Ofin, b_Ofin = P.sb("Ofin", [128, NCH, 384], F32)
    def mk(i):
        w = K()
        w.qk, w.b_qk = P.sb("cqk%d" % i, [128, 6, 128], BF16)
        w.kt, w.b_kt = P.sb("ckt%d" % i, [128, 384], BF16)
        w.vt, w.b_vt = P.sb("cvt%d" % i, [128, 384], BF16)
        w.rhsD, w.b_rhsD = P.sb("rhsD%d" % i, [128, 6, 2, 128], F32)
        w.Em, w.b_Em = P.sb("Em%d" % i, [128, 6, 256], F32)
        w.EB, w.b_EB = P.sb("EB%d" % i, [128, 6, 256], F32)
        w.qkTm, w.b_qkTm = P.sb("qkTm%d" % i, [128, 6, 128], BF16)
        def grp(nm, shape, dt):
            xs = [P.sb("%s%d_%d" % (nm, i, g), shape, dt) for g in range(2)]
            return [x[0] for x in xs], [x[1] for x in xs]
        w.Z, w.b_Zg = grp("Z", [128, 3, 3, 128], BF16)
        w.T32, w.b_T32g = grp("T32", [128, 3, 128], F32)
        w.O1T, w.b_O1Tg = grp("O1T", [128, 3, 128], BF16)
        w.O2T, w.b_O2Tg = grp("O2T", [128, 3, 128], BF16)
        w.NY, w.b_NYg = grp("NY", [128, 3, 128], BF16)
        w.QdT, w.b_QdT = P.sb("QdT%d" % i, [128, 3, 128], BF16)
        w.RwT, w.b_RwT = P.sb("RwT%d" % i, [128, 3, 128], BF16)
        w.KD, w.b_KD = P.sb("KD%d" % i, [128, 3, 2, 128], BF16)
        w.Ru, w.b_Ru = P.sb("Ru%d" % i, [128, 6, 64], F32)
        S.op("pool", lambda e: e.memset(w.KD[:], 0.0), w=[w.b_KD])
        return w
    WS = [mk(0), mk(1)]
    S32 = [P.sb("S32_%d" % i, [128, 64], F32) for i in range(6)]
    Sbf = [P.sb("Sbf%d" % i, [128, 128], BF16) for i in range(6)]
    Xb = [P.sb("Xb%d" % i, [128, 128], BF16) for i in range(3)]
    vnb = [P.sb("vnb%d" % i, [128, 128], BF16) for i in range(3)]
    for i in range(6):
        S.op("pool", lambda e, i=i: e.memset(S32[i][0][:], 0.0), w=[S32[i][1]])
        S.op("pool", lambda e, i=i: e.memset(Sbf[i][0][:], 0.0), w=[Sbf[i][1]])
    bSc = [bS[0], bS[1], bS[0]]

    def build_rhsD(w, n, d):
        for h in range(6):
            u = d * 6 + h
            S.op("dve", lambda e, h=h, u=u: e.tensor_scalar(w.rhsD[:, h, 0, :], MI[:, d, :], G[:, n, u:u + 1], None, ALU.mult),
                 r=[b_MI, b_G], w=[w.b_rhsD])
            S.op("dve", lambda e, h=h, u=u: e.scalar_tensor_tensor(out=w.rhsD[:, h, 1, :], in0=idf[:], scalar=LB[:, n, u:u + 1],
                                                                    in1=w.rhsD[:, h, 0, :], op0=ALU.mult, op1=ALU.add),
                 r=[b_idf, b_LB, w.b_rhsD], w=[w.b_rhsD])
            yield
        w.rhsD_ready = True

    def prep(w, n, d):
        t0 = n * 128
        S.dma(w.qk[:, 0:3, :], k.QnT[:, :, t0:t0 + 128].rearrange("c p t -> p c t"), w=[w.b_qk], sem=w.b_qk)
        S.dma(w.qk[:, 3:6, :], k.KnT[:, :, t0:t0 + 128].rearrange("c p t -> p c t"), w=[w.b_qk], sem=w.b_qk)
        S.dma(w.kt[:], k.Ktok[t0:t0 + 128, :], w=[w.b_kt], sem=w.b_kt)
        S.dma(w.vt[:], k.Vtok2[t0:t0 + 128, :], w=[w.b_vt], sem=w.b_vt)
        if not getattr(w, "rhsD_ready", False):
            for _ in build_rhsD(w, n, d):
                pass
        w.rhsD_ready = False
        for c in range(3):
            for (lhs, lhsb, (pt, pb), dst, dstb) in ((SU[:, d, :], b_SU, bA[c], w.Em, w.b_Em), (on128[:], b_on128, PBK[3 + c], w.EB, w.b_EB)):
                def mmD(e, c=c, pt=pt, lhs=lhs):
                    last = None
                    for hp in range(2):
                        last = e.matmul(pt[:, hp * 256:(hp + 1) * 256], lhsT=lhs,
                                        rhs=w.rhsD[:, 2 * c + hp, :, :].rearrange("p a b -> p (a b)"), start=True, stop=True)
                    return last
                S.group("pe", mmD, r=[lhsb, w.b_rhsD], w=[pb])
                S.op("act", lambda e, c=c, pt=pt, dst=dst: e.activation(out=dst[:, 2 * c:2 * c + 2, :].rearrange("p a b -> p (a b)"), in_=pt[:], func=AF.Exp),
                     r=[pb], w=[dstb])
        S.op("dve", lambda e: e.tensor_tensor(w.Em[:], w.Em[:], EMK[:, d, :].unsqueeze(1).broadcast_to([128, 6, 256]), ALU.mult),
             r=[w.b_Em, b_EMK], w=[w.b_Em])
        tick()
        kslots = [([0, 2], bA[0]), ([1, 3], bA[1]), ([4], bA[2]), ([5], bM[0])]
        for heads, (pt, pb) in kslots:
            def mmK(e, heads=heads, pt=pt):
                last = None
                for i, h in enumerate(heads):
                    ps_ = slice((h % 2) * 64, (h % 2) * 64 + 64)
                    cq = h // 2
                    e.matmul(pt[:, i * 256:i * 256 + 128], lhsT=w.qk[ps_, 3 + cq, :], rhs=w.qk[ps_, cq, :], start=True, stop=True)
                    last = e.matmul(pt[:, i * 256 + 128:i * 256 + 256], lhsT=w.qk[ps_, 3 + cq, :], rhs=w.qk[ps_, 3 + cq, :],
                                    start=True, stop=True)
                return last
            S.group("pe", mmK, r=[w.b_qk], w=[pb])
            nh = len(heads)
            hs = slice(heads[0], heads[-1] + 1, 2)
            pv = pt[:, 0:nh * 256].rearrange("p (h x c) -> p h x c", h=nh, x=2)
            S.op("dve", lambda e, hs=hs, pv=pv: e.tensor_tensor(w.qkTm[:, hs, :], pv[:, :, 0, :], w.Em[:, hs, 0:128], ALU.mult),
                 r=[pb, w.b_Em], w=[w.b_qkTm])
            for i, h in enumerate(heads):
                S.op("dve", lambda e, i=i, h=h, pv=pv: e.tensor_tensor(w.T32[h // 3][:, h % 3, :], pv[:, i, 1, :], w.Em[:, h, 128:256], ALU.mult),
                     r=[pb, w.b_Em], w=[w.b_T32g[h // 3]])
        tick()
        m3 = lambda t: t[:].unsqueeze(1).broadcast_to([128, 3, 128])
        for g in range(2):
            T32, T32b = w.T32[g], w.b_T32g[g]
            S.op("dve", lambda e, g=g, T32=T32: e.tensor_tensor(w.Z[g][:, :, 1, :], T32[:], m3(MK[2]), ALU.mult), r=[T32b, b_MK], w=[w.b_Zg[g]])
            S.op("pool", lambda e, g=g: e.tensor_tensor(w.Z[g][:, :, 2, :], w.Z[g][:, :, 1, :], m3(idf), ALU.add), r=[w.b_Zg[g], b_idf], w=[w.b_Zg[g]])
            S.op("pool", lambda e, g=g, T32=T32: e.tensor_tensor(w.O1T[g][:], T32[:], m3(MK[0]), ALU.mult), r=[T32b, b_MK], w=[w.b_O1Tg[g]])
            S.op("pool", lambda e, g=g, T32=T32: e.tensor_tensor(w.O2T[g][:], T32[:], m3(MK[1]), ALU.mult), r=[T32b, b_MK], w=[w.b_O2Tg[g]])
        tick()

        def pe3(g, fn, r):
            pG, pGb = PG[g]

            def f(e):
                last = None
                for i in range(3):
                    last = fn(e, pG, i)
                return last
            S.group("pe", f, r=r, w=pGb)
        for g in range(2):
            Z = w.Z[g]
            pe3(g, lambda e, pG, i, Z=Z: e.matmul(pG[:, i, 0:128], lhsT=Z[:, i, 1, :], rhs=k.identb[:], start=True, stop=True), [w.b_Zg[g], k.b_identb])
        for g in range(2):
            pG, pGb = PG[g]
            S.op("act", lambda e, g=g, pG=pG: e.activation(out=w.Z[g][:, :, 0, :], in_=pG[:, :, 0:128], func=AF.Copy), r=pGb, w=[w.b_Zg[g]])
        for j in range(5):
            tick()
            for g in range(2):
                Z = w.Z[g]
                if j < 4:
                    pe3(g, lambda e, pG, i, Z=Z: e.matmul(pG[:, i, 0:128], lhsT=Z[:, i, 1, :], rhs=Z[:, i, 0, :], start=True, stop=True), [w.b_Zg[g]])
                if j == 0:
                    pe3(g, lambda e, pG, i, Z=Z: e.matmul(pG[:, i, 128:256], lhsT=Z[:, i, 0, :], rhs=Z[:, i, 1, :], start=True, stop=True), [w.b_Zg[g]])
                elif j < 4:
                    pe3(g, lambda e, pG, i, Z=Z: e.matmul(pG[:, i, 128:384], lhsT=Z[:, i, 0, :], rhs=Z[:, i, 1:3, :].rearrange("p a b -> p (a b)"),
                                                          start=True, stop=True), [w.b_Zg[g]])
                else:
                    pe3(g, lambda e, pG, i, Z=Z: e.matmul(pG[:, i, 256:384], lhsT=Z[:, i, 0, :], rhs=Z[:, i, 2, :], start=True, stop=True), [w.b_Zg[g]])
            for g in range(2):
                pG, pGb = PG[g]
                Z = w.Z[g]
                if j >= 1:
                    S.op("dve", lambda e, Z=Z, pG=pG: e.tensor_tensor(Z[:, :, 2, :], Z[:, :, 2, :], pG[:, :, 256:384], ALU.add), r=pGb + [w.b_Zg[g]], w=[w.b_Zg[g]])
                if j < 4:
                    S.op("act", lambda e, Z=Z, pG=pG: e.activation(out=Z[:, :, 0:2, :].rearrange("p h a b -> p h (a b)"), in_=pG[:, :, 0:256], func=AF.Copy),
                         r=pGb, w=[w.b_Zg[g]])
        tick()
        for g in range(2):
            Z = w.Z[g]
            pe3(g, lambda e, pG, i, Z=Z: e.matmul(pG[:, i, 128:256], lhsT=Z[:, i, 2, :], rhs=k.identb[:], start=True, stop=True), [w.b_Zg[g], k.b_identb])
        for g in range(2):
            pG, pGb = PG[g]
            S.op("act", lambda e, g=g, pG=pG: e.activation(out=w.Z[g][:, :, 1, :], in_=pG[:, :, 128:256], func=AF.Copy), r=pGb, w=[w.b_Zg[g]])
        for mi in range(2):
            tick()
            OT = w.O1T if mi == 0 else w.O2T
            OTb = w.b_O1Tg if mi == 0 else w.b_O2Tg
            for g in range(2):
                Z = w.Z[g]
                pe3(g, lambda e, pG, i, Z=Z, OTg=OT[g]: e.matmul(pG[:, i, 0:128], lhsT=OTg[:, i, :], rhs=Z[:, i, 1, :], start=True, stop=True), [OTb[g], w.b_Zg[g]])
            for g in range(2):
                pG, pGb = PG[g]
                S.op("act", lambda e, g=g, pG=pG: e.activation(out=w.NY[g][:], in_=pG[:, :, 0:128], func=AF.Copy), r=pGb, w=[w.b_NYg[g]])
            for g in range(2):
                Z = w.Z[g]
                pe3(g, lambda e, pG, i, Z=Z, NYg=w.NY[g]: e.matmul(pG[:, i, 256:384], lhsT=NYg[:, i, :], rhs=Z[:, i, 2, :], start=True, stop=True), [w.b_NYg[g], w.b_Zg[g]])
                if mi == 0:
                    pe3(g, lambda e, pG, i, Z=Z, NYg=w.NY[g]: e.matmul(pG[:, i, 128:256], lhsT=Z[:, i, 2, :], rhs=NYg[:, i, :], start=True, stop=True), [w.b_NYg[g], w.b_Zg[g]])
            for g in range(2):
                pG, pGb = PG[g]
                Z = w.Z[g]
                if mi == 0:
                    S.op("dve", lambda e, Z=Z, pG=pG: e.tensor_tensor(Z[:, :, 1:3, :].rearrange("p h a b -> p h (a b)"), Z[:, :, 1:3, :].rearrange("p h a b -> p h (a b)"),
                                                                       pG[:, :, 128:384], ALU.add), r=pGb + [w.b_Zg[g]], w=[w.b_Zg[g]])
                else:
                    S.op("dve", lambda e, Z=Z, pG=pG: e.tensor_tensor(Z[:, :, 2, :], Z[:, :, 2, :], pG[:, :, 256:384], ALU.add), r=pGb + [w.b_Zg[g]], w=[w.b_Zg[g]])
        for half in range(2):
            ps_ = slice(half * 64, half * 64 + 64)
            S.op("pool", lambda e, ps_=ps_, half=half: e.tensor_tensor(w.QdT[ps_, :, :], w.qk[ps_, 0:3, :], w.EB[ps_, half:6:2, 0:128], ALU.mult),
                 r=[w.b_qk, w.b_EB], w=[w.b_QdT])
            S.op("pool", lambda e, ps_=ps_, half=half: e.tensor_tensor(w.RwT[ps_, :, :], w.qk[ps_, 3:6, :], w.EB[ps_, half:6:2, 128:256], ALU.mult),
                 r=[w.b_qk, w.b_EB], w=[w.b_RwT])
        KDv = w.KD[:].rearrange("p c a b -> p c (a b)").rearrange("p c (x y) -> p c x y", y=64)[:, :, 0:4:3, :]
        S.op("dve", lambda e: e.tensor_tensor(KDv, w.kt[:].rearrange("p (c a y) -> p c a y", c=3, a=2),
                                              KDS[:, n, d * 6:d * 6 + 6].rearrange("p (c a) -> p c a", a=2).unsqueeze(3).broadcast_to([128, 3, 2, 64]),
                                              ALU.mult), r=[w.b_kt, b_KDS], w=[w.b_KD])
        S.op("dve", lambda e: e.tensor_tensor(w.Ru[:], w.vt[:].rearrange("p (h y) -> p h y", y=64),
                                              BETA[:, n, d * 6:d * 6 + 6].unsqueeze(2).broadcast_to([128, 6, 64]), ALU.mult),
             r=[w.b_vt, b_BETA], w=[w.b_Ru])

    def scan(w, n, d, first_pass):
        prs = []
        for c in range(3):
            bank = bS[c % 2]
            prs.append((c, (bank[0][:, (c // 2) * 256:(c // 2) * 256 + 256], bank[1]), S32[d * 3 + c], Sbf[d * 3 + c], Xb[c], vnb[c]))
        for (c, (bk, bkb), (s32, s32b), (sbf, sbfb), (xb, xbb), (vb, vbb)) in prs:
            S.op("pe", lambda e, bk=bk, c=c, sbf=sbf: e.matmul(bk[:, 0:128], lhsT=w.RwT[:, c, :], rhs=sbf[:], start=True, stop=True),
                 r=[w.b_RwT, sbfb], w=[bkb])
            if c == 1:
                yield
        yield
        for (c, (bk, bkb), (s32, s32b), (sbf, sbfb), (xb, xbb), (vb, vbb)) in prs:
            S.op("dve", lambda e, bk=bk, c=c, xb=xb: e.tensor_tensor(xb[:], w.Ru[:, 2 * c:2 * c + 2, :].rearrange("p a b -> p (a b)"), bk[:, 0:128], ALU.subtract),
                 r=[w.b_Ru, bkb], w=[xbb])

            def mmV(e, bk=bk, c=c, xb=xb):
                last = None
                for hp in range(2):
                    last = e.matmul(bk[:, 128 + hp * 64:128 + hp * 64 + 64], lhsT=w.Z[(2 * c + hp) // 3][:, (2 * c + hp) % 3, 2, :], rhs=xb[:, hp * 64:hp * 64 + 64],
                                    start=True, stop=True)
                return last
            S.group("pe", mmV, r=[w.b_Zg[0], w.b_Zg[1], xbb], w=[bkb])
            if c == 1:
                yield
        yield
        for (c, (bk, bkb), (s32, s32b), (sbf, sbfb), (xb, xbb), (vb, vbb)) in prs:
            S.op("act", lambda e, bk=bk, vb=vb: e.activation(out=vb[:], in_=bk[:, 128:256], func=AF.Copy), r=[bkb], w=[vbb])

            def mmO(e, bk=bk, c=c, sbf=sbf, vb=vb):
                e.matmul(bk[:, 0:128], lhsT=w.QdT[:, c, :], rhs=sbf[:], start=True, stop=False)
                for hp in range(2):
                    e.matmul(bk[:, hp * 64:hp * 64 + 64], lhsT=w.qkTm[:, 2 * c + hp, :], rhs=vb[:, hp * 64:hp * 64 + 64],
                             start=False, stop=(hp == 1))
                e.matmul(bk[:, 128:192], lhsT=w.KD[:, c, 0, :], rhs=vb[:, 0:64], start=True, stop=False)
                last = e.matmul(bk[:, 128:192], lhsT=w.KD[:, c, 1, :], rhs=vb[:, 64:128], start=False, stop=True)
                return last
            S.group("pe", mmO, r=[w.b_QdT, sbfb, w.b_qkTm, vbb, w.b_KD], w=[bkb])
            if c == 1:
                yield
        yield
        for (c, (bk, bkb), (s32, s32b), (sbf, sbfb), (xb, xbb), (vb, vbb)) in prs:
            S.op("dve", lambda e, bk=bk, c=c, s32=s32: e.scalar_tensor_tensor(out=s32[:], in0=s32[:], scalar=GLP[:, n, d, c:c + 1], in1=bk[:, 128:192],
                                                                             op0=ALU.mult, op1=ALU.add), r=[s32b, b_GLP, bkb], w=[s32b])
            S.op("pool", lambda e, sbf=sbf, s32=s32: e.tensor_copy(sbf[0:64, 0:64], s32[0:64, :]), r=[s32b], w=[sbfb])
            S.op("pool", lambda e, sbf=sbf, s32=s32: e.tensor_copy(sbf[64:128, 64:128], s32[64:128, :]), r=[s32b], w=[sbfb])
            ov = Ofin[:, n, 2 * c * 64:(2 * c + 2) * 64]
            if first_pass:
                S.op("act", lambda e, ov=ov, bk=bk: e.activation(out=ov, in_=bk[:, 0:128], func=AF.Copy), r=[bkb], w=[b_Ofin])
            else:
                S.op("dve", lambda e, ov=ov, bk=bk: e.tensor_tensor(ov, ov, bk[:, 0:128], ALU.add), r=[bkb, b_Ofin], w=[b_Ofin])
            if c == 1:
                yield

    cur_scan = [None]
    cur_aux = [None]

    def tick():
        for slot in (cur_scan, cur_aux):
            g_ = slot[0]
            if g_ is not None:
                try:
                    next(g_)
                except StopIteration:
                    slot[0] = None

    def drain():
        while cur_scan[0] is not None or cur_aux[0] is not None:
            tick()

    orders = [list(range(NCH)), [1, 0] + list(range(NCH - 1, 1, -1))]
    step = 0
    if GST in (2, 3):
        w = WS[0]
        DN, DD = int(os.environ.get("DN", "0")), int(os.environ.get("DD", "0"))
        prep(w, DN, DD)
        if GST == 3:
            cur_scan[0] = scan(w, DN, DD, True)
            drain()
        S.barrier()
        def dump(name, ap, shape, dt=F32):
            o = nc.dram_tensor(name, list(shape), dt, kind="ExternalOutput").ap()
            S.dma(o, ap, sem=Buf(name))
        dump("d_G", G[:].rearrange("p n c -> p (n c)"), [128, NCH * 12])
        dump("d_LB", LB[:].rearrange("p n c -> p (n c)"), [128, NCH * 12])
        dump("d_KDS", KDS[:].rearrange("p n c -> p (n c)"), [128, NCH * 12])
        dump("d_GLP", GLP[:].rearrange("p n d c -> p (n d c)"), [128, NCH * 6])
        dump("d_Em", w.Em[:].rearrange("p a b -> p (a b)"), [128, 6 * 256])
        dump("d_EB", w.EB[:].rearrange("p a b -> p (a b)"), [128, 6 * 256])
        dump("d_qkTm", w.qkTm[:].rearrange("p a b -> p (a b)"), [128, 6 * 128], BF16)
        dump("d_PT32", w.T32[0][:], [128, 3, 128])
        dump("d_MP", w.Z[0][:, :, 2, :], [128, 3, 128], BF16)
        dump("d_QdT", w.QdT[:].rearrange("p a b -> p (a b)"), [128, 384], BF16)
        dump("d_RwT", w.RwT[:].rearrange("p a b -> p (a b)"), [128, 384], BF16)
        dump("d_KD", w.KD[:].rearrange("p a b c -> p (a b c)"), [128, 768], BF16)
        dump("d_Ru", w.Ru[:].rearrange("p a b -> p (a b)"), [128, 384])
        dump("d_Ofin", Ofin[:, DN, :], [128, 384])
        dump("d_S32", S32[DD * 3][0][:], [128, 64])
        S.barrier(); P.close(); return
    flat = [(n, d) for d in range(2) for n in orders[d]]
    prep(WS[0], flat[0][0], flat[0][1])
    for si_, (n, d) in enumerate(flat):
        w = WS[si_ % 2]
        cur_scan[0] = scan(w, n, d, d == 0)
        if si_ + 2 < len(flat):
            cur_aux[0] = build_rhsD(w, flat[si_ + 2][0], flat[si_ + 2][1])
        if si_ + 1 < len(flat):
            prep(WS[(si_ + 1) % 2], flat[si_ + 1][0], flat[si_ + 1][1])
        drain()
    if GST == 4:
        S.barrier(); P.close(); return

    zt = [P.sb("zt%d" % i, [128, 384], F32) for i in range(2)]
    sqo, b_sqo = P.sb("sqo", [128, 384], F32)
    ssum, b_ssum = P.sb("ssum", [128, 6], F32)
    yb = [P.sb("yb%d" % i, [128, 384], F32) for i in range(2)]
    yT = [P.sb("yT%d" % i, [128, 3, 512], BF16) for i in range(2)]
    tiles = list(range(0 if need_ctx else 2, NCH))
    groups = []
    if need_ctx:
        groups.append([0, 1])
    for g0 in range(2, NCH, 4):
        groups.append(list(range(g0, g0 + 4)))
    cntr = 0
    for gi, grp in enumerate(groups):
        yt_, ytb = yT[gi % 2]
        for ti, n in enumerate(grp):
            z_, zb = zt[cntr % 2]
            y_, ybb = yb[cntr % 2]
            cntr += 1
            S.dma(z_[:], k.zab[n * 128:(n + 1) * 128, 0:384], w=[zb], sem=zb)
            S.op("act", lambda e, z_=z_: e.activation(out=z_[:], in_=z_[:], func=AF.Silu), r=[zb], w=[zb])
            o_ = Ofin[:, n, :]
            S.op("dve", lambda e, o_=o_: e.tensor_tensor(sqo[:], o_, o_, ALU.mult), r=[b_Ofin], w=[b_sqo])
            S.op("dve", lambda e: e.reduce_sum(ssum[:], sqo[:].rearrange("p (h y) -> p h y", y=64), axis=AX.X), r=[b_sqo], w=[b_ssum])
            S.op("act", lambda e: e.activation(out=ssum[:], in_=ssum[:], func=AF.Ln, scale=1.0 / 64, bias=k.epsc[:, 0:1]), r=[b_ssum, k.b_epsc], w=[b_ssum])
            S.op("act", lambda e: e.activation(out=ssum[:], in_=ssum[:], func=AF.Exp, scale=-0.5), r=[b_ssum], w=[b_ssum])
            S.op("dve", lambda e, o_=o_: e.tensor_tensor(sqo[:].rearrange("p (h y) -> p h y", y=64), o_.rearrange("p (h y) -> p h y", y=64),
                                                          ssum[:].unsqueeze(2).broadcast_to([128, 6, 64]), ALU.mult), r=[b_Ofin, b_ssum], w=[b_sqo])
            S.op("pool", lambda e: e.tensor_tensor(sqo[:].rearrange("p (h y) -> p h y", y=64), sqo[:].rearrange("p (h y) -> p h y", y=64),
                                                   gnb[:].unsqueeze(1).broadcast_to([128, 6, 64]), ALU.mult), r=[b_sqo, b_gnb], w=[b_sqo])
            S.op("pool", lambda e, z_=z_, y_=y_: e.tensor_tensor(y_[:], sqo[:], z_[:], ALU.mult), r=[b_sqo, zb], w=[ybb])

            def trY(e, y_=y_):
                last = None
                for c in range(3):
                    last = e.transpose(bX[0][:, c * 128:(c + 1) * 128], y_[:, c * 128:(c + 1) * 128], idf[:])
                return last
            S.group("pe", trY, r=[ybb, b_idf], w=[bX[1]])
            S.op("act", lambda e, ti=ti, yt_=yt_: e.activation(out=yt_[:, :, ti * 128:(ti + 1) * 128],
                                                               in_=bX[0][:, 0:384].rearrange("p (c t) -> p c t", c=3), func=AF.Copy), r=[bX[1]], w=[ytb])
        t0 = grp[0] * 128
        nt = len(grp) * 128
        S.dma(k.mixT[5:8, :, t0:t0 + nt].rearrange("c p t -> p c t"), yt_[:, :, :nt], r=[ytb], sem=ytb)
    S.barrier()
    P.close()


def ph_out(k, l, hsrc, hdst, need_ctx):
    S, nc = k.S, k.nc
    P = Phase(k)
    wbf, b_w = P.sb("w_out_bf", [128, 8, 1024], BF16)
    hb = [P.sb("ohb%d" % i, [128, 8, 512], F32) for i in range(2)]
    ho = [P.sb("oho%d" % i, [128, 8, 512], F32) for i in range(2)]
    mx = [P.sb("omx%d" % i, [128, 8, 512], BF16) for i in range(2)]
    pp = [P.ps("opp%d" % i, [128, 512]) for i in range(4)]
    wv = k.w_out[l].rearrange("(k p) n -> p k n", p=128)
    for ci in range(2):
        st, sbuf_ = ho[ci]
        S.dma(st[:], wv[:, :, ci * 512:(ci + 1) * 512], w=[sbuf_], sem=sbuf_)
        S.op("pool", lambda e, st=st, ci=ci: e.tensor_copy(wbf[:, :, ci * 512:(ci + 1) * 512], st[:]), r=[sbuf_], w=[b_w])
    hv = hsrc.rearrange("(k p) t -> p k t", p=128)
    ov = hdst.rearrange("(k p) t -> p k t", p=128)
    blocks = list(range(0 if need_ctx else 1, len(BLOCKS)))

    def load(b):
        t0, nt = BLOCKS[b]
        S.dma(hb[b % 2][0][:, :, :nt], hv[:, :, t0:t0 + nt], w=[hb[b % 2][1]], sem=hb[b % 2][1])
        S.dma(mx[b % 2][0][:, :, :nt], k.mixT[:, :, t0:t0 + nt].rearrange("c p t -> p c t"), w=[mx[b % 2][1]], sem=mx[b % 2][1])
    load(blocks[0])
    cnt = 0
    for bi, b in enumerate(blocks):
        t0, nt = BLOCKS[b]
        j = 1 if b == 0 else 0
        if bi + 1 < len(blocks):
            load(blocks[bi + 1])
        ht, hbb = hb[b % 2]
        mt, mtb = mx[b % 2]
        ot, otb = ho[b % 2]
        for n in range(8):
            pt, pb = pp[cnt % 4]
            cnt += 1

            def mm(e, n=n, pt=pt):
                last = None
                for kk in range(8):
                    last = e.matmul(pt[:, :nt], lhsT=wbf[:, kk, n * 128:(n + 1) * 128], rhs=mt[:, kk, :nt], start=(kk == 0), stop=(kk == 7))
                return last
            S.group("pe", mm, r=[b_w, mtb], w=[pb])
            S.op("dve", lambda e, n=n, pt=pt: e.scalar_tensor_tensor(out=ot[:, n, :nt], in0=pt[:, :nt], scalar=k.mod[:, 16 + n, j:j + 1],
                                                                     in1=ht[:, n, :nt], op0=ALU.mult, op1=ALU.add),
                 r=[pb, k.b_mod, hbb], w=[otb])
        S.dma(ov[:, :, t0:t0 + nt], ot[:, :, :nt], r=[otb], sem=otb)
    S.barrier()
    P.close()


def ph_ffn(k, l, hsrc, hdst, need_ctx, final):
    S, nc = k.S, k.nc
    P = Phase(k)
    NH = 22
    wgu, b_wgu = P.sb("wgu", [128, 8, 2 * 2816], BF16)
    wdn, b_wdn = P.sb("wdn", [128, NH, 1024], BF16)
    hb = [P.sb("fhb%d" % i, [128, 8, 256], F32) for i in range(2)]
    tmp, b_tmp = P.sb("ftmp", [128, 8, 256], F32)
    stg = [hb[0], hb[1], (tmp, b_tmp)]
    si = 0
    wv = k.w_gu[l].rearrange("(k p) n -> p k n", p=128)
    for ci in range(22):
        st, sbuf_ = stg[si % 3]
        si += 1
        S.dma(st[:], wv[:, :, ci * 256:(ci + 1) * 256], w=[sbuf_], sem=sbuf_)
        S.op("pool" if ci % 2 == 0 else "act",
             (lambda e, st=st, ci=ci: e.tensor_copy(wgu[:, :, ci * 256:(ci + 1) * 256], st[:])) if ci % 2 == 0
             else (lambda e, st=st, ci=ci: e.activation(out=wgu[:, :, ci * 256:(ci + 1) * 256], in_=st[:], func=AF.Copy)),
             r=[sbuf_], w=[b_wgu])
    wv = k.w_down[l].rearrange("(k p) n -> p k n", p=128)
    for m0 in (0, 8, 16):
        nm = min(8, NH - m0)
        for n0 in range(0, 1024, 256):
            st, sbuf_ = stg[si % 3]
            si += 1
            S.dma(st[:, :nm, :], wv[:, m0:m0 + nm, n0:n0 + 256], w=[sbuf_], sem=sbuf_)
            S.op("pool" if si % 2 == 0 else "act",
                 (lambda e, st=st, m0=m0, nm=nm, n0=n0: e.tensor_copy(wdn[:, m0:m0 + nm, n0:n0 + 256], st[:, :nm, :])) if si % 2 == 0
                 else (lambda e, st=st, m0=m0, nm=nm, n0=n0: e.activation(out=wdn[:, m0:m0 + nm, n0:n0 + 256], in_=st[:, :nm, :], func=AF.Copy)),
                 r=[sbuf_], w=[b_wdn])
    sq, b_sq = P.sb("fsq", [128, 8, 256], BF16)
    rstd, b_rstd = P.sb("frstd", [128, 256], F32)
    aT, b_aT = P.sb("faT", [128, 8, 256], BF16)
    hid, b_hid = P.sb("fhid", [128, NH, 256], BF16)
    sil = [P.sb("fsil%d" % i, [128, 256], F32) for i in range(2)]
    h2, b_h2 = P.sb("fh2", [128, 8, 256], F32)
    fg, b_fg = P.sb("ffg", [128, 8], F32)
    S.dma(fg[:], k.final_g, w=[b_fg], sem=b_fg)
    p_ss, b_pss = P.ps("fp_ss", [128, 512])
    p_gu = [P.ps("fp_gu%d" % i, [128, 512]) for i in range(4)]
    p_dn = [P.ps("fp_dn%d" % i, [128, 512]) for i in range(3)]
    hv = hsrc.rearrange("(k p) t -> p k t", p=128)
    ov = hdst.rearrange("(k p) t -> p k t", p=128) if hdst is not None else None
    NT = 256
    blocks = list(range(0 if need_ctx else 1, T // NT))

    def load(b):
        S.dma(hb[b % 2][0][:], hv[:, :, b * NT:(b + 1) * NT], w=[hb[b % 2][1]], sem=hb[b % 2][1])
    load(blocks[0])
    cg = 0
    cd = 0
    for bi, b in enumerate(blocks):
        t0 = b * NT
        j = 1 if b == 0 else 0
        if bi + 1 < len(blocks):
            load(blocks[bi + 1])
        ht, hbb = hb[b % 2]

        def norm(src, srcb, dst, dstb, gcol, bcol):
            S.op("act", lambda e: e.activation(out=sq[:], in_=src[:], func=AF.Square), r=[srcb], w=[b_sq])

            def mm_ss(e):
                last = None
                for kk in range(8):
                    last = e.matmul(p_ss[:, :NT], lhsT=k.ones_d[:], rhs=sq[:, kk, :], start=(kk == 0), stop=(kk == 7))
                return last
            S.group("pe", mm_ss, r=[b_sq, k.b_ones_d], w=[b_pss])
            rsqrt_eps(k, rstd[:], b_rstd, p_ss[:, :NT], b_pss)
            S.op("dve", lambda e: e.tensor_tensor(tmp[:], src[:], rstd[:].unsqueeze(1).broadcast_to([128, 8, NT]), ALU.mult),
                 r=[srcb, b_rstd], w=[b_tmp])
            for kk in range(8):
                if bcol is not None:
                    S.op("act", lambda e, kk=kk: e.activation(out=dst[:, kk, :], in_=tmp[:, kk, :], func=AF.Identity,
                                                               scale=gcol(kk), bias=bcol(kk)), r=[b_tmp, k.b_G2, k.b_mod, b_fg], w=[dstb])
                else:
                    S.op("act", lambda e, kk=kk: e.activation(out=dst[:, kk, :], in_=tmp[:, kk, :], func=AF.Copy, scale=gcol(kk)),
                         r=[b_tmp, b_fg], w=[dstb])
        if bi == 0 or final:
            norm(ht, hbb, aT, b_aT, lambda kk: k.G2[:, kk, j:j + 1], lambda kk: k.mod[:, 24 + kk, j:j + 1])
        for m in range(NH):
            pt, pb = p_gu[cg % 4]
            st_, sb_ = sil[cg % 2]
            cg += 1

            def mm(e, m=m, pt=pt):
                last = None
                for half in range(2):
                    c0 = half * 2816 + m * 128
                    for kk in range(8):
                        last = e.matmul(pt[:, half * 256:(half + 1) * 256], lhsT=wgu[:, kk, c0:c0 + 128], rhs=aT[:, kk, :],
                                        start=(kk == 0), stop=(kk == 7))
                return last
            S.group("pe", mm, r=[b_wgu, b_aT], w=[pb])
            S.op("act", lambda e, pt=pt, st_=st_: e.activation(out=st_[:], in_=pt[:, 0:256], func=AF.Silu), r=[pb], w=[sb_])
            S.op("dve", lambda e, pt=pt, st_=st_, m=m: e.tensor_tensor(hid[:, m, :], st_[:], pt[:, 256:512], ALU.mult), r=[pb, sb_], w=[b_hid])
        if bi + 1 < len(blocks) and not final:
            nb_ = blocks[bi + 1]
            jn = 1 if nb_ == 0 else 0
            norm(hb[nb_ % 2][0], hb[nb_ % 2][1], aT, b_aT, lambda kk: k.G2[:, kk, jn:jn + 1], lambda kk: k.mod[:, 24 + kk, jn:jn + 1])
        for n in range(8):
            pt, pb = p_dn[cd % 3]
            cd += 1

            def mm(e, n=n, pt=pt):
                last = None
                for m in range(NH):
                    last = e.matmul(pt[:, :NT], lhsT=wdn[:, m, n * 128:(n + 1) * 128], rhs=hid[:, m, :], start=(m == 0), stop=(m == NH - 1))
                return last
            S.group("pe", mm, r=[b_wdn, b_hid], w=[pb])
            S.op("dve", lambda e, n=n, pt=pt: e.scalar_tensor_tensor(out=h2[:, n, :], in0=pt[:, :NT], scalar=k.mod[:, 40 + n, j:j + 1],
                                                                     in1=ht[:, n, :], op0=ALU.mult, op1=ALU.add),
                 r=[pb, k.b_mod, hbb], w=[b_h2])
        if not final:
            S.dma(ov[:, :, t0:t0 + NT], h2[:], r=[b_h2], sem=b_h2)
        else:
            norm(h2, b_h2, h2, b_h2, lambda kk: fg[:, kk:kk + 1], None)
            S.dma(k.outT.rearrange("(k p) t -> p k t", p=128)[:, :, t0 - NCTX:t0 - NCTX + NT], h2[:], r=[b_h2], sem=b_h2)
    S.barrier()
    P.close()


_CONSTS = None


def kernel(**inputs):
    global _CONSTS
    inp = {k_: np.asarray(v) for k_, v in inputs.items()}
    nc = build(depth=2)
    if _CONSTS is None:
        _CONSTS = host_consts()
    in_maps = []
    for b in range(8):
        d = host_inputs(inp, b)
        d.update(_CONSTS)
        in_maps.append(d)
    res = run_bass_kernel_spmd(nc, in_maps, core_ids=list(range(8)))
    out = np.stack([np.ascontiguousarray(np.asarray(res.results[b]["outT"]).T) for b in range(8)])
    return out.astype(np.float32)
```

```python
import numpy as np, contextlib, math
import concourse.bass as bass
import concourse.mybir as mybir
from concourse.bass_utils import run_bass_kernel_spmd

F32 = mybir.dt.float32
BF16 = mybir.dt.bfloat16
AF = mybir.ActivationFunctionType
ALU = mybir.AluOpType
AX = mybir.AxisListType


class Buf:
    __slots__ = ("name", "w", "r", "dsem", "excl")

    def __init__(self, name, excl=False):
        self.name = name
        self.excl = excl
        self.w = None
        self.r = {}
        self.dsem = None


class Sched:
    ENG = ("pe", "act", "dve", "pool", "sp")

    def __init__(self, nc, es):
        self.nc = nc
        self.es = es
        self.eng = {"pe": nc.tensor, "act": nc.scalar, "dve": nc.vector,
                    "pool": nc.gpsimd, "sp": nc.sync}
        self.sems = {}
        self.cnt = {}
        for e in self.ENG:
            self.sems[e] = es.enter_context(nc.semaphore("s_" + e))
            self.cnt[e] = 0
        self.seen = {e: {} for e in self.ENG}
        self.free_dsems = []
        self.n_dsem = 0
        self.live_dsems = []

    def _dsem(self, buf):
        if buf.dsem is None:
            if self.free_dsems:
                key = self.free_dsems.pop()
            else:
                key = "d%d" % self.n_dsem
                self.n_dsem += 1
                self.sems[key] = self.es.enter_context(self.nc.semaphore("s_" + key))
                self.cnt[key] = 0
            buf.dsem = key
            self.live_dsems.append(buf)
        return buf.dsem

    def _deps(self, e, r, w):
        deps = {}

        def need(ev, raw):
            if ev is None:
                return
            key, val = ev
            if key == e and (e == "pe" or not raw):
                return
            if self.seen[e].get(key, 0) >= val:
                return
            if deps.get(key, 0) < val:
                deps[key] = val
        for b in r:
            need(b.w, True)
            if b.excl:
                for k, v in b.r.items():
                    need((k, v), False)
        for b in w:
            need(b.w, False)
            for k, v in b.r.items():
                need((k, v), False)
        return deps

    def _emit(self, e, deps, fn):
        eng = self.eng[e]
        items = list(deps.items())
        for key, val in items[:-1]:
            eng.wait_ge(self.sems[key], val)
        ins = fn(eng)
        if isinstance(ins, (list, tuple)):
            first, last = ins[0], ins[-1]
            multi = len(ins) > 1
        else:
            first = last = ins
            multi = False
        if items:
            key, val = items[-1]
            if multi:
                raise RuntimeError("multi-instruction op must use op_group")
            first._wait_ge(self.sems[key], val)
        for key, val in items:
            self.seen[e][key] = val
        return last

    def op(self, e, fn, r=(), w=()):
        deps = self._deps(e, r, w)
        last = self._emit(e, deps, fn)
        self.cnt[e] += 1
        last.then_inc(self.sems[e], 1)
        ev = (e, self.cnt[e])
        self._mark(ev, r, w)
        return ev

    def group(self, e, fn, r=(), w=()):
        deps = self._deps(e, r, w)
        eng = self.eng[e]
        for key, val in deps.items():
            eng.wait_ge(self.sems[key], val)
            self.seen[e][key] = val
        last = fn(eng)
        self.cnt[e] += 1
        last.then_inc(self.sems[e], 1)
        ev = (e, self.cnt[e])
        self._mark(ev, r, w)
        return ev

    def _mark(self, ev, r, w):
        key, val = ev
        for b in r:
            if b.excl:
                b.r = {key: val}
            elif b.r.get(key, 0) < val:
                b.r[key] = val
        for b in w:
            b.w = ev
            b.r = {}

    def dma(self, out, in_, r=(), w=(), sem=None, q="sp", **kw):
        deps = self._deps(q, r, w)
        key = self._dsem(sem)
        eng = self.eng[q]
        for k, v in deps.items():
            eng.wait_ge(self.sems[k], v)
            self.seen[q][k] = v
        ins = eng.dma_start(out=out, in_=in_, **kw)
        self.cnt[key] += 16
        ins.then_inc(self.sems[key], 16)
        ev = (key, self.cnt[key])
        self._mark(ev, r, w)
        return ev

    def barrier(self):
        sp = self.eng["sp"]
        for key in list(self.sems.keys()):
            if key == "sp":
                continue
            v = self.cnt[key]
            if v > 0 and self.seen["sp"].get(key, 0) < v:
                sp.wait_ge(self.sems[key], v)
                self.seen["sp"][key] = v
        self.cnt["sp"] += 1
        sp.sem_inc(self.sems["sp"], 1)
        for e in self.ENG:
            if e == "sp":
                continue
            self.eng[e].wait_ge(self.sems["sp"], self.cnt["sp"])
        for e in self.ENG:
            for key in self.sems:
                self.seen[e][key] = self.cnt[key]
        for b in self.live_dsems:
            self.free_dsems.append(b.dsem)
            b.dsem = None
        self.live_dsems = []

    def finish(self):
        self.barrier()


import os
import ml_dtypes

T = 4352
NCTX = 256
D = 1024
IN_DIM = 2968
EPS = 1e-6
BLOCKS = [(0, 256)] + [(256 + 512 * i, 512) for i in range(8)]


class K:
    pass


def build(depth=2, stop_after=None, debug=False):
    nc = bass.Bass("TRN2", target_bir_lowering=False)
    es = contextlib.ExitStack()
    S = Sched(nc, es)
    k = K()
    k.nc, k.S, k.es = nc, S, es
    k.debug = debug

    def din(name, shape, dt=F32):
        return nc.dram_tensor(name, list(shape), dt, kind="ExternalInput").ap()

    def dscr(name, shape, dt=F32):
        kind = "ExternalOutput" if debug else "Internal"
        return nc.dram_tensor(name, list(shape), dt, kind=kind).ap()

    k.xT = din("xT", [D, T])
    k.ccin = din("ccin", [128, 8, 2])
    k.ada_w = din("ada_w", [depth, D, 6 * D])
    k.ada_b = din("ada_b", [depth, 128, 48])
    k.norm1_g = din("norm1_g", [depth, 128, 8])
    k.norm2_g = din("norm2_g", [depth, 128, 8])
    k.final_g = din("final_g", [128, 8])
    k.w_in = din("w_in", [depth, D, IN_DIM])
    k.w_out = din("w_out", [depth, D, D])
    k.w_gu = din("w_gu", [depth, D, 2 * 2816])
    k.w_down = din("w_down", [depth, 2816, D])
    k.outT = nc.dram_tensor("outT", [D, 4096], F32, kind="ExternalOutput").ap()
    k.qk_g = din("qk_g", [depth, 128, 2])
    k.c_ones_d = din("c_ones_d", [128, 128], BF16)
    k.c_blk64 = din("c_blk64", [128, 128], BF16)
    k.c_rp_d = din("c_rp_d", [128, 128], BF16)
    k.c_rp_g = din("c_rp_g", [128, 128], BF16)
    k.c_rope = din("c_rope", [4, 128, 4096])
    k.c_place = din("c_place", [64, 2, 128], BF16)
    k.diff_lambda = din("diff_lambda", [depth, 128])
    k.diff_g = din("diff_g", [depth, 128, 1])
    k.conv_w = din("conv_w", [depth, 128, 9, 5])
    k.c_blk64s = din("c_blk64s", [128, 128], BF16)
    k.c_identb = din("c_identb", [128, 128], BF16)
    k.c_identf = din("c_identf", [128, 128])
    k.c_MI = din("c_MI", [128, 2, 128])
    k.c_SU = din("c_SU", [128, 2, 128])
    k.c_EMK = din("c_EMK", [128, 2, 256])
    k.c_MK = din("c_MK", [3, 128, 128])
    k.gdn_g = din("gdn_g", [depth, 64])
    k.gdn_alog = din("gdn_alog", [depth, 12])
    k.gdn_dtb = din("gdn_dtb", [depth, 12])
    k.hA = dscr("hA", [D, T])
    k.hB = dscr("hB", [D, T])
    k.QdT = dscr("QdT", [2, 128, T], BF16)
    k.KdT = dscr("KdT", [2, 128, T], BF16)
    k.QgT = dscr("QgT", [3, 128, T], BF16)
    k.KgT = dscr("KgT", [1, 128, T], BF16)
    k.Vtok = dscr("Vtok", [T, 390], BF16)
    k.gqkvT = dscr("gqkvT", [9, 128, T])
    k.zab = dscr("zab", [T, 408])
    k.mixT = dscr("mixT", [8, 128, T], BF16)
    k.QnT = dscr("QnT", [3, 128, T], BF16)
    k.KnT = dscr("KnT", [3, 128, T], BF16)
    k.Ktok = dscr("Ktok", [T, 384], BF16)
    k.Vtok2 = dscr("Vtok2", [T, 384], BF16)
    if debug:
        k.dbg_mod = nc.dram_tensor("dbg_mod", [128, 96], F32, kind="ExternalOutput").ap()

    def sbp(name, shape, dt):
        t = es.enter_context(nc.sbuf_tensor(name, list(shape), dt))
        return t, Buf(name)

    k.ones_d, k.b_ones_d = sbp("ones_d", [128, 128], BF16)
    k.blk64, k.b_blk64 = sbp("blk64", [128, 128], BF16)
    k.rp_d, k.b_rp_d = sbp("rp_d", [128, 128], BF16)
    k.rp_g, k.b_rp_g = sbp("rp_g", [128, 128], BF16)
    k.scc, k.b_scc = sbp("scc", [128, 8, 2], F32)
    k.mod, k.b_mod = sbp("mod", [128, 48, 2], F32)
    k.G1, k.b_G1 = sbp("G1", [128, 8, 2], F32)
    k.G2, k.b_G2 = sbp("G2", [128, 8, 2], F32)
    k.qkg, k.b_qkg = sbp("qkg", [128, 2], F32)
    k.epsc, k.b_epsc = sbp("epsc", [128, 2], F32)
    k.place, k.b_place = sbp("place", [64, 2, 128], BF16)
    k.blk64s, k.b_blk64s = sbp("blk64s", [128, 128], BF16)
    k.identb, k.b_identb = sbp("identb", [128, 128], BF16)
    S.dma(k.blk64s[:], k.c_blk64s, w=[k.b_blk64s], sem=k.b_blk64s)
    S.dma(k.identb[:], k.c_identb, w=[k.b_identb], sem=k.b_identb)
    S.dma(k.place[:], k.c_place, w=[k.b_place], sem=k.b_place)
    S.op("pool", lambda e: e.memset(k.epsc[:], EPS), w=[k.b_epsc])

    S.dma(k.ones_d[:], k.c_ones_d, w=[k.b_ones_d], sem=k.b_ones_d)
    S.dma(k.blk64[:], k.c_blk64, w=[k.b_blk64], sem=k.b_blk64)
    S.dma(k.rp_d[:], k.c_rp_d, w=[k.b_rp_d], sem=k.b_rp_d)
    S.dma(k.rp_g[:], k.c_rp_g, w=[k.b_rp_g], sem=k.b_rp_g)
    S.dma(k.scc[:], k.ccin, w=[k.b_scc], sem=k.b_scc)
    S.op("act", lambda e: e.activation(out=k.scc[:], in_=k.scc[:], func=AF.Silu), r=[k.b_scc], w=[k.b_scc])
    S.barrier()

    for l in range(depth):
        ph_mod(k, l)
        S.barrier()
        if stop_after == ("mod", l):
            break
        ph_proj(k, l, k.xT if l == 0 else k.hB)
        S.barrier()
        if stop_after == ("proj", l):
            break
        ph_att(k, l, need_ctx=(l < depth - 1))
        S.barrier()
        if stop_after == ("att", l):
            break
        ph_gdn_a(k, l)
        S.barrier()
        if stop_after == ("gdna", l):
            break
        ph_gdn_b(k, l, need_ctx=(l < depth - 1))
        S.barrier()
        if stop_after == ("gdnb", l):
            break
        last = (l == depth - 1)
        hsrc = k.xT if l == 0 else k.hB
        ph_out(k, l, hsrc, k.hA, need_ctx=not last)
        S.barrier()
        if stop_after == ("out", l):
            break
        ph_ffn(k, l, k.hA, None if last else k.hB, need_ctx=not last, final=last)
        S.barrier()
        if stop_after == ("ffn", l):
            break
    S.finish()
    es.close()
    return nc


def rsqrt_eps(k, dst, dst_b, src, src_b, eps=EPS):
    S = k.S
    npart = dst.shape[0]
    S.op("act", lambda e: e.activation(out=dst, in_=src, func=AF.Ln, bias=k.epsc[0:npart, 0:1]), r=[src_b, k.b_epsc], w=[dst_b])
    S.op("act", lambda e: e.activation(out=dst, in_=dst, func=AF.Exp, scale=-0.5), r=[dst_b], w=[dst_b])


class Phase:
    def __init__(self, k):
        self.k = k
        self.es = contextlib.ExitStack()
        self.nps = 0

    _uid = [0]

    def sb(self, name, shape, dt):
        Phase._uid[0] += 1
        name = "%s_u%d" % (name, Phase._uid[0])
        t = self.es.enter_context(self.k.nc.sbuf_tensor(name, list(shape), dt))
        return t, Buf(name)

    def ps(self, name, shape, dt=F32):
        Phase._uid[0] += 1
        name = "%s_u%d" % (name, Phase._uid[0])
        t = self.es.enter_context(self.k.nc.psum_tensor(name, list(shape), dt))
        return t, Buf(name, excl=True)

    def close(self):
        self.es.close()


def ph_mod(k, l):
    S, nc = k.S, k.nc
    P = Phase(k)
    wst = [P.sb("adaw%d" % i, [128, 8, 512], F32) for i in range(2)]
    adab, b_adab = P.sb("adab", [128, 48], F32)
    n1, b_n1 = P.sb("n1g", [128, 8], F32)
    n2, b_n2 = P.sb("n2g", [128, 8], F32)
    pm, b_pm = P.ps("pm", [128, 96])
    S.dma(adab[:], k.ada_b[l], w=[b_adab], sem=b_adab)
    S.dma(n1[:], k.norm1_g[l], w=[b_n1], sem=b_n1)
    S.dma(n2[:], k.norm2_g[l], w=[b_n2], sem=b_n2)
    S.dma(k.qkg[:], k.qk_g[l], w=[k.b_qkg], sem=k.b_qkg)
    wv = k.ada_w[l].rearrange("(k p) n -> p k n", p=128)
    for ci in range(12):
        wt, wb = wst[ci % 2]
        S.dma(wt[:], wv[:, :, ci * 512:(ci + 1) * 512], w=[wb], sem=wb)
        for j in range(4):
            c = ci * 4 + j

            def mm(e, c=c, j=j, wt=wt):
                last = None
                for kk in range(8):
                    last = e.matmul(pm[:, 2 * c:2 * c + 2], lhsT=wt[:, kk, j * 128:(j + 1) * 128],
                                    rhs=k.scc[:, kk, :], start=(kk == 0), stop=(kk == 7))
                return last
            S.group("pe", mm, r=[wb, k.b_scc], w=[b_pm])
    S.op("dve", lambda e: e.tensor_tensor(k.mod[:], pm[:].rearrange("p (c j) -> p c j", j=2),
                                          adab[:].unsqueeze(2).broadcast_to([128, 48, 2]), ALU.add),
         r=[b_pm, b_adab], w=[k.b_mod])
    S.op("dve", lambda e: e.scalar_tensor_tensor(out=k.G1[:], in0=k.mod[:, 8:16, :], scalar=1.0,
                                                 in1=n1[:].unsqueeze(2).broadcast_to([128, 8, 2]),
                                                 op0=ALU.add, op1=ALU.mult),
         r=[k.b_mod, b_n1], w=[k.b_G1])
    S.op("dve", lambda e: e.scalar_tensor_tensor(out=k.G2[:], in0=k.mod[:, 32:40, :], scalar=1.0,
                                                 in1=n2[:].unsqueeze(2).broadcast_to([128, 8, 2]),
                                                 op0=ALU.add, op1=ALU.mult),
         r=[k.b_mod, b_n2], w=[k.b_G2])
    if k.debug:
        S.dma(k.dbg_mod, k.mod[:].rearrange("p c j -> p (c j)"), r=[k.b_mod], sem=k.b_mod)
    S.barrier()
    P.close()


def ph_proj(k, l, hsrc):
    S, nc = k.S, k.nc
    P = Phase(k)
    wbf, b_w = P.sb("w_in_bf", [128, 8, IN_DIM], BF16)
    wv = k.w_in[l].rearrange("(k p) n -> p k n", p=128)
    hb = [P.sb("hblk%d" % i, [128, 8, 512], F32) for i in range(2)]
    tmp, b_tmp = P.sb("tmp", [128, 8, 512], F32)
    stg = [(tmp, b_tmp), hb[1]]
    for ci, c0 in enumerate(range(0, IN_DIM, 512)):
        n = min(512, IN_DIM - c0)
        st, sbuf_ = stg[ci % 2]
        S.dma(st[:, :, :n], wv[:, :, c0:c0 + n], w=[sbuf_], sem=sbuf_)
        S.op("pool", lambda e, st=st, c0=c0, n=n: e.tensor_copy(wbf[:, :, c0:c0 + n], st[:, :, :n]),
             r=[sbuf_], w=[b_w])

    sq, b_sq = P.sb("sq", [128, 8, 512], BF16)
    rstd, b_rstd = P.sb("rstd", [128, 512], F32)
    aT = [P.sb("aT%d" % i, [128, 8, 512], BF16) for i in range(2)]
    rope = [P.sb("rope%d" % i, [128, 4, 512], F32) for i in range(2)]
    p_ss, b_pss = P.ps("p_ss", [128, 512])
    p_fm = [P.ps("p_fm%d" % i, [128, 512]) for i in range(3)]
    p_aux = [P.ps("p_aux%d" % i, [128, 512]) for i in range(2)]
    p_tm = [P.ps("p_tm%d" % i, [128, 512]) for i in range(2)]
    qb = [P.sb("qb%d" % i, [128, 512], BF16) for i in range(2)]
    sqh = [P.sb("sqh%d" % i, [128, 512], BF16) for i in range(2)]
    t1 = [P.sb("t1_%d" % i, [128, 512], F32) for i in range(2)]
    t2 = [P.sb("t2_%d" % i, [128, 512], F32) for i in range(2)]
    rs2 = [P.sb("rs2_%d" % i, [128, 512], F32) for i in range(2)]
    ob = [P.sb("ob%d" % i, [128, 512], BF16) for i in range(3)]
    of = [P.sb("of%d" % i, [128, 512], F32) for i in range(3)]
    vt = [P.sb("vt%d" % i, [128, 4, 390], BF16) for i in range(2)]
    zt = [P.sb("zt%d" % i, [128, 4, 408], F32) for i in range(2)]
    for i in range(2):
        S.op("pool", lambda e, i=i: e.memset(vt[i][0][:], 1.0), w=[vt[i][1]])

    hv = hsrc.rearrange("(k p) t -> p k t", p=128)
    cnt = {"fm": 0, "aux": 0, "tm": 0, "w": 0, "ob": 0, "of": 0}

    def load_block(b):
        t0, nt = BLOCKS[b]
        ht, hbuf = hb[b % 2]
        S.dma(ht[:, :, :nt], hv[:, :, t0:t0 + nt], w=[hbuf], sem=hbuf)
        if b > 0:
            rt, rbuf = rope[b % 2]
            S.dma(rt[:, :, :nt], k.c_rope[:, :, t0 - NCTX:t0 - NCTX + nt].rearrange("c p t -> p c t"),
                  w=[rbuf], sem=rbuf)

    def norm_block(b):
        t0, nt = BLOCKS[b]
        j = 1 if b == 0 else 0
        ht, hbuf = hb[b % 2]
        at, abuf = aT[b % 2]
        S.op("act", lambda e: e.activation(out=sq[:, :, :nt], in_=ht[:, :, :nt], func=AF.Square),
             r=[hbuf], w=[b_sq])

        def mm_ss(e):
            last = None
            for kk in range(8):
                last = e.matmul(p_ss[:, :nt], lhsT=k.ones_d[:], rhs=sq[:, kk, :nt], start=(kk == 0), stop=(kk == 7))
            return last
        S.group("pe", mm_ss, r=[b_sq, k.b_ones_d], w=[b_pss])
        rsqrt_eps(k, rstd[:, :nt], b_rstd, p_ss[:, :nt], b_pss)
        S.op("dve", lambda e: e.tensor_tensor(tmp[:, :, :nt], ht[:, :, :nt],
                                              rstd[:, :nt].unsqueeze(1).broadcast_to([128, 8, nt]), ALU.mult),
             r=[hbuf, b_rstd], w=[b_tmp])
        for kk in range(8):
            S.op("act", lambda e, kk=kk: e.activation(out=at[:, kk, :nt], in_=tmp[:, kk, :nt], func=AF.Identity,
                                                       scale=k.G1[:, kk, j:j + 1], bias=k.mod[:, kk, j:j + 1]),
                 r=[b_tmp, k.b_G1, k.b_mod], w=[abuf])

    load_block(0)
    for b in range(len(BLOCKS)):
        t0, nt = BLOCKS[b]
        j = 1 if b == 0 else 0
        if b + 1 < len(BLOCKS):
            load_block(b + 1)
        ht, hbuf = hb[b % 2]
        rt, rbuf = rope[b % 2]
        at, abuf = aT[b % 2]
        if b == 0:
            norm_block(0)
        if os.environ.get("BISECT") == "1":
            continue
        def fm_matmul(c0):
            pt, pb = p_fm[cnt["fm"] % 3]
            cnt["fm"] += 1

            def mm(e):
                last = None
                for kk in range(8):
                    last = e.matmul(pt[:, :nt], lhsT=wbf[:, kk, c0:c0 + 128], rhs=at[:, kk, :nt],
                                    start=(kk == 0), stop=(kk == 7))
                return last
            S.group("pe", mm, r=[b_w, abuf], w=[pb])
            return pt, pb

        def store(dst, src_t, src_b):
            S.dma(dst, src_t, r=[src_b], sem=src_b)

        def rope_chunk(pt, pb, kind, dst, gcol=None):
            i = cnt["w"] % 2
            cnt["w"] += 1
            qt, qbuf = qb[i]
            o_t, o_b = ob[cnt["ob"] % 3]
            cnt["ob"] += 1
            rp = k.rp_d if kind == "d" else k.rp_g
            rpb = k.b_rp_d if kind == "d" else k.b_rp_g
            ci, si = (0, 1) if kind == "d" else (2, 3)
            if kind == "d":
                S.op("act", lambda e: e.activation(out=qt[:, :nt], in_=pt[:, :nt], func=AF.Copy), r=[pb], w=[qbuf])
            else:
                S.op("act", lambda e: e.activation(out=qt[:, :nt], in_=pt[:, :nt], func=AF.Copy, scale=gcol),
                     r=[pb, k.b_qkg], w=[qbuf])
                st_, sb_ = sqh[i]
                S.op("act", lambda e: e.activation(out=st_[:, :nt], in_=pt[:, :nt], func=AF.Square), r=[pb], w=[sb_])
                pa, pab = p_aux[cnt["aux"] % 2]
                cnt["aux"] += 1
                S.op("pe", lambda e: e.matmul(pa[:, :nt], lhsT=k.blk64[:], rhs=st_[:, :nt], start=True, stop=True),
                     r=[sb_, k.b_blk64], w=[pab])
                r2, r2b = rs2[i]
                rsqrt_eps(k, r2[:, :nt], r2b, pa[:, :nt], pab)
            if b == 0:
                if kind == "d":
                    store(dst, qt[:, :nt], qbuf)
                else:
                    S.op("pool", lambda e: e.tensor_tensor(o_t[:, :nt], qt[:, :nt], r2[:, :nt], ALU.mult),
                         r=[qbuf, r2b], w=[o_b])
                    store(dst, o_t[:, :nt], o_b)
                return
            pa, pab = p_aux[cnt["aux"] % 2]
            cnt["aux"] += 1
            S.op("pe", lambda e: e.matmul(pa[:, :nt], lhsT=rp[:], rhs=qt[:, :nt], start=True, stop=True),
                 r=[qbuf, rpb], w=[pab])
            a1, a1b = t1[i]
            a2, a2b = t2[i]
            if kind == "d":
                S.op("dve", lambda e: e.tensor_tensor(a1[:, :nt], pt[:, :nt], rt[:, ci, :nt], ALU.mult),
                     r=[pb, rbuf], w=[a1b])
            else:
                S.op("pool", lambda e: e.tensor_tensor(a1[:, :nt], qt[:, :nt], rt[:, ci, :nt], ALU.mult),
                     r=[qbuf, rbuf], w=[a1b])
            S.op("dve", lambda e: e.tensor_tensor(a2[:, :nt], pa[:, :nt], rt[:, si, :nt], ALU.mult),
                 r=[pab, rbuf], w=[a2b])
            if kind == "d":
                S.op("pool", lambda e: e.tensor_tensor(o_t[:, :nt], a1[:, :nt], a2[:, :nt], ALU.add),
                     r=[a1b, a2b], w=[o_b])
            else:
                S.op("pool", lambda e: e.tensor_tensor(a1[:, :nt], a1[:, :nt], a2[:, :nt], ALU.add),
                     r=[a1b, a2b], w=[a1b])
                S.op("pool", lambda e: e.tensor_tensor(o_t[:, :nt], a1[:, :nt], r2[:, :nt], ALU.mult),
                     r=[a1b, r2b], w=[o_b])
            store(dst, o_t[:, :nt], o_b)

        B2 = os.environ.get("BISECT2", "dgn")
        for c in range(2 if "d" in B2 else 0):
            pt, pb = fm_matmul(c * 128)
            rope_chunk(pt, pb, "d", k.QdT[c, :, t0:t0 + nt])
        for c in range(2 if "d" in B2 else 0):
            pt, pb = fm_matmul(256 + c * 128)
            rope_chunk(pt, pb, "d", k.KdT[c, :, t0:t0 + nt])
        for c in range(3 if "g" in B2 else 0):
            pt, pb = fm_matmul(768 + c * 128)
            rope_chunk(pt, pb, "g", k.QgT[c, :, t0:t0 + nt], gcol=k.qkg[:, 0:1])
        if "g" in B2:
            pt, pb = fm_matmul(1152)
            rope_chunk(pt, pb, "g", k.KgT[0, :, t0:t0 + nt], gcol=k.qkg[:, 1:2])
        for c in range(9 if "n" in B2 else 0):
            pt, pb = fm_matmul(1408 + c * 128)
            o_t, o_b = of[cnt["of"] % 3]
            cnt["of"] += 1
            if c % 2 == 0:
                S.op("dve", lambda e: e.tensor_copy(o_t[:, :nt], pt[:, :nt]), r=[pb], w=[o_b])
            else:
                S.op("act", lambda e: e.activation(out=o_t[:, :nt], in_=pt[:, :nt], func=AF.Copy), r=[pb], w=[o_b])
            store(k.gqkvT[c, :, t0:t0 + nt], o_t[:, :nt], o_b)

        if os.environ.get("BISECT") == "2":
            continue
        if b + 1 < len(BLOCKS):
            norm_block(b + 1)
        v_t, v_b = vt[b % 2]
        z_t, z_b = zt[b % 2]
        ntile = nt // 128
        for tt in range(ntile):
            for (c0, n, kind) in ((512, 256, "dv"), (1280, 128, "gv"), (2560, 408, "zab")):
                pt, pb = p_tm[cnt["tm"] % 2]
                cnt["tm"] += 1

                def mm(e, c0=c0, n=n, pt=pt):
                    last = None
                    for kk in range(8):
                        last = e.matmul(pt[:, :n], lhsT=at[:, kk, tt * 128:(tt + 1) * 128], rhs=wbf[:, kk, c0:c0 + n],
                                        start=(kk == 0), stop=(kk == 7))
                    return last
                S.group("pe", mm, r=[b_w, abuf], w=[pb])
                if kind == "dv":
                    S.op("act", lambda e, pt=pt: e.activation(
                        out=v_t[:, tt, 0:260].rearrange("p (h d) -> p h d", d=65)[:, :, 0:64],
                        in_=pt[:, 0:256].rearrange("p (h d) -> p h d", d=64), func=AF.Copy), r=[pb], w=[v_b])
                elif kind == "gv":
                    S.op("act", lambda e, pt=pt: e.activation(
                        out=v_t[:, tt, 260:390].rearrange("p (h d) -> p h d", d=65)[:, :, 0:64],
                        in_=pt[:, 0:128].rearrange("p (h d) -> p h d", d=64), func=AF.Copy), r=[pb], w=[v_b])
                else:
                    S.op("dve", lambda e, pt=pt: e.tensor_copy(z_t[:, tt, :], pt[:, 0:408]), r=[pb], w=[z_b])
        S.dma(k.Vtok[t0:t0 + nt, :].rearrange("(n p) c -> p n c", p=128), v_t[:, :ntile, :], r=[v_b], sem=v_b)
        S.dma(k.zab[t0:t0 + nt, :].rearrange("(n p) c -> p n c", p=128), z_t[:, :ntile, :], r=[z_b], sem=z_b)
    S.barrier()
    P.close()


def host_consts():
    bf = ml_dtypes.bfloat16
    c = {}
    c["c_ones_d"] = np.full((128, 128), 1.0 / 1024, np.float32).astype(bf)
    blk = np.zeros((128, 128), np.float32)
    blk[:64, :64] = 1.0 / 64
    blk[64:, 64:] = 1.0 / 64
    c["c_blk64"] = blk.astype(bf)
    c["c_blk64s"] = (blk * 64).astype(bf)
    c["c_identb"] = np.eye(128, dtype=np.float32).astype(bf)
    c["c_identf"] = np.eye(128, dtype=np.float32)
    ti = np.arange(128)
    le = (ti[:, None] <= ti[None, :]).astype(np.float32)
    gt = (ti[:, None] > ti[None, :]).astype(np.float32)
    c["c_MI"] = np.ascontiguousarray(np.stack([le, le.T], axis=1))
    c["c_SU"] = np.ascontiguousarray(np.stack([gt, gt.T], axis=1))
    incl_f = (ti[None, :] >= ti[:, None]).astype(np.float32); strict_f = (ti[None, :] > ti[:, None]).astype(np.float32)
    emk_f = np.concatenate([incl_f, -strict_f], axis=1)
    emk_b = np.concatenate([incl_f.T, -strict_f.T], axis=1)
    c["c_EMK"] = np.ascontiguousarray(np.stack([emk_f, emk_b], axis=1))
    blk = lambda b_: (ti[:, None] // b_) == (ti[None, :] // b_)
    c["c_MK"] = np.stack([blk(64) & ~blk(32), blk(128) & ~blk(64), blk(32)]).astype(np.float32)

    def rp(group, nf):
        Rp = np.zeros((128, 128), np.float32)
        for g0 in range(0, 128, group):
            for a in range(2):
                for f in range(nf):
                    i0 = g0 + a * 2 * nf + f
                    i1 = g0 + a * 2 * nf + nf + f
                    Rp[i0, i1] = -1.0
                    Rp[i1, i0] = 1.0
        return np.ascontiguousarray(Rp.T).astype(bf)
    c["c_rp_d"] = rp(32, 8)
    c["c_rp_g"] = rp(64, 16)
    t = np.arange(4096)
    row = (t // 64).astype(np.float32)
    col = (t % 64).astype(np.float32)

    def tables(group, nf):
        inv = (np.float32(10000.0) ** (-np.arange(nf, dtype=np.float32) / np.float32(nf))).astype(np.float32)
        cos = np.zeros((128, 4096), np.float32)
        sin = np.zeros((128, 4096), np.float32)
        for p in range(128):
            w = p % group
            a = w // (2 * nf)
            f = w % nf
            pos = row if a == 0 else col
            ang = (pos * inv[f]).astype(np.float32)
            cos[p] = np.cos(ang)
            sin[p] = np.sin(ang)
        return cos, sin
    cd, sd = tables(32, 8)
    cg, sg = tables(64, 16)
    c["c_rope"] = np.stack([cd, sd, cg, sg]).astype(np.float32)
    pl = np.zeros((64, 2, 128), np.float32)
    for i in range(64):
        pl[i, 0, i] = 1.0
        pl[i, 1, 64 + i] = 1.0
    c["c_place"] = pl.astype(bf)
    return c


def col_layout(v, nch):
    v = np.asarray(v)
    return np.ascontiguousarray(np.swapaxes(v.reshape(v.shape[:-1] + (nch, 128)), -1, -2))


def host_inputs(inp, b):
    d = {}
    xT = np.concatenate([inp["ctx"][b], inp["x"][b]], axis=0).T
    d["xT"] = np.ascontiguousarray(xT)
    cc = np.stack([col_layout(inp["c"][b], 8), col_layout(inp["c_ctx"], 8)], axis=-1)
    d["ccin"] = np.ascontiguousarray(cc.astype(np.float32))
    d["ada_w"] = inp["ada_w"]
    d["ada_b"] = col_layout(inp["ada_b"], 48)
    d["norm1_g"] = col_layout(inp["norm1_g"], 8)
    d["norm2_g"] = col_layout(inp["norm2_g"], 8)
    d["final_g"] = col_layout(inp["final_norm_g"], 8)
    d["w_in"] = inp["w_in"]
    d["w_out"] = inp["w_out"]
    d["w_gu"] = inp["ffn_w_gu"]
    d["w_down"] = inp["ffn_w_down"]
    d["gdn_g"] = inp["gdn_norm_g"]
    d["gdn_alog"] = np.ascontiguousarray(inp["gdn_a_log"].reshape(-1, 12))
    d["gdn_dtb"] = np.ascontiguousarray(inp["gdn_dt_bias"].reshape(-1, 12))
    cwl = inp["gdn_conv_w"]
    d["conv_w"] = np.ascontiguousarray(cwl.reshape(cwl.shape[0], 5, 9, 128).transpose(0, 3, 2, 1))
    d["diff_lambda"] = np.ascontiguousarray(inp["diff_lambda"].reshape(-1, 128))
    d["diff_g"] = np.ascontiguousarray(np.tile(inp["diff_norm_g"], (1, 2))[:, :, None].astype(np.float32))
    qg = np.tile(inp["q_norm_g"], (1, 2))
    kg = np.tile(inp["k_norm_g"], (1, 2))
    d["qk_g"] = np.ascontiguousarray(np.stack([qg, kg], axis=-1).astype(np.float32))
    return d


def ph_att(k, l, need_ctx):
    S, nc = k.S, k.nc
    P = Phase(k)
    lam_init = 0.8 - 0.6 * math.exp(-0.3 * l)
    Kd, b_Kd = P.sb("Kd", [128, 2, 2, T], BF16)
    Kg, b_Kg = P.sb("Kg", [128, 3, T], BF16)
    V, b_V = P.sb("V", [128, 34, 390], BF16)
    S.op("pool", lambda e: e.memset(Kd[:], 0.0), w=[b_Kd])
    for c in range(2):
        for hi in range(2):
            for cc in range(2):
                r0 = hi * 64 + cc * 32
                S.dma(Kd[r0:r0 + 32, c, cc, :], k.KdT[c, r0:r0 + 32, :], w=[b_Kd], sem=b_Kd)
    for vi, (ka, kb) in enumerate(((0, 0), (0, 1), (1, 1))):
        S.dma(Kg[0:64, vi, :], k.KgT[0, ka * 64:ka * 64 + 64, :], w=[b_Kg], sem=b_Kg)
        S.dma(Kg[64:128, vi, :], k.KgT[0, kb * 64:kb * 64 + 64, :], w=[b_Kg], sem=b_Kg)
    S.dma(V[:], k.Vtok.rearrange("(n p) c -> p n c", p=128), w=[b_V], sem=b_V)
    lamt, b_lamt = P.sb("lamt", [128, 4, 32], F32)
    lamp, b_lamp = P.sb("lamp", [128, 2, 32], F32)
    lams, b_lams = P.sb("lams", [128, 5], F32)
    gd, b_gd = P.sb("gd", [128, 1], F32)
    ones_r, b_ones_r = P.sb("ones_r", [128, 64], F32)
    S.op("pool", lambda e: e.memset(ones_r[:], 1.0), w=[b_ones_r])
    S.dma(lamt[:].rearrange("p a b -> p (a b)"), k.diff_lambda[l:l + 1, :].partition_broadcast(128), w=[b_lamt], sem=b_lamt)
    S.dma(gd[:], k.diff_g[l], w=[b_gd], sem=b_gd)
    S.op("act", lambda e: e.mul(gd[:], gd[:], 1.0 - lam_init), r=[b_gd], w=[b_gd])
    S.op("dve", lambda e: e.tensor_tensor(lamp[:], lamt[:, 0:4:2, :], lamt[:, 1:4:2, :], ALU.mult), r=[b_lamt], w=[b_lamp])
    S.op("dve", lambda e: e.reduce_sum(lams[:, 0:2], lamp[:], axis=AX.X), r=[b_lamp], w=[b_lams])
    S.op("act", lambda e: e.activation(out=lams[:, 0:2], in_=lams[:, 0:2], func=AF.Exp), r=[b_lams], w=[b_lams])
    S.op("dve", lambda e: e.tensor_tensor(lams[:, 2:3], lams[:, 0:1], lams[:, 1:2], ALU.subtract), r=[b_lams], w=[b_lams])
    S.op("dve", lambda e: e.tensor_scalar(lams[:, 3:4], lams[:, 2:3], lam_init, None, ALU.add), r=[b_lams], w=[b_lams])
    S.op("dve", lambda e: e.tensor_scalar(lams[:, 4:5], lams[:, 3:4], -1.0, None, ALU.mult), r=[b_lams], w=[b_lams])
    lam_ap = lams[:, 3:4]
    neglam_ap = lams[:, 4:5]

    Qd = [P.sb("Qd%d" % i, [128, 2, 512], BF16) for i in range(2)]
    Qg = [P.sb("Qg%d" % i, [128, 3, 512], BF16) for i in range(2)]
    pT = [P.sb("pT%d" % i, [128, 2, 512], BF16) for i in range(3)]
    Oalls = [P.sb("Oall%d" % i, [128, 14, 512], F32) for i in range(2)]
    ps_s = [P.ps("ps_s%d" % i, [128, 2, 512]) for i in range(2)]
    ps_o = [P.ps("ps_o%d" % i, [128, 512]) for i in range(2)]
    ps_x = [P.ps("ps_x%d" % i, [128, 512]) for i in range(2)]
    w1 = [P.sb("w1_%d" % i, [64, 512], F32) for i in range(2)]
    w2 = [P.sb("w2_%d" % i, [64, 512], F32) for i in range(2)]
    wsq = [P.sb("wsq%d" % i, [64, 512], BF16) for i in range(2)]
    wr = [P.sb("wr%d" % i, [64, 512], F32) for i in range(2)]
    obf = [P.sb("obf%d" % i, [64, 512], BF16) for i in range(4)]
    mixc = [P.sb("mixc%d" % i, [128, 512], BF16) for i in range(2)]
    rcp = [P.sb("rcp%d" % i, [64, 512], F32) for i in range(2)]
    cnt = {"s": 0, "o": 0, "x": 0, "w": 0, "obf": 0, "mix": 0, "rc": 0}

    def load_q(b):
        t0, nt = BLOCKS[b]
        qd, qdb = Qd[b % 2]
        qg, qgb = Qg[b % 2]
        S.dma(qd[:, :, :nt], k.QdT[:, :, t0:t0 + nt].rearrange("c p t -> p c t"), w=[qdb], sem=qdb)
        S.dma(qg[:, :, :nt], k.QgT[:, :, t0:t0 + nt].rearrange("c p t -> p c t"), w=[qgb], sem=qgb)

    post_gen = [None]

    def tick_post():
        g_ = post_gen[0]
        if g_ is not None:
            try:
                next(g_)
            except StopIteration:
                post_gen[0] = None

    def drain_post():
        while post_gen[0] is not None:
            tick_post()

    def post(t0, nt, Oall, b_Oall):
            def bcast(u):
                px, pxb = ps_x[cnt["x"] % 2]
                cnt["x"] += 1
                S.op("pe", lambda e: e.matmul(px[0:64, :nt], lhsT=ones_r[64:65, 0:64], rhs=Oall[64:65, u, :nt], start=True, stop=True),
                     r=[b_Oall, b_ones_r], w=[pxb])
                rc, rcb = rcp[cnt["rc"] % 2]
                cnt["rc"] += 1
                S.op("dve", lambda e: e.reciprocal(rc[:, :nt], px[0:64, :nt]), r=[pxb], w=[rcb])
                return rc, rcb

            def place(chunk, parts):
                px, pxb = ps_x[cnt["x"] % 2]
                cnt["x"] += 1
                for i, (ot, otb, hi) in enumerate(parts):
                    S.op("pe", lambda e, ot=ot, hi=hi, i=i: e.matmul(px[:, :nt], lhsT=k.place[0:64, hi, :], rhs=ot[:, :nt],
                                                                     start=(i == 0), stop=(i == len(parts) - 1)),
                         r=[otb, k.b_place], w=[pxb])
                mt, mtb = mixc[cnt["mix"] % 2]
                cnt["mix"] += 1
                S.op("act", lambda e: e.activation(out=mt[:, :nt], in_=px[:, :nt], func=AF.Copy), r=[pxb], w=[mtb])
                S.dma(k.mixT[chunk, :, t0:t0 + nt], mt[:, :nt], r=[mtb], sem=mtb)

            parts = []
            for hh in range(4):
                i = cnt["w"] % 2
                cnt["w"] += 1
                a1, a1b = w1[i]
                a2, a2b = w2[i]
                sqt, sqb = wsq[i]
                rt_, rtb = wr[i]
                px, pxb = bcast(2 * hh)
                S.op("dve", lambda e: e.tensor_tensor(a1[:, :nt], Oall[0:64, 2 * hh, :nt], px[:, :nt], ALU.mult),
                     r=[b_Oall, pxb], w=[a1b])
                yield
                px, pxb = bcast(2 * hh + 1)
                S.op("dve", lambda e: e.tensor_tensor(a2[:, :nt], Oall[0:64, 2 * hh + 1, :nt], px[:, :nt], ALU.mult),
                     r=[b_Oall, pxb], w=[a2b])
                S.op("dve", lambda e: e.scalar_tensor_tensor(out=a1[:, :nt], in0=a2[:, :nt], scalar=neglam_ap[0:64, :], in1=a1[:, :nt],
                                                             op0=ALU.mult, op1=ALU.add), r=[a1b, a2b, b_lams], w=[a1b])
                yield
                S.op("act", lambda e: e.activation(out=sqt[:, :nt], in_=a1[:, :nt], func=AF.Square), r=[a1b], w=[sqb])
                px, pxb = ps_x[cnt["x"] % 2]
                cnt["x"] += 1
                S.op("pe", lambda e: e.matmul(px[0:64, :nt], lhsT=k.blk64[0:64, 0:64], rhs=sqt[:, :nt], start=True, stop=True),
                     r=[sqb, k.b_blk64], w=[pxb])
                yield
                rsqrt_eps(k, rt_[:, :nt], rtb, px[0:64, :nt], pxb)
                yield
                ot, otb = obf[cnt["obf"] % 4]
                cnt["obf"] += 1
                S.op("dve", lambda e: e.scalar_tensor_tensor(out=ot[:, :nt], in0=a1[:, :nt], scalar=gd[0:64, :], in1=rt_[:, :nt],
                                                             op0=ALU.mult, op1=ALU.mult), r=[a1b, rtb, b_gd], w=[otb])
                parts.append((ot, otb, hh % 2))
                yield
                if hh % 2 == 1:
                    place(hh // 2, parts)
                    parts = []
                    yield
            for h in range(6):
                u = 8 + h
                px, pxb = bcast(u)
                ot, otb = obf[cnt["obf"] % 4]
                cnt["obf"] += 1
                S.op("dve", lambda e: e.tensor_tensor(ot[:, :nt], Oall[0:64, u, :nt], px[:, :nt], ALU.mult),
                     r=[b_Oall, pxb], w=[otb])
                parts.append((ot, otb, h % 2))
                yield
                if h % 2 == 1:
                    place(2 + h // 2, parts)
                    parts = []
                    yield

    blocks = list(range(0 if need_ctx else 1, len(BLOCKS)))
    load_q(blocks[0])
    for bi, b in enumerate(blocks):
        t0, nt = BLOCKS[b]
        if bi + 1 < len(blocks):
            load_q(blocks[bi + 1])
        qd, qdb = Qd[b % 2]
        qg, qgb = Qg[b % 2]
        Oall, b_Oall = Oalls[bi % 2]
        kts = [0, 1] if b == 0 else list(range(34))
        pairs = []
        for c in range(2):
            for cc in range(2):
                pairs.append(("d", c, cc, [(2 * c + hi) * 2 + cc for hi in range(2)], [(2 * c + hi) * 65 for hi in range(2)], 32 ** -0.5))
        for c in range(3):
            kva, kvb = (2 * c) // 3, (2 * c + 1) // 3
            pairs.append(("g", c, kva + kvb, [8 + 2 * c, 8 + 2 * c + 1], [260 + kva * 65, 260 + kvb * 65], 64 ** -0.5))
        steps = []
        for pi, pr in enumerate(pairs):
            for ki, kt in enumerate(kts):
                steps.append((pi, pr, ki, kt))
        LAG = 1
        pend = []
        po_of = {}

        def emit_pv(st):
            (pi, pr, ki, kt, ptile, ptb) = st
            (kind, c, var, us, v0s, sc) = pr
            if ki == 0:
                po_of[pi] = [ps_o[(cnt["o"] + i) % 2] for i in range(2)]
                cnt["o"] += 2
            for hi in range(2):
                po, pob = po_of[pi][hi]
                S.op("pe", lambda e, hi=hi, po=po: e.matmul(po[0:65, :nt], lhsT=V[:, kt, v0s[hi]:v0s[hi] + 65], rhs=ptile[:, hi, :nt],
                                                        start=(ki == 0), stop=(ki == len(kts) - 1)), r=[b_V, ptb], w=[pob])
                if ki == len(kts) - 1:
                    S.op("dve", lambda e, hi=hi, po=po: e.tensor_copy(Oall[0:65, us[hi], :nt], po[0:65, :nt]), r=[pob], w=[b_Oall])

        for si_, (pi, pr, ki, kt) in enumerate(steps):
            if si_ % 5 == 4:
                tick_post()
            (kind, c, var, us, v0s, sc) = pr
            pst, psb = ps_s[cnt["s"] % 2]
            ptile, ptb = pT[cnt["s"] % 3]
            cnt["s"] += 1
            qt, qb_ = (qd, qdb) if kind == "d" else (qg, qgb)
            Kt, Kb = (Kd, b_Kd) if kind == "d" else (Kg, b_Kg)

            def mmQK(e, pst=pst, Kt=Kt, qt=qt, kind=kind, c=c, var=var, kt=kt):
                last = None
                for hi in range(2):
                    rows = slice(hi * 64, hi * 64 + 64)
                    if kind == "d":
                        lhsT = Kt[rows, c, var, kt * 128:(kt + 1) * 128]
                    else:
                        lhsT = Kt[rows, var, kt * 128:(kt + 1) * 128]
                    last = e.matmul(pst[:, hi, :nt], lhsT=lhsT, rhs=qt[rows, c, :nt], start=True, stop=True)
                return last
            S.group("pe", mmQK, r=[Kb, qb_], w=[psb])
            S.op("act", lambda e, pst=pst, ptile=ptile, sc=sc: e.activation(out=ptile[:, :, :nt], in_=pst[:, :, :nt], func=AF.Exp, scale=sc),
                 r=[psb], w=[ptb])
            pend.append((pi, pr, ki, kt, ptile, ptb))
            if len(pend) > LAG:
                emit_pv(pend.pop(0))
        while pend:
            emit_pv(pend.pop(0))
        drain_post()
        post_gen[0] = post(t0, nt, Oall, b_Oall)
    drain_post()
    S.barrier()
    P.close()


def ph_gdn_a(k, l):
    S, nc = k.S, k.nc
    P = Phase(k)
    W = T + 8
    segs = [(2, 0, 256), (262, 256, 4096)]
    xin = [P.sb("xin%d" % i, [128, W], F32) for i in range(2)]
    acc = [P.sb("acc%d" % i, [128, T], F32) for i in range(2)]
    sl, b_sl = P.sb("sl", [128, T], F32)
    sqb, b_sqb = P.sb("sqb", [128, T], BF16)
    rn, b_rn = P.sb("rn", [128, T], F32)
    ob = [P.sb("gob%d" % i, [128, T], BF16) for i in range(2)]
    cw, b_cw = P.sb("cw", [128, 9, 5], F32)
    tk = [P.sb("tk%d" % i, [128, 4, 128], BF16) for i in range(2)]
    ps_n = [P.ps("ps_n%d" % i, [128, 512]) for i in range(2)]
    ps_t = [P.ps("ps_t%d" % i, [128, 4, 128], BF16) for i in range(2)]
    S.dma(cw[:], k.conv_w[l], w=[b_cw], sem=b_cw)
    for i in range(2):
        S.op("pool", lambda e, i=i: e.memset(xin[i][0][:], 0.0), w=[xin[i][1]])
    cnt = {"n": 0, "t": 0}

    def conv(c):
        xt, xb = xin[c % 2]
        at, ab = acc[c % 2]
        for (p0, t0, n) in segs:
            S.dma(xt[:, p0:p0 + n], k.gqkvT[c, :, t0:t0 + n], w=[xb], sem=xb)
        for (p0, t0, n) in segs:
            for tap in range(5):
                src = xt[:, p0 + tap - 2:p0 + tap - 2 + n]
                if tap == 0:
                    S.op("dve", lambda e, src=src, t0=t0, n=n: e.tensor_scalar(at[:, t0:t0 + n], src, cw[:, c, 0:1], None, ALU.mult),
                         r=[xb, b_cw], w=[ab])
                else:
                    S.op("dve", lambda e, src=src, t0=t0, n=n, tap=tap: e.scalar_tensor_tensor(
                        out=at[:, t0:t0 + n], in0=src, scalar=cw[:, c, tap:tap + 1], in1=at[:, t0:t0 + n],
                        op0=ALU.mult, op1=ALU.add), r=[xb, b_cw, ab], w=[ab])

    conv(0)
    for c in range(9):
        at, ab = acc[c % 2]
        if c + 1 < 9:
            conv(c + 1)
        o_t, o_b = ob[c % 2]
        if c >= 6:
            S.op("act", lambda e: e.activation(out=o_t[:], in_=at[:], func=AF.Silu), r=[ab], w=[o_b])
        else:
            S.op("act", lambda e: e.activation(out=sl[:], in_=at[:], func=AF.Silu), r=[ab], w=[b_sl])
            S.op("act", lambda e: e.activation(out=sqb[:], in_=sl[:], func=AF.Square), r=[b_sl], w=[b_sqb])
            for t0 in range(0, T, 512):
                n = min(512, T - t0)
                pn, pnb = ps_n[cnt["n"] % 2]
                cnt["n"] += 1
                S.op("pe", lambda e, t0=t0, n=n, pn=pn: e.matmul(pn[:, :n], lhsT=k.blk64s[:], rhs=sqb[:, t0:t0 + n], start=True, stop=True),
                     r=[b_sqb, k.b_blk64s], w=[pnb])
                S.op("act", lambda e, t0=t0, n=n, pn=pn: e.activation(out=rn[:, t0:t0 + n], in_=pn[:, :n], func=AF.Ln, bias=k.epsc[:, 0:1]),
                     r=[pnb, k.b_epsc], w=[b_rn])
            S.op("act", lambda e: e.activation(out=rn[:], in_=rn[:], func=AF.Exp, scale=-0.5), r=[b_rn], w=[b_rn])
            sc = 0.125 if c < 3 else 1.0
            S.op("dve", lambda e: e.scalar_tensor_tensor(out=o_t[:], in0=sl[:], scalar=sc, in1=rn[:], op0=ALU.mult, op1=ALU.mult),
                 r=[b_sl, b_rn], w=[o_b])
            dst = k.QnT[c] if c < 3 else k.KnT[c - 3]
            S.dma(dst, o_t[:], r=[o_b], sem=o_b)
        if c >= 3:
            dstT = k.Ktok if c < 6 else k.Vtok2
            cc = (c - 3) % 3
            for g0 in range(0, 34, 4):
                ng = min(4, 34 - g0)
                pt, ptb = ps_t[cnt["t"] % 2]
                tt, ttb = tk[cnt["t"] % 2]
                cnt["t"] += 1

                def tr(e, g0=g0, ng=ng, pt=pt):
                    last = None
                    for i in range(ng):
                        last = e.transpose(pt[:, i, :], o_t[:, (g0 + i) * 128:(g0 + i + 1) * 128], k.identb[:])
                    return last
                S.group("pe", tr, r=[o_b, k.b_identb], w=[ptb])
                S.op("act" if (g0 // 4) % 2 == 0 else "dve",
                     (lambda e, pt=pt, tt=tt, ng=ng: e.activation(out=tt[:, :ng, :], in_=pt[:, :ng, :], func=AF.Copy)) if (g0 // 4) % 2 == 0
                     else (lambda e, pt=pt, tt=tt, ng=ng: e.tensor_copy(tt[:, :ng, :], pt[:, :ng, :])), r=[ptb], w=[ttb])
                S.dma(dstT[g0 * 128:(g0 + ng) * 128, cc * 128:(cc + 1) * 128].rearrange("(n p) c -> p n c", p=128), tt[:, :ng, :],
                      r=[ttb], sem=ttb)
    S.barrier()
    P.close()


def ph_gdn_b(k, l, need_ctx):
    S, nc = k.S, k.nc
    P = Phase(k)
    NCH = 34
    MI, b_MI = P.sb("MI", [128, 2, 128], F32)
    SU, b_SU = P.sb("SU", [128, 2, 128], F32)
    EMK, b_EMK = P.sb("EMK", [128, 2, 256], F32)
    idf, b_idf = P.sb("idf", [128, 128], F32)
    on128, b_on128 = P.sb("on128", [128, 128], F32)
    gnb, b_gnb = P.sb("gnb", [128, 64], F32)
    nal, b_nal = P.sb("nal", [128, 12], F32)
    dtb, b_dtb = P.sb("dtb", [128, 12], F32)
    onec, b_onec = P.sb("onec", [128, 1], F32)
    S.dma(MI[:], k.c_MI, w=[b_MI], sem=b_MI)
    S.dma(SU[:], k.c_SU, w=[b_SU], sem=b_SU)
    S.dma(EMK[:], k.c_EMK, w=[b_EMK], sem=b_EMK)
    S.dma(idf[:], k.c_identf, w=[b_idf], sem=b_idf)
    mk_ = [P.sb("MK%d" % i, [128, 128], F32) for i in range(3)]
    MK = [x[0] for x in mk_]
    b_MK = Buf("MK")
    for i in range(3):
        S.dma(MK[i][:], k.c_MK[i], w=[b_MK], sem=b_MK)
    S.op("pool", lambda e: e.memset(on128[:], 1.0), w=[b_on128])
    S.op("pool", lambda e: e.memset(onec[:], 1.0), w=[b_onec])
    SUb, b_SUb = P.sb("SUb", [128, 2, 128], BF16)
    on128b, _ = P.sb("on128b", [128, 128], BF16)
    S.op("dve", lambda e: e.tensor_copy(SUb[:], SU[:]), r=[b_SU], w=[b_SUb])
    S.op("pool", lambda e: e.memset(on128b[:], 1.0), w=[b_SUb])
    S.dma(gnb[:], k.gdn_g[l:l + 1, :].partition_broadcast(128), w=[b_gnb], sem=b_gnb)
    S.dma(nal[:], k.gdn_alog[l:l + 1, :].partition_broadcast(128), w=[b_nal], sem=b_nal)
    S.dma(dtb[:], k.gdn_dtb[l:l + 1, :].partition_broadcast(128), w=[b_dtb], sem=b_dtb)
    S.op("act", lambda e: e.activation(out=nal[:], in_=nal[:], func=AF.Exp), r=[b_nal], w=[b_nal])
    S.op("dve", lambda e: e.tensor_scalar(nal[:], nal[:], -1.0, None, ALU.mult), r=[b_nal], w=[b_nal])
    ab, b_ab = P.sb("ab", [128, NCH, 24], F32)
    G, b_G = P.sb("G", [128, NCH, 12], F32)
    LB, b_LB = P.sb("LB", [128, NCH, 12], F32)
    BETA, b_BETA = P.sb("BETA", [128, NCH, 12], F32)
    KDS, b_KDS = P.sb("KDS", [128, NCH, 12], F32)
    GL, b_GL = P.sb("GL", [128, NCH, 12], F32)
    GLP, b_GLP = P.sb("GLP", [128, NCH, 2, 3], F32)
    S.dma(ab[:], k.zab[:, 384:408].rearrange("(n p) c -> p n c", p=128), w=[b_ab], sem=b_ab)
    S.op("dve", lambda e: e.tensor_tensor(G[:], ab[:, :, 0:12], dtb[:].unsqueeze(1).broadcast_to([128, NCH, 12]), ALU.add),
         r=[b_ab, b_dtb], w=[b_G])
    S.op("act", lambda e: e.activation(out=G[:], in_=G[:], func=AF.Exp), r=[b_G], w=[b_G])
    S.op("act", lambda e: e.activation(out=G[:], in_=G[:], func=AF.Ln, bias=onec[:, 0:1]), r=[b_G, b_onec], w=[b_G])
    S.op("dve", lambda e: e.tensor_tensor(G[:], G[:], nal[:].unsqueeze(1).broadcast_to([128, NCH, 12]), ALU.mult),
         r=[b_G, b_nal], w=[b_G])
    S.op("act", lambda e: e.activation(out=LB[:], in_=ab[:, :, 12:24], func=AF.Exp, scale=-1.0), r=[b_ab], w=[b_LB])
    S.op("act", lambda e: e.activation(out=LB[:], in_=LB[:], func=AF.Ln, bias=onec[:, 0:1]), r=[b_LB, b_onec], w=[b_LB])
    S.op("dve", lambda e: e.tensor_scalar(LB[:], LB[:], -1.0, None, ALU.mult), r=[b_LB], w=[b_LB])
    S.op("act", lambda e: e.activation(out=BETA[:], in_=LB[:], func=AF.Exp), r=[b_LB], w=[b_BETA])
    PG = []
    for g in range(2):
        t_, _ = P.ps("PG%d" % g, [128, 3, 512])
        PG.append((t_, [Buf("PG%d_%d" % (g, i), excl=True) for i in range(3)]))
    bA = [(PG[0][0][:, i, :], PG[0][1][i]) for i in range(3)]
    bM = [(PG[1][0][:, i, :], PG[1][1][i]) for i in range(2)]
    bX = (PG[1][0][:, 2, :], PG[1][1][2])
    PBK = [bA[0], bA[1], bA[2], bM[0], bM[1], bX]
    bS = [P.ps("bS%d" % i, [128, 512]) for i in range(2)]
    for half in range(2):
        pt, pb = bA[half]
        n0 = half * 17

        def mm(e, pt=pt, n0=n0):
            last = None
            for i in range(17):
                n = n0 + i
                for d in range(2):
                    last = e.matmul(pt[:, i * 24 + d * 6:i * 24 + d * 6 + 6], lhsT=SU[:, d, :], rhs=G[:, n, d * 6:d * 6 + 6],
                                    start=True, stop=True)
                last = e.matmul(pt[:, i * 24 + 12:i * 24 + 24], lhsT=on128[:], rhs=G[:, n, :], start=True, stop=True)
            return last
        S.group("pe", mm, r=[b_SU, b_G, b_on128], w=[pb])
        S.op("act", lambda e, pt=pt, n0=n0: e.activation(out=KDS[:, n0:n0 + 17, :],
                                                         in_=pt[:, 0:408].rearrange("p (n c) -> p n c", c=24)[:, :, 0:12], func=AF.Exp),
             r=[pb], w=[b_KDS])
        S.op("act", lambda e, pt=pt, n0=n0: e.activation(out=GL[:, n0:n0 + 17, :],
                                                         in_=pt[:, 0:408].rearrange("p (n c) -> p n c", c=24)[:, :, 12:24], func=AF.Exp),
             r=[pb], w=[b_GL])
    GLv = GL[:].rearrange("p n (d c h) -> p n d c h", d=2, c=3)
    S.op("dve", lambda e: e.tensor_copy(GLP[0:64], GLv[0:64, :, :, :, 0]), r=[b_GL], w=[b_GLP])
    S.op("dve", lambda e: e.tensor_copy(GLP[64:128], GLv[64:128, :, :, :, 1]), r=[b_GL], w=[b_GLP])

    GST = int(os.environ.get("GST", "99"))
    if GST <= 1:
        S.barrier(); P.close(); return
    Ofin, b_Ofin = P.sb("Ofin", [128, NCH, 384], F32)
    def mk(i):
        w = K()
        w.qk, w.b_qk = P.sb("cqk%d" % i, [128, 6, 128], BF16)
        w.kt, w.b_kt = P.sb("ckt%d" % i, [128, 384], BF16)
        w.vt, w.b_vt = P.sb("cvt%d" % i, [128, 384], BF16)
        w.rhsD, w.b_rhsD = P.sb("rhsD%d" % i, [128, 6, 2, 128], F32)
        w.rhsDh, w.b_rhsDh = P.sb("rhsDh%d" % i, [128, 6, 2, 128], BF16)
        w.rhsDl, w.b_rhsDl = P.sb("rhsDl%d" % i, [128, 6, 2, 128], BF16)
        w.Em, w.b_Em = P.sb("Em%d" % i, [128, 6, 256], F32)
        w.EB, w.b_EB = P.sb("EB%d" % i, [128, 6, 256], F32)
        w.qkTm, w.b_qkTm = P.sb("qkTm%d" % i, [128, 6, 128], BF16)
        def grp(nm, shape, dt):
            xs = [P.sb("%s%d_%d" % (nm, i, g), shape, dt) for g in range(2)]
            return [x[0] for x in xs], [x[1] for x in xs]
        w.Z, w.b_Zg = grp("Z", [128, 3, 3, 128], BF16)
        w.T32, w.b_T32g = grp("T32", [128, 3, 128], F32)
        w.O1T, w.b_O1Tg = grp("O1T", [128, 3, 128], BF16)
        w.O2T, w.b_O2Tg = grp("O2T", [128, 3, 128], BF16)
        w.NY, w.b_NYg = grp("NY", [128, 3, 128], BF16)
        w.QdT, w.b_QdT = P.sb("QdT%d" % i, [128, 3, 128], BF16)
        w.RwT, w.b_RwT = P.sb("RwT%d" % i, [128, 3, 128], BF16)
        w.KD, w.b_KD = P.sb("KD%d" % i, [128, 3, 2, 128], BF16)
        w.Ru, w.b_Ru = P.sb("Ru%d" % i, [128, 6, 64], F32)
        S.op("pool", lambda e: e.memset(w.KD[:], 0.0), w=[w.b_KD])
        return w
    WS = [mk(0), mk(1)]
    S32 = [P.sb("S32_%d" % i, [128, 64], F32) for i in range(6)]
    Sbf = [P.sb("Sbf%d" % i, [128, 128], BF16) for i in range(6)]
    Xb = [P.sb("Xb%d" % i, [128, 128], BF16) for i in range(3)]
    vnb = [P.sb("vnb%d" % i, [128, 128], BF16) for i in range(3)]
    for i in range(6):
        S.op("pool", lambda e, i=i: e.memset(S32[i][0][:], 0.0), w=[S32[i][1]])
        S.op("pool", lambda e, i=i: e.memset(Sbf[i][0][:], 0.0), w=[Sbf[i][1]])
    bSc = [bS[0], bS[1], bS[0]]

    def build_rhsD(w, n, d):
        for h in range(6):
            u = d * 6 + h
            S.op("dve", lambda e, h=h, u=u: e.tensor_scalar(w.rhsD[:, h, 0, :], MI[:, d, :], G[:, n, u:u + 1], None, ALU.mult),
                 r=[b_MI, b_G], w=[w.b_rhsD])
            S.op("dve", lambda e, h=h, u=u: e.scalar_tensor_tensor(out=w.rhsD[:, h, 1, :], in0=idf[:], scalar=LB[:, n, u:u + 1],
                                                                    in1=w.rhsD[:, h, 0, :], op0=ALU.mult, op1=ALU.add),
                 r=[b_idf, b_LB, w.b_rhsD], w=[w.b_rhsD])
            yield
        S.op("dve", lambda e: e.tensor_copy(w.rhsDh[:], w.rhsD[:]), r=[w.b_rhsD], w=[w.b_rhsDh])
        yield
        S.op("dve", lambda e: e.tensor_tensor(w.rhsDl[:], w.rhsD[:], w.rhsDh[:], ALU.subtract), r=[w.b_rhsD, w.b_rhsDh], w=[w.b_rhsDl])
        yield
        w.rhsD_ready = True

    def prep(w, n, d):
        t0 = n * 128
        S.dma(w.qk[:, 0:3, :], k.QnT[:, :, t0:t0 + 128].rearrange("c p t -> p c t"), w=[w.b_qk], sem=w.b_qk)
        S.dma(w.qk[:, 3:6, :], k.KnT[:, :, t0:t0 + 128].rearrange("c p t -> p c t"), w=[w.b_qk], sem=w.b_qk)
        S.dma(w.kt[:], k.Ktok[t0:t0 + 128, :], w=[w.b_kt], sem=w.b_kt)
        S.dma(w.vt[:], k.Vtok2[t0:t0 + 128, :], w=[w.b_vt], sem=w.b_vt)
        if not getattr(w, "rhsD_ready", False):
            for _ in build_rhsD(w, n, d):
                pass
        w.rhsD_ready = False
        for c in range(3):
            for (lhs, lhsb, (pt, pb), dst, dstb) in ((SUb[:, d, :], b_SUb, bA[c], w.Em, w.b_Em), (on128b[:], b_SUb, PBK[3 + c], w.EB, w.b_EB)):
                def mmD(e, c=c, pt=pt, lhs=lhs):
                    last = None
                    for hp in range(2):
                        e.matmul(pt[:, hp * 256:(hp + 1) * 256], lhsT=lhs,
                                 rhs=w.rhsDh[:, 2 * c + hp, :, :].rearrange("p a b -> p (a b)"), start=True, stop=False)
                        last = e.matmul(pt[:, hp * 256:(hp + 1) * 256], lhsT=lhs,
                                        rhs=w.rhsDl[:, 2 * c + hp, :, :].rearrange("p a b -> p (a b)"), start=False, stop=True)
                    return last
                S.group("pe", mmD, r=[lhsb, w.b_rhsDh, w.b_rhsDl], w=[pb])
                S.op("act", lambda e, c=c, pt=pt, dst=dst: e.activation(out=dst[:, 2 * c:2 * c + 2, :].rearrange("p a b -> p (a b)"), in_=pt[:], func=AF.Exp),
                     r=[pb], w=[dstb])
        S.op("dve", lambda e: e.tensor_tensor(w.Em[:], w.Em[:], EMK[:, d, :].unsqueeze(1).broadcast_to([128, 6, 256]), ALU.mult),
             r=[w.b_Em, b_EMK], w=[w.b_Em])
        tick()
        kslots = [([0, 2], bA[0]), ([1, 3], bA[1]), ([4], bA[2]), ([5], bM[0])]
        for heads, (pt, pb) in kslots:
            def mmK(e, heads=heads, pt=pt):
                last = None
                for i, h in enumerate(heads):
                    ps_ = slice((h % 2) * 64, (h % 2) * 64 + 64)
                    cq = h // 2
                    e.matmul(pt[:, i * 256:i * 256 + 128], lhsT=w.qk[ps_, 3 + cq, :], rhs=w.qk[ps_, cq, :], start=True, stop=True)
                    last = e.matmul(pt[:, i * 256 + 128:i * 256 + 256], lhsT=w.qk[ps_, 3 + cq, :], rhs=w.qk[ps_, 3 + cq, :],
                                    start=True, stop=True)
                return last
            S.group("pe", mmK, r=[w.b_qk], w=[pb])
            nh = len(heads)
            hs = slice(heads[0], heads[-1] + 1, 2)
            pv = pt[:, 0:nh * 256].rearrange("p (h x c) -> p h x c", h=nh, x=2)
            S.op("dve", lambda e, hs=hs, pv=pv: e.tensor_tensor(w.qkTm[:, hs, :], pv[:, :, 0, :], w.Em[:, hs, 0:128], ALU.mult),
                 r=[pb, w.b_Em], w=[w.b_qkTm])
            for i, h in enumerate(heads):
                S.op("dve", lambda e, i=i, h=h, pv=pv: e.tensor_tensor(w.T32[h // 3][:, h % 3, :], pv[:, i, 1, :], w.Em[:, h, 128:256], ALU.mult),
                     r=[pb, w.b_Em], w=[w.b_T32g[h // 3]])
        tick()
        m3 = lambda t: t[:].unsqueeze(1).broadcast_to([128, 3, 128])
        for g in range(2):
            T32, T32b = w.T32[g], w.b_T32g[g]
            S.op("dve", lambda e, g=g, T32=T32: e.tensor_tensor(w.Z[g][:, :, 1, :], T32[:], m3(MK[2]), ALU.mult), r=[T32b, b_MK], w=[w.b_Zg[g]])
            S.op("pool", lambda e, g=g: e.tensor_tensor(w.Z[g][:, :, 2, :], w.Z[g][:, :, 1, :], m3(idf), ALU.add), r=[w.b_Zg[g], b_idf], w=[w.b_Zg[g]])
            S.op("pool", lambda e, g=g, T32=T32: e.tensor_tensor(w.O1T[g][:], T32[:], m3(MK[0]), ALU.mult), r=[T32b, b_MK], w=[w.b_O1Tg[g]])
            S.op("pool", lambda e, g=g, T32=T32: e.tensor_tensor(w.O2T[g][:], T32[:], m3(MK[1]), ALU.mult), r=[T32b, b_MK], w=[w.b_O2Tg[g]])
        tick()

        def pe3(g, fn, r):
            pG, pGb = PG[g]

            def f(e):
                last = None
                for i in range(3):
                    last = fn(e, pG, i)
                return last
            S.group("pe", f, r=r, w=pGb)
        for g in range(2):
            Z = w.Z[g]
            pe3(g, lambda e, pG, i, Z=Z: e.matmul(pG[:, i, 0:128], lhsT=Z[:, i, 1, :], rhs=k.identb[:], start=True, stop=True), [w.b_Zg[g], k.b_identb])
        for g in range(2):
            pG, pGb = PG[g]
            S.op("act", lambda e, g=g, pG=pG: e.activation(out=w.Z[g][:, :, 0, :], in_=pG[:, :, 0:128], func=AF.Copy), r=pGb, w=[w.b_Zg[g]])
        for j in range(5):
            tick()
            for g in range(2):
                Z = w.Z[g]
                if j < 4:
                    pe3(g, lambda e, pG, i, Z=Z: e.matmul(pG[:, i, 0:128], lhsT=Z[:, i, 1, :], rhs=Z[:, i, 0, :], start=True, stop=True), [w.b_Zg[g]])
                if j == 0:
                    pe3(g, lambda e, pG, i, Z=Z: e.matmul(pG[:, i, 128:256], lhsT=Z[:, i, 0, :], rhs=Z[:, i, 1, :], start=True, stop=True), [w.b_Zg[g]])
                elif j < 4:
                    pe3(g, lambda e, pG, i, Z=Z: e.matmul(pG[:, i, 128:384], lhsT=Z[:, i, 0, :], rhs=Z[:, i, 1:3, :].rearrange("p a b -> p (a b)"),
                                                          start=True, stop=True), [w.b_Zg[g]])
                else:
                    pe3(g, lambda e, pG, i, Z=Z: e.matmul(pG[:, i, 256:384], lhsT=Z[:, i, 0, :], rhs=Z[:, i, 2, :], start=True, stop=True), [w.b_Zg[g]])
            for g in range(2):
                pG, pGb = PG[g]
                Z = w.Z[g]
                if j >= 1:
                    S.op("dve", lambda e, Z=Z, pG=pG: e.tensor_tensor(Z[:, :, 2, :], Z[:, :, 2, :], pG[:, :, 256:384], ALU.add), r=pGb + [w.b_Zg[g]], w=[w.b_Zg[g]])
                if j < 4:
                    S.op("act", lambda e, Z=Z, pG=pG: e.activation(out=Z[:, :, 0:2, :].rearrange("p h a b -> p h (a b)"), in_=pG[:, :, 0:256], func=AF.Copy),
                         r=pGb, w=[w.b_Zg[g]])
        tick()
        for g in range(2):
            Z = w.Z[g]
            pe3(g, lambda e, pG, i, Z=Z: e.matmul(pG[:, i, 128:256], lhsT=Z[:, i, 2, :], rhs=k.identb[:], start=True, stop=True), [w.b_Zg[g], k.b_identb])
        for g in range(2):
            pG, pGb = PG[g]
            S.op("act", lambda e, g=g, pG=pG: e.activation(out=w.Z[g][:, :, 1, :], in_=pG[:, :, 128:256], func=AF.Copy), r=pGb, w=[w.b_Zg[g]])
        for mi in range(2):
            tick()
            OT = w.O1T if mi == 0 else w.O2T
            OTb = w.b_O1Tg if mi == 0 else w.b_O2Tg
            for g in range(2):
                Z = w.Z[g]
                pe3(g, lambda e, pG, i, Z=Z, OTg=OT[g]: e.matmul(pG[:, i, 0:128], lhsT=OTg[:, i, :], rhs=Z[:, i, 1, :], start=True, stop=True), [OTb[g], w.b_Zg[g]])
            for g in range(2):
                pG, pGb = PG[g]
                S.op("act", lambda e, g=g, pG=pG: e.activation(out=w.NY[g][:], in_=pG[:, :, 0:128], func=AF.Copy), r=pGb, w=[w.b_NYg[g]])
            for g in range(2):
                Z = w.Z[g]
                pe3(g, lambda e, pG, i, Z=Z, NYg=w.NY[g]: e.matmul(pG[:, i, 256:384], lhsT=NYg[:, i, :], rhs=Z[:, i, 2, :], start=True, stop=True), [w.b_NYg[g], w.b_Zg[g]])
                if mi == 0:
                    pe3(g, lambda e, pG, i, Z=Z, NYg=w.NY[g]: e.matmul(pG[:, i, 128:256], lhsT=Z[:, i, 2, :], rhs=NYg[:, i, :], start=True, stop=True), [w.b_NYg[g], w.b_Zg[g]])
            for g in range(2):
                pG, pGb = PG[g]
                Z = w.Z[g]
                if mi == 0:
                    S.op("dve", lambda e, Z=Z, pG=pG: e.tensor_tensor(Z[:, :, 1:3, :].rearrange("p h a b -> p h (a b)"), Z[:, :, 1:3, :].rearrange("p h a b -> p h (a b)"),
                                                                       pG[:, :, 128:384], ALU.add), r=pGb + [w.b_Zg[g]], w=[w.b_Zg[g]])
                else:
                    S.op("dve", lambda e, Z=Z, pG=pG: e.tensor_tensor(Z[:, :, 2, :], Z[:, :, 2, :], pG[:, :, 256:384], ALU.add), r=pGb + [w.b_Zg[g]], w=[w.b_Zg[g]])
        for half in range(2):
            ps_ = slice(half * 64, half * 64 + 64)
            S.op("pool", lambda e, ps_=ps_, half=half: e.tensor_tensor(w.QdT[ps_, :, :], w.qk[ps_, 0:3, :], w.EB[ps_, half:6:2, 0:128], ALU.mult),
                 r=[w.b_qk, w.b_EB], w=[w.b_QdT])
            S.op("pool", lambda e, ps_=ps_, half=half: e.tensor_tensor(w.RwT[ps_, :, :], w.qk[ps_, 3:6, :], w.EB[ps_, half:6:2, 128:256], ALU.mult),
                 r=[w.b_qk, w.b_EB], w=[w.b_RwT])
        KDv = w.KD[:].rearrange("p c a b -> p c (a b)").rearrange("p c (x y) -> p c x y", y=64)[:, :, 0:4:3, :]
        S.op("dve", lambda e: e.tensor_tensor(KDv, w.kt[:].rearrange("p (c a y) -> p c a y", c=3, a=2),
                                              KDS[:, n, d * 6:d * 6 + 6].rearrange("p (c a) -> p c a", a=2).unsqueeze(3).broadcast_to([128, 3, 2, 64]),
                                              ALU.mult), r=[w.b_kt, b_KDS], w=[w.b_KD])
        S.op("dve", lambda e: e.tensor_tensor(w.Ru[:], w.vt[:].rearrange("p (h y) -> p h y", y=64),
                                              BETA[:, n, d * 6:d * 6 + 6].unsqueeze(2).broadcast_to([128, 6, 64]), ALU.mult),
             r=[w.b_vt, b_BETA], w=[w.b_Ru])

    def scan(w, n, d, first_pass):
        prs = []
        for c in range(3):
            bank = bS[c % 2]
            prs.append((c, (bank[0][:, (c // 2) * 256:(c // 2) * 256 + 256], bank[1]), S32[d * 3 + c], Sbf[d * 3 + c], Xb[c], vnb[c]))
        for (c, (bk, bkb), (s32, s32b), (sbf, sbfb), (xb, xbb), (vb, vbb)) in prs:
            S.op("pe", lambda e, bk=bk, c=c, sbf=sbf: e.matmul(bk[:, 0:128], lhsT=w.RwT[:, c, :], rhs=sbf[:], start=True, stop=True),
                 r=[w.b_RwT, sbfb], w=[bkb])
            if c == 1:
                yield
        yield
        for (c, (bk, bkb), (s32, s32b), (sbf, sbfb), (xb, xbb), (vb, vbb)) in prs:
            S.op("dve", lambda e, bk=bk, c=c, xb=xb: e.tensor_tensor(xb[:], w.Ru[:, 2 * c:2 * c + 2, :].rearrange("p a b -> p (a b)"), bk[:, 0:128], ALU.subtract),
                 r=[w.b_Ru, bkb], w=[xbb])

            def mmV(e, bk=bk, c=c, xb=xb):
                last = None
                for hp in range(2):
                    last = e.matmul(bk[:, 128 + hp * 64:128 + hp * 64 + 64], lhsT=w.Z[(2 * c + hp) // 3][:, (2 * c + hp) % 3, 2, :], rhs=xb[:, hp * 64:hp * 64 + 64],
                                    start=True, stop=True)
                return last
            S.group("pe", mmV, r=[w.b_Zg[0], w.b_Zg[1], xbb], w=[bkb])
            if c == 1:
                yield
        yield
        for (c, (bk, bkb), (s32, s32b), (sbf, sbfb), (xb, xbb), (vb, vbb)) in prs:
            S.op("act", lambda e, bk=bk, vb=vb: e.activation(out=vb[:], in_=bk[:, 128:256], func=AF.Copy), r=[bkb], w=[vbb])

            def mmO(e, bk=bk, c=c, sbf=sbf, vb=vb):
                e.matmul(bk[:, 0:128], lhsT=w.QdT[:, c, :], rhs=sbf[:], start=True, stop=False)
                for hp in range(2):
                    e.matmul(bk[:, hp * 64:hp * 64 + 64], lhsT=w.qkTm[:, 2 * c + hp, :], rhs=vb[:, hp * 64:hp * 64 + 64],
                             start=False, stop=(hp == 1))
                e.matmul(bk[:, 128:192], lhsT=w.KD[:, c, 0, :], rhs=vb[:, 0:64], start=True, stop=False)
                last = e.matmul(bk[:, 128:192], lhsT=w.KD[:, c, 1, :], rhs=vb[:, 64:128], start=False, stop=True)
                return last
            S.group("pe", mmO, r=[w.b_QdT, sbfb, w.b_qkTm, vbb, w.b_KD], w=[bkb])
            if c == 1:
                yield
        yield
        for (c, (bk, bkb), (s32, s32b), (sbf, sbfb), (xb, xbb), (vb, vbb)) in prs:
            S.op("dve", lambda e, bk=bk, c=c, s32=s32: e.scalar_tensor_tensor(out=s32[:], in0=s32[:], scalar=GLP[:, n, d, c:c + 1], in1=bk[:, 128:192],
                                                                             op0=ALU.mult, op1=ALU.add), r=[s32b, b_GLP, bkb], w=[s32b])
            S.op("pool", lambda e, sbf=sbf, s32=s32: e.tensor_copy(sbf[0:64, 0:64], s32[0:64, :]), r=[s32b], w=[sbfb])
            S.op("pool", lambda e, sbf=sbf, s32=s32: e.tensor_copy(sbf[64:128, 64:128], s32[64:128, :]), r=[s32b], w=[sbfb])
            ov = Ofin[:, n, 2 * c * 64:(2 * c + 2) * 64]
            if first_pass:
                S.op("act", lambda e, ov=ov, bk=bk: e.activation(out=ov, in_=bk[:, 0:128], func=AF.Copy), r=[bkb], w=[b_Ofin])
            else:
                S.op("dve", lambda e, ov=ov, bk=bk: e.tensor_tensor(ov, ov, bk[:, 0:128], ALU.add), r=[bkb, b_Ofin], w=[b_Ofin])
            if c == 1:
                yield

    cur_scan = [None]
    cur_aux = [None]

    def tick():
        for slot in (cur_scan, cur_aux):
            g_ = slot[0]
            if g_ is not None:
                try:
                    next(g_)
                except StopIteration:
                    slot[0] = None

    def drain():
        while cur_scan[0] is not None or cur_aux[0] is not None:
            tick()

    orders = [list(range(NCH)), [1, 0] + list(range(NCH - 1, 1, -1))]
    step = 0
    if GST in (2, 3):
        w = WS[0]
        DN, DD = int(os.environ.get("DN", "0")), int(os.environ.get("DD", "0"))
        prep(w, DN, DD)
        if GST == 3:
            cur_scan[0] = scan(w, DN, DD, True)
            drain()
        S.barrier()
        def dump(name, ap, shape, dt=F32):
            o = nc.dram_tensor(name, list(shape), dt, kind="ExternalOutput").ap()
            S.dma(o, ap, sem=Buf(name))
        dump("d_G", G[:].rearrange("p n c -> p (n c)"), [128, NCH * 12])
        dump("d_LB", LB[:].rearrange("p n c -> p (n c)"), [128, NCH * 12])
        dump("d_KDS", KDS[:].rearrange("p n c -> p (n c)"), [128, NCH * 12])
        dump("d_GLP", GLP[:].rearrange("p n d c -> p (n d c)"), [128, NCH * 6])
        dump("d_Em", w.Em[:].rearrange("p a b -> p (a b)"), [128, 6 * 256])
        dump("d_EB", w.EB[:].rearrange("p a b -> p (a b)"), [128, 6 * 256])
        dump("d_qkTm", w.qkTm[:].rearrange("p a b -> p (a b)"), [128, 6 * 128], BF16)
        dump("d_PT32", w.T32[0][:], [128, 3, 128])
        dump("d_MP", w.Z[0][:, :, 2, :], [128, 3, 128], BF16)
        dump("d_QdT", w.QdT[:].rearrange("p a b -> p (a b)"), [128, 384], BF16)
        dump("d_RwT", w.RwT[:].rearrange("p a b -> p (a b)"), [128, 384], BF16)
        dump("d_KD", w.KD[:].rearrange("p a b c -> p (a b c)"), [128, 768], BF16)
        dump("d_Ru", w.Ru[:].rearrange("p a b -> p (a b)"), [128, 384])
        dump("d_Ofin", Ofin[:, DN, :], [128, 384])
        dump("d_S32", S32[DD * 3][0][:], [128, 64])
        S.barrier(); P.close(); return
    flat = [(n, d) for d in range(2) for n in orders[d]]
    prep(WS[0], flat[0][0], flat[0][1])
    for si_, (n, d) in enumerate(flat):
        w = WS[si_ % 2]
        cur_scan[0] = scan(w, n, d, d == 0)
        if si_ + 2 < len(flat):
            cur_aux[0] = build_rhsD(w, flat[si_ + 2][0], flat[si_ + 2][1])
        if si_ + 1 < len(flat):
            prep(WS[(si_ + 1) % 2], flat[si_ + 1][0], flat[si_ + 1][1])
        drain()
    if GST == 4:
        S.barrier(); P.close(); return

    zt = [P.sb("zt%d" % i, [128, 384], F32) for i in range(2)]
    sqo, b_sqo = P.sb("sqo", [128, 384], F32)
    ssum, b_ssum = P.sb("ssum", [128, 6], F32)
    yb = [P.sb("yb%d" % i, [128, 384], F32) for i in range(2)]
    yT = [P.sb("yT%d" % i, [128, 3, 512], BF16) for i in range(2)]
    tiles = list(range(0 if need_ctx else 2, NCH))
    groups = []
    if need_ctx:
        groups.append([0, 1])
    for g0 in range(2, NCH, 4):
        groups.append(list(range(g0, g0 + 4)))
    cntr = 0
    for gi, grp in enumerate(groups):
        yt_, ytb = yT[gi % 2]
        for ti, n in enumerate(grp):
            z_, zb = zt[cntr % 2]
            y_, ybb = yb[cntr % 2]
            cntr += 1
            S.dma(z_[:], k.zab[n * 128:(n + 1) * 128, 0:384], w=[zb], sem=zb)
            S.op("act", lambda e, z_=z_: e.activation(out=z_[:], in_=z_[:], func=AF.Silu), r=[zb], w=[zb])
            o_ = Ofin[:, n, :]
            S.op("dve", lambda e, o_=o_: e.tensor_tensor(sqo[:], o_, o_, ALU.mult), r=[b_Ofin], w=[b_sqo])
            S.op("dve", lambda e: e.reduce_sum(ssum[:], sqo[:].rearrange("p (h y) -> p h y", y=64), axis=AX.X), r=[b_sqo], w=[b_ssum])
            S.op("act", lambda e: e.activation(out=ssum[:], in_=ssum[:], func=AF.Ln, scale=1.0 / 64, bias=k.epsc[:, 0:1]), r=[b_ssum, k.b_epsc], w=[b_ssum])
            S.op("act", lambda e: e.activation(out=ssum[:], in_=ssum[:], func=AF.Exp, scale=-0.5), r=[b_ssum], w=[b_ssum])
            S.op("dve", lambda e, o_=o_: e.tensor_tensor(sqo[:].rearrange("p (h y) -> p h y", y=64), o_.rearrange("p (h y) -> p h y", y=64),
                                                          ssum[:].unsqueeze(2).broadcast_to([128, 6, 64]), ALU.mult), r=[b_Ofin, b_ssum], w=[b_sqo])
            S.op("pool", lambda e: e.tensor_tensor(sqo[:].rearrange("p (h y) -> p h y", y=64), sqo[:].rearrange("p (h y) -> p h y", y=64),
                                                   gnb[:].unsqueeze(1).broadcast_to([128, 6, 64]), ALU.mult), r=[b_sqo, b_gnb], w=[b_sqo])
            S.op("pool", lambda e, z_=z_, y_=y_: e.tensor_tensor(y_[:], sqo[:], z_[:], ALU.mult), r=[b_sqo, zb], w=[ybb])

            def trY(e, y_=y_):
                last = None
                for c in range(3):
                    last = e.transpose(bX[0][:, c * 128:(c + 1) * 128], y_[:, c * 128:(c + 1) * 128], idf[:])
                return last
            S.group("pe", trY, r=[ybb, b_idf], w=[bX[1]])
            S.op("act", lambda e, ti=ti, yt_=yt_: e.activation(out=yt_[:, :, ti * 128:(ti + 1) * 128],
                                                               in_=bX[0][:, 0:384].rearrange("p (c t) -> p c t", c=3), func=AF.Copy), r=[bX[1]], w=[ytb])
        t0 = grp[0] * 128
        nt = len(grp) * 128
        S.dma(k.mixT[5:8, :, t0:t0 + nt].rearrange("c p t -> p c t"), yt_[:, :, :nt], r=[ytb], sem=ytb)
    S.barrier()
    P.close()


def ph_out(k, l, hsrc, hdst, need_ctx):
    S, nc = k.S, k.nc
    P = Phase(k)
    wbf, b_w = P.sb("w_out_bf", [128, 8, 1024], BF16)
    hb = [P.sb("ohb%d" % i, [128, 8, 512], F32) for i in range(2)]
    ho = [P.sb("oho%d" % i, [128, 8, 512], F32) for i in range(2)]
    mx = [P.sb("omx%d" % i, [128, 8, 512], BF16) for i in range(2)]
    pp = [P.ps("opp%d" % i, [128, 512]) for i in range(4)]
    wv = k.w_out[l].rearrange("(k p) n -> p k n", p=128)
    for ci in range(2):
        st, sbuf_ = ho[ci]
        S.dma(st[:], wv[:, :, ci * 512:(ci + 1) * 512], w=[sbuf_], sem=sbuf_)
        S.op("pool", lambda e, st=st, ci=ci: e.tensor_copy(wbf[:, :, ci * 512:(ci + 1) * 512], st[:]), r=[sbuf_], w=[b_w])
    hv = hsrc.rearrange("(k p) t -> p k t", p=128)
    ov = hdst.rearrange("(k p) t -> p k t", p=128)
    blocks = list(range(0 if need_ctx else 1, len(BLOCKS)))

    def load(b):
        t0, nt = BLOCKS[b]
        S.dma(hb[b % 2][0][:, :, :nt], hv[:, :, t0:t0 + nt], w=[hb[b % 2][1]], sem=hb[b % 2][1])
        S.dma(mx[b % 2][0][:, :, :nt], k.mixT[:, :, t0:t0 + nt].rearrange("c p t -> p c t"), w=[mx[b % 2][1]], sem=mx[b % 2][1])
    load(blocks[0])
    cnt = 0
    for bi, b in enumerate(blocks):
        t0, nt = BLOCKS[b]
        j = 1 if b == 0 else 0
        if bi + 1 < len(blocks):
            load(blocks[bi + 1])
        ht, hbb = hb[b % 2]
        mt, mtb = mx[b % 2]
        ot, otb = ho[b % 2]
        for n in range(8):
            pt, pb = pp[cnt % 4]
            cnt += 1

            def mm(e, n=n, pt=pt):
                last = None
                for kk in range(8):
                    last = e.matmul(pt[:, :nt], lhsT=wbf[:, kk, n * 128:(n + 1) * 128], rhs=mt[:, kk, :nt], start=(kk == 0), stop=(kk == 7))
                return last
            S.group("pe", mm, r=[b_w, mtb], w=[pb])
            S.op("dve", lambda e, n=n, pt=pt: e.scalar_tensor_tensor(out=ot[:, n, :nt], in0=pt[:, :nt], scalar=k.mod[:, 16 + n, j:j + 1],
                                                                     in1=ht[:, n, :nt], op0=ALU.mult, op1=ALU.add),
                 r=[pb, k.b_mod, hbb], w=[otb])
        S.dma(ov[:, :, t0:t0 + nt], ot[:, :, :nt], r=[otb], sem=otb)
    S.barrier()
    P.close()


def ph_ffn(k, l, hsrc, hdst, need_ctx, final):
    S, nc = k.S, k.nc
    P = Phase(k)
    NH = 22
    wgu, b_wgu = P.sb("wgu", [128, 8, 2 * 2816], BF16)
    wdn, b_wdn = P.sb("wdn", [128, NH, 1024], BF16)
    hb = [P.sb("fhb%d" % i, [128, 8, 256], F32) for i in range(2)]
    tmp, b_tmp = P.sb("ftmp", [128, 8, 256], F32)
    stg = [hb[0], hb[1], (tmp, b_tmp)]
    si = 0
    wv = k.w_gu[l].rearrange("(k p) n -> p k n", p=128)
    for ci in range(22):
        st, sbuf_ = stg[si % 3]
        si += 1
        S.dma(st[:], wv[:, :, ci * 256:(ci + 1) * 256], w=[sbuf_], sem=sbuf_)
        S.op("pool" if ci % 2 == 0 else "act",
             (lambda e, st=st, ci=ci: e.tensor_copy(wgu[:, :, ci * 256:(ci + 1) * 256], st[:])) if ci % 2 == 0
             else (lambda e, st=st, ci=ci: e.activation(out=wgu[:, :, ci * 256:(ci + 1) * 256], in_=st[:], func=AF.Copy)),
             r=[sbuf_], w=[b_wgu])
    wv = k.w_down[l].rearrange("(k p) n -> p k n", p=128)
    for m0 in (0, 8, 16):
        nm = min(8, NH - m0)
        for n0 in range(0, 1024, 256):
            st, sbuf_ = stg[si % 3]
            si += 1
            S.dma(st[:, :nm, :], wv[:, m0:m0 + nm, n0:n0 + 256], w=[sbuf_], sem=sbuf_)
            S.op("pool" if si % 2 == 0 else "act",
                 (lambda e, st=st, m0=m0, nm=nm, n0=n0: e.tensor_copy(wdn[:, m0:m0 + nm, n0:n0 + 256], st[:, :nm, :])) if si % 2 == 0
                 else (lambda e, st=st, m0=m0, nm=nm, n0=n0: e.activation(out=wdn[:, m0:m0 + nm, n0:n0 + 256], in_=st[:, :nm, :], func=AF.Copy)),
                 r=[sbuf_], w=[b_wdn])
    sq, b_sq = P.sb("fsq", [128, 8, 256], BF16)
    rstd, b_rstd = P.sb("frstd", [128, 256], F32)
    aT, b_aT = P.sb("faT", [128, 8, 256], BF16)
    hid, b_hid = P.sb("fhid", [128, NH, 256], BF16)
    sil = [P.sb("fsil%d" % i, [128, 256], F32) for i in range(2)]
    h2, b_h2 = P.sb("fh2", [128, 8, 256], F32)
    fg, b_fg = P.sb("ffg", [128, 8], F32)
    S.dma(fg[:], k.final_g, w=[b_fg], sem=b_fg)
    p_ss, b_pss = P.ps("fp_ss", [128, 512])
    p_gu = [P.ps("fp_gu%d" % i, [128, 512]) for i in range(4)]
    p_dn = [P.ps("fp_dn%d" % i, [128, 512]) for i in range(3)]
    hv = hsrc.rearrange("(k p) t -> p k t", p=128)
    ov = hdst.rearrange("(k p) t -> p k t", p=128) if hdst is not None else None
    NT = 256
    blocks = list(range(0 if need_ctx else 1, T // NT))

    def load(b):
        S.dma(hb[b % 2][0][:], hv[:, :, b * NT:(b + 1) * NT], w=[hb[b % 2][1]], sem=hb[b % 2][1])
    load(blocks[0])
    cg = 0
    cd = 0
    for bi, b in enumerate(blocks):
        t0 = b * NT
        j = 1 if b == 0 else 0
        if bi + 1 < len(blocks):
            load(blocks[bi + 1])
        ht, hbb = hb[b % 2]

        def norm(src, srcb, dst, dstb, gcol, bcol):
            S.op("act", lambda e: e.activation(out=sq[:], in_=src[:], func=AF.Square), r=[srcb], w=[b_sq])

            def mm_ss(e):
                last = None
                for kk in range(8):
                    last = e.matmul(p_ss[:, :NT], lhsT=k.ones_d[:], rhs=sq[:, kk, :], start=(kk == 0), stop=(kk == 7))
                return last
            S.group("pe", mm_ss, r=[b_sq, k.b_ones_d], w=[b_pss])
            rsqrt_eps(k, rstd[:], b_rstd, p_ss[:, :NT], b_pss)
            S.op("dve", lambda e: e.tensor_tensor(tmp[:], src[:], rstd[:].unsqueeze(1).broadcast_to([128, 8, NT]), ALU.mult),
                 r=[srcb, b_rstd], w=[b_tmp])
            for kk in range(8):
                if bcol is not None:
                    S.op("act", lambda e, kk=kk: e.activation(out=dst[:, kk, :], in_=tmp[:, kk, :], func=AF.Identity,
                                                               scale=gcol(kk), bias=bcol(kk)), r=[b_tmp, k.b_G2, k.b_mod, b_fg], w=[dstb])
                else:
                    S.op("act", lambda e, kk=kk: e.activation(out=dst[:, kk, :], in_=tmp[:, kk, :], func=AF.Copy, scale=gcol(kk)),
                         r=[b_tmp, b_fg], w=[dstb])
        if bi == 0 or final:
            norm(ht, hbb, aT, b_aT, lambda kk: k.G2[:, kk, j:j + 1], lambda kk: k.mod[:, 24 + kk, j:j + 1])
        for m in range(NH):
            pt, pb = p_gu[cg % 4]
            st_, sb_ = sil[cg % 2]
            cg += 1

            def mm(e, m=m, pt=pt):
                last = None
                for half in range(2):
                    c0 = half * 2816 + m * 128
                    for kk in range(8):
                        last = e.matmul(pt[:, half * 256:(half + 1) * 256], lhsT=wgu[:, kk, c0:c0 + 128], rhs=aT[:, kk, :],
                                        start=(kk == 0), stop=(kk == 7))
                return last
            S.group("pe", mm, r=[b_wgu, b_aT], w=[pb])
            S.op("act", lambda e, pt=pt, st_=st_: e.activation(out=st_[:], in_=pt[:, 0:256], func=AF.Silu), r=[pb], w=[sb_])
            S.op("dve", lambda e, pt=pt, st_=st_, m=m: e.tensor_tensor(hid[:, m, :], st_[:], pt[:, 256:512], ALU.mult), r=[pb, sb_], w=[b_hid])
        if bi + 1 < len(blocks) and not final:
            nb_ = blocks[bi + 1]
            jn = 1 if nb_ == 0 else 0
            norm(hb[nb_ % 2][0], hb[nb_ % 2][1], aT, b_aT, lambda kk: k.G2[:, kk, jn:jn + 1], lambda kk: k.mod[:, 24 + kk, jn:jn + 1])
        for n in range(8):
            pt, pb = p_dn[cd % 3]
            cd += 1

            def mm(e, n=n, pt=pt):
                last = None
                for m in range(NH):
                    last = e.matmul(pt[:, :NT], lhsT=wdn[:, m, n * 128:(n + 1) * 128], rhs=hid[:, m, :], start=(m == 0), stop=(m == NH - 1))
                return last
            S.group("pe", mm, r=[b_wdn, b_hid], w=[pb])
            S.op("dve", lambda e, n=n, pt=pt: e.scalar_tensor_tensor(out=h2[:, n, :], in0=pt[:, :NT], scalar=k.mod[:, 40 + n, j:j + 1],
                                                                     in1=ht[:, n, :], op0=ALU.mult, op1=ALU.add),
                 r=[pb, k.b_mod, hbb], w=[b_h2])
        if not final:
            S.dma(ov[:, :, t0:t0 + NT], h2[:], r=[b_h2], sem=b_h2)
        else:
            norm(h2, b_h2, h2, b_h2, lambda kk: fg[:, kk:kk + 1], None)
            S.dma(k.outT.rearrange("(k p) t -> p k t", p=128)[:, :, t0 - NCTX:t0 - NCTX + NT], h2[:], r=[b_h2], sem=b_h2)
    S.barrier()
    P.close()


_CONSTS = None


def kernel(**inputs):
    global _CONSTS
    inp = {k_: np.asarray(v) for k_, v in inputs.items()}
    nc = build(depth=2)
    if _CONSTS is None:
        _CONSTS = host_consts()
    in_maps = []
    for b in range(8):
        d = host_inputs(inp, b)
        d.update(_CONSTS)
        in_maps.append(d)
    res = run_bass_kernel_spmd(nc, in_maps, core_ids=list(range(8)))
    out = np.stack([np.ascontiguousarray(np.asarray(res.results[b]["outT"]).T) for b in range(8)])
    return out.astype(np.float32)
```
